# Optimizing a Trainium2 kernel written in Bass

```python
import jax, jax.numpy as jnp
from jax import lax
import numpy as np

D_MODEL = 1024
BATCH = 8
SEQ = 2048
DEPTH = 2
DEC_BATCH = 128
DEC_SEQ = 8
PAST_LEN = 16384
PAGE_SIZE = 128

N_MIXERS = 2
N_A_LAYERS = (DEPTH + N_MIXERS - 1) // N_MIXERS
N_B_LAYERS = DEPTH // N_MIXERS

N_MEM = 256
X_WIDTH = D_MODEL // 4
X_HEADS = 4
X_HEAD_DIM = X_WIDTH // X_HEADS

GDN_WIDTH = 3 * D_MODEL // 4
GDN_DK = 128
GDN_DV = 128
GDN_HEADS = GDN_WIDTH // GDN_DV
QKV_WIDTH = GDN_HEADS * (2 * GDN_DK + GDN_DV)
GDN_CONV = 4
GDN_CHUNK = 64

SC_WIDTH = 3 * D_MODEL // 4
SC_CONV = 3

BRANCH_WIDTH = GDN_WIDTH + X_WIDTH
COLS_A = QKV_WIDTH + 2 * GDN_HEADS + BRANCH_WIDTH + X_WIDTH
COLS_B = 3 * SC_WIDTH + BRANCH_WIDTH + X_WIDTH
EPS = 1e-6

kernel_name = 'hybrid_gdn_shortconv_memxattn_step'


def _split(t, sizes):
    return jnp.split(t, [int(s) for s in np.cumsum(sizes)[:-1]], axis=-1)


def rmsnorm(x, g):
    xf = x.astype(jnp.float32)
    r = lax.rsqrt(jnp.mean(xf * xf, axis=-1, keepdims=True) + EPS)
    return (xf * r * g.astype(jnp.float32)).astype(x.dtype)


def l2norm(x):
    return x * lax.rsqrt(jnp.sum(x * x, axis=-1, keepdims=True) + EPS)


def causal_dwconv(x, buf, w):
    width = w.shape[0]
    L = x.shape[1]
    xp = jnp.concatenate([buf.astype(x.dtype), x], axis=1)
    y = sum(xp[:, j:j + L] * w[j].astype(x.dtype) for j in range(width))
    return y, xp[:, xp.shape[1] - (width - 1):]


def gated_delta_rule(q, k, v, g, beta, S0):
    Bn, L, H, DK = q.shape
    C = min(GDN_CHUNK, L)
    n = -(-L // C)
    pad = n * C - L

    def blocks(t):
        t = jnp.pad(t, [(0, 0), (0, pad)] + [(0, 0)] * (t.ndim - 2))
        t = t.reshape((Bn, n, C) + t.shape[2:])
        return jnp.moveaxis(t, 3, 2)

    q, k, v, g, beta = (blocks(t) for t in (q, k, v, g, beta))
    G = jnp.cumsum(g, axis=-1)
    tril = jnp.tril(jnp.ones((C, C), bool))
    decay = jnp.exp(jnp.where(tril, G[..., :, None] - G[..., None, :], -jnp.inf))
    kb = k * beta[..., None]
    A = jnp.tril(jnp.einsum('bnhid,bnhjd->bnhij', kb, k) * decay, -1)
    rhs = jnp.concatenate([kb * jnp.exp(G)[..., None], v * beta[..., None]], axis=-1)
    wu = lax.linalg.triangular_solve(A + jnp.eye(C, dtype=A.dtype), rhs, left_side=True,
                                     lower=True, unit_diagonal=True)
    w, u = wu[..., :DK], wu[..., DK:]
    Aqk = jnp.einsum('bnhid,bnhjd->bnhij', q, k) * decay
    qd = q * jnp.exp(G)[..., None]
    Gl = G[..., -1]
    kd = k * jnp.exp(Gl[..., None] - G)[..., None]
    xs = tuple(jnp.moveaxis(t, 1, 0) for t in (w, u, qd, kd, Aqk, Gl))

    def step(S, inp):
        w_c, u_c, qd_c, kd_c, aqk_c, gl_c = inp
        v_new = u_c - jnp.einsum('bhcd,bhde->bhce', w_c, S)
        o_c = (jnp.einsum('bhcd,bhde->bhce', qd_c, S)
               + jnp.einsum('bhij,bhje->bhie', aqk_c, v_new))
        S = S * jnp.exp(gl_c)[..., None, None] + jnp.einsum('bhcd,bhce->bhde', kd_c, v_new)
        return S, o_c

    S, o = lax.scan(step, S0, xs)
    o = jnp.transpose(o, (1, 0, 3, 2, 4)).reshape(Bn, n * C, H, -1)[:, :L]
    return o, S


def mem_attend(xq, mk, mv):
    s = jnp.einsum('blhd,bmhd->bhlm', xq, mk.astype(xq.dtype)).astype(jnp.float32)
    p = jax.nn.softmax(s * (X_HEAD_DIM ** -0.5), axis=-1).astype(xq.dtype)
    return jnp.einsum('bhlm,bmhd->blhd', p, mv.astype(xq.dtype))


def gdn_branch(h, w_in, conv_w, a_log, dt_bias, o_norm_g, S0, conv0):
    Bn, L, _ = h.shape
    proj = h @ w_in.astype(h.dtype)
    qkv, b, a, gate, xq = _split(proj, [QKV_WIDTH, GDN_HEADS, GDN_HEADS, BRANCH_WIDTH, X_WIDTH])
    qkv, conv_new = causal_dwconv(qkv, conv0, conv_w)
    qkv = jax.nn.silu(qkv).astype(jnp.float32)
    q, k, v = _split(qkv, [GDN_HEADS * GDN_DK, GDN_HEADS * GDN_DK, GDN_HEADS * GDN_DV])
    q = l2norm(q.reshape(Bn, L, GDN_HEADS, GDN_DK)) * (GDN_DK ** -0.5)
    k = l2norm(k.reshape(Bn, L, GDN_HEADS, GDN_DK))
    v = v.reshape(Bn, L, GDN_HEADS, GDN_DV)
    beta = jax.nn.sigmoid(b.astype(jnp.float32))
    g = -jnp.exp(a_log.astype(jnp.float32)) * jax.nn.softplus(
        a.astype(jnp.float32) + dt_bias.astype(jnp.float32))
    o, S = gated_delta_rule(q, k, v, g, beta, S0.astype(jnp.float32))
    o = rmsnorm(o, o_norm_g).reshape(Bn, L, GDN_WIDTH).astype(h.dtype)
    return o, gate, xq, S, conv_new


def sconv_branch(h, w_in, conv_w, conv0):
    proj = h @ w_in.astype(h.dtype)
    bg, cg, xs, gate, xq = _split(proj, [SC_WIDTH, SC_WIDTH, SC_WIDTH, BRANCH_WIDTH, X_WIDTH])
    y, conv_new = causal_dwconv(cg * xs, conv0, conv_w)
    return bg * y, gate, xq, conv_new


def trunk(x, mem_k, mem_v, gdn_S, gdn_conv, sc_conv, norm_g, w_in_a, conv_w_a, a_log,
          dt_bias, o_norm_g, w_in_b, conv_w_b, w_out, final_norm_g):
    Bn, L, _ = x.shape
    S_new, gconv_new, sconv_new = [], [], []
    for i in range(DEPTH):
        h = rmsnorm(x, norm_g[i])
        j = i // N_MIXERS
        if i % N_MIXERS == 0:
            tok, gate, xq, S, cbuf = gdn_branch(h, w_in_a[j], conv_w_a[j], a_log[j], dt_bias[j],
                                               o_norm_g[j], gdn_S[j], gdn_conv[j])
            S_new.append(S)
            gconv_new.append(cbuf)
        else:
            tok, gate, xq, cbuf = sconv_branch(h, w_in_b[j], conv_w_b[j], sc_conv[j])
            sconv_new.append(cbuf)
        xo = mem_attend(xq.reshape(Bn, L, X_HEADS, X_HEAD_DIM), mem_k[i], mem_v[i])
        br = jnp.concatenate([tok, xo.reshape(Bn, L, X_WIDTH)], axis=-1) * jax.nn.silu(gate)
        x = x + br @ w_out[i].astype(x.dtype)
    y = rmsnorm(x, final_norm_g)
    return y, jnp.stack(S_new), jnp.stack(gconv_new), jnp.stack(sconv_new)


def setup_inputs(seed: int = 0) -> dict:
    key = jax.random.key(seed)
    ks = jax.random.split(key, 24)

    def nrm(k, shape, scale):
        return jax.random.normal(k, shape, jnp.float32) * scale

    dt = jnp.exp(jax.random.uniform(ks[11], (N_A_LAYERS, GDN_HEADS), jnp.float32,
                                    np.log(1e-3), np.log(1e-1)))
    return {
        'x_prompt': nrm(ks[0], (BATCH, SEQ, D_MODEL), 1.0),
        'x_sample': nrm(ks[1], (DEC_BATCH, DEC_SEQ, D_MODEL), 1.0),
        'mem_prompt': nrm(ks[2], (BATCH, N_MEM, D_MODEL), 1.0),
        'state_gdn': nrm(ks[3], (N_A_LAYERS, DEC_BATCH, GDN_HEADS, GDN_DK, GDN_DV), 0.05),
        'state_gdn_conv': nrm(ks[4], (N_A_LAYERS, DEC_BATCH, GDN_CONV - 1, QKV_WIDTH), 1.0),
        'state_sconv': nrm(ks[5], (N_B_LAYERS, DEC_BATCH, SC_CONV - 1, SC_WIDTH), 1.0),
        'cache_mem_k': nrm(ks[6], (DEPTH, DEC_BATCH, N_MEM, X_HEADS, X_HEAD_DIM), 1.0),
        'cache_mem_v': nrm(ks[7], (DEPTH, DEC_BATCH, N_MEM, X_HEADS, X_HEAD_DIM), 1.0),
        'norm_g': 1.0 + nrm(ks[8], (DEPTH, D_MODEL), 0.02),
        'w_in_a': nrm(ks[9], (N_A_LAYERS, D_MODEL, COLS_A), D_MODEL ** -0.5),
        'conv_w_a': nrm(ks[10], (N_A_LAYERS, GDN_CONV, QKV_WIDTH), GDN_CONV ** -0.5),
        'a_log': jnp.log(jax.random.uniform(ks[12], (N_A_LAYERS, GDN_HEADS), jnp.float32, 1.0, 16.0)),
        'dt_bias': dt + jnp.log(-jnp.expm1(-dt)),
        'o_norm_g': 1.0 + nrm(ks[13], (N_A_LAYERS, GDN_DV), 0.02),
        'w_in_b': nrm(ks[14], (N_B_LAYERS, D_MODEL, COLS_B), D_MODEL ** -0.5),
        'conv_w_b': nrm(ks[15], (N_B_LAYERS, SC_CONV, SC_WIDTH), SC_CONV ** -0.5),
        'mem_norm_g': 1.0 + nrm(ks[16], (D_MODEL,), 0.02),
        'w_mem_kv': nrm(ks[17], (DEPTH, D_MODEL, 2 * X_WIDTH), D_MODEL ** -0.5),
        'w_out': nrm(ks[18], (DEPTH, BRANCH_WIDTH, D_MODEL), BRANCH_WIDTH ** -0.5),
        'final_norm_g': 1.0 + nrm(ks[19], (D_MODEL,), 0.02),
    }


def reference(x_prompt, x_sample, mem_prompt, state_gdn, state_gdn_conv, state_sconv,
              cache_mem_k, cache_mem_v, norm_g, w_in_a, conv_w_a, a_log, dt_bias, o_norm_g,
              w_in_b, conv_w_b, mem_norm_g, w_mem_kv, w_out, final_norm_g):
    Bp = x_prompt.shape[0]
    n_mem = mem_prompt.shape[1]
    mem_n = rmsnorm(mem_prompt, mem_norm_g)
    mkv = jnp.einsum('bmd,lde->lbme', mem_n, w_mem_kv.astype(mem_n.dtype))
    mem_k_p = mkv[..., :X_WIDTH].reshape(DEPTH, Bp, n_mem, X_HEADS, X_HEAD_DIM)
    mem_v_p = mkv[..., X_WIDTH:].reshape(DEPTH, Bp, n_mem, X_HEADS, X_HEAD_DIM)
    S0_p = jnp.zeros((N_A_LAYERS, Bp, GDN_HEADS, GDN_DK, GDN_DV), jnp.float32)
    gc0_p = jnp.zeros((N_A_LAYERS, Bp, GDN_CONV - 1, QKV_WIDTH), x_prompt.dtype)
    sc0_p = jnp.zeros((N_B_LAYERS, Bp, SC_CONV - 1, SC_WIDTH), x_prompt.dtype)
    y_prompt, S_p, gc_p, sc_p = trunk(x_prompt, mem_k_p, mem_v_p, S0_p, gc0_p, sc0_p,
                                      norm_g, w_in_a, conv_w_a, a_log, dt_bias, o_norm_g,
                                      w_in_b, conv_w_b, w_out, final_norm_g)
    y_sample, S_s, gc_s, sc_s = trunk(x_sample, cache_mem_k, cache_mem_v, state_gdn,
                                      state_gdn_conv, state_sconv,
                                      norm_g, w_in_a, conv_w_a, a_log, dt_bias, o_norm_g,
                                      w_in_b, conv_w_b, w_out, final_norm_g)
    return (y_prompt, y_sample, S_p, gc_p, sc_p, mem_k_p, mem_v_p, S_s, gc_s, sc_s)
```

```python
import numpy as np
from contextlib import ExitStack
import concourse.bass as bass
import concourse.mybir as mybir
from concourse.bass_utils import run_bass_kernel_spmd

F32 = mybir.dt.float32
BF16 = mybir.dt.bfloat16
AF = mybir.ActivationFunctionType
ALU = mybir.AluOpType
AX = mybir.AxisListType

NT = 17
EPS = 1e-6
CA = 3596
CB = 3584


class Buf:
    __slots__ = ("name", "last_w", "readers", "excl")

    def __init__(self, name, excl=False):
        self.name = name
        self.last_w = None
        self.readers = {}
        self.excl = excl


class Sched:
    def __init__(self, nc, es, n_dma_sems=(24, 4, 12)):
        self.nc = nc
        self.eng = {"pe": nc.tensor, "act": nc.scalar, "dve": nc.vector, "pool": nc.gpsimd, "sp": nc.sync}
        self.sem, self.cnt, self.seen = {}, {}, {}
        for k in self.eng:
            self.sem[k] = es.enter_context(nc.semaphore("s_" + k))
            self.cnt[k] = 0
            self.seen[k] = {}
        self.dma_sems, self.dma_rr = {}, {}
        for q, n in zip(("sp", "act", "pool"), n_dma_sems):
            self.dma_sems[q] = []
            self.dma_rr[q] = 0
            for i in range(n):
                key = "d%s%d" % (q, i)
                self.sem[key] = es.enter_context(nc.semaphore("s_" + key))
                self.cnt[key] = 0
                self.dma_sems[q].append(key)
        self.n_inst = 0
        self.n_wait = 0
        self.pe_last = None
        self.sep = None

    def _wait(self, e, deps):
        best = {}
        for (k, c) in deps:
            if c > best.get(k, 0):
                best[k] = c
        for k, c in best.items():
            if self.seen[e].get(k, 0) >= c:
                continue
            self.eng[e].wait_ge(self.sem[k], c)
            self.seen[e][k] = c
            self.n_wait += 1

    @staticmethod
    def _deps(reads, writes):
        deps = []
        for b in reads:
            if b.last_w is not None:
                deps.append(b.last_w)
        for b in writes:
            if b.last_w is not None:
                deps.append(b.last_w)
            deps.extend(b.readers.items())
        return deps

    def op(self, e, fn, reads=(), writes=(), kind=None):
        if e == "pe":
            if kind == "bf" and self.pe_last == "f32" and self.sep is not None:
                self.pe_last = "tr"
                self.sep()
            if kind is not None:
                self.pe_last = kind
        ex = [b for b in reads if b.excl]
        if ex:
            writes = list(writes) + [b for b in ex if b not in writes]
            reads = [b for b in reads if not b.excl]
        deps = self._deps(reads, writes)
        if e == "pe":
            deps = [d for d in deps if d[0] != "pe"]
        self._wait(e, deps)
        ins = fn(self.eng[e])
        ins.then_inc(self.sem[e], 1)
        self.cnt[e] += 1
        c = self.cnt[e]
        for b in writes:
            b.last_w = (e, c)
            b.readers = {}
        for b in reads:
            if b not in writes:
                b.readers[e] = c
        self.n_inst += 1
        return ins

    def dma(self, q, out, in_, reads=(), writes=(), **kw):
        key = self.dma_sems[q][self.dma_rr[q] % len(self.dma_sems[q])]
        self.dma_rr[q] += 1
        deps = self._deps(reads, writes)
        if self.cnt[key] > 0:
            deps.append((key, self.cnt[key]))
        self._wait(q, deps)
        ins = self.eng[q].dma_start(out=out, in_=in_, **kw)
        ins.then_inc(self.sem[key], 16)
        self.cnt[key] += 16
        c = self.cnt[key]
        for b in writes:
            b.last_w = (key, c)
            b.readers = {}
        for b in reads:
            b.readers[key] = c
        self.n_inst += 1
        return ins

    def finish(self):
        deps = [(k, c) for k, c in self.cnt.items() if c > 0]
        for e in ("sp", "act", "pool", "dve", "pe"):
            self._wait(e, [d for d in deps if d[0] != e])


def build_program(stage=99):
    nc = bass.Bass("TRN2", target_bir_lowering=False)

    def din(name, shape):
        return nc.dram_tensor(name, list(shape), F32, kind="ExternalInput").ap()

    def dout(name, shape):
        return nc.dram_tensor(name, list(shape), F32, kind="ExternalOutput").ap()

    x_d = din("x", [NT * 128, 1024])
    mem_d = din("mem", [256, 1024])
    sgdn_d = din("sgdn", [16 * 6 * 128, 128])
    sgc_d = din("sgc", [48, 2304])
    ssc_d = din("ssc", [32, 768])
    cmk_d = din("cmk", [2 * 16 * 256, 256])
    cmv_d = din("cmv", [2 * 16 * 256, 256])
    wina_d = din("wina", [1024, CA])
    winb_d = din("winb", [1024, CB])
    wout_d = din("wout", [2048, 1024])
    wkv_d = din("wkv", [2048, 512])
    gfm_d = din("gfm", [128, 24])
    cwa_d = din("cwa", [128, 18 * 4])
    cwb_d = din("cwb", [128, 6 * 3])
    small_d = din("small", [128, 12])
    ogb_d = din("ogb", [128, 128])
    gfin_d = din("gfin", [128, 1024])
    cst_d = din("cst", [128, 6 * 128])
    smask_d = din("smask", [128, 16])

    y_o = dout("y", [NT * 128, 1024])
    sp_o = dout("sp_o", [6 * 128, 128])
    gcp_o = dout("gcp_o", [3, 2304])
    scp_o = dout("scp_o", [2, 768])
    mkp_o = dout("mkp_o", [512, 256])
    mvp_o = dout("mvp_o", [512, 256])
    ss_o = dout("ss_o", [16 * 6 * 128, 128])
    gcs_o = dout("gcs_o", [48, 2304])
    scs_o = dout("scs_o", [32, 768])

    es = ExitStack()
    S = Sched(nc, es)
    bufs = {}

    def sb(name, shape, dt=F32):
        t = es.enter_context(nc.sbuf_tensor("sb_" + name, list(shape), dt))
        bufs[name] = Buf(name)
        return t

    def B(*names):
        return [bufs[n] for n in names]

    def nb(name):
        bufs[name] = Buf(name)
        return bufs[name]

    xres = sb("xres", [128, NT, 1024])
    for t in range(NT):
        nb("x%d" % t)
    win = sb("win", [128, 8, CA], BF16)
    for k in range(8):
        nb("win%d" % k)
    wout = sb("wout", [128, 8, 1024], BF16)
    cst = sb("cst", [128, 6, 128])
    identb = sb("identb", [128, 128], BF16)
    onesb = sb("onesb", [128, 128], BF16)
    smask = sb("smask", [128, 16])
    gfm = sb("gfm", [128, 24])
    cwa = sb("cwa", [128, 18, 4])
    cwb = sb("cwb", [128, 6, 3])
    small = sb("small", [128, 12])
    nA = sb("nA", [128, 6])
    ogb = sb("ogb", [128, 128])
    epsc = sb("epsc", [128, 2])
    rstd = sb("rstd", [128, 3 * NT + 2])
    for i in range(3 * NT + 2):
        nb("rstd_%d" % i)
    ssq = sb("ssq", [128, 3 * NT + 2])
    mkT = sb("mkT", [128, 2, 2, 256], BF16)
    mvp = sb("mvp", [128, 2, 2, 256], BF16)
    hb = sb("hb", [128, 1024], BF16)
    hT = sb("hT", [128, 8, 128], BF16)
    xp = [sb("xp%d" % i, [128, 16 * 11]) for i in range(2)]
    hist = sb("hist", [128, 18, 3])
    hist2 = sb("hist2", [128, 6, 2])
    cacc = [sb("cacc%d" % i, [128, 128]) for i in range(2)]
    sil = [sb("sil%d" % i, [128, 128]) for i in range(2)]
    sqb = [sb("sqb%d" % i, [128, 3, 128], BF16) for i in range(2)]
    QKT = sb("QKT", [128, 12, 128], BF16)
    QT, KT = QKT[:, 0:6], QKT[:, 6:12]
    VT = sb("VT", [128, 6, 128], BF16)
    for h in range(6):
        nb("QT%d" % h), nb("KT%d" % h), nb("VT%d" % h)
    sgT = sb("sgT", [128, 8, 128], BF16)
    xqT = sb("xqT", [128, 2, 128], BF16)
    gat = sb("gat", [128, 48])
    cum = sb("cum", [128, 18 + 96])
    kkv3 = sb("kkv3", [128, 3, 6, 128], BF16)
    KBG, KD, VB = kkv3[:, 0], kkv3[:, 1], kkv3[:, 2]
    nb("KBG"), nb("KD"), nb("VB")
    gfin = kkv3[:].rearrange("p a b c -> p (a b c)")[:, 0:2048].bitcast(F32)
    Dl = [sb("Dl%d" % i, [128, 128]) for i in range(3)]
    Du = [sb("Du%d" % i, [128, 128]) for i in range(3)]
    AqT = [sb("AqT%d" % i, [128, 128], BF16) for i in range(3)]
    Rb = [[sb("R%d_%d" % (i, j), [128, 128]) for j in range(2)] for i in range(3)]
    Qb = [[sb("Q%d_%d" % (i, j), [128, 128]) for j in range(2)] for i in range(3)]
    Xb = [sb("X%d" % i, [128, 128]) for i in range(3)]
    Xbf = [sb("Xbf%d" % i, [128, 128], BF16) for i in range(3)]
    wTn = [sb("wTn%d" % i, [128, 128], BF16) for i in range(3)]
    vnew = [sb("vnew%d" % i, [128, 128], BF16) for i in range(3)]
    onb = [sb("onb%d" % i, [128, 128], BF16) for i in range(3)]
    Sst = sb("Sst", [128, 6, 128])
    Sbf = sb("Sbf", [128, 6, 128], BF16)
    for h in range(6):
        nb("S%d" % h)
    Pb = sb("Pb", [128, 4, 256], BF16)
    PTb = hb[:].rearrange("p (h m t) -> p h m t", h=4, m=2)
    sm = sb("sm", [128, 16])
    stg = [sb("stg%d" % i, [128, 512]) for i in range(2)]
    S0bf = sb("S0bf", [128, 8, 128], BF16)
    S0f = [sb("S0f0", [128, 128]), sil[0]]
    Snw = [sb("Snw0", [128, 128]), sil[1]]
    bufs["S0f1"] = bufs["sil0"]; bufs["Snw1"] = bufs["sil1"]
    for k8 in range(8):
        nb("S0bf_%d" % k8)
    tsb = [sb("tsb%d" % i, [128, 128], BF16) for i in range(2)]
    kdm = [sb("kdm%d" % i, [128, 128], BF16) for i in range(2)]
    xqm = [sb("xqm%d" % i, [128, 2, 128], BF16) for i in range(2)]
    ckv = [sb("ckv%d" % i, [128, 2, 2, 256], BF16) for i in range(2)]
    mkTs = [sb("mkTs%d" % i, [128, 2, 256], BF16) for i in range(2)]
    hst = [sb("hst%d" % i, [48, 128]) for i in range(2)]

    S0v = [hb[:].bitcast(F32).rearrange("p (s e) -> p s e", s=4),
           Pb[:].rearrange("p a b -> p (a b)").bitcast(F32).rearrange("p (s e) -> p s e", s=4)]
    S0n = ["hb", "Pb"]
    Snw4 = [stg[0][:].rearrange("p (s e) -> p s e", s=4), stg[1][:].rearrange("p (s e) -> p s e", s=4)]
    stgKV = [stg[0][:].rearrange("p (m c) -> p m c", m=2), stg[1][:].rearrange("p (m c) -> p m c", m=2)]
    sg4 = sgdn_d.rearrange("(s h d) e -> d s h e", s=16, h=6)
    ss4 = ss_o.rearrange("(s h d) e -> d s h e", s=16, h=6)

    pbank = [es.enter_context(nc.psum_tensor("pb%d" % i, [128, 512], F32)) for i in range(8)]
    for i in range(8):
        bk = nb("pbank%d" % i)
        bk.excl = True
        for j in range(4):
            bufs["p%d_%d" % (i, j)] = bk

    def PS(bank, slot, n=1):
        return pbank[bank][:, slot * 128:(slot + n) * 128], B("p%d_0" % bank)

    def PSB(bank, slot, n=1):
        v = pbank[bank][:, slot * 128:(slot + n) * 128].bitcast(BF16)
        return v, B("p%d_0" % bank)

    I_F, ONES_F, U_, SL_, US_, SLS_ = range(6)
    ML_, MU_, MLS_, MUS_ = SL_, U_, SLS_, US_

    def C(i):
        return cst[:, i, :]

    S.dma("sp", cst[:].rearrange("p a b -> p (a b)"), cst_d[:, :], writes=B("cst"))
    S.dma("sp", smask[:], smask_d[:, :], writes=B("smask"))
    S.dma("sp", gfm[:], gfm_d[:, :], writes=B("gfm"))
    S.dma("sp", cwa[:].rearrange("p a b -> p (a b)"), cwa_d[:, :], writes=B("cwa"))
    S.dma("sp", cwb[:].rearrange("p a b -> p (a b)"), cwb_d[:, :], writes=B("cwb"))
    S.dma("sp", small[:], small_d[:, :], writes=B("small"))
    S.dma("sp", ogb[:], ogb_d[:, :], writes=B("ogb"))
    S.op("dve", lambda e: e.tensor_copy(identb[:], C(I_F)), reads=B("cst"), writes=B("identb"))
    S.op("dve", lambda e: e.tensor_copy(onesb[:], C(ONES_F)), reads=B("cst"), writes=B("onesb"))
    S.op("dve", lambda e: e.memset(epsc[:, 0:1], EPS), writes=B("epsc"))
    S.op("dve", lambda e: e.memset(epsc[:, 1:2], 128.0 * EPS), writes=B("epsc"))
    S.op("dve", lambda e: e.memset(hist[:], 0.0), writes=B("hist"))
    S.op("dve", lambda e: e.memset(hist2[:], 0.0), writes=B("hist2"))
    S.op("dve", lambda e: e.memset(ssq[:], 0.0), writes=B("ssq"))
    S.op("dve", lambda e: e.memset(sm[:], 0.0), writes=B("sm"))
    S.op("dve", lambda e: e.memset(gat[:], 0.0), writes=B("gat"))
    S.op("dve", lambda e: e.memset(rstd[:], 0.0), writes=B(*["rstd_%d" % i for i in range(3 * NT + 2)]))
    S.op("act", lambda e: e.activation(nA[:], small[:, 0:6], AF.Exp), reads=B("small"), writes=B("nA"))
    S.op("dve", lambda e: e.tensor_scalar(nA[:], nA[:], -1.0, None, op0=ALU.mult), reads=B("nA"), writes=B("nA"))

    for l in range(2):
        for kc in range(8):
            S.dma("pool", wout[:, kc, l * 512:(l + 1) * 512], wkv_d[l * 1024 + kc * 128:l * 1024 + (kc + 1) * 128, :],
                  writes=B("wout"))
    for kc in range(8):
        S.dma("pool", win[:, kc, 0:CA], wina_d[kc * 128:(kc + 1) * 128, :], writes=B("win%d" % kc))
    for mt in range(2):
        S.dma("sp", xres[:, mt, :], mem_d[mt * 128:(mt + 1) * 128, :], writes=B("x%d" % mt))

    def rstd_from_ssq(col, n_feat):
        S.op("act", lambda e: e.activation(rstd[:, col:col + 1], ssq[:, col:col + 1], AF.Ln, bias=epsc[:, 0:1], scale=1.0 / n_feat),
             reads=B("ssq", "epsc"), writes=B("rstd_%d" % col))
        S.op("act", lambda e: e.activation(rstd[:, col:col + 1], rstd[:, col:col + 1], AF.Exp, scale=-0.5),
             reads=B("rstd_%d" % col), writes=B("rstd_%d" % col))

    def ssq_of_tile(t, col):
        S.op("act", lambda e: e.activation(hb[:], xres[:, t, :], AF.Square, accum_out=ssq[:, col:col + 1]),
             reads=B("x%d" % t, "ssq"), writes=B("hb", "ssq"))

    def norm_transpose(t, rcol, gcol0):
        S.op("dve", lambda e: e.tensor_scalar(hb[:], xres[:, t, :], rstd[:, rcol:rcol + 1], None, op0=ALU.mult),
             reads=B("x%d" % t, "rstd_%d" % rcol), writes=B("hb"))
        pv, pbf = PSB(0, 0, 4)
        pv3 = pv.rearrange("p (k t) -> p k t", k=8)
        for kc in range(8):
            S.op("pe", lambda e, kc=kc: e.transpose(pv3[:, kc, :], hb[:, kc * 128:(kc + 1) * 128], identb[:]),
                 reads=B("hb", "identb"), writes=pbf)
        g3 = gfm[:, gcol0:gcol0 + 8].unsqueeze(2).to_broadcast([128, 8, 128])
        S.op("dve", lambda e: e.tensor_tensor(hT[:], pv3, g3, op=ALU.mult), reads=pbf + B("gfm"), writes=B("hT"))

    def pe_sep():
        sv, sbk = PSB(3, 3, 1)
        S.op("pe", lambda e: e.transpose(sv[:, 0:128], identb[:], identb[:]), reads=B("identb"), writes=sbk, kind="tr")

    S.sep = pe_sep
    def run_interleaved(gens):
        gens = list(gens)
        while gens:
            for g in list(gens):
                try:
                    next(g)
                except StopIteration:
                    gens.remove(g)

    proj_rr = [0]

    def proj_chunk(col0, ncols=128):
        bank = (1, 7)[proj_rr[0] % 2]
        proj_rr[0] += 1
        pv, pbf = PS(bank, 0)
        for kc in range(8):
            S.op("pe", lambda e, kc=kc: e.matmul(pv[0:ncols, :], win[:, kc, col0:col0 + ncols], hT[:, kc, :], start=(kc == 0), stop=(kc == 7)),
                 reads=B("win%d" % kc, "hT"), writes=pbf)
        return pv, pbf

    for mt in range(2):
        ssq_of_tile(mt, 3 * NT + mt)
        rstd_from_ssq(3 * NT + mt, 1024)
    memT = kkv3[:].rearrange("p a b c -> p (a b c)")[:, 0:2048].rearrange("p (k m) -> p k m", k=8)
    MEMB = B("KBG", "KD", "VB")
    for mt in range(2):
        norm_transpose(mt, 3 * NT + mt, 16)
        S.op("act", lambda e, mt=mt: e.copy(memT[:, :, mt * 128:(mt + 1) * 128], hT[:]), reads=B("hT"), writes=MEMB)
    for l in range(2):
        for mt in range(2):
            pv, pbf = PS(2, 0, 4)
            for kc in range(8):
                S.op("pe", lambda e, kc=kc: e.matmul(pv, memT[:, kc, mt * 128:(mt + 1) * 128], wout[:, kc, l * 512:(l + 1) * 512], start=(kc == 0), stop=(kc == 7)),
                     reads=MEMB + B("wout"), writes=pbf)
            st = stg[(l * 2 + mt) % 2]
            sn = "stg%d" % ((l * 2 + mt) % 2)
            S.op("act", lambda e, st=st: e.copy(st[:], pv), reads=pbf, writes=B(sn))
            S.dma("sp", mkp_o[l * 256 + mt * 128:l * 256 + (mt + 1) * 128, :], st[:, 0:256], reads=B(sn))
            S.dma("sp", mvp_o[l * 256 + mt * 128:l * 256 + (mt + 1) * 128, :], st[:, 256:512], reads=B(sn))
            S.op("dve", lambda e, st=st: e.tensor_copy(mvp[:, l, mt, :], st[:, 256:512]), reads=B(sn), writes=B("mvp"))
        for c in range(2):
            pv, pbf = PS(3, 0, 2)
            for kc in range(8):
                S.op("pe", lambda e, kc=kc: e.matmul(pv, wout[:, kc, l * 512 + c * 128:l * 512 + (c + 1) * 128], memT[:, kc, :], start=(kc == 0), stop=(kc == 7)),
                     reads=MEMB + B("wout"), writes=pbf)
            S.op("act", lambda e, c=c: e.copy(mkT[:, l, c, :], pv), reads=pbf, writes=B("mkT"))

    for t in range(NT):
        S.dma("sp", xres[:, t, :], x_d[t * 128:(t + 1) * 128, :], writes=B("x%d" % t))

    def load_win(l):
        wd = wina_d if l == 0 else winb_d
        ncol = CA if l == 0 else CB
        for kc in range(8):
            S.dma("pool", win[:, kc, 0:ncol], wd[kc * 128:(kc + 1) * 128, :], writes=B("win%d" % kc))

    def load_wout(l):
        for kc in range(8):
            S.dma("pool", wout[:, kc, :], wout_d[l * 1024 + kc * 128:l * 1024 + (kc + 1) * 128, :], writes=B("wout"))

    def load_layer_weights(l):
        load_win(l)
        load_wout(l)

    load_wout(0)
    for t in range(NT):
        ssq_of_tile(t, t)
        rstd_from_ssq(t, 1024)

    def attention_and_out(l, t):
        sample = (t == NT - 1)
        scv, scb = [], []
        for bk in (3, 4, 5, 6):
            v_, b_ = PS(bk, 0, 2)
            scv.append(v_)
            scb.append(b_)
        if not sample:
            for h in range(4):
                po = (h % 2) * 64
                if _SUB == 700 and h != 0:
                    continue
                if _SUB == 701 and h != 1:
                    continue
                S.op("pe", lambda e, h=h, po=po: e.matmul(scv[h], xqT[po:po + 64, h // 2, :], mkT[po:po + 64, l, h // 2, :], start=True, stop=True),
                     reads=B("xqT", "mkT"), writes=scb[h])
        else:
            for s in range(16):
                i = s % 2
                r0 = (l * 16 + s) * 256
                S.dma("sp", stgKV[i], cmk_d[r0:r0 + 256, :].rearrange("(mt p) c -> p mt c", p=128), writes=B("stg%d" % i))
                S.op("act", lambda e, i=i: e.copy(ckv[i][:, 0, :, :], stgKV[i]), reads=B("stg%d" % i), writes=B("ckv%d" % i))
                tv, tb = PSB(0, 0, 2)
                tv3 = tv.rearrange("p (c m) -> p c m", c=2)
                for c in range(2):
                    for mt in range(2):
                        S.op("pe", lambda e, c=c, mt=mt, i=i: e.transpose(tv3[:, c, mt * 128:(mt + 1) * 128], ckv[i][:, 0, mt, c * 128:(c + 1) * 128], identb[:]),
                             reads=B("ckv%d" % i, "identb"), writes=tb)
                S.op("act", lambda e, i=i: e.copy(mkTs[i][:], tv3), reads=tb, writes=B("mkTs%d" % i))
                S.op("pool", lambda e, i=i: e.memset(xqm[i][:].rearrange("p a b -> p (a b)"), 0.0), writes=B("xqm%d" % i))
                S.op("pool", lambda e, i=i, s=s: e.tensor_copy(xqm[i][:, :, s * 8:(s + 1) * 8], xqT[:, :, s * 8:(s + 1) * 8]),
                     reads=B("xqT"), writes=B("xqm%d" % i))
                for h in range(4):
                    po = (h % 2) * 64
                    S.op("pe", lambda e, h=h, po=po, i=i: e.matmul(scv[h], xqm[i][po:po + 64, h // 2, :], mkTs[i][po:po + 64, h // 2, :], start=(s == 0), stop=(s == 15)),
                         reads=B("xqm%d" % i, "mkTs%d" % i), writes=scb[h])
        ovs = [PS(0, 0, 4), PS(1, 0, 4)]
        for half in range(2):
            for c in range(6):
                S.op("pe", lambda e, c=c, half=half: e.matmul(ovs[half][0], sgT[:, c, :], wout[:, c, half * 512:(half + 1) * 512], start=(c == 0), stop=False),
                     reads=B("sgT", "wout"), writes=ovs[half][1], kind="bf")
        for h in range(4):
            S.op("dve", lambda e, h=h: e.tensor_reduce(sm[:, h:h + 1], scv[h], axis=AX.X, op=ALU.max), reads=scb[h], writes=B("sm"))
        S.op("dve", lambda e: e.tensor_scalar(sm[:, 4:8], sm[:, 0:4], -0.125, None, op0=ALU.mult), reads=B("sm"), writes=B("sm"))
        S.op("dve", lambda e: e.memset(sm[:, 8:12], 0.0), writes=B("sm"))
        for h in range(4):
            S.op("act", lambda e, h=h: e.activation(Pb[:, h, :], scv[h], AF.Exp, bias=sm[:, 4 + h:5 + h], scale=0.125, accum_out=sm[:, 8 + h:9 + h]),
                 reads=scb[h] + B("sm"), writes=B("Pb", "sm"))
        S.op("dve", lambda e: e.reciprocal(sm[:, 12:16], sm[:, 8:12]), reads=B("sm"), writes=B("sm"))
        S.op("dve", lambda e: e.tensor_tensor(Pb[:], Pb[:], sm[:, 12:16].unsqueeze(2).to_broadcast([128, 4, 256]), op=ALU.mult),
             reads=B("Pb", "sm"), writes=B("Pb"))
        if _SUB == 71:
            return
        tv, tb = PSB(5, 0, 4)
        tv4 = tv.rearrange("p (h m t) -> p h m t", h=4, m=2)
        for h in range(4):
            for mt in range(2):
                S.op("pe", lambda e, h=h, mt=mt: e.transpose(tv4[:, h, mt, :], Pb[:, h, mt * 128:(mt + 1) * 128], identb[:]),
                     reads=B("Pb", "identb"), writes=tb)
        S.op("act", lambda e: e.copy(PTb, tv4), reads=tb, writes=B("hb"))
        if _SUB == 72:
            return
        xo_v, xo_b = PS(6, 0, 4)
        xo4 = xo_v.rearrange("p (c k t) -> p c k t", c=2, k=2)
        if not sample:
            for c in range(2):
                for hh in range(2):
                    h = 2 * c + hh
                    for mt in range(2):
                        S.op("pe", lambda e, h=h, mt=mt, c=c, hh=hh: e.matmul(xo4[:, c, hh, :], mvp[:, l, mt, c * 128:(c + 1) * 128], PTb[:, h, mt, :], start=(mt == 0), stop=(mt == 1)),
                             reads=B("mvp", "hb"), writes=xo_b)
        else:
            for s in range(16):
                i = s % 2
                r0 = (l * 16 + s) * 256
                S.dma("sp", stgKV[i], cmv_d[r0:r0 + 256, :].rearrange("(mt p) c -> p mt c", p=128), writes=B("stg%d" % i))
                S.op("act", lambda e, i=i: e.copy(ckv[i][:, 1, :, :], stgKV[i]), reads=B("stg%d" % i), writes=B("ckv%d" % i))
                for c in range(2):
                    for hh in range(2):
                        h = 2 * c + hh
                        for mt in range(2):
                            S.op("pe", lambda e, h=h, mt=mt, c=c, hh=hh, i=i, s=s: e.matmul(xo4[:, c, hh, s * 8:(s + 1) * 8], ckv[i][:, 1, mt, c * 128:(c + 1) * 128], PTb[:, h, mt, s * 8:(s + 1) * 8], start=(mt == 0), stop=(mt == 1)),
                                 reads=B("ckv%d" % i, "hb"), writes=xo_b)
        if _SUB == 73:
            return
        for c in range(2):
            for hh in range(2):
                po = hh * 64
                S.op("dve", lambda e, c=c, hh=hh, po=po: e.tensor_tensor(sgT[po:po + 64, 6 + c, :], xo4[po:po + 64, c, hh, :], sgT[po:po + 64, 6 + c, :], op=ALU.mult),
                     reads=xo_b + B("sgT"), writes=B("sgT"))
        if _SUB == 74:
            return
        for half in range(2):
            for c in (6, 7):
                S.op("pe", lambda e, c=c, half=half: e.matmul(ovs[half][0], sgT[:, c, :], wout[:, c, half * 512:(half + 1) * 512], start=False, stop=(c == 7)),
                     reads=B("sgT", "wout"), writes=ovs[half][1], kind="bf")
            S.op("dve", lambda e, half=half: e.tensor_tensor(xres[:, t, half * 512:(half + 1) * 512], xres[:, t, half * 512:(half + 1) * 512], ovs[half][0], op=ALU.add),
                 reads=ovs[half][1] + B("x%d" % t), writes=B("x%d" % t))
        if _SUB == 75:
            return
        col = (l + 1) * NT + t
        ssq_of_tile(t, col)
        rstd_from_ssq(col, 1024)
        if l == 1:
            st = stg[t % 2]
            for half in range(2):
                S.op("dve", lambda e, half=half, st=st: e.scalar_tensor_tensor(st[:], xres[:, t, half * 512:(half + 1) * 512], rstd[:, col:col + 1], gfin[:, half * 512:(half + 1) * 512], op0=ALU.mult, op1=ALU.mult),
                     reads=B("x%d" % t, "rstd_%d" % col, "KBG", "KD", "VB"), writes=B("stg%d" % (t % 2)))
                S.dma("sp", y_o[t * 128:(t + 1) * 128, half * 512:(half + 1) * 512], st[:], reads=B("stg%d" % (t % 2)))

    def conv_setup_hist(l, t, c, i, nh):
        sample = (t == NT - 1)
        hbuf = hist if l == 0 else hist2
        hname = "hist" if l == 0 else "hist2"
        if not sample:
            S.op("pool", lambda e: e.tensor_copy(xp[i][:, 0:nh], hbuf[:, c, :]), reads=B(hname), writes=B("xp%d" % i))
        else:
            srcd = sgc_d if l == 0 else ssc_d
            nr = 16 * nh
            S.dma("sp", hst[i][0:nr, :], srcd[0:nr, c * 128:(c + 1) * 128], writes=B("hst%d" % i))
            pv, pbf = PS(2, 2, 1)
            S.op("pe", lambda e: e.transpose(pv[:, 0:nr], hst[i][0:nr, :], C(I_F)[0:nr, 0:nr]),
                 reads=B("hst%d" % i, "cst"), writes=pbf)
            L = 8
            xp3 = xp[i][:, 0:16 * (nh + L)].rearrange("p (s k) -> p s k", s=16)
            S.op("act", lambda e: e.copy(xp3[:, :, 0:nh], pv[:, 0:nr].rearrange("p (s j) -> p s j", s=16)), reads=pbf, writes=B("xp%d" % i))

    def xp_views(t, i, nh):
        sample = (t == NT - 1)
        if not sample:
            cur = xp[i][:, nh:nh + 128]
            taps = [xp[i][:, j:j + 128] for j in range(nh + 1)]
            last = xp[i][:, 128:128 + nh]
            return cur, taps, last
        L = 8
        xp3 = xp[i][:, 0:16 * (nh + L)].rearrange("p (s k) -> p s k", s=16)
        cur = xp3[:, :, nh:nh + L]
        taps = [xp3[:, :, j:j + L] for j in range(nh + 1)]
        return cur, taps, None

    def as3(ap, t):
        if t == NT - 1:
            return ap.rearrange("p (s k) -> p s k", s=16)
        return ap

    def layer0_tile(t):
        sample = (t == NT - 1)
        norm_transpose(t, t, 0)
        for c in range(8):
            pv, pbf = proj_chunk(2316 + c * 128)
            S.op("act", lambda e, c=c, pv=pv: e.activation(sgT[:, c, :], pv, AF.Silu), reads=pbf, writes=B("sgT"))
        for c in range(2):
            pv, pbf = proj_chunk(2316 + 1024 + c * 128)
            S.op("act", lambda e, c=c, pv=pv: e.copy(xqT[:, c, :], pv), reads=pbf, writes=B("xqT"))
        if _SUB == 1:
            return
        def stA(c):
            i = c % 2
            pv, pbf = proj_chunk(c * 128)
            conv_setup_hist(0, t, c, i, 3)
            cur, taps, last = xp_views(t, i, 3)
            S.op("act", lambda e: e.copy(cur, as3(pv, t)), reads=pbf, writes=B("xp%d" % i))
            if not sample:
                S.op("pool", lambda e: e.tensor_copy(hist[:, c, :], last), reads=B("xp%d" % i), writes=B("hist"))
            acc = as3(cacc[i][:], t)
            S.op("act", lambda e: e.activation(acc, taps[3], AF.Identity, scale=cwa[:, c, 3:4]),
                 reads=B("xp%d" % i, "cwa"), writes=B("cacc%d" % i))

        def stB(c):
            i = c % 2
            cur, taps, last = xp_views(t, i, 3)
            acc = as3(cacc[i][:], t)
            for j in range(3):
                S.op("dve", lambda e, j=j: e.scalar_tensor_tensor(acc, taps[j], cwa[:, c, j:j + 1], acc, op0=ALU.mult, op1=ALU.add),
                     reads=B("xp%d" % i, "cwa", "cacc%d" % i), writes=B("cacc%d" % i))

        def stC(c):
            i = c % 2
            if c >= 12:
                dst, h, dn = VT, c - 12, "VT%d" % (c - 12)
            elif c < 6:
                dst, h, dn = QT, c, "QT%d" % c
            else:
                dst, h, dn = KT, c - 6, "KT%d" % (c - 6)
            S.op("act", lambda e: e.activation(dst[:, h, :], cacc[i][:], AF.Silu), reads=B("cacc%d" % i), writes=B(dn))

        stA(0)
        stB(0)
        for c in range(18):
            if c + 1 < 18:
                stA(c + 1)
            stC(c)
            if c + 1 < 18:
                stB(c + 1)
        gcol = gat[:, 24:30]
        Um, SLm = (US_, SLS_) if sample else (U_, SL_)
        eG, eGrev, eGl = cum[:, 0:6], cum[:, 6:12], cum[:, 12:18]
        eGls = cum[:, 18:114].rearrange("p (s h) -> p s h", s=16)

        def gen_norm():
            for g in range(4):
                i = g % 2
                isq = (g < 2)
                c0 = g * 3
                names = [("QT%d" % (c0 + k)) if isq else ("KT%d" % (c0 - 6 + k)) for k in range(3)]
                blk = QKT[:, c0:c0 + 3, :]
                S.op("act", lambda e: e.activation(sqb[i][:], blk, AF.Square), reads=B(*names), writes=B("sqb%d" % i))
                qv, qb = PS((2, 0)[i], 0, 3)
                S.op("pe", lambda e: e.matmul(qv, onesb[:], sqb[i][:].rearrange("p a b -> p (a b)"), start=True, stop=True), reads=B("onesb", "sqb%d" % i), writes=qb, kind="bf")
                yield
                S.op("act", lambda e: e.activation(qv, qv, AF.Ln, bias=epsc[:, 0:1], scale=1.0), reads=qb + B("epsc"), writes=qb)
                S.op("act", lambda e: e.activation(qv, qv, AF.Exp, scale=-0.5), reads=qb, writes=qb)
                qv3 = qv.rearrange("p (a b) -> p a b", a=3)
                if isq:
                    S.op("dve", lambda e: e.scalar_tensor_tensor(blk, blk, float(128.0 ** -0.5), qv3, op0=ALU.mult, op1=ALU.mult), reads=qb + B(*names), writes=B(*names))
                else:
                    S.op("dve", lambda e: e.tensor_tensor(blk, blk, qv3, op=ALU.mult), reads=qb + B(*names), writes=B(*names))
                yield

        def gen_gates():
            bav, bab = PS(3, 0)
            for kc in range(8):
                S.op("pe", lambda e, kc=kc: e.matmul(bav[:, 0:12], hT[:, kc, :], win[:, kc, 2304:2316], start=(kc == 0), stop=(kc == 7)),
                     reads=B("hT", "win%d" % kc), writes=bab, kind="bf")
            yield
            S.op("act", lambda e: e.activation(gat[:, 0:6], bav[:, 0:6], AF.Exp, scale=-1.0), reads=bab, writes=B("gat"))
            S.op("dve", lambda e: e.tensor_tensor(gat[:, 6:12], bav[:, 6:12], small[:, 6:12], op=ALU.add), reads=bab + B("small", "gat"), writes=B("gat"))
            S.op("act", lambda e: e.activation(gat[:, 6:12], gat[:, 6:12], AF.Exp), reads=B("gat"), writes=B("gat"))
            S.op("act", lambda e: e.activation(gat[:, 6:12], gat[:, 6:12], AF.Ln, bias=1.0), reads=B("gat"), writes=B("gat"))
            yield
            S.op("dve", lambda e: e.tensor_scalar(gat[:, 0:6], gat[:, 0:6], 1.0, None, op0=ALU.add), reads=B("gat"), writes=B("gat"))
            S.op("dve", lambda e: e.reciprocal(gat[:, 12:18], gat[:, 0:6]), reads=B("gat"), writes=B("gat"))
            S.op("dve", lambda e: e.tensor_scalar(gat[:, 18:24], gat[:, 12:18], -1.0, None, op0=ALU.mult), reads=B("gat"), writes=B("gat"))
            S.op("dve", lambda e: e.tensor_tensor(gat[:, 24:30], gat[:, 6:12], nA[:], op=ALU.mult), reads=B("gat", "nA"), writes=B("gat"))
            cv, cb = PS(3, 1)
            S.op("pe", lambda e: e.matmul(cv[:, 0:6], C(Um), gcol, start=True, stop=True), reads=B("cst", "gat"), writes=cb, kind="f32")
            S.op("pe", lambda e: e.matmul(cv[:, 6:12], C(SLm), gcol, start=True, stop=True), reads=B("cst", "gat"), writes=cb, kind="f32")
            if not sample:
                S.op("pe", lambda e: e.matmul(cv[:, 12:18], C(ONES_F), gcol, start=True, stop=True), reads=B("cst", "gat"), writes=cb, kind="f32")
                yield
                S.op("act", lambda e: e.activation(cum[:, 0:18], cv[:, 0:18], AF.Exp), reads=cb, writes=B("cum"))
            else:
                gm = cum[:, 18:114].rearrange("p (s h) -> p s h", s=16)
                S.op("dve", lambda e: e.tensor_tensor(gm, gcol.unsqueeze(1).to_broadcast([128, 16, 6]), smask[:].unsqueeze(2).to_broadcast([128, 16, 6]), op=ALU.mult),
                     reads=B("gat", "smask", "cum"), writes=B("cum"))
                cv2, cb2 = PS(3, 2)
                S.op("pe", lambda e: e.matmul(cv2[:, 0:96], C(ONES_F), cum[:, 18:114], start=True, stop=True), reads=B("cst", "cum"), writes=cb2, kind="f32")
                yield
                S.op("act", lambda e: e.activation(cum[:, 0:12], cv[:, 0:12], AF.Exp), reads=cb, writes=B("cum"))
                S.op("act", lambda e: e.activation(cum[:, 18:114], cv2[:, 0:96], AF.Exp), reads=cb2, writes=B("cum"))
            S.op("dve", lambda e: e.tensor_tensor(gat[:, 30:36], gat[:, 12:18], eG, op=ALU.mult), reads=B("gat", "cum"), writes=B("gat"))
            tv_, tb_ = PSB(7, 0, 3)
            tvv = tv_[:, 0:768].rearrange("p (h d) -> p h d", h=6)
            for h in range(6):
                S.op("pe", lambda e, h=h: e.transpose(tvv[:, h, :], VT[:, h, :], identb[:]), reads=B("VT%d" % h, "identb"), writes=tb_, kind="tr")
            yield
            S.op("dve", lambda e: e.tensor_tensor(VB, tvv, gat[:, 12:18].unsqueeze(2).to_broadcast([128, 6, 128]), op=ALU.mult), reads=tb_ + B("gat"), writes=B("VB"))

        run_interleaved([gen_norm(), gen_gates()])
        if _SUB == 2:
            return
        tv, tb = PSB(0, 0, 3)
        tv3 = tv.rearrange("p (h d) -> p h d", h=6)
        for h in range(6):
            S.op("pe", lambda e, h=h: e.transpose(tv3[:, h, :], KT[:, h, :], identb[:]), reads=B("KT%d" % h, "identb"), writes=tb)
        S.op("dve", lambda e: e.tensor_tensor(KBG, tv3, gat[:, 30:36].unsqueeze(2).to_broadcast([128, 6, 128]), op=ALU.mult), reads=tb + B("gat"), writes=B("KBG"))
        S.op("dve", lambda e: e.tensor_tensor(KD, tv3, eGrev.unsqueeze(2).to_broadcast([128, 6, 128]), op=ALU.mult), reads=tb + B("cum"), writes=B("KD"))
        if _SUB == 3:
            return
        if t >= NT - 2:
            for blk in range(5):
                c0 = blk * 512
                w_ = min(512, 2304 - c0)
                pv, pbf = PS(7, 0, 4)
                for kc in range(8):
                    S.op("pe", lambda e, kc=kc, pv=pv, c0=c0, w_=w_: e.matmul(pv[:, 0:w_], hT[:, kc, :], win[:, kc, c0:c0 + w_], start=(kc == 0), stop=(kc == 7)),
                         reads=B("hT", "win%d" % kc), writes=pbf)
                st = stg[blk % 2]
                S.op("act", lambda e, st=st, pv=pv, w_=w_: e.copy(st[:, 0:w_], pv[:, 0:w_]), reads=pbf, writes=B("stg%d" % (blk % 2)))
                if not sample:
                    S.dma("sp", gcp_o[0:3, c0:c0 + w_], st[125:128, 0:w_], reads=B("stg%d" % (blk % 2)))
                else:
                    for s in range(16):
                        S.dma("sp", gcs_o[s * 3:(s + 1) * 3, c0:c0 + w_], st[s * 8 + 5:s * 8 + 8, 0:w_], reads=B("stg%d" % (blk % 2)))
        if t == NT - 1:
            load_win(1)
        MLm, MUm = (MLS_, MUS_) if sample else (ML_, MU_)
        nlev = 3 if sample else 7
        have_state = sample or t > 0
        HB = ((4, 5), (0, 1), (7, 2))

        def head_gen(h, i):
            bA, bB = HB[i]
            dlv, dlb = PS(bA, 0)
            duv, dub = PS(bA, 1)
            kkv, kkb = PS(bB, 0)
            kqv, kqb = PS(bB, 1)
            UGv, SLGv = Dl[i], Du[i]
            S.op("dve", lambda e: e.tensor_scalar(UGv[:], C(Um), gat[:, 24 + h:25 + h], None, op0=ALU.mult), reads=B("cst", "gat"), writes=B("Dl%d" % i))
            S.op("dve", lambda e: e.tensor_scalar(SLGv[:], C(SLm), gat[:, 24 + h:25 + h], None, op0=ALU.mult), reads=B("cst", "gat"), writes=B("Du%d" % i))
            S.op("pe", lambda e: e.matmul(dlv, UGv[:], C(SLm), start=True, stop=True), reads=B("Dl%d" % i, "cst"), writes=dlb, kind="f32")
            S.op("pe", lambda e: e.matmul(duv, SLGv[:], C(Um), start=True, stop=True), reads=B("Du%d" % i, "cst"), writes=dub, kind="f32")
            S.op("pe", lambda e: e.matmul(kkv, KT[:, h, :], KT[:, h, :], start=True, stop=True), reads=B("KT%d" % h), writes=kkb, kind="bf")
            S.op("pe", lambda e: e.matmul(kqv, KT[:, h, :], QT[:, h, :], start=True, stop=True), reads=B("KT%d" % h, "QT%d" % h), writes=kqb, kind="bf")
            yield
            S.op("act", lambda e: e.activation(Dl[i][:], dlv, AF.Exp), reads=dlb, writes=B("Dl%d" % i))
            S.op("act", lambda e: e.activation(Du[i][:], duv, AF.Exp), reads=dub, writes=B("Du%d" % i))
            S.op("dve", lambda e: e.tensor_tensor(Dl[i][:], Dl[i][:], C(MLm), op=ALU.mult), reads=B("Dl%d" % i, "cst"), writes=B("Dl%d" % i))
            S.op("dve", lambda e: e.tensor_tensor(Du[i][:], Du[i][:], C(MUm), op=ALU.mult), reads=B("Du%d" % i, "cst"), writes=B("Du%d" % i))
            R0 = Rb[i][0]
            S.op("dve", lambda e: e.scalar_tensor_tensor(R0[:], kkv, gat[:, 18 + h:19 + h], Dl[i][:], op0=ALU.mult, op1=ALU.mult),
                 reads=kkb + B("gat", "Dl%d" % i), writes=B("R%d_0" % i))
            S.op("dve", lambda e: e.tensor_tensor(AqT[i][:], kqv, Du[i][:], op=ALU.mult), reads=kqb + B("Du%d" % i), writes=B("AqT%d" % i))
            yield
            ntv, ntb = PS(bA, 2)
            S.op("pe", lambda e: e.transpose(ntv, R0[:], C(I_F)), reads=B("R%d_0" % i, "cst"), writes=ntb, kind="tr")
            yield
            S.op("act", lambda e: e.copy(Qb[i][0][:], ntv), reads=ntb, writes=B("Q%d_0" % i))
            S.op("dve", lambda e: e.tensor_tensor(Xb[i][:], ntv, C(I_F), op=ALU.add), reads=ntb + B("cst"), writes=B("X%d" % i))
            for k in range(1, nlev):
                a, b_ = (k - 1) % 2, k % 2
                rv, rb_ = PS(bA, 0)
                qv_, qb_ = PS(bA, 1)
                xv, xb_ = PS(bB, 0)
                S.op("pe", lambda e: e.matmul(rv, Qb[i][a][:], Rb[i][a][:], start=True, stop=True),
                     reads=B("Q%d_%d" % (i, a), "R%d_%d" % (i, a)), writes=rb_, kind="f32")
                if k < nlev - 1:
                    S.op("pe", lambda e: e.matmul(qv_, Rb[i][a][:], Qb[i][a][:], start=True, stop=True),
                         reads=B("Q%d_%d" % (i, a), "R%d_%d" % (i, a)), writes=qb_, kind="f32")
                yield
                S.op("act", lambda e: e.copy(Rb[i][b_][:], rv), reads=rb_, writes=B("R%d_%d" % (i, b_)))
                if k < nlev - 1:
                    S.op("act", lambda e: e.copy(Qb[i][b_][:], qv_), reads=qb_, writes=B("Q%d_%d" % (i, b_)))
                S.op("pe", lambda e: e.matmul(xv, Rb[i][b_][:], Xb[i][:], start=True, stop=True),
                     reads=B("R%d_%d" % (i, b_), "X%d" % i), writes=xb_, kind="f32")
                yield
                S.op("dve", lambda e: e.tensor_tensor(Xb[i][:], xv, Xb[i][:], op=ALU.add), reads=xb_ + B("X%d" % i), writes=B("X%d" % i))
            S.op("act", lambda e: e.copy(Xbf[i][:], Xb[i][:]), reads=B("X%d" % i), writes=B("Xbf%d" % i))
            wv, wb = PS(bA, 2)
            S.op("pe", lambda e: e.matmul(wv, KBG[:, h, :], Xbf[i][:], start=True, stop=True), reads=B("KBG", "Xbf%d" % i), writes=wb, kind="bf")
            yield
            S.op("act", lambda e: e.activation(wTn[i][:], wv, AF.Identity, scale=-1.0), reads=wb, writes=B("wTn%d" % i))
            vv, vb = PS(bB, 1)
            qsv, qsb = PS(bB, 2)
            if not sample:
                S.op("pe", lambda e: e.matmul(vv, Xbf[i][:], VB[:, h, :], start=True, stop=(not have_state)), reads=B("Xbf%d" % i, "VB"), writes=vb, kind="bf")
                if have_state:
                    S.op("pe", lambda e: e.matmul(vv, wTn[i][:], Sbf[:, h, :], start=False, stop=True), reads=B("wTn%d" % i, "S%d" % h), writes=vb, kind="bf")
                    S.op("pe", lambda e: e.matmul(qsv, QT[:, h, :], Sbf[:, h, :], start=True, stop=True), reads=B("QT%d" % h, "S%d" % h), writes=qsb, kind="bf")
            else:
                t1v, t1b = PS(bA, 0)
                t2v, t2b = PS(bA, 1)
                for q4 in range(4):
                    j = q4 % 2
                    S.dma("sp", S0v[j], sg4[:, q4 * 4:(q4 + 1) * 4, h, :], writes=B(S0n[j]))
                    S.op("act", lambda e, j=j: e.copy(S0bf[:, j * 4:j * 4 + 4, :], S0v[j]), reads=B(S0n[j]), writes=B(*["S0bf_%d" % (j * 4 + k) for k in range(4)]))
                    for k in range(4):
                        s, s8 = q4 * 4 + k, j * 4 + k
                        S.op("pe", lambda e, s=s, s8=s8: e.matmul(t1v[:, s * 8:(s + 1) * 8], S0bf[:, s8, :], wTn[i][:, s * 8:(s + 1) * 8], start=True, stop=True),
                             reads=B("S0bf_%d" % s8, "wTn%d" % i), writes=t1b, kind="bf")
                        S.op("pe", lambda e, s=s, s8=s8: e.matmul(t2v[:, s * 8:(s + 1) * 8], S0bf[:, s8, :], QT[:, h, s * 8:(s + 1) * 8], start=True, stop=True),
                             reads=B("S0bf_%d" % s8, "QT%d" % h), writes=t2b, kind="bf")
                S.op("act", lambda e: e.copy(tsb[0][:], t1v), reads=t1b, writes=B("tsb0"))
                S.op("act", lambda e: e.copy(tsb[1][:], t2v), reads=t2b, writes=B("tsb1"))
                S.op("pe", lambda e: e.matmul(vv, Xbf[i][:], VB[:, h, :], start=True, stop=False), reads=B("Xbf%d" % i, "VB"), writes=vb, kind="bf")
                S.op("pe", lambda e: e.matmul(vv, tsb[0][:], identb[:], start=False, stop=True), reads=B("tsb0", "identb"), writes=vb, kind="bf")
                S.op("pe", lambda e: e.matmul(qsv, tsb[1][:], identb[:], start=True, stop=True), reads=B("tsb1", "identb"), writes=qsb, kind="bf")
            yield
            S.op("act", lambda e: e.copy(vnew[i][:], vv), reads=vb, writes=B("vnew%d" % i))
            av, ab = PS(bA, 3)
            S.op("pe", lambda e: e.matmul(av, AqT[i][:], vnew[i][:], start=True, stop=True), reads=B("AqT%d" % i, "vnew%d" % i), writes=ab, kind="bf")
            if not sample:
                suv, sub = PS(6, i)
                S.op("pe", lambda e: e.matmul(suv, KD[:, h, :], vnew[i][:], start=True, stop=True), reads=B("KD", "vnew%d" % i), writes=sub, kind="bf")
            yield
            ofpv = Dl[i]
            avsv = Du[i]
            if have_state:
                S.op("act", lambda e: e.copy(avsv[:], av), reads=ab, writes=B("Du%d" % i))
                S.op("dve", lambda e: e.scalar_tensor_tensor(ofpv[:], qsv, eG[:, h:h + 1], avsv[:], op0=ALU.mult, op1=ALU.add),
                     reads=qsb + B("cum", "Du%d" % i), writes=B("Dl%d" % i))
            else:
                S.op("act", lambda e: e.copy(ofpv[:], av), reads=ab, writes=B("Dl%d" % i))
            if not sample:
                if have_state:
                    S.op("dve", lambda e: e.scalar_tensor_tensor(Sst[:, h, :], Sst[:, h, :], eGl[:, h:h + 1], suv, op0=ALU.mult, op1=ALU.add),
                         reads=sub + B("cum", "S%d" % h), writes=B("S%d" % h))
                else:
                    S.op("dve", lambda e: e.tensor_copy(Sst[:, h, :], suv), reads=sub, writes=B("S%d" % h))
                S.op("act", lambda e: e.copy(Sbf[:, h, :], Sst[:, h, :]), reads=B("S%d" % h), writes=B("S%d" % h))
                if t == NT - 2:
                    S.dma("sp", sp_o[h * 128:(h + 1) * 128, :], Sst[:, h, :], reads=B("S%d" % h))
            S.op("dve", lambda e: e.memset(gat[:, 36 + h:37 + h], 0.0), reads=B("gat"), writes=B("gat"))
            S.op("act", lambda e: e.activation(sqb[0][:, 0, :], ofpv[:], AF.Square, accum_out=gat[:, 36 + h:37 + h]), reads=B("Dl%d" % i, "gat"), writes=B("sqb0", "gat"))
            S.op("act", lambda e: e.activation(gat[:, 42 + h:43 + h], gat[:, 36 + h:37 + h], AF.Ln, bias=epsc[:, 0:1], scale=1.0 / 128), reads=B("gat", "epsc"), writes=B("gat"))
            S.op("act", lambda e: e.activation(gat[:, 42 + h:43 + h], gat[:, 42 + h:43 + h], AF.Exp, scale=-0.5), reads=B("gat"), writes=B("gat"))
            S.op("dve", lambda e: e.scalar_tensor_tensor(onb[i][:], ofpv[:], gat[:, 42 + h:43 + h], ogb[:], op0=ALU.mult, op1=ALU.mult),
                 reads=B("Dl%d" % i, "gat", "ogb"), writes=B("onb%d" % i))
            otv, otb = PSB(3, i, 1)
            S.op("pe", lambda e: e.transpose(otv[:, 0:128], onb[i][:], identb[:]), reads=B("onb%d" % i, "identb"), writes=otb, kind="tr")
            yield
            S.op("dve", lambda e: e.tensor_tensor(sgT[:, h, :], otv[:, 0:128], sgT[:, h, :], op=ALU.mult), reads=otb + B("sgT"), writes=B("sgT"))
            if sample:
                for q4 in range(4):
                    j = q4 % 2
                    S.dma("sp", S0v[j], sg4[:, q4 * 4:(q4 + 1) * 4, h, :], writes=B(S0n[j]))
                    def mk_mask(k):
                        s = q4 * 4 + k
                        jj = s % 2
                        S.op("dve", lambda e: e.tensor_scalar(kdm[jj][:], KD[:, h, :], smask[:, s:s + 1], None, op0=ALU.mult),
                             reads=B("KD", "smask"), writes=B("kdm%d" % jj))
                        suv, sub = PS((bA, bB)[jj], 0)
                        S.op("pe", lambda e: e.matmul(suv, kdm[jj][:], vnew[i][:], start=True, stop=True), reads=B("kdm%d" % jj, "vnew%d" % i), writes=sub, kind="bf")
                        return suv, sub

                    def fin(k, suv, sub):
                        s = q4 * 4 + k
                        S.op("dve", lambda e: e.scalar_tensor_tensor(Snw4[j][:, k, :], S0v[j][:, k, :], eGls[:, s, h:h + 1], suv, op0=ALU.mult, op1=ALU.add),
                             reads=sub + B("cum", S0n[j]), writes=B("stg%d" % j))

                    pend = mk_mask(0)
                    for k in range(4):
                        nxt = mk_mask(k + 1) if k + 1 < 4 else None
                        fin(k, *pend)
                        pend = nxt
                    S.dma("sp", ss4[:, q4 * 4:(q4 + 1) * 4, h, :], Snw4[j], reads=B("stg%d" % j))
                    yield

        STAG = 0 if sample else 5
        slots = [None, None, None]
        steps = [0, 0, 0]
        nxt_h = 0
        tick = 0
        while nxt_h < 6 or any(g is not None for g in slots):
            for i in range(3):
                if slots[i] is None and nxt_h < 6 and tick >= nxt_h * STAG:
                    slots[i] = head_gen(nxt_h, i)
                    nxt_h += 1
                if slots[i] is not None:
                    try:
                        next(slots[i])
                    except StopIteration:
                        slots[i] = None
            tick += 1
        if _SUB == 6:
            return
        attention_and_out(0, t)

    def layer1_tile(t):
        sample = (t == NT - 1)
        norm_transpose(t, NT + t, 8)
        for c in range(8):
            pv, pbf = proj_chunk(2304 + c * 128)
            S.op("act", lambda e, c=c, pv=pv: e.activation(sgT[:, c, :], pv, AF.Silu), reads=pbf, writes=B("sgT"))
        for c in range(2):
            pv, pbf = proj_chunk(2304 + 1024 + c * 128)
            S.op("act", lambda e, c=c, pv=pv: e.copy(xqT[:, c, :], pv), reads=pbf, writes=B("xqT"))
        def l1A(c):
            i = c % 2
            pv, pbf = proj_chunk(768 + c * 128)
            S.op("act", lambda e: e.copy(sil[i][:], pv), reads=pbf, writes=B("sil%d" % i))
            pv2, pbf2 = proj_chunk(1536 + c * 128)
            conv_setup_hist(1, t, c, i, 2)
            cur, taps, last = xp_views(t, i, 2)
            S.op("dve", lambda e: e.tensor_tensor(cur, as3(pv2, t), as3(sil[i][:], t), op=ALU.mult), reads=pbf2 + B("sil%d" % i), writes=B("xp%d" % i))
            if not sample:
                S.op("pool", lambda e: e.tensor_copy(hist2[:, c, :], last), reads=B("xp%d" % i), writes=B("hist2"))
            acc = as3(cacc[i][:], t)
            S.op("act", lambda e: e.activation(acc, taps[2], AF.Identity, scale=cwb[:, c, 2:3]), reads=B("xp%d" % i, "cwb"), writes=B("cacc%d" % i))

        def l1B(c):
            i = c % 2
            cur, taps, last = xp_views(t, i, 2)
            acc = as3(cacc[i][:], t)
            for j in range(2):
                S.op("dve", lambda e, j=j: e.scalar_tensor_tensor(acc, taps[j], cwb[:, c, j:j + 1], acc, op0=ALU.mult, op1=ALU.add),
                     reads=B("xp%d" % i, "cwb", "cacc%d" % i), writes=B("cacc%d" % i))
            pv3, pbf3 = proj_chunk(c * 128)
            S.op("dve", lambda e: e.tensor_tensor(cacc[i][:], pv3, cacc[i][:], op=ALU.mult), reads=pbf3 + B("cacc%d" % i), writes=B("cacc%d" % i))
            S.op("dve", lambda e: e.tensor_tensor(sgT[:, c, :], cacc[i][:], sgT[:, c, :], op=ALU.mult), reads=B("cacc%d" % i, "sgT"), writes=B("sgT"))

        l1A(0)
        for c in range(6):
            if c + 1 < 6:
                l1A(c + 1)
            l1B(c)
        if t >= NT - 2:
            for blk in range(2):
                c0 = blk * 384
                pv, pbf = PS(7, 0, 4)
                pw, pwb = PS(6, 0, 4)
                for kc in range(8):
                    S.op("pe", lambda e, kc=kc, pv=pv, c0=c0: e.matmul(pv[:, 0:384], hT[:, kc, :], win[:, kc, 768 + c0:768 + c0 + 384], start=(kc == 0), stop=(kc == 7)),
                         reads=B("hT", "win%d" % kc), writes=pbf)
                for kc in range(8):
                    S.op("pe", lambda e, kc=kc, pw=pw, c0=c0: e.matmul(pw[:, 0:384], hT[:, kc, :], win[:, kc, 1536 + c0:1536 + c0 + 384], start=(kc == 0), stop=(kc == 7)),
                         reads=B("hT", "win%d" % kc), writes=pwb)
                st = stg[blk % 2]
                S.op("act", lambda e, st=st, pv=pv: e.copy(st[:, 0:384], pv[:, 0:384]), reads=pbf, writes=B("stg%d" % (blk % 2)))
                S.op("dve", lambda e, st=st, pw=pw: e.tensor_tensor(st[:, 0:384], pw[:, 0:384], st[:, 0:384], op=ALU.mult), reads=pwb + B("stg%d" % (blk % 2)), writes=B("stg%d" % (blk % 2)))
                if not sample:
                    S.dma("sp", scp_o[0:2, c0:c0 + 384], st[126:128, 0:384], reads=B("stg%d" % (blk % 2)))
                else:
                    for s in range(16):
                        S.dma("sp", scs_o[s * 2:(s + 1) * 2, c0:c0 + 384], st[s * 8 + 6:s * 8 + 8, 0:384], reads=B("stg%d" % (blk % 2)))
        attention_and_out(1, t)

    order = list(range(NT))
    if stage >= 2:
        for t in order[:(stage - 1) if stage < 10 else NT]:
            layer0_tile(t)
    if stage >= 20:
        load_wout(1)
        S.dma("sp", gfin, gfin_d[:, :], writes=B("KBG", "KD", "VB"))
        for t in order[:(stage - 19) if stage < 30 else NT]:
            layer1_tile(t)
    S.finish()
    es.close()
    return nc, S


_CACHE = {}
_STAGE = 99
_SUB = 99


def _consts():
    i = np.arange(128)
    same = (i[:, None] // 8) == (i[None, :] // 8)
    ident = np.eye(128)
    ones = np.ones((128, 128))
    U = (i[:, None] <= i[None, :])
    SL = (i[:, None] > i[None, :])
    ML = (i[:, None] > i[None, :])
    MU = (i[:, None] <= i[None, :])
    mats = [ident, ones, U, SL, U & same, SL & same]
    cst = np.stack([m.astype(np.float32) for m in mats], axis=1).reshape(128, 6 * 128)
    smask = ((i[:, None] // 8) == np.arange(16)[None, :]).astype(np.float32)
    return np.ascontiguousarray(cst), np.ascontiguousarray(smask)


def kernel(x_prompt, x_sample, mem_prompt, state_gdn, state_gdn_conv, state_sconv,
           cache_mem_k, cache_mem_v, norm_g, w_in_a, conv_w_a, a_log, dt_bias, o_norm_g,
           w_in_b, conv_w_b, mem_norm_g, w_mem_kv, w_out, final_norm_g):
    f = lambda a: np.ascontiguousarray(np.asarray(a, dtype=np.float32))
    if "nc" not in _CACHE:
        _CACHE["nc"] = build_program(_STAGE)[0]
    nc = _CACHE["nc"]
    cst, smask = _consts()
    x_prompt, x_sample, mem_prompt = f(x_prompt), f(x_sample), f(mem_prompt)
    state_gdn, state_gdn_conv, state_sconv = f(state_gdn), f(state_gdn_conv), f(state_sconv)
    cache_mem_k, cache_mem_v = f(cache_mem_k), f(cache_mem_v)
    gfm = np.concatenate([f(norm_g).reshape(2, 8, 128), f(mem_norm_g).reshape(1, 8, 128)], axis=0)
    gfm = np.ascontiguousarray(gfm.transpose(2, 0, 1).reshape(128, 24))
    cwa = np.ascontiguousarray(f(conv_w_a)[0].reshape(4, 18, 128).transpose(2, 1, 0).reshape(128, 72))
    cwb = np.ascontiguousarray(f(conv_w_b)[0].reshape(3, 6, 128).transpose(2, 1, 0).reshape(128, 18))
    small = np.ascontiguousarray(np.broadcast_to(np.concatenate([f(a_log)[0], f(dt_bias)[0]])[None, :], (128, 12)))
    ogb = np.ascontiguousarray(np.broadcast_to(f(o_norm_g)[0][None, :], (128, 128)))
    gfin = np.ascontiguousarray(np.broadcast_to(f(final_norm_g)[None, :], (128, 1024)))
    shared = {
        "wina": f(w_in_a)[0], "winb": f(w_in_b)[0], "wout": f(w_out).reshape(2048, 1024),
        "wkv": f(w_mem_kv).reshape(2048, 512), "gfm": gfm, "cwa": cwa, "cwb": cwb, "small": small,
        "ogb": ogb, "gfin": gfin, "cst": cst, "smask": smask,
    }
    in_maps = []
    for c in range(8):
        sl = slice(c * 16, (c + 1) * 16)
        m = dict(shared)
        m["x"] = np.ascontiguousarray(np.concatenate([x_prompt[c], x_sample[sl].reshape(128, 1024)], axis=0))
        m["mem"] = mem_prompt[c]
        m["sgdn"] = np.ascontiguousarray(state_gdn[0, sl].reshape(16 * 6 * 128, 128))
        m["sgc"] = np.ascontiguousarray(state_gdn_conv[0, sl].reshape(48, 2304))
        m["ssc"] = np.ascontiguousarray(state_sconv[0, sl].reshape(32, 768))
        m["cmk"] = np.ascontiguousarray(cache_mem_k[:, sl].reshape(2 * 16 * 256, 256))
        m["cmv"] = np.ascontiguousarray(cache_mem_v[:, sl].reshape(2 * 16 * 256, 256))
        in_maps.append(m)
    res = run_bass_kernel_spmd(nc, in_maps, core_ids=list(range(8)))
    R = res.results
    y = np.stack([r["y"] for r in R])
    y_prompt = np.ascontiguousarray(y[:, :2048])
    y_sample = np.ascontiguousarray(y[:, 2048:].reshape(128, 8, 1024))
    S_p = np.stack([r["sp_o"].reshape(6, 128, 128) for r in R])[None]
    gc_p = np.stack([r["gcp_o"] for r in R])[None]
    sc_p = np.stack([r["scp_o"] for r in R])[None]
    mk_p = np.stack([r["mkp_o"].reshape(2, 256, 4, 64) for r in R], axis=1)
    mv_p = np.stack([r["mvp_o"].reshape(2, 256, 4, 64) for r in R], axis=1)
    S_s = np.concatenate([r["ss_o"].reshape(16, 6, 128, 128) for r in R])[None]
    gc_s = np.concatenate([r["gcs_o"].reshape(16, 3, 2304) for r in R])[None]
    sc_s = np.concatenate([r["scs_o"].reshape(16, 2, 768) for r in R])[None]
    outs = (y_prompt, y_sample, S_p, gc_p, sc_p, mk_p, mv_p, S_s, gc_s, sc_s)
    return tuple(np.ascontiguousarray(o, dtype=np.float32) for o in outs)
```

```python
import numpy as np
from contextlib import ExitStack
import concourse.bass as bass
import concourse.mybir as mybir
from concourse.bass_utils import run_bass_kernel_spmd

F32 = mybir.dt.float32
BF16 = mybir.dt.bfloat16
AF = mybir.ActivationFunctionType
ALU = mybir.AluOpType
AX = mybir.AxisListType

NT = 17
EPS = 1e-6
CA = 3596
CB = 3584


class Buf:
    __slots__ = ("name", "last_w", "readers", "excl")

    def __init__(self, name, excl=False):
        self.name = name
        self.last_w = None
        self.readers = {}
        self.excl = excl


class Sched:
    def __init__(self, nc, es, n_dma_sems=(24, 4, 12)):
        self.nc = nc
        self.eng = {"pe": nc.tensor, "act": nc.scalar, "dve": nc.vector, "pool": nc.gpsimd, "sp": nc.sync}
        self.sem, self.cnt, self.seen = {}, {}, {}
        for k in self.eng:
            self.sem[k] = es.enter_context(nc.semaphore("s_" + k))
            self.cnt[k] = 0
            self.seen[k] = {}
        self.dma_sems, self.dma_rr = {}, {}
        for q, n in zip(("sp", "act", "pool"), n_dma_sems):
            self.dma_sems[q] = []
            self.dma_rr[q] = 0
            for i in range(n):
                key = "d%s%d" % (q, i)
                self.sem[key] = es.enter_context(nc.semaphore("s_" + key))
                self.cnt[key] = 0
                self.dma_sems[q].append(key)
        self.n_inst = 0
        self.n_wait = 0
        self.pe_last = None
        self.sep = None

    def _wait(self, e, deps):
        best = {}
        for (k, c) in deps:
            if c > best.get(k, 0):
                best[k] = c
        for k, c in best.items():
            if self.seen[e].get(k, 0) >= c:
                continue
            self.eng[e].wait_ge(self.sem[k], c)
            self.seen[e][k] = c
            self.n_wait += 1

    @staticmethod
    def _deps(reads, writes):
        deps = []
        for b in reads:
            if b.last_w is not None:
                deps.append(b.last_w)
        for b in writes:
            if b.last_w is not None:
                deps.append(b.last_w)
            deps.extend(b.readers.items())
        return deps

    def op(self, e, fn, reads=(), writes=(), kind=None):
        if e == "pe":
            if kind == "bf" and self.pe_last == "f32" and self.sep is not None:
                self.pe_last = "tr"
                self.sep()
            if kind is not None:
                self.pe_last = kind
        ex = [b for b in reads if b.excl]
        if ex:
            writes = list(writes) + [b for b in ex if b not in writes]
            reads = [b for b in reads if not b.excl]
        deps = self._deps(reads, writes)
        if e == "pe":
            deps = [d for d in deps if d[0] != "pe"]
        self._wait(e, deps)
        ins = fn(self.eng[e])
        ins.then_inc(self.sem[e], 1)
        self.cnt[e] += 1
        c = self.cnt[e]
        for b in writes:
            b.last_w = (e, c)
            b.readers = {}
        for b in reads:
            if b not in writes:
                b.readers[e] = c
        self.n_inst += 1
        return ins

    def dma(self, q, out, in_, reads=(), writes=(), **kw):
        key = self.dma_sems[q][self.dma_rr[q] % len(self.dma_sems[q])]
        self.dma_rr[q] += 1
        deps = self._deps(reads, writes)
        if self.cnt[key] > 0:
            deps.append((key, self.cnt[key]))
        self._wait(q, deps)
        ins = self.eng[q].dma_start(out=out, in_=in_, **kw)
        ins.then_inc(self.sem[key], 16)
        self.cnt[key] += 16
        c = self.cnt[key]
        for b in writes:
            b.last_w = (key, c)
            b.readers = {}
        for b in reads:
            b.readers[key] = c
        self.n_inst += 1
        return ins

    def finish(self):
        deps = [(k, c) for k, c in self.cnt.items() if c > 0]
        for e in ("sp", "act", "pool", "dve", "pe"):
            self._wait(e, [d for d in deps if d[0] != e])


def build_program(stage=99):
    nc = bass.Bass("TRN2", target_bir_lowering=False)

    def din(name, shape):
        return nc.dram_tensor(name, list(shape), F32, kind="ExternalInput").ap()

    def dout(name, shape):
        return nc.dram_tensor(name, list(shape), F32, kind="ExternalOutput").ap()

    x_d = din("x", [NT * 128, 1024])
    mem_d = din("mem", [256, 1024])
    sgdn_d = din("sgdn", [16 * 6 * 128, 128])
    sgc_d = din("sgc", [48, 2304])
    ssc_d = din("ssc", [32, 768])
    cmk_d = din("cmk", [2 * 16 * 256, 256])
    cmv_d = din("cmv", [2 * 16 * 256, 256])
    wina_d = din("wina", [1024, CA])
    winb_d = din("winb", [1024, CB])
    wout_d = din("wout", [2048, 1024])
    wkv_d = din("wkv", [2048, 512])
    gfm_d = din("gfm", [128, 24])
    cwa_d = din("cwa", [128, 18 * 4])
    cwb_d = din("cwb", [128, 6 * 3])
    small_d = din("small", [128, 12])
    ogb_d = din("ogb", [128, 128])
    gfin_d = din("gfin", [128, 1024])
    cst_d = din("cst", [128, 6 * 128])
    smask_d = din("smask", [128, 16])

    y_o = dout("y", [NT * 128, 1024])
    sp_o = dout("sp_o", [6 * 128, 128])
    gcp_o = dout("gcp_o", [3, 2304])
    scp_o = dout("scp_o", [2, 768])
    mkp_o = dout("mkp_o", [512, 256])
    mvp_o = dout("mvp_o", [512, 256])
    ss_o = dout("ss_o", [16 * 6 * 128, 128])
    gcs_o = dout("gcs_o", [48, 2304])
    scs_o = dout("scs_o", [32, 768])

    es = ExitStack()
    S = Sched(nc, es)
    bufs = {}

    def sb(name, shape, dt=F32):
        t = es.enter_context(nc.sbuf_tensor("sb_" + name, list(shape), dt))
        bufs[name] = Buf(name)
        return t

    def B(*names):
        return [bufs[n] for n in names]

    def nb(name):
        bufs[name] = Buf(name)
        return bufs[name]

    xres = sb("xres", [128, NT, 1024])
    for t in range(NT):
        nb("x%d" % t)
    win = sb("win", [128, 8, CA], BF16)
    for k in range(8):
        nb("win%d" % k)
    wout = sb("wout", [128, 8, 1024], BF16)
    cst = sb("cst", [128, 6, 128])
    identb = sb("identb", [128, 128], BF16)
    onesb = sb("onesb", [128, 128], BF16)
    smask = sb("smask", [128, 16])
    gfm = sb("gfm", [128, 24])
    cwa = sb("cwa", [128, 18, 4])
    cwb = sb("cwb", [128, 6, 3])
    small = sb("small", [128, 12])
    nA = sb("nA", [128, 6])
    ogb = sb("ogb", [128, 128])
    epsc = sb("epsc", [128, 2])
    rstd = sb("rstd", [128, 3 * NT + 2])
    for i in range(3 * NT + 2):
        nb("rstd_%d" % i)
    ssq = sb("ssq", [128, 3 * NT + 2])
    mkT = sb("mkT", [128, 2, 2, 256], BF16)
    mvp = sb("mvp", [128, 2, 2, 256], BF16)
    hb = sb("hb", [128, 1024], BF16)
    hT = sb("hT", [128, 8, 128], BF16)
    xp = [sb("xp%d" % i, [128, 16 * 11]) for i in range(2)]
    hist = sb("hist", [128, 18, 3])
    hist2 = sb("hist2", [128, 6, 2])
    cacc = [sb("cacc%d" % i, [128, 128]) for i in range(2)]
    sil = [sb("sil%d" % i, [128, 128]) for i in range(2)]
    sqb = [sb("sqb%d" % i, [128, 3, 128], BF16) for i in range(2)]
    QKT = sb("QKT", [128, 12, 128], BF16)
    QT, KT = QKT[:, 0:6], QKT[:, 6:12]
    VT = sb("VT", [128, 6, 128], BF16)
    for h in range(6):
        nb("QT%d" % h), nb("KT%d" % h), nb("VT%d" % h)
    sgT = sb("sgT", [128, 8, 128], BF16)
    xqT = sb("xqT", [128, 2, 128], BF16)
    gat = sb("gat", [128, 48])
    cum = sb("cum", [128, 18 + 96])
    kkv3 = sb("kkv3", [128, 3, 6, 128], BF16)
    KBG, KD, VB = kkv3[:, 0], kkv3[:, 1], kkv3[:, 2]
    nb("KBG"), nb("KD"), nb("VB")
    for h6 in range(6):
        nb("gatn%d" % h6)
    gfin = kkv3[:].rearrange("p a b c -> p (a b c)")[:, 0:2048].bitcast(F32)
    Dl = [sb("Dl%d" % i, [128, 128]) for i in range(3)]
    Du = [sb("Du%d" % i, [128, 128]) for i in range(3)]
    AqT = [sb("AqT%d" % i, [128, 128], BF16) for i in range(3)]
    Rb = [[sb("R%d_%d" % (i, j), [128, 128]) for j in range(2)] for i in range(3)]
    Qb = [[sb("Q%d_%d" % (i, j), [128, 128]) for j in range(2)] for i in range(3)]
    Xb = [sb("X%d" % i, [128, 128]) for i in range(3)]
    Xbf = [sb("Xbf%d" % i, [128, 128], BF16) for i in range(3)]
    wTn = [sb("wTn%d" % i, [128, 128], BF16) for i in range(3)]
    vnew = [sb("vnew%d" % i, [128, 128], BF16) for i in range(3)]
    onb = [sb("onb%d" % i, [128, 128], BF16) for i in range(3)]
    Sst = sb("Sst", [128, 6, 128])
    Sbf = sb("Sbf", [128, 6, 128], BF16)
    for h in range(6):
        nb("S%d" % h)
    Pb = sb("Pb", [128, 4, 256], BF16)
    PTb = hb[:].rearrange("p (h m t) -> p h m t", h=4, m=2)
    sm = sb("sm", [128, 16])
    stg = [sb("stg%d" % i, [128, 512]) for i in range(2)]
    S0bf = sb("S0bf", [128, 8, 128], BF16)
    S0f = [sb("S0f0", [128, 128]), sil[0]]
    Snw = [sb("Snw0", [128, 128]), sil[1]]
    bufs["S0f1"] = bufs["sil0"]; bufs["Snw1"] = bufs["sil1"]
    for k8 in range(8):
        nb("S0bf_%d" % k8)
    tsb = [sb("tsb%d" % i, [128, 128], BF16) for i in range(2)]
    kdm = [sb("kdm%d" % i, [128, 128], BF16) for i in range(2)]
    xqm = [sb("xqm%d" % i, [128, 2, 128], BF16) for i in range(2)]
    ckv = [sb("ckv%d" % i, [128, 2, 2, 256], BF16) for i in range(2)]
    mkTs = [sb("mkTs%d" % i, [128, 2, 256], BF16) for i in range(2)]
    hst = [sb("hst%d" % i, [48, 128]) for i in range(2)]

    S0v = [hb[:].bitcast(F32).rearrange("p (s e) -> p s e", s=4),
           Pb[:].rearrange("p a b -> p (a b)").bitcast(F32).rearrange("p (s e) -> p s e", s=4)]
    S0n = ["hb", "Pb"]
    Snw4 = [stg[0][:].rearrange("p (s e) -> p s e", s=4), stg[1][:].rearrange("p (s e) -> p s e", s=4)]
    stgKV = [stg[0][:].rearrange("p (m c) -> p m c", m=2), stg[1][:].rearrange("p (m c) -> p m c", m=2)]
    sg4 = sgdn_d.rearrange("(s h d) e -> d s h e", s=16, h=6)
    ss4 = ss_o.rearrange("(s h d) e -> d s h e", s=16, h=6)

    pbank = [es.enter_context(nc.psum_tensor("pb%d" % i, [128, 512], F32)) for i in range(8)]
    for i in range(8):
        bk = nb("pbank%d" % i)
        bk.excl = True
        for j in range(4):
            bufs["p%d_%d" % (i, j)] = bk

    def PS(bank, slot, n=1):
        return pbank[bank][:, slot * 128:(slot + n) * 128], B("p%d_0" % bank)

    def PSB(bank, slot, n=1):
        v = pbank[bank][:, slot * 128:(slot + n) * 128].bitcast(BF16)
        return v, B("p%d_0" % bank)

    I_F, ONES_F, U_, SL_, US_, SLS_ = range(6)
    ML_, MU_, MLS_, MUS_ = SL_, U_, SLS_, US_

    def C(i):
        return cst[:, i, :]

    S.dma("sp", cst[:].rearrange("p a b -> p (a b)"), cst_d[:, :], writes=B("cst"))
    S.dma("sp", smask[:], smask_d[:, :], writes=B("smask"))
    S.dma("sp", gfm[:], gfm_d[:, :], writes=B("gfm"))
    S.dma("sp", cwa[:].rearrange("p a b -> p (a b)"), cwa_d[:, :], writes=B("cwa"))
    S.dma("sp", cwb[:].rearrange("p a b -> p (a b)"), cwb_d[:, :], writes=B("cwb"))
    S.dma("sp", small[:], small_d[:, :], writes=B("small"))
    S.dma("sp", ogb[:], ogb_d[:, :], writes=B("ogb"))
    S.op("dve", lambda e: e.tensor_copy(identb[:], C(I_F)), reads=B("cst"), writes=B("identb"))
    S.op("dve", lambda e: e.tensor_copy(onesb[:], C(ONES_F)), reads=B("cst"), writes=B("onesb"))
    S.op("dve", lambda e: e.memset(epsc[:, 0:1], EPS), writes=B("epsc"))
    S.op("dve", lambda e: e.memset(epsc[:, 1:2], 128.0 * EPS), writes=B("epsc"))
    S.op("dve", lambda e: e.memset(hist[:], 0.0), writes=B("hist"))
    S.op("dve", lambda e: e.memset(hist2[:], 0.0), writes=B("hist2"))
    S.op("dve", lambda e: e.memset(ssq[:], 0.0), writes=B("ssq"))
    S.op("dve", lambda e: e.memset(sm[:], 0.0), writes=B("sm"))
    S.op("dve", lambda e: e.memset(gat[:], 0.0), writes=B("gat"))
    S.op("dve", lambda e: e.memset(rstd[:], 0.0), writes=B(*["rstd_%d" % i for i in range(3 * NT + 2)]))
    S.op("act", lambda e: e.activation(nA[:], small[:, 0:6], AF.Exp), reads=B("small"), writes=B("nA"))
    S.op("dve", lambda e: e.tensor_scalar(nA[:], nA[:], -1.0, None, op0=ALU.mult), reads=B("nA"), writes=B("nA"))

    for l in range(2):
        for kc in range(8):
            S.dma("pool", wout[:, kc, l * 512:(l + 1) * 512], wkv_d[l * 1024 + kc * 128:l * 1024 + (kc + 1) * 128, :],
                  writes=B("wout"))
    for kc in range(8):
        S.dma("pool", win[:, kc, 0:CA], wina_d[kc * 128:(kc + 1) * 128, :], writes=B("win%d" % kc))
    for mt in range(2):
        S.dma("sp", xres[:, mt, :], mem_d[mt * 128:(mt + 1) * 128, :], writes=B("x%d" % mt))

    def rstd_from_ssq(col, n_feat):
        S.op("act", lambda e: e.activation(rstd[:, col:col + 1], ssq[:, col:col + 1], AF.Ln, bias=epsc[:, 0:1], scale=1.0 / n_feat),
             reads=B("ssq", "epsc"), writes=B("rstd_%d" % col))
        S.op("act", lambda e: e.activation(rstd[:, col:col + 1], rstd[:, col:col + 1], AF.Exp, scale=-0.5),
             reads=B("rstd_%d" % col), writes=B("rstd_%d" % col))

    def ssq_of_tile(t, col):
        S.op("act", lambda e: e.activation(hb[:], xres[:, t, :], AF.Square, accum_out=ssq[:, col:col + 1]),
             reads=B("x%d" % t, "ssq"), writes=B("hb", "ssq"))

    def norm_transpose(t, rcol, gcol0):
        S.op("dve", lambda e: e.tensor_scalar(hb[:], xres[:, t, :], rstd[:, rcol:rcol + 1], None, op0=ALU.mult),
             reads=B("x%d" % t, "rstd_%d" % rcol), writes=B("hb"))
        pv, pbf = PSB(0, 0, 4)
        pv3 = pv.rearrange("p (k t) -> p k t", k=8)
        for kc in range(8):
            S.op("pe", lambda e, kc=kc: e.transpose(pv3[:, kc, :], hb[:, kc * 128:(kc + 1) * 128], identb[:]),
                 reads=B("hb", "identb"), writes=pbf)
        g3 = gfm[:, gcol0:gcol0 + 8].unsqueeze(2).to_broadcast([128, 8, 128])
        S.op("dve", lambda e: e.tensor_tensor(hT[:], pv3, g3, op=ALU.mult), reads=pbf + B("gfm"), writes=B("hT"))

    def pe_sep():
        sv, sbk = PSB(3, 3, 1)
        S.op("pe", lambda e: e.transpose(sv[:, 0:128], identb[:], identb[:]), reads=B("identb"), writes=sbk, kind="tr")

    S.sep = pe_sep
    def run_interleaved(gens):
        gens = list(gens)
        while gens:
            for g in list(gens):
                try:
                    next(g)
                except StopIteration:
                    gens.remove(g)

    proj_rr = [0]

    def proj_chunk(col0, ncols=128):
        bank = (1, 7)[proj_rr[0] % 2]
        proj_rr[0] += 1
        pv, pbf = PS(bank, 0)
        for kc in range(8):
            S.op("pe", lambda e, kc=kc: e.matmul(pv[0:ncols, :], win[:, kc, col0:col0 + ncols], hT[:, kc, :], start=(kc == 0), stop=(kc == 7)),
                 reads=B("win%d" % kc, "hT"), writes=pbf)
        return pv, pbf

    for mt in range(2):
        ssq_of_tile(mt, 3 * NT + mt)
        rstd_from_ssq(3 * NT + mt, 1024)
    memT = kkv3[:].rearrange("p a b c -> p (a b c)")[:, 0:2048].rearrange("p (k m) -> p k m", k=8)
    MEMB = B("KBG", "KD", "VB")
    for mt in range(2):
        norm_transpose(mt, 3 * NT + mt, 16)
        S.op("act", lambda e, mt=mt: e.copy(memT[:, :, mt * 128:(mt + 1) * 128], hT[:]), reads=B("hT"), writes=MEMB)
    for l in range(2):
        for mt in range(2):
            pv, pbf = PS(2, 0, 4)
            for kc in range(8):
                S.op("pe", lambda e, kc=kc: e.matmul(pv, memT[:, kc, mt * 128:(mt + 1) * 128], wout[:, kc, l * 512:(l + 1) * 512], start=(kc == 0), stop=(kc == 7)),
                     reads=MEMB + B("wout"), writes=pbf)
            st = stg[(l * 2 + mt) % 2]
            sn = "stg%d" % ((l * 2 + mt) % 2)
            S.op("act", lambda e, st=st: e.copy(st[:], pv), reads=pbf, writes=B(sn))
            S.dma("sp", mkp_o[l * 256 + mt * 128:l * 256 + (mt + 1) * 128, :], st[:, 0:256], reads=B(sn))
            S.dma("sp", mvp_o[l * 256 + mt * 128:l * 256 + (mt + 1) * 128, :], st[:, 256:512], reads=B(sn))
            S.op("dve", lambda e, st=st: e.tensor_copy(mvp[:, l, mt, :], st[:, 256:512]), reads=B(sn), writes=B("mvp"))
        for c in range(2):
            pv, pbf = PS(3, 0, 2)
            for kc in range(8):
                S.op("pe", lambda e, kc=kc: e.matmul(pv, wout[:, kc, l * 512 + c * 128:l * 512 + (c + 1) * 128], memT[:, kc, :], start=(kc == 0), stop=(kc == 7)),
                     reads=MEMB + B("wout"), writes=pbf)
            S.op("act", lambda e, c=c: e.copy(mkT[:, l, c, :], pv), reads=pbf, writes=B("mkT"))

    for t in range(NT):
        S.dma("sp", xres[:, t, :], x_d[t * 128:(t + 1) * 128, :], writes=B("x%d" % t))

    def load_win(l):
        wd = wina_d if l == 0 else winb_d
        ncol = CA if l == 0 else CB
        for kc in range(8):
            S.dma("pool", win[:, kc, 0:ncol], wd[kc * 128:(kc + 1) * 128, :], writes=B("win%d" % kc))

    def load_wout(l):
        for kc in range(8):
            S.dma("pool", wout[:, kc, :], wout_d[l * 1024 + kc * 128:l * 1024 + (kc + 1) * 128, :], writes=B("wout"))

    def load_layer_weights(l):
        load_win(l)
        load_wout(l)

    load_wout(0)
    for t in range(NT):
        ssq_of_tile(t, t)
        rstd_from_ssq(t, 1024)

    def attention_and_out(l, t):
        sample = (t == NT - 1)
        scv, scb = [], []
        for bk in (3, 4, 5, 6):
            v_, b_ = PS(bk, 0, 2)
            scv.append(v_)
            scb.append(b_)
        if not sample:
            for h in range(4):
                po = (h % 2) * 64
                if _SUB == 700 and h != 0:
                    continue
                if _SUB == 701 and h != 1:
                    continue
                S.op("pe", lambda e, h=h, po=po: e.matmul(scv[h], xqT[po:po + 64, h // 2, :], mkT[po:po + 64, l, h // 2, :], start=True, stop=True),
                     reads=B("xqT", "mkT"), writes=scb[h])
        else:
            for s in range(16):
                i = s % 2
                r0 = (l * 16 + s) * 256
                S.dma("sp", stgKV[i], cmk_d[r0:r0 + 256, :].rearrange("(mt p) c -> p mt c", p=128), writes=B("stg%d" % i))
                S.op("act", lambda e, i=i: e.copy(ckv[i][:, 0, :, :], stgKV[i]), reads=B("stg%d" % i), writes=B("ckv%d" % i))
                tv, tb = PSB(0, 0, 2)
                tv3 = tv.rearrange("p (c m) -> p c m", c=2)
                for c in range(2):
                    for mt in range(2):
                        S.op("pe", lambda e, c=c, mt=mt, i=i: e.transpose(tv3[:, c, mt * 128:(mt + 1) * 128], ckv[i][:, 0, mt, c * 128:(c + 1) * 128], identb[:]),
                             reads=B("ckv%d" % i, "identb"), writes=tb)
                S.op("act", lambda e, i=i: e.copy(mkTs[i][:], tv3), reads=tb, writes=B("mkTs%d" % i))
                S.op("pool", lambda e, i=i: e.memset(xqm[i][:].rearrange("p a b -> p (a b)"), 0.0), writes=B("xqm%d" % i))
                S.op("pool", lambda e, i=i, s=s: e.tensor_copy(xqm[i][:, :, s * 8:(s + 1) * 8], xqT[:, :, s * 8:(s + 1) * 8]),
                     reads=B("xqT"), writes=B("xqm%d" % i))
                for h in range(4):
                    po = (h % 2) * 64
                    S.op("pe", lambda e, h=h, po=po, i=i: e.matmul(scv[h], xqm[i][po:po + 64, h // 2, :], mkTs[i][po:po + 64, h // 2, :], start=(s == 0), stop=(s == 15)),
                         reads=B("xqm%d" % i, "mkTs%d" % i), writes=scb[h])
        ovs = [PS(0, 0, 4), PS(1, 0, 4)]
        for half in range(2):
            for c in range(6):
                S.op("pe", lambda e, c=c, half=half: e.matmul(ovs[half][0], sgT[:, c, :], wout[:, c, half * 512:(half + 1) * 512], start=(c == 0), stop=False),
                     reads=B("sgT", "wout"), writes=ovs[half][1], kind="bf")
        for h in range(4):
            S.op("dve", lambda e, h=h: e.tensor_reduce(sm[:, h:h + 1], scv[h], axis=AX.X, op=ALU.max), reads=scb[h], writes=B("sm"))
        S.op("dve", lambda e: e.tensor_scalar(sm[:, 4:8], sm[:, 0:4], -0.125, None, op0=ALU.mult), reads=B("sm"), writes=B("sm"))
        S.op("dve", lambda e: e.memset(sm[:, 8:12], 0.0), writes=B("sm"))
        for h in range(4):
            S.op("act", lambda e, h=h: e.activation(Pb[:, h, :], scv[h], AF.Exp, bias=sm[:, 4 + h:5 + h], scale=0.125, accum_out=sm[:, 8 + h:9 + h]),
                 reads=scb[h] + B("sm"), writes=B("Pb", "sm"))
        S.op("dve", lambda e: e.reciprocal(sm[:, 12:16], sm[:, 8:12]), reads=B("sm"), writes=B("sm"))
        S.op("dve", lambda e: e.tensor_tensor(Pb[:], Pb[:], sm[:, 12:16].unsqueeze(2).to_broadcast([128, 4, 256]), op=ALU.mult),
             reads=B("Pb", "sm"), writes=B("Pb"))
        if _SUB == 71:
            return
        tv, tb = PSB(5, 0, 4)
        tv4 = tv.rearrange("p (h m t) -> p h m t", h=4, m=2)
        for h in range(4):
            for mt in range(2):
                S.op("pe", lambda e, h=h, mt=mt: e.transpose(tv4[:, h, mt, :], Pb[:, h, mt * 128:(mt + 1) * 128], identb[:]),
                     reads=B("Pb", "identb"), writes=tb)
        S.op("act", lambda e: e.copy(PTb, tv4), reads=tb, writes=B("hb"))
        if _SUB == 72:
            return
        xo_v, xo_b = PS(6, 0, 4)
        xo4 = xo_v.rearrange("p (c k t) -> p c k t", c=2, k=2)
        if not sample:
            for c in range(2):
                for hh in range(2):
                    h = 2 * c + hh
                    for mt in range(2):
                        S.op("pe", lambda e, h=h, mt=mt, c=c, hh=hh: e.matmul(xo4[:, c, hh, :], mvp[:, l, mt, c * 128:(c + 1) * 128], PTb[:, h, mt, :], start=(mt == 0), stop=(mt == 1)),
                             reads=B("mvp", "hb"), writes=xo_b)
        else:
            for s in range(16):
                i = s % 2
                r0 = (l * 16 + s) * 256
                S.dma("sp", stgKV[i], cmv_d[r0:r0 + 256, :].rearrange("(mt p) c -> p mt c", p=128), writes=B("stg%d" % i))
                S.op("act", lambda e, i=i: e.copy(ckv[i][:, 1, :, :], stgKV[i]), reads=B("stg%d" % i), writes=B("ckv%d" % i))
                for c in range(2):
                    for hh in range(2):
                        h = 2 * c + hh
                        for mt in range(2):
                            S.op("pe", lambda e, h=h, mt=mt, c=c, hh=hh, i=i, s=s: e.matmul(xo4[:, c, hh, s * 8:(s + 1) * 8], ckv[i][:, 1, mt, c * 128:(c + 1) * 128], PTb[:, h, mt, s * 8:(s + 1) * 8], start=(mt == 0), stop=(mt == 1)),
                                 reads=B("ckv%d" % i, "hb"), writes=xo_b)
        if _SUB == 73:
            return
        for c in range(2):
            for hh in range(2):
                po = hh * 64
                S.op("dve", lambda e, c=c, hh=hh, po=po: e.tensor_tensor(sgT[po:po + 64, 6 + c, :], xo4[po:po + 64, c, hh, :], sgT[po:po + 64, 6 + c, :], op=ALU.mult),
                     reads=xo_b + B("sgT"), writes=B("sgT"))
        if _SUB == 74:
            return
        for half in range(2):
            for c in (6, 7):
                S.op("pe", lambda e, c=c, half=half: e.matmul(ovs[half][0], sgT[:, c, :], wout[:, c, half * 512:(half + 1) * 512], start=False, stop=(c == 7)),
                     reads=B("sgT", "wout"), writes=ovs[half][1], kind="bf")
            S.op("dve", lambda e, half=half: e.tensor_tensor(xres[:, t, half * 512:(half + 1) * 512], xres[:, t, half * 512:(half + 1) * 512], ovs[half][0], op=ALU.add),
                 reads=ovs[half][1] + B("x%d" % t), writes=B("x%d" % t))
        if _SUB == 75:
            return
        col = (l + 1) * NT + t
        ssq_of_tile(t, col)
        rstd_from_ssq(col, 1024)
        if l == 1:
            st = stg[t % 2]
            for half in range(2):
                S.op("dve", lambda e, half=half, st=st: e.scalar_tensor_tensor(st[:], xres[:, t, half * 512:(half + 1) * 512], rstd[:, col:col + 1], gfin[:, half * 512:(half + 1) * 512], op0=ALU.mult, op1=ALU.mult),
                     reads=B("x%d" % t, "rstd_%d" % col, "KBG", "KD", "VB"), writes=B("stg%d" % (t % 2)))
                S.dma("sp", y_o[t * 128:(t + 1) * 128, half * 512:(half + 1) * 512], st[:], reads=B("stg%d" % (t % 2)))

    def conv_setup_hist(l, t, c, i, nh):
        sample = (t == NT - 1)
        hbuf = hist if l == 0 else hist2
        hname = "hist" if l == 0 else "hist2"
        if not sample:
            S.op("pool", lambda e: e.tensor_copy(xp[i][:, 0:nh], hbuf[:, c, :]), reads=B(hname), writes=B("xp%d" % i))
        else:
            srcd = sgc_d if l == 0 else ssc_d
            nr = 16 * nh
            S.dma("sp", hst[i][0:nr, :], srcd[0:nr, c * 128:(c + 1) * 128], writes=B("hst%d" % i))
            pv, pbf = PS(2, 2, 1)
            S.op("pe", lambda e: e.transpose(pv[:, 0:nr], hst[i][0:nr, :], C(I_F)[0:nr, 0:nr]),
                 reads=B("hst%d" % i, "cst"), writes=pbf)
            L = 8
            xp3 = xp[i][:, 0:16 * (nh + L)].rearrange("p (s k) -> p s k", s=16)
            S.op("act", lambda e: e.copy(xp3[:, :, 0:nh], pv[:, 0:nr].rearrange("p (s j) -> p s j", s=16)), reads=pbf, writes=B("xp%d" % i))

    def xp_views(t, i, nh):
        sample = (t == NT - 1)
        if not sample:
            cur = xp[i][:, nh:nh + 128]
            taps = [xp[i][:, j:j + 128] for j in range(nh + 1)]
            last = xp[i][:, 128:128 + nh]
            return cur, taps, last
        L = 8
        xp3 = xp[i][:, 0:16 * (nh + L)].rearrange("p (s k) -> p s k", s=16)
        cur = xp3[:, :, nh:nh + L]
        taps = [xp3[:, :, j:j + L] for j in range(nh + 1)]
        return cur, taps, None

    def as3(ap, t):
        if t == NT - 1:
            return ap.rearrange("p (s k) -> p s k", s=16)
        return ap

    def layer0_tile(t):
        sample = (t == NT - 1)
        norm_transpose(t, t, 0)
        for c in range(8):
            pv, pbf = proj_chunk(2316 + c * 128)
            S.op("act", lambda e, c=c, pv=pv: e.activation(sgT[:, c, :], pv, AF.Silu), reads=pbf, writes=B("sgT"))
        for c in range(2):
            pv, pbf = proj_chunk(2316 + 1024 + c * 128)
            S.op("act", lambda e, c=c, pv=pv: e.copy(xqT[:, c, :], pv), reads=pbf, writes=B("xqT"))
        if _SUB == 1:
            return
        def stA(c):
            i = c % 2
            pv, pbf = proj_chunk(c * 128)
            conv_setup_hist(0, t, c, i, 3)
            cur, taps, last = xp_views(t, i, 3)
            S.op("act", lambda e: e.copy(cur, as3(pv, t)), reads=pbf, writes=B("xp%d" % i))
            if not sample:
                S.op("pool", lambda e: e.tensor_copy(hist[:, c, :], last), reads=B("xp%d" % i), writes=B("hist"))
            acc = as3(cacc[i][:], t)
            S.op("act", lambda e: e.activation(acc, taps[3], AF.Identity, scale=cwa[:, c, 3:4]),
                 reads=B("xp%d" % i, "cwa"), writes=B("cacc%d" % i))

        def stB(c):
            i = c % 2
            cur, taps, last = xp_views(t, i, 3)
            acc = as3(cacc[i][:], t)
            for j in range(3):
                S.op("dve", lambda e, j=j: e.scalar_tensor_tensor(acc, taps[j], cwa[:, c, j:j + 1], acc, op0=ALU.mult, op1=ALU.add),
                     reads=B("xp%d" % i, "cwa", "cacc%d" % i), writes=B("cacc%d" % i))

        def stC(c):
            i = c % 2
            if c >= 12:
                dst, h, dn = VT, c - 12, "VT%d" % (c - 12)
            elif c < 6:
                dst, h, dn = QT, c, "QT%d" % c
            else:
                dst, h, dn = KT, c - 6, "KT%d" % (c - 6)
            S.op("act", lambda e: e.activation(dst[:, h, :], cacc[i][:], AF.Silu), reads=B("cacc%d" % i), writes=B(dn))

        stA(0)
        stB(0)
        for c in range(18):
            if c + 1 < 18:
                stA(c + 1)
            stC(c)
            if c + 1 < 18:
                stB(c + 1)
        gcol = gat[:, 24:30]
        Um, SLm = (US_, SLS_) if sample else (U_, SL_)
        eG, eGrev, eGl = cum[:, 0:6], cum[:, 6:12], cum[:, 12:18]
        eGls = cum[:, 18:114].rearrange("p (s h) -> p s h", s=16)

        def gen_norm():
            for g in range(4):
                i = g % 2
                isq = (g < 2)
                c0 = g * 3
                names = [("QT%d" % (c0 + k)) if isq else ("KT%d" % (c0 - 6 + k)) for k in range(3)]
                blk = QKT[:, c0:c0 + 3, :]
                S.op("act", lambda e: e.activation(sqb[i][:], blk, AF.Square), reads=B(*names), writes=B("sqb%d" % i))
                qv, qb = PS((2, 0)[i], 0, 3)
                S.op("pe", lambda e: e.matmul(qv, onesb[:], sqb[i][:].rearrange("p a b -> p (a b)"), start=True, stop=True), reads=B("onesb", "sqb%d" % i), writes=qb, kind="bf")
                yield
                S.op("act", lambda e: e.activation(qv, qv, AF.Ln, bias=epsc[:, 0:1], scale=1.0), reads=qb + B("epsc"), writes=qb)
                S.op("act", lambda e: e.activation(qv, qv, AF.Exp, scale=-0.5), reads=qb, writes=qb)
                qv3 = qv.rearrange("p (a b) -> p a b", a=3)
                if isq:
                    S.op("dve", lambda e: e.scalar_tensor_tensor(blk, blk, float(128.0 ** -0.5), qv3, op0=ALU.mult, op1=ALU.mult), reads=qb + B(*names), writes=B(*names))
                else:
                    S.op("dve", lambda e: e.tensor_tensor(blk, blk, qv3, op=ALU.mult), reads=qb + B(*names), writes=B(*names))
                yield

        def gen_gates():
            bav, bab = PS(3, 0)
            for kc in range(8):
                S.op("pe", lambda e, kc=kc: e.matmul(bav[:, 0:12], hT[:, kc, :], win[:, kc, 2304:2316], start=(kc == 0), stop=(kc == 7)),
                     reads=B("hT", "win%d" % kc), writes=bab, kind="bf")
            yield
            S.op("act", lambda e: e.activation(gat[:, 0:6], bav[:, 0:6], AF.Exp, scale=-1.0), reads=bab, writes=B("gat"))
            S.op("dve", lambda e: e.tensor_tensor(gat[:, 6:12], bav[:, 6:12], small[:, 6:12], op=ALU.add), reads=bab + B("small", "gat"), writes=B("gat"))
            S.op("act", lambda e: e.activation(gat[:, 6:12], gat[:, 6:12], AF.Exp), reads=B("gat"), writes=B("gat"))
            S.op("act", lambda e: e.activation(gat[:, 6:12], gat[:, 6:12], AF.Ln, bias=1.0), reads=B("gat"), writes=B("gat"))
            yield
            S.op("dve", lambda e: e.tensor_scalar(gat[:, 0:6], gat[:, 0:6], 1.0, None, op0=ALU.add), reads=B("gat"), writes=B("gat"))
            S.op("dve", lambda e: e.reciprocal(gat[:, 12:18], gat[:, 0:6]), reads=B("gat"), writes=B("gat"))
            S.op("dve", lambda e: e.tensor_scalar(gat[:, 18:24], gat[:, 12:18], -1.0, None, op0=ALU.mult), reads=B("gat"), writes=B("gat"))
            S.op("dve", lambda e: e.tensor_tensor(gat[:, 24:30], gat[:, 6:12], nA[:], op=ALU.mult), reads=B("gat", "nA"), writes=B("gat"))
            cv, cb = PS(3, 1)
            S.op("pe", lambda e: e.matmul(cv[:, 0:6], C(Um), gcol, start=True, stop=True), reads=B("cst", "gat"), writes=cb, kind="f32")
            S.op("pe", lambda e: e.matmul(cv[:, 6:12], C(SLm), gcol, start=True, stop=True), reads=B("cst", "gat"), writes=cb, kind="f32")
            if not sample:
                S.op("pe", lambda e: e.matmul(cv[:, 12:18], C(ONES_F), gcol, start=True, stop=True), reads=B("cst", "gat"), writes=cb, kind="f32")
                yield
                S.op("act", lambda e: e.activation(cum[:, 0:18], cv[:, 0:18], AF.Exp), reads=cb, writes=B("cum"))
            else:
                gm = cum[:, 18:114].rearrange("p (s h) -> p s h", s=16)
                S.op("dve", lambda e: e.tensor_tensor(gm, gcol.unsqueeze(1).to_broadcast([128, 16, 6]), smask[:].unsqueeze(2).to_broadcast([128, 16, 6]), op=ALU.mult),
                     reads=B("gat", "smask", "cum"), writes=B("cum"))
                cv2, cb2 = PS(3, 2)
                S.op("pe", lambda e: e.matmul(cv2[:, 0:96], C(ONES_F), cum[:, 18:114], start=True, stop=True), reads=B("cst", "cum"), writes=cb2, kind="f32")
                yield
                S.op("act", lambda e: e.activation(cum[:, 0:12], cv[:, 0:12], AF.Exp), reads=cb, writes=B("cum"))
                S.op("act", lambda e: e.activation(cum[:, 18:114], cv2[:, 0:96], AF.Exp), reads=cb2, writes=B("cum"))
            S.op("dve", lambda e: e.tensor_tensor(gat[:, 30:36], gat[:, 12:18], eG, op=ALU.mult), reads=B("gat", "cum"), writes=B("gat"))
            tv_, tb_ = PSB(7, 0, 3)
            tvv = tv_[:, 0:768].rearrange("p (h d) -> p h d", h=6)
            for h in range(6):
                S.op("pe", lambda e, h=h: e.transpose(tvv[:, h, :], VT[:, h, :], identb[:]), reads=B("VT%d" % h, "identb"), writes=tb_, kind="tr")
            yield
            S.op("dve", lambda e: e.tensor_tensor(VB, tvv, gat[:, 12:18].unsqueeze(2).to_broadcast([128, 6, 128]), op=ALU.mult), reads=tb_ + B("gat"), writes=B("VB"))

        run_interleaved([gen_norm(), gen_gates()])
        if _SUB == 2:
            return
        tv, tb = PSB(0, 0, 3)
        tv3 = tv.rearrange("p (h d) -> p h d", h=6)
        for h in range(6):
            S.op("pe", lambda e, h=h: e.transpose(tv3[:, h, :], KT[:, h, :], identb[:]), reads=B("KT%d" % h, "identb"), writes=tb)
        S.op("dve", lambda e: e.tensor_tensor(KBG, tv3, gat[:, 30:36].unsqueeze(2).to_broadcast([128, 6, 128]), op=ALU.mult), reads=tb + B("gat"), writes=B("KBG"))
        S.op("dve", lambda e: e.tensor_tensor(KD, tv3, eGrev.unsqueeze(2).to_broadcast([128, 6, 128]), op=ALU.mult), reads=tb + B("cum"), writes=B("KD"))
        if _SUB == 3:
            return
        if t >= NT - 2:
            for blk in range(5):
                c0 = blk * 512
                w_ = min(512, 2304 - c0)
                pv, pbf = PS(7, 0, 4)
                for kc in range(8):
                    S.op("pe", lambda e, kc=kc, pv=pv, c0=c0, w_=w_: e.matmul(pv[:, 0:w_], hT[:, kc, :], win[:, kc, c0:c0 + w_], start=(kc == 0), stop=(kc == 7)),
                         reads=B("hT", "win%d" % kc), writes=pbf)
                st = stg[blk % 2]
                S.op("act", lambda e, st=st, pv=pv, w_=w_: e.copy(st[:, 0:w_], pv[:, 0:w_]), reads=pbf, writes=B("stg%d" % (blk % 2)))
                if not sample:
                    S.dma("sp", gcp_o[0:3, c0:c0 + w_], st[125:128, 0:w_], reads=B("stg%d" % (blk % 2)))
                else:
                    for s in range(16):
                        S.dma("sp", gcs_o[s * 3:(s + 1) * 3, c0:c0 + w_], st[s * 8 + 5:s * 8 + 8, 0:w_], reads=B("stg%d" % (blk % 2)))
        if t == NT - 1:
            load_win(1)
        MLm, MUm = (MLS_, MUS_) if sample else (ML_, MU_)
        nlev = 3 if sample else 7
        have_state = sample or t > 0
        HB = ((4, 5), (0, 1), (7, 2))

        def head_gen(h, i):
            bA, bB = HB[i]
            dlv, dlb = PS(bA, 0)
            duv, dub = PS(bA, 1)
            kkv, kkb = PS(bB, 0)
            kqv, kqb = PS(bB, 1)
            UGv, SLGv = Dl[i], Du[i]
            S.op("dve", lambda e: e.tensor_scalar(UGv[:], C(Um), gat[:, 24 + h:25 + h], None, op0=ALU.mult), reads=B("cst", "gat"), writes=B("Dl%d" % i))
            S.op("dve", lambda e: e.tensor_scalar(SLGv[:], C(SLm), gat[:, 24 + h:25 + h], None, op0=ALU.mult), reads=B("cst", "gat"), writes=B("Du%d" % i))
            S.op("pe", lambda e: e.matmul(dlv, UGv[:], C(SLm), start=True, stop=True), reads=B("Dl%d" % i, "cst"), writes=dlb, kind="f32")
            S.op("pe", lambda e: e.matmul(duv, SLGv[:], C(Um), start=True, stop=True), reads=B("Du%d" % i, "cst"), writes=dub, kind="f32")
            S.op("pe", lambda e: e.matmul(kkv, KT[:, h, :], KT[:, h, :], start=True, stop=True), reads=B("KT%d" % h), writes=kkb, kind="bf")
            S.op("pe", lambda e: e.matmul(kqv, KT[:, h, :], QT[:, h, :], start=True, stop=True), reads=B("KT%d" % h, "QT%d" % h), writes=kqb, kind="bf")
            yield
            S.op("act", lambda e: e.activation(Dl[i][:], dlv, AF.Exp), reads=dlb, writes=B("Dl%d" % i))
            S.op("act", lambda e: e.activation(Du[i][:], duv, AF.Exp), reads=dub, writes=B("Du%d" % i))
            S.op("dve", lambda e: e.tensor_tensor(Dl[i][:], Dl[i][:], C(MLm), op=ALU.mult), reads=B("Dl%d" % i, "cst"), writes=B("Dl%d" % i))
            S.op("dve", lambda e: e.tensor_tensor(Du[i][:], Du[i][:], C(MUm), op=ALU.mult), reads=B("Du%d" % i, "cst"), writes=B("Du%d" % i))
            R0 = Rb[i][0]
            S.op("dve", lambda e: e.scalar_tensor_tensor(R0[:], kkv, gat[:, 18 + h:19 + h], Dl[i][:], op0=ALU.mult, op1=ALU.mult),
                 reads=kkb + B("gat", "Dl%d" % i), writes=B("R%d_0" % i))
            S.op("dve", lambda e: e.tensor_tensor(AqT[i][:], kqv, Du[i][:], op=ALU.mult), reads=kqb + B("Du%d" % i), writes=B("AqT%d" % i))
            yield
            ntv, ntb = PS(bA, 2)
            S.op("pe", lambda e: e.transpose(ntv, R0[:], C(I_F)), reads=B("R%d_0" % i, "cst"), writes=ntb, kind="tr")
            yield
            S.op("act", lambda e: e.copy(Qb[i][0][:], ntv), reads=ntb, writes=B("Q%d_0" % i))
            S.op("dve", lambda e: e.tensor_tensor(Xb[i][:], ntv, C(I_F), op=ALU.add), reads=ntb + B("cst"), writes=B("X%d" % i))
            for k in range(1, nlev):
                a, b_ = (k - 1) % 2, k % 2
                rv, rb_ = PS(bA, 0)
                qv_, qb_ = PS(bA, 1)
                xv, xb_ = PS(bB, 0)
                S.op("pe", lambda e: e.matmul(rv, Qb[i][a][:], Rb[i][a][:], start=True, stop=True),
                     reads=B("Q%d_%d" % (i, a), "R%d_%d" % (i, a)), writes=rb_, kind="f32")
                if k < nlev - 1:
                    S.op("pe", lambda e: e.matmul(qv_, Rb[i][a][:], Qb[i][a][:], start=True, stop=True),
                         reads=B("Q%d_%d" % (i, a), "R%d_%d" % (i, a)), writes=qb_, kind="f32")
                yield
                S.op("act", lambda e: e.copy(Rb[i][b_][:], rv), reads=rb_, writes=B("R%d_%d" % (i, b_)))
                if k < nlev - 1:
                    S.op("act", lambda e: e.copy(Qb[i][b_][:], qv_), reads=qb_, writes=B("Q%d_%d" % (i, b_)))
                S.op("pe", lambda e: e.matmul(xv, Rb[i][b_][:], Xb[i][:], start=True, stop=True),
                     reads=B("R%d_%d" % (i, b_), "X%d" % i), writes=xb_, kind="f32")
                yield
                S.op("dve", lambda e: e.tensor_tensor(Xb[i][:], xv, Xb[i][:], op=ALU.add), reads=xb_ + B("X%d" % i), writes=B("X%d" % i))
            S.op("act", lambda e: e.copy(Xbf[i][:], Xb[i][:]), reads=B("X%d" % i), writes=B("Xbf%d" % i))
            wv, wb = PS(bA, 2)
            S.op("pe", lambda e: e.matmul(wv, KBG[:, h, :], Xbf[i][:], start=True, stop=True), reads=B("KBG", "Xbf%d" % i), writes=wb, kind="bf")
            yield
            S.op("act", lambda e: e.activation(wTn[i][:], wv, AF.Identity, scale=-1.0), reads=wb, writes=B("wTn%d" % i))
            vv, vb = PS(bB, 1)
            qsv, qsb = PS(bB, 2)
            if not sample:
                S.op("pe", lambda e: e.matmul(vv, Xbf[i][:], VB[:, h, :], start=True, stop=(not have_state)), reads=B("Xbf%d" % i, "VB"), writes=vb, kind="bf")
                if have_state:
                    S.op("pe", lambda e: e.matmul(vv, wTn[i][:], Sbf[:, h, :], start=False, stop=True), reads=B("wTn%d" % i, "S%d" % h), writes=vb, kind="bf")
                    S.op("pe", lambda e: e.matmul(qsv, QT[:, h, :], Sbf[:, h, :], start=True, stop=True), reads=B("QT%d" % h, "S%d" % h), writes=qsb, kind="bf")
            else:
                t1v, t1b = PS(bA, 0)
                t2v, t2b = PS(bA, 1)
                for q4 in range(4):
                    j = q4 % 2
                    S.dma("sp", S0v[j], sg4[:, q4 * 4:(q4 + 1) * 4, h, :], writes=B(S0n[j]))
                    S.op("act", lambda e, j=j: e.copy(S0bf[:, j * 4:j * 4 + 4, :], S0v[j]), reads=B(S0n[j]), writes=B(*["S0bf_%d" % (j * 4 + k) for k in range(4)]))
                    for k in range(4):
                        s, s8 = q4 * 4 + k, j * 4 + k
                        S.op("pe", lambda e, s=s, s8=s8: e.matmul(t1v[:, s * 8:(s + 1) * 8], S0bf[:, s8, :], wTn[i][:, s * 8:(s + 1) * 8], start=True, stop=True),
                             reads=B("S0bf_%d" % s8, "wTn%d" % i), writes=t1b, kind="bf")
                        S.op("pe", lambda e, s=s, s8=s8: e.matmul(t2v[:, s * 8:(s + 1) * 8], S0bf[:, s8, :], QT[:, h, s * 8:(s + 1) * 8], start=True, stop=True),
                             reads=B("S0bf_%d" % s8, "QT%d" % h), writes=t2b, kind="bf")
                S.op("act", lambda e: e.copy(tsb[0][:], t1v), reads=t1b, writes=B("tsb0"))
                S.op("act", lambda e: e.copy(tsb[1][:], t2v), reads=t2b, writes=B("tsb1"))
                S.op("pe", lambda e: e.matmul(vv, Xbf[i][:], VB[:, h, :], start=True, stop=False), reads=B("Xbf%d" % i, "VB"), writes=vb, kind="bf")
                S.op("pe", lambda e: e.matmul(vv, tsb[0][:], identb[:], start=False, stop=True), reads=B("tsb0", "identb"), writes=vb, kind="bf")
                S.op("pe", lambda e: e.matmul(qsv, tsb[1][:], identb[:], start=True, stop=True), reads=B("tsb1", "identb"), writes=qsb, kind="bf")
            yield
            S.op("act", lambda e: e.copy(vnew[i][:], vv), reads=vb, writes=B("vnew%d" % i))
            av, ab = PS(bA, 3)
            S.op("pe", lambda e: e.matmul(av, AqT[i][:], vnew[i][:], start=True, stop=True), reads=B("AqT%d" % i, "vnew%d" % i), writes=ab, kind="bf")
            if not sample:
                suv, sub = PS(6, i)
                S.op("pe", lambda e: e.matmul(suv, KD[:, h, :], vnew[i][:], start=True, stop=True), reads=B("KD", "vnew%d" % i), writes=sub, kind="bf")
            yield
            ofpv = Dl[i]
            avsv = Du[i]
            if have_state:
                S.op("act", lambda e: e.copy(avsv[:], av), reads=ab, writes=B("Du%d" % i))
                S.op("dve", lambda e: e.scalar_tensor_tensor(ofpv[:], qsv, eG[:, h:h + 1], avsv[:], op0=ALU.mult, op1=ALU.add),
                     reads=qsb + B("cum", "Du%d" % i), writes=B("Dl%d" % i))
            else:
                S.op("act", lambda e: e.copy(ofpv[:], av), reads=ab, writes=B("Dl%d" % i))
            if not sample:
                if have_state:
                    S.op("dve", lambda e: e.scalar_tensor_tensor(Sst[:, h, :], Sst[:, h, :], eGl[:, h:h + 1], suv, op0=ALU.mult, op1=ALU.add),
                         reads=sub + B("cum", "S%d" % h), writes=B("S%d" % h))
                else:
                    S.op("dve", lambda e: e.tensor_copy(Sst[:, h, :], suv), reads=sub, writes=B("S%d" % h))
                S.op("act", lambda e: e.copy(Sbf[:, h, :], Sst[:, h, :]), reads=B("S%d" % h), writes=B("S%d" % h))
                if t == NT - 2:
                    S.dma("sp", sp_o[h * 128:(h + 1) * 128, :], Sst[:, h, :], reads=B("S%d" % h))
            gn = "gatn%d" % h
            S.op("dve", lambda e: e.memset(gat[:, 36 + h:37 + h], 0.0), writes=B(gn))
            S.op("act", lambda e: e.activation(wTn[i][:], ofpv[:], AF.Square, accum_out=gat[:, 36 + h:37 + h]), reads=B("Dl%d" % i, gn), writes=B("wTn%d" % i, gn))
            S.op("act", lambda e: e.activation(gat[:, 42 + h:43 + h], gat[:, 36 + h:37 + h], AF.Ln, bias=epsc[:, 0:1], scale=1.0 / 128), reads=B(gn, "epsc"), writes=B(gn))
            S.op("act", lambda e: e.activation(gat[:, 42 + h:43 + h], gat[:, 42 + h:43 + h], AF.Exp, scale=-0.5), reads=B(gn), writes=B(gn))
            S.op("dve", lambda e: e.scalar_tensor_tensor(onb[i][:], ofpv[:], gat[:, 42 + h:43 + h], ogb[:], op0=ALU.mult, op1=ALU.mult),
                 reads=B("Dl%d" % i, gn, "ogb"), writes=B("onb%d" % i))
            otv, otb = PSB(3, i, 1)
            S.op("pe", lambda e: e.transpose(otv[:, 0:128], onb[i][:], identb[:]), reads=B("onb%d" % i, "identb"), writes=otb, kind="tr")
            yield
            S.op("dve", lambda e: e.tensor_tensor(sgT[:, h, :], otv[:, 0:128], sgT[:, h, :], op=ALU.mult), reads=otb + B("sgT"), writes=B("sgT"))
            if sample:
                for q4 in range(4):
                    j = q4 % 2
                    S.dma("sp", S0v[j], sg4[:, q4 * 4:(q4 + 1) * 4, h, :], writes=B(S0n[j]))
                    def mk_mask(k):
                        s = q4 * 4 + k
                        jj = s % 2
                        S.op("dve", lambda e: e.tensor_scalar(kdm[jj][:], KD[:, h, :], smask[:, s:s + 1], None, op0=ALU.mult),
                             reads=B("KD", "smask"), writes=B("kdm%d" % jj))
                        suv, sub = PS((bA, bB)[jj], 0)
                        S.op("pe", lambda e: e.matmul(suv, kdm[jj][:], vnew[i][:], start=True, stop=True), reads=B("kdm%d" % jj, "vnew%d" % i), writes=sub, kind="bf")
                        return suv, sub

                    def fin(k, suv, sub):
                        s = q4 * 4 + k
                        S.op("dve", lambda e: e.scalar_tensor_tensor(Snw4[j][:, k, :], S0v[j][:, k, :], eGls[:, s, h:h + 1], suv, op0=ALU.mult, op1=ALU.add),
                             reads=sub + B("cum", S0n[j]), writes=B("stg%d" % j))

                    pend = mk_mask(0)
                    for k in range(4):
                        nxt = mk_mask(k + 1) if k + 1 < 4 else None
                        fin(k, *pend)
                        pend = nxt
                    S.dma("sp", ss4[:, q4 * 4:(q4 + 1) * 4, h, :], Snw4[j], reads=B("stg%d" % j))
                    yield

        STAG = 0
        slots = [None, None, None]
        steps = [0, 0, 0]
        nxt_h = 0
        tick = 0
        while nxt_h < 6 or any(g is not None for g in slots):
            for i in range(3):
                if slots[i] is None and nxt_h < 6 and tick >= nxt_h * STAG:
                    slots[i] = head_gen(nxt_h, i)
                    nxt_h += 1
                if slots[i] is not None:
                    try:
                        next(slots[i])
                    except StopIteration:
                        slots[i] = None
            tick += 1
        if _SUB == 6:
            return
        attention_and_out(0, t)

    def layer1_tile(t):
        sample = (t == NT - 1)
        norm_transpose(t, NT + t, 8)
        for c in range(8):
            pv, pbf = proj_chunk(2304 + c * 128)
            S.op("act", lambda e, c=c, pv=pv: e.activation(sgT[:, c, :], pv, AF.Silu), reads=pbf, writes=B("sgT"))
        for c in range(2):
            pv, pbf = proj_chunk(2304 + 1024 + c * 128)
            S.op("act", lambda e, c=c, pv=pv: e.copy(xqT[:, c, :], pv), reads=pbf, writes=B("xqT"))
        def l1A(c):
            i = c % 2
            pv, pbf = proj_chunk(768 + c * 128)
            S.op("act", lambda e: e.copy(sil[i][:], pv), reads=pbf, writes=B("sil%d" % i))
            pv2, pbf2 = proj_chunk(1536 + c * 128)
            conv_setup_hist(1, t, c, i, 2)
            cur, taps, last = xp_views(t, i, 2)
            S.op("dve", lambda e: e.tensor_tensor(cur, as3(pv2, t), as3(sil[i][:], t), op=ALU.mult), reads=pbf2 + B("sil%d" % i), writes=B("xp%d" % i))
            if not sample:
                S.op("pool", lambda e: e.tensor_copy(hist2[:, c, :], last), reads=B("xp%d" % i), writes=B("hist2"))
            acc = as3(cacc[i][:], t)
            S.op("act", lambda e: e.activation(acc, taps[2], AF.Identity, scale=cwb[:, c, 2:3]), reads=B("xp%d" % i, "cwb"), writes=B("cacc%d" % i))

        def l1B(c):
            i = c % 2
            cur, taps, last = xp_views(t, i, 2)
            acc = as3(cacc[i][:], t)
            for j in range(2):
                S.op("dve", lambda e, j=j: e.scalar_tensor_tensor(acc, taps[j], cwb[:, c, j:j + 1], acc, op0=ALU.mult, op1=ALU.add),
                     reads=B("xp%d" % i, "cwb", "cacc%d" % i), writes=B("cacc%d" % i))
            pv3, pbf3 = proj_chunk(c * 128)
            S.op("dve", lambda e: e.tensor_tensor(cacc[i][:], pv3, cacc[i][:], op=ALU.mult), reads=pbf3 + B("cacc%d" % i), writes=B("cacc%d" % i))
            S.op("dve", lambda e: e.tensor_tensor(sgT[:, c, :], cacc[i][:], sgT[:, c, :], op=ALU.mult), reads=B("cacc%d" % i, "sgT"), writes=B("sgT"))

        l1A(0)
        for c in range(6):
            if c + 1 < 6:
                l1A(c + 1)
            l1B(c)
        if t >= NT - 2:
            for blk in range(2):
                c0 = blk * 384
                pv, pbf = PS(7, 0, 4)
                pw, pwb = PS(6, 0, 4)
                for kc in range(8):
                    S.op("pe", lambda e, kc=kc, pv=pv, c0=c0: e.matmul(pv[:, 0:384], hT[:, kc, :], win[:, kc, 768 + c0:768 + c0 + 384], start=(kc == 0), stop=(kc == 7)),
                         reads=B("hT", "win%d" % kc), writes=pbf)
                for kc in range(8):
                    S.op("pe", lambda e, kc=kc, pw=pw, c0=c0: e.matmul(pw[:, 0:384], hT[:, kc, :], win[:, kc, 1536 + c0:1536 + c0 + 384], start=(kc == 0), stop=(kc == 7)),
                         reads=B("hT", "win%d" % kc), writes=pwb)
                st = stg[blk % 2]
                S.op("act", lambda e, st=st, pv=pv: e.copy(st[:, 0:384], pv[:, 0:384]), reads=pbf, writes=B("stg%d" % (blk % 2)))
                S.op("dve", lambda e, st=st, pw=pw: e.tensor_tensor(st[:, 0:384], pw[:, 0:384], st[:, 0:384], op=ALU.mult), reads=pwb + B("stg%d" % (blk % 2)), writes=B("stg%d" % (blk % 2)))
                if not sample:
                    S.dma("sp", scp_o[0:2, c0:c0 + 384], st[126:128, 0:384], reads=B("stg%d" % (blk % 2)))
                else:
                    for s in range(16):
                        S.dma("sp", scs_o[s * 2:(s + 1) * 2, c0:c0 + 384], st[s * 8 + 6:s * 8 + 8, 0:384], reads=B("stg%d" % (blk % 2)))
        attention_and_out(1, t)

    order = list(range(NT))
    if stage >= 2:
        for t in order[:(stage - 1) if stage < 10 else NT]:
            layer0_tile(t)
    if stage >= 20:
        load_wout(1)
        S.dma("sp", gfin, gfin_d[:, :], writes=B("KBG", "KD", "VB"))
        for t in order[:(stage - 19) if stage < 30 else NT]:
            layer1_tile(t)
    S.finish()
    es.close()
    return nc, S


_CACHE = {}
_STAGE = 99
_SUB = 99


def _consts():
    i = np.arange(128)
    same = (i[:, None] // 8) == (i[None, :] // 8)
    ident = np.eye(128)
    ones = np.ones((128, 128))
    U = (i[:, None] <= i[None, :])
    SL = (i[:, None] > i[None, :])
    ML = (i[:, None] > i[None, :])
    MU = (i[:, None] <= i[None, :])
    mats = [ident, ones, U, SL, U & same, SL & same]
    cst = np.stack([m.astype(np.float32) for m in mats], axis=1).reshape(128, 6 * 128)
    smask = ((i[:, None] // 8) == np.arange(16)[None, :]).astype(np.float32)
    return np.ascontiguousarray(cst), np.ascontiguousarray(smask)


def kernel(x_prompt, x_sample, mem_prompt, state_gdn, state_gdn_conv, state_sconv,
           cache_mem_k, cache_mem_v, norm_g, w_in_a, conv_w_a, a_log, dt_bias, o_norm_g,
           w_in_b, conv_w_b, mem_norm_g, w_mem_kv, w_out, final_norm_g):
    f = lambda a: np.ascontiguousarray(np.asarray(a, dtype=np.float32))
    if "nc" not in _CACHE:
        _CACHE["nc"] = build_program(_STAGE)[0]
    nc = _CACHE["nc"]
    cst, smask = _consts()
    x_prompt, x_sample, mem_prompt = f(x_prompt), f(x_sample), f(mem_prompt)
    state_gdn, state_gdn_conv, state_sconv = f(state_gdn), f(state_gdn_conv), f(state_sconv)
    cache_mem_k, cache_mem_v = f(cache_mem_k), f(cache_mem_v)
    gfm = np.concatenate([f(norm_g).reshape(2, 8, 128), f(mem_norm_g).reshape(1, 8, 128)], axis=0)
    gfm = np.ascontiguousarray(gfm.transpose(2, 0, 1).reshape(128, 24))
    cwa = np.ascontiguousarray(f(conv_w_a)[0].reshape(4, 18, 128).transpose(2, 1, 0).reshape(128, 72))
    cwb = np.ascontiguousarray(f(conv_w_b)[0].reshape(3, 6, 128).transpose(2, 1, 0).reshape(128, 18))
    small = np.ascontiguousarray(np.broadcast_to(np.concatenate([f(a_log)[0], f(dt_bias)[0]])[None, :], (128, 12)))
    ogb = np.ascontiguousarray(np.broadcast_to(f(o_norm_g)[0][None, :], (128, 128)))
    gfin = np.ascontiguousarray(np.broadcast_to(f(final_norm_g)[None, :], (128, 1024)))
    shared = {
        "wina": f(w_in_a)[0], "winb": f(w_in_b)[0], "wout": f(w_out).reshape(2048, 1024),
        "wkv": f(w_mem_kv).reshape(2048, 512), "gfm": gfm, "cwa": cwa, "cwb": cwb, "small": small,
        "ogb": ogb, "gfin": gfin, "cst": cst, "smask": smask,
    }
    in_maps = []
    for c in range(8):
        sl = slice(c * 16, (c + 1) * 16)
        m = dict(shared)
        m["x"] = np.ascontiguousarray(np.concatenate([x_prompt[c], x_sample[sl].reshape(128, 1024)], axis=0))
        m["mem"] = mem_prompt[c]
        m["sgdn"] = np.ascontiguousarray(state_gdn[0, sl].reshape(16 * 6 * 128, 128))
        m["sgc"] = np.ascontiguousarray(state_gdn_conv[0, sl].reshape(48, 2304))
        m["ssc"] = np.ascontiguousarray(state_sconv[0, sl].reshape(32, 768))
        m["cmk"] = np.ascontiguousarray(cache_mem_k[:, sl].reshape(2 * 16 * 256, 256))
        m["cmv"] = np.ascontiguousarray(cache_mem_v[:, sl].reshape(2 * 16 * 256, 256))
        in_maps.append(m)
    res = run_bass_kernel_spmd(nc, in_maps, core_ids=list(range(8)))
    R = res.results
    y = np.stack([r["y"] for r in R])
    y_prompt = np.ascontiguousarray(y[:, :2048])
    y_sample = np.ascontiguousarray(y[:, 2048:].reshape(128, 8, 1024))
    S_p = np.stack([r["sp_o"].reshape(6, 128, 128) for r in R])[None]
    gc_p = np.stack([r["gcp_o"] for r in R])[None]
    sc_p = np.stack([r["scp_o"] for r in R])[None]
    mk_p = np.stack([r["mkp_o"].reshape(2, 256, 4, 64) for r in R], axis=1)
    mv_p = np.stack([r["mvp_o"].reshape(2, 256, 4, 64) for r in R], axis=1)
    S_s = np.concatenate([r["ss_o"].reshape(16, 6, 128, 128) for r in R])[None]
    gc_s = np.concatenate([r["gcs_o"].reshape(16, 3, 2304) for r in R])[None]
    sc_s = np.concatenate([r["scs_o"].reshape(16, 2, 768) for r in R])[None]
    outs = (y_prompt, y_sample, S_p, gc_p, sc_p, mk_p, mv_p, S_s, gc_s, sc_s)
    return tuple(np.ascontiguousarray(o, dtype=np.float32) for o in outs)
```

```python
import numpy as np
from contextlib import ExitStack
import concourse.bass as bass
import concourse.mybir as mybir
from concourse.bass_utils import run_bass_kernel_spmd

F32 = mybir.dt.float32
BF16 = mybir.dt.bfloat16
AF = mybir.ActivationFunctionType
ALU = mybir.AluOpType
AX = mybir.AxisListType

NT = 17
EPS = 1e-6
CA = 3596
CB = 3584


class Buf:
    __slots__ = ("name", "last_w", "readers", "excl")

    def __init__(self, name, excl=False):
        self.name = name
        self.last_w = None
        self.readers = {}
        self.excl = excl


class Sched:
    def __init__(self, nc, es, n_dma_sems=(24, 4, 12)):
        self.nc = nc
        self.eng = {"pe": nc.tensor, "act": nc.scalar, "dve": nc.vector, "pool": nc.gpsimd, "sp": nc.sync}
        self.sem, self.cnt, self.seen = {}, {}, {}
        for k in self.eng:
            self.sem[k] = es.enter_context(nc.semaphore("s_" + k))
            self.cnt[k] = 0
            self.seen[k] = {}
        self.dma_sems, self.dma_rr = {}, {}
        for q, n in zip(("sp", "act", "pool"), n_dma_sems):
            self.dma_sems[q] = []
            self.dma_rr[q] = 0
            for i in range(n):
                key = "d%s%d" % (q, i)
                self.sem[key] = es.enter_context(nc.semaphore("s_" + key))
                self.cnt[key] = 0
                self.dma_sems[q].append(key)
        self.n_inst = 0
        self.n_wait = 0
        self.pe_last = None
        self.sep = None
        self.attach = True

    def _wait(self, e, deps, defer=False):
        best = {}
        for (k, c) in deps:
            if c > best.get(k, 0):
                best[k] = c
        need = [(k, c) for k, c in best.items() if self.seen[e].get(k, 0) < c]
        last = None
        if defer and need:
            last = need.pop()
        for k, c in need:
            self.eng[e].wait_ge(self.sem[k], c)
            self.seen[e][k] = c
            self.n_wait += 1
        if last is not None:
            self.seen[e][last[0]] = last[1]
        return last

    @staticmethod
    def _deps(reads, writes):
        deps = []
        for b in reads:
            if b.last_w is not None:
                deps.append(b.last_w)
        for b in writes:
            if b.last_w is not None:
                deps.append(b.last_w)
            deps.extend(b.readers.items())
        return deps

    def op(self, e, fn, reads=(), writes=(), kind=None):
        if e == "pe":
            if kind == "bf" and self.pe_last == "f32" and self.sep is not None:
                self.pe_last = "tr"
                self.sep()
            if kind is not None:
                self.pe_last = kind
        ex = [b for b in reads if b.excl]
        if ex:
            writes = list(writes) + [b for b in ex if b not in writes]
            reads = [b for b in reads if not b.excl]
        deps = self._deps(reads, writes)
        if e == "pe":
            deps = [d for d in deps if d[0] != "pe"]
        last = self._wait(e, deps, defer=self.attach)
        ins = fn(self.eng[e])
        if last is not None:
            ins._wait_ge(self.sem[last[0]], last[1])
        ins.then_inc(self.sem[e], 1)
        self.cnt[e] += 1
        c = self.cnt[e]
        for b in writes:
            b.last_w = (e, c)
            b.readers = {}
        for b in reads:
            if b not in writes:
                b.readers[e] = c
        self.n_inst += 1
        return ins

    def dma(self, q, out, in_, reads=(), writes=(), **kw):
        key = self.dma_sems[q][self.dma_rr[q] % len(self.dma_sems[q])]
        self.dma_rr[q] += 1
        deps = self._deps(reads, writes)
        if self.cnt[key] > 0:
            deps.append((key, self.cnt[key]))
        self._wait(q, deps)
        ins = self.eng[q].dma_start(out=out, in_=in_, **kw)
        ins.then_inc(self.sem[key], 16)
        self.cnt[key] += 16
        c = self.cnt[key]
        for b in writes:
            b.last_w = (key, c)
            b.readers = {}
        for b in reads:
            b.readers[key] = c
        self.n_inst += 1
        return ins

    def finish(self):
        deps = [(k, c) for k, c in self.cnt.items() if c > 0]
        for e in ("sp", "act", "pool", "dve", "pe"):
            self._wait(e, [d for d in deps if d[0] != e])


def build_program(stage=99):
    nc = bass.Bass("TRN2", target_bir_lowering=False)

    def din(name, shape):
        return nc.dram_tensor(name, list(shape), F32, kind="ExternalInput").ap()

    def dout(name, shape):
        return nc.dram_tensor(name, list(shape), F32, kind="ExternalOutput").ap()

    x_d = din("x", [NT * 128, 1024])
    mem_d = din("mem", [256, 1024])
    sgdn_d = din("sgdn", [16 * 6 * 128, 128])
    sgc_d = din("sgc", [48, 2304])
    ssc_d = din("ssc", [32, 768])
    cmk_d = din("cmk", [2 * 16 * 256, 256])
    cmv_d = din("cmv", [2 * 16 * 256, 256])
    wina_d = din("wina", [1024, CA])
    winb_d = din("winb", [1024, CB])
    wout_d = din("wout", [2048, 1024])
    wkv_d = din("wkv", [2048, 512])
    gfm_d = din("gfm", [128, 24])
    cwa_d = din("cwa", [128, 18 * 4])
    cwb_d = din("cwb", [128, 6 * 3])
    small_d = din("small", [128, 12])
    ogb_d = din("ogb", [128, 128])
    gfin_d = din("gfin", [128, 1024])
    cst_d = din("cst", [128, 6 * 128])
    smask_d = din("smask", [128, 16])

    y_o = dout("y", [NT * 128, 1024])
    sp_o = dout("sp_o", [6 * 128, 128])
    gcp_o = dout("gcp_o", [3, 2304])
    scp_o = dout("scp_o", [2, 768])
    mkp_o = dout("mkp_o", [512, 256])
    mvp_o = dout("mvp_o", [512, 256])
    ss_o = dout("ss_o", [16 * 6 * 128, 128])
    gcs_o = dout("gcs_o", [48, 2304])
    scs_o = dout("scs_o", [32, 768])

    es = ExitStack()
    S = Sched(nc, es)
    bufs = {}

    def sb(name, shape, dt=F32):
        t = es.enter_context(nc.sbuf_tensor("sb_" + name, list(shape), dt))
        bufs[name] = Buf(name)
        return t

    def B(*names):
        return [bufs[n] for n in names]

    def nb(name):
        bufs[name] = Buf(name)
        return bufs[name]

    xres = sb("xres", [128, NT, 1024])
    for t in range(NT):
        nb("x%d" % t)
    win = sb("win", [128, 8, CA], BF16)
    for k in range(8):
        nb("win%d" % k)
    wout = sb("wout", [128, 8, 1024], BF16)
    cst = sb("cst", [128, 6, 128])
    identb = sb("identb", [128, 128], BF16)
    onesb = sb("onesb", [128, 128], BF16)
    smask = sb("smask", [128, 16])
    gfm = sb("gfm", [128, 24])
    cwa = sb("cwa", [128, 18, 4])
    cwb = sb("cwb", [128, 6, 3])
    small = sb("small", [128, 12])
    nA = sb("nA", [128, 6])
    ogb = sb("ogb", [128, 128])
    epsc = sb("epsc", [128, 2])
    rstd = sb("rstd", [128, 3 * NT + 2])
    for i in range(3 * NT + 2):
        nb("rstd_%d" % i)
    ssq = sb("ssq", [128, 3 * NT + 2])
    mkT = sb("mkT", [128, 2, 2, 256], BF16)
    mvp = sb("mvp", [128, 2, 2, 256], BF16)
    hb = sb("hb", [128, 1024], BF16)
    hT = sb("hT", [128, 8, 128], BF16)
    xp = [sb("xp%d" % i, [128, 16 * 11]) for i in range(2)]
    hist = sb("hist", [128, 18, 3])
    hist2 = sb("hist2", [128, 6, 2])
    cacc = [sb("cacc%d" % i, [128, 128]) for i in range(2)]
    sil = [sb("sil%d" % i, [128, 128]) for i in range(2)]
    sqb = [sb("sqb%d" % i, [128, 3, 128], BF16) for i in range(2)]
    QKT = sb("QKT", [128, 12, 128], BF16)
    QT, KT = QKT[:, 0:6], QKT[:, 6:12]
    VT = sb("VT", [128, 6, 128], BF16)
    for h in range(6):
        nb("QT%d" % h), nb("KT%d" % h), nb("VT%d" % h)
    sgT2 = [sb("sgT%d" % i, [128, 8, 128], BF16) for i in range(2)]
    xqT2 = [sb("xqT%d" % i, [128, 2, 128], BF16) for i in range(2)]
    gat = sb("gat", [128, 48])
    cum = sb("cum", [128, 18 + 96])
    kkv3 = sb("kkv3", [128, 3, 6, 128], BF16)
    KBG, KD, VB = kkv3[:, 0], kkv3[:, 1], kkv3[:, 2]
    nb("KBG"), nb("KD"), nb("VB")
    for h6 in range(6):
        nb("gatn%d" % h6)
    gfin = kkv3[:].rearrange("p a b c -> p (a b c)")[:, 0:2048].bitcast(F32)
    Dl = [sb("Dl%d" % i, [128, 128]) for i in range(3)]
    Du = [sb("Du%d" % i, [128, 128]) for i in range(3)]
    AqT = [sb("AqT%d" % i, [128, 128], BF16) for i in range(3)]
    Rb = [[sb("R%d_%d" % (i, j), [128, 128]) for j in range(2)] for i in range(3)]
    Qb = [[sb("Q%d_%d" % (i, j), [128, 128]) for j in range(2)] for i in range(3)]
    Xb = [sb("X%d" % i, [128, 128]) for i in range(3)]
    Xbf = [sb("Xbf%d" % i, [128, 128], BF16) for i in range(3)]
    wTn = [sb("wTn%d" % i, [128, 128], BF16) for i in range(3)]
    vnew = [sb("vnew%d" % i, [128, 128], BF16) for i in range(3)]
    onb = [sb("onb%d" % i, [128, 128], BF16) for i in range(3)]
    Sst = sb("Sst", [128, 6, 128])
    Sbf = sb("Sbf", [128, 6, 128], BF16)
    for h in range(6):
        nb("S%d" % h)
    Pb = sb("Pb", [128, 4, 256], BF16)
    PTb = hb[:].rearrange("p (h m t) -> p h m t", h=4, m=2)
    sm = sb("sm", [128, 16])
    stg = [sb("stg%d" % i, [128, 512]) for i in range(2)]
    for k8 in range(8):
        nb("S0bf_%d" % k8)
    tsb = [sb("tsb%d" % i, [128, 128], BF16) for i in range(2)]
    kdm = [sb("kdm%d" % i, [128, 128], BF16) for i in range(2)]
    xqm = [sb("xqm%d" % i, [128, 2, 128], BF16) for i in range(2)]
    ckv = [sb("ckv%d" % i, [128, 2, 2, 256], BF16) for i in range(2)]
    S0bf = ckv[0][:].rearrange("p a b c -> p (a b c)").rearrange("p (s e) -> p s e", s=8)
    S0BN = ["S0bf_%d" % k8 for k8 in range(8)]
    mkTs = [sb("mkTs%d" % i, [128, 2, 256], BF16) for i in range(2)]
    hst = [sb("hst%d" % i, [48, 128]) for i in range(2)]

    S0v = [hb[:].bitcast(F32).rearrange("p (s e) -> p s e", s=4),
           Pb[:].rearrange("p a b -> p (a b)").bitcast(F32).rearrange("p (s e) -> p s e", s=4)]
    S0n = ["hb", "Pb"]
    Snw4 = [stg[0][:].rearrange("p (s e) -> p s e", s=4), stg[1][:].rearrange("p (s e) -> p s e", s=4)]
    stgKV = [stg[0][:].rearrange("p (m c) -> p m c", m=2), stg[1][:].rearrange("p (m c) -> p m c", m=2)]
    sg4 = sgdn_d.rearrange("(s h d) e -> d s h e", s=16, h=6)
    ss4 = ss_o.rearrange("(s h d) e -> d s h e", s=16, h=6)

    pbank = [es.enter_context(nc.psum_tensor("pb%d" % i, [128, 512], F32)) for i in range(8)]
    for i in range(8):
        bk = nb("pbank%d" % i)
        bk.excl = True
        for j in range(4):
            bufs["p%d_%d" % (i, j)] = bk

    def PS(bank, slot, n=1):
        return pbank[bank][:, slot * 128:(slot + n) * 128], B("p%d_0" % bank)

    def PSB(bank, slot, n=1):
        v = pbank[bank][:, slot * 128:(slot + n) * 128].bitcast(BF16)
        return v, B("p%d_0" % bank)

    I_F, ONES_F, U_, SL_, US_, SLS_ = range(6)
    ML_, MU_, MLS_, MUS_ = SL_, U_, SLS_, US_

    def C(i):
        return cst[:, i, :]

    S.dma("sp", cst[:].rearrange("p a b -> p (a b)"), cst_d[:, :], writes=B("cst"))
    S.dma("sp", smask[:], smask_d[:, :], writes=B("smask"))
    S.dma("sp", gfm[:], gfm_d[:, :], writes=B("gfm"))
    S.dma("sp", cwa[:].rearrange("p a b -> p (a b)"), cwa_d[:, :], writes=B("cwa"))
    S.dma("sp", cwb[:].rearrange("p a b -> p (a b)"), cwb_d[:, :], writes=B("cwb"))
    S.dma("sp", small[:], small_d[:, :], writes=B("small"))
    S.dma("sp", ogb[:], ogb_d[:, :], writes=B("ogb"))
    S.op("dve", lambda e: e.tensor_copy(identb[:], C(I_F)), reads=B("cst"), writes=B("identb"))
    S.op("dve", lambda e: e.tensor_copy(onesb[:], C(ONES_F)), reads=B("cst"), writes=B("onesb"))
    S.op("dve", lambda e: e.memset(epsc[:, 0:1], EPS), writes=B("epsc"))
    S.op("dve", lambda e: e.memset(epsc[:, 1:2], 128.0 * EPS), writes=B("epsc"))
    S.op("dve", lambda e: e.memset(hist[:], 0.0), writes=B("hist"))
    S.op("dve", lambda e: e.memset(hist2[:], 0.0), writes=B("hist2"))
    S.op("dve", lambda e: e.memset(ssq[:], 0.0), writes=B("ssq"))
    S.op("dve", lambda e: e.memset(sm[:], 0.0), writes=B("sm"))
    S.op("dve", lambda e: e.memset(gat[:], 0.0), writes=B("gat"))
    S.op("dve", lambda e: e.memset(rstd[:], 0.0), writes=B(*["rstd_%d" % i for i in range(3 * NT + 2)]))
    S.op("act", lambda e: e.activation(nA[:], small[:, 0:6], AF.Exp), reads=B("small"), writes=B("nA"))
    S.op("dve", lambda e: e.tensor_scalar(nA[:], nA[:], -1.0, None, op0=ALU.mult), reads=B("nA"), writes=B("nA"))

    for l in range(2):
        for kc in range(8):
            S.dma("pool", wout[:, kc, l * 512:(l + 1) * 512], wkv_d[l * 1024 + kc * 128:l * 1024 + (kc + 1) * 128, :],
                  writes=B("wout"))
    for kc in range(8):
        S.dma("pool", win[:, kc, 0:CA], wina_d[kc * 128:(kc + 1) * 128, :], writes=B("win%d" % kc))
    for mt in range(2):
        S.dma("sp", xres[:, mt, :], mem_d[mt * 128:(mt + 1) * 128, :], writes=B("x%d" % mt))

    def rstd_from_ssq(col, n_feat):
        S.op("act", lambda e: e.activation(rstd[:, col:col + 1], ssq[:, col:col + 1], AF.Ln, bias=epsc[:, 0:1], scale=1.0 / n_feat),
             reads=B("ssq", "epsc"), writes=B("rstd_%d" % col))
        S.op("act", lambda e: e.activation(rstd[:, col:col + 1], rstd[:, col:col + 1], AF.Exp, scale=-0.5),
             reads=B("rstd_%d" % col), writes=B("rstd_%d" % col))

    def ssq_of_tile(t, col):
        S.op("act", lambda e: e.activation(hb[:], xres[:, t, :], AF.Square, accum_out=ssq[:, col:col + 1]),
             reads=B("x%d" % t, "ssq"), writes=B("hb", "ssq"))

    def norm_transpose(t, rcol, gcol0):
        hbx = QKT[:].rearrange("p a b -> p (a b)")[:, 0:1024]
        HBN = B("QT0", "QT1", "QT2", "QT3", "QT4", "QT5", "KT0", "KT1")
        S.op("dve", lambda e: e.tensor_scalar(hbx, xres[:, t, :], rstd[:, rcol:rcol + 1], None, op0=ALU.mult),
             reads=B("x%d" % t, "rstd_%d" % rcol), writes=HBN)
        pv, pbf = PSB(7, 0, 4)
        pv3 = pv.rearrange("p (k t) -> p k t", k=8)
        for kc in range(8):
            S.op("pe", lambda e, kc=kc: e.transpose(pv3[:, kc, :], hbx[:, kc * 128:(kc + 1) * 128], identb[:]),
                 reads=HBN + B("identb"), writes=pbf, kind="tr")
        g3 = gfm[:, gcol0:gcol0 + 8].unsqueeze(2).to_broadcast([128, 8, 128])
        S.op("dve", lambda e: e.tensor_tensor(hT[:], pv3, g3, op=ALU.mult), reads=pbf + B("gfm"), writes=B("hT"))

    def pe_sep():
        sv, sbk = PSB(3, 3, 1)
        S.op("pe", lambda e: e.transpose(sv[:, 0:128], identb[:], identb[:]), reads=B("identb"), writes=sbk, kind="tr")

    S.sep = pe_sep
    def run_interleaved(gens):
        gens = list(gens)
        while gens:
            for g in list(gens):
                try:
                    next(g)
                except StopIteration:
                    gens.remove(g)

    proj_rr = [0]

    def proj_chunk(col0, ncols=128):
        bank = (1, 7)[proj_rr[0] % 2]
        proj_rr[0] += 1
        pv, pbf = PS(bank, 0)
        for kc in range(8):
            S.op("pe", lambda e, kc=kc: e.matmul(pv[0:ncols, :], win[:, kc, col0:col0 + ncols], hT[:, kc, :], start=(kc == 0), stop=(kc == 7)),
                 reads=B("win%d" % kc, "hT"), writes=pbf)
        return pv, pbf

    for mt in range(2):
        ssq_of_tile(mt, 3 * NT + mt)
        rstd_from_ssq(3 * NT + mt, 1024)
    memT = kkv3[:].rearrange("p a b c -> p (a b c)")[:, 0:2048].rearrange("p (k m) -> p k m", k=8)
    MEMB = B("KBG", "KD", "VB")
    for mt in range(2):
        norm_transpose(mt, 3 * NT + mt, 16)
        S.op("act", lambda e, mt=mt: e.copy(memT[:, :, mt * 128:(mt + 1) * 128], hT[:]), reads=B("hT"), writes=MEMB)
    for l in range(2):
        for mt in range(2):
            pv, pbf = PS(2, 0, 4)
            for kc in range(8):
                S.op("pe", lambda e, kc=kc: e.matmul(pv, memT[:, kc, mt * 128:(mt + 1) * 128], wout[:, kc, l * 512:(l + 1) * 512], start=(kc == 0), stop=(kc == 7)),
                     reads=MEMB + B("wout"), writes=pbf)
            st = stg[(l * 2 + mt) % 2]
            sn = "stg%d" % ((l * 2 + mt) % 2)
            S.op("act", lambda e, st=st: e.copy(st[:], pv), reads=pbf, writes=B(sn))
            S.dma("sp", mkp_o[l * 256 + mt * 128:l * 256 + (mt + 1) * 128, :], st[:, 0:256], reads=B(sn))
            S.dma("sp", mvp_o[l * 256 + mt * 128:l * 256 + (mt + 1) * 128, :], st[:, 256:512], reads=B(sn))
            S.op("dve", lambda e, st=st: e.tensor_copy(mvp[:, l, mt, :], st[:, 256:512]), reads=B(sn), writes=B("mvp"))
        for c in range(2):
            pv, pbf = PS(3, 0, 2)
            for kc in range(8):
                S.op("pe", lambda e, kc=kc: e.matmul(pv, wout[:, kc, l * 512 + c * 128:l * 512 + (c + 1) * 128], memT[:, kc, :], start=(kc == 0), stop=(kc == 7)),
                     reads=MEMB + B("wout"), writes=pbf)
            S.op("act", lambda e, c=c: e.copy(mkT[:, l, c, :], pv), reads=pbf, writes=B("mkT"))

    for t in range(NT):
        S.dma("sp", xres[:, t, :], x_d[t * 128:(t + 1) * 128, :], writes=B("x%d" % t))

    def load_win(l):
        wd = wina_d if l == 0 else winb_d
        ncol = CA if l == 0 else CB
        for kc in range(8):
            S.dma("pool", win[:, kc, 0:ncol], wd[kc * 128:(kc + 1) * 128, :], writes=B("win%d" % kc))

    def load_wout(l):
        for kc in range(8):
            S.dma("pool", wout[:, kc, :], wout_d[l * 1024 + kc * 128:l * 1024 + (kc + 1) * 128, :], writes=B("wout"))

    def load_layer_weights(l):
        load_win(l)
        load_wout(l)

    load_wout(0)
    for t in range(NT):
        ssq_of_tile(t, t)
        rstd_from_ssq(t, 1024)

    def att_gen(l, t, pg):
        sample = (t == NT - 1)
        sgT, xqT = sgT2[pg], xqT2[pg]
        SG, XQ = "sgT%d" % pg, "xqT%d" % pg
        scv, scb = [], []
        for bk in (3, 4, 5, 6):
            v_, b_ = PS(bk, 0, 2)
            scv.append(v_)
            scb.append(b_)
        if not sample:
            for h in range(4):
                po = (h % 2) * 64
                if _SUB == 700 and h != 0:
                    continue
                if _SUB == 701 and h != 1:
                    continue
                S.op("pe", lambda e, h=h, po=po: e.matmul(scv[h], xqT[po:po + 64, h // 2, :], mkT[po:po + 64, l, h // 2, :], start=True, stop=True),
                     reads=B(XQ, "mkT"), writes=scb[h])
        else:
            for s in range(16):
                i = s % 2
                r0 = (l * 16 + s) * 256
                S.dma("sp", stgKV[i], cmk_d[r0:r0 + 256, :].rearrange("(mt p) c -> p mt c", p=128), writes=B("stg%d" % i))
                S.op("act", lambda e, i=i: e.copy(ckv[i][:, 0, :, :], stgKV[i]), reads=B("stg%d" % i), writes=B("ckv%d" % i) + (B(*S0BN) if i == 0 else []))
                tv, tb = PSB(0, 0, 2)
                tv3 = tv.rearrange("p (c m) -> p c m", c=2)
                for c in range(2):
                    for mt in range(2):
                        S.op("pe", lambda e, c=c, mt=mt, i=i: e.transpose(tv3[:, c, mt * 128:(mt + 1) * 128], ckv[i][:, 0, mt, c * 128:(c + 1) * 128], identb[:]),
                             reads=B("ckv%d" % i, "identb"), writes=tb)
                S.op("act", lambda e, i=i: e.copy(mkTs[i][:], tv3), reads=tb, writes=B("mkTs%d" % i))
                S.op("pool", lambda e, i=i: e.memset(xqm[i][:].rearrange("p a b -> p (a b)"), 0.0), writes=B("xqm%d" % i))
                S.op("pool", lambda e, i=i, s=s: e.tensor_copy(xqm[i][:, :, s * 8:(s + 1) * 8], xqT[:, :, s * 8:(s + 1) * 8]),
                     reads=B(XQ), writes=B("xqm%d" % i))
                for h in range(4):
                    po = (h % 2) * 64
                    S.op("pe", lambda e, h=h, po=po, i=i: e.matmul(scv[h], xqm[i][po:po + 64, h // 2, :], mkTs[i][po:po + 64, h // 2, :], start=(s == 0), stop=(s == 15)),
                         reads=B("xqm%d" % i, "mkTs%d" % i), writes=scb[h])
                yield
        yield
        ovs = [PS(0, 0, 4), PS(2, 0, 4)]
        for half in range(2):
            for c in range(6):
                S.op("pe", lambda e, c=c, half=half: e.matmul(ovs[half][0], sgT[:, c, :], wout[:, c, half * 512:(half + 1) * 512], start=(c == 0), stop=False),
                     reads=B(SG, "wout"), writes=ovs[half][1], kind="bf")
        yield
        for h in range(4):
            S.op("dve", lambda e, h=h: e.tensor_reduce(sm[:, h:h + 1], scv[h], axis=AX.X, op=ALU.max), reads=scb[h], writes=B("sm"))
        S.op("dve", lambda e: e.tensor_scalar(sm[:, 4:8], sm[:, 0:4], -0.125, None, op0=ALU.mult), reads=B("sm"), writes=B("sm"))
        S.op("dve", lambda e: e.memset(sm[:, 8:12], 0.0), writes=B("sm"))
        yield
        for h in range(4):
            S.op("act", lambda e, h=h: e.activation(Pb[:, h, :], scv[h], AF.Exp, bias=sm[:, 4 + h:5 + h], scale=0.125, accum_out=sm[:, 8 + h:9 + h]),
                 reads=scb[h] + B("sm"), writes=B("Pb", "sm"))
        yield
        S.op("dve", lambda e: e.reciprocal(sm[:, 12:16], sm[:, 8:12]), reads=B("sm"), writes=B("sm"))
        S.op("dve", lambda e: e.tensor_tensor(Pb[:], Pb[:], sm[:, 12:16].unsqueeze(2).to_broadcast([128, 4, 256]), op=ALU.mult),
             reads=B("Pb", "sm"), writes=B("Pb"))
        yield
        if _SUB == 71:
            return
        tv, tb = PSB(5, 0, 4)
        tv4 = tv.rearrange("p (h m t) -> p h m t", h=4, m=2)
        for h in range(4):
            for mt in range(2):
                S.op("pe", lambda e, h=h, mt=mt: e.transpose(tv4[:, h, mt, :], Pb[:, h, mt * 128:(mt + 1) * 128], identb[:]),
                     reads=B("Pb", "identb"), writes=tb)
        yield
        S.op("act", lambda e: e.copy(PTb, tv4), reads=tb, writes=B("hb"))
        yield
        if _SUB == 72:
            return
        xo_v, xo_b = PS(6, 0, 4)
        xo4 = xo_v.rearrange("p (c k t) -> p c k t", c=2, k=2)
        if not sample:
            for c in range(2):
                for hh in range(2):
                    h = 2 * c + hh
                    for mt in range(2):
                        S.op("pe", lambda e, h=h, mt=mt, c=c, hh=hh: e.matmul(xo4[:, c, hh, :], mvp[:, l, mt, c * 128:(c + 1) * 128], PTb[:, h, mt, :], start=(mt == 0), stop=(mt == 1)),
                             reads=B("mvp", "hb"), writes=xo_b)
        else:
            for s in range(16):
                i = s % 2
                r0 = (l * 16 + s) * 256
                S.dma("sp", stgKV[i], cmv_d[r0:r0 + 256, :].rearrange("(mt p) c -> p mt c", p=128), writes=B("stg%d" % i))
                S.op("act", lambda e, i=i: e.copy(ckv[i][:, 1, :, :], stgKV[i]), reads=B("stg%d" % i), writes=B("ckv%d" % i) + (B(*S0BN) if i == 0 else []))
                for c in range(2):
                    for hh in range(2):
                        h = 2 * c + hh
                        for mt in range(2):
                            S.op("pe", lambda e, h=h, mt=mt, c=c, hh=hh, i=i, s=s: e.matmul(xo4[:, c, hh, s * 8:(s + 1) * 8], ckv[i][:, 1, mt, c * 128:(c + 1) * 128], PTb[:, h, mt, s * 8:(s + 1) * 8], start=(mt == 0), stop=(mt == 1)),
                                 reads=B("ckv%d" % i, "hb"), writes=xo_b)
                yield
        yield
        for c in range(2):
            for hh in range(2):
                po = hh * 64
                S.op("dve", lambda e, c=c, hh=hh, po=po: e.tensor_tensor(sgT[po:po + 64, 6 + c, :], xo4[po:po + 64, c, hh, :], sgT[po:po + 64, 6 + c, :], op=ALU.mult),
                     reads=xo_b + B(SG), writes=B(SG))
        yield
        for half in range(2):
            for c in (6, 7):
                S.op("pe", lambda e, c=c, half=half: e.matmul(ovs[half][0], sgT[:, c, :], wout[:, c, half * 512:(half + 1) * 512], start=False, stop=(c == 7)),
                     reads=B(SG, "wout"), writes=ovs[half][1], kind="bf")
            S.op("dve", lambda e, half=half: e.tensor_tensor(xres[:, t, half * 512:(half + 1) * 512], xres[:, t, half * 512:(half + 1) * 512], ovs[half][0], op=ALU.add),
                 reads=ovs[half][1] + B("x%d" % t), writes=B("x%d" % t))
    def att_tail(l, t):
        col = (l + 1) * NT + t
        ssq_of_tile(t, col)
        rstd_from_ssq(col, 1024)
        if l == 1:
            st = stg[t % 2]
            for half in range(2):
                S.op("dve", lambda e, half=half, st=st: e.scalar_tensor_tensor(st[:], xres[:, t, half * 512:(half + 1) * 512], rstd[:, col:col + 1], gfin[:, half * 512:(half + 1) * 512], op0=ALU.mult, op1=ALU.mult),
                     reads=B("x%d" % t, "rstd_%d" % col, "KBG", "KD", "VB"), writes=B("stg%d" % (t % 2)))
                S.dma("sp", y_o[t * 128:(t + 1) * 128, half * 512:(half + 1) * 512], st[:], reads=B("stg%d" % (t % 2)))

    def conv_setup_hist(l, t, c, i, nh):
        sample = (t == NT - 1)
        hbuf = hist if l == 0 else hist2
        hname = "hist" if l == 0 else "hist2"
        if not sample:
            S.op("pool", lambda e: e.tensor_copy(xp[i][:, 0:nh], hbuf[:, c, :]), reads=B(hname), writes=B("xp%d" % i))
        else:
            srcd = sgc_d if l == 0 else ssc_d
            nr = 16 * nh
            S.dma("sp", hst[i][0:nr, :], srcd[0:nr, c * 128:(c + 1) * 128], writes=B("hst%d" % i))
            pv, pbf = PS(2, 2, 1)
            S.op("pe", lambda e: e.transpose(pv[:, 0:nr], hst[i][0:nr, :], C(I_F)[0:nr, 0:nr]),
                 reads=B("hst%d" % i, "cst"), writes=pbf)
            L = 8
            xp3 = xp[i][:, 0:16 * (nh + L)].rearrange("p (s k) -> p s k", s=16)
            S.op("act", lambda e: e.copy(xp3[:, :, 0:nh], pv[:, 0:nr].rearrange("p (s j) -> p s j", s=16)), reads=pbf, writes=B("xp%d" % i))

    def xp_views(t, i, nh):
        sample = (t == NT - 1)
        if not sample:
            cur = xp[i][:, nh:nh + 128]
            taps = [xp[i][:, j:j + 128] for j in range(nh + 1)]
            last = xp[i][:, 128:128 + nh]
            return cur, taps, last
        L = 8
        xp3 = xp[i][:, 0:16 * (nh + L)].rearrange("p (s k) -> p s k", s=16)
        cur = xp3[:, :, nh:nh + L]
        taps = [xp3[:, :, j:j + L] for j in range(nh + 1)]
        return cur, taps, None

    def as3(ap, t):
        if t == NT - 1:
            return ap.rearrange("p (s k) -> p s k", s=16)
        return ap

    def front0_a(t, pg):
        sample = (t == NT - 1)
        sgT, xqT = sgT2[pg], xqT2[pg]
        SG, XQ = "sgT%d" % pg, "xqT%d" % pg
        norm_transpose(t, t, 0)
        yield
        for c in range(8):
            pv, pbf = proj_chunk(2316 + c * 128)
            S.op("act", lambda e, c=c, pv=pv: e.activation(sgT[:, c, :], pv, AF.Silu), reads=pbf, writes=B(SG))
            yield
        for c in range(2):
            pv, pbf = proj_chunk(2316 + 1024 + c * 128)
            S.op("act", lambda e, c=c, pv=pv: e.copy(xqT[:, c, :], pv), reads=pbf, writes=B(XQ))
        yield
        def stA(c):
            i = c % 2
            pv, pbf = proj_chunk(c * 128)
            conv_setup_hist(0, t, c, i, 3)
            cur, taps, last = xp_views(t, i, 3)
            S.op("act", lambda e: e.copy(cur, as3(pv, t)), reads=pbf, writes=B("xp%d" % i))
            if not sample:
                S.op("pool", lambda e: e.tensor_copy(hist[:, c, :], last), reads=B("xp%d" % i), writes=B("hist"))
            acc = as3(cacc[i][:], t)
            S.op("act", lambda e: e.activation(acc, taps[3], AF.Identity, scale=cwa[:, c, 3:4]),
                 reads=B("xp%d" % i, "cwa"), writes=B("cacc%d" % i))

        def stB(c):
            i = c % 2
            cur, taps, last = xp_views(t, i, 3)
            acc = as3(cacc[i][:], t)
            for j in range(3):
                S.op("dve", lambda e, j=j: e.scalar_tensor_tensor(acc, taps[j], cwa[:, c, j:j + 1], acc, op0=ALU.mult, op1=ALU.add),
                     reads=B("xp%d" % i, "cwa", "cacc%d" % i), writes=B("cacc%d" % i))

        def stC(c):
            i = c % 2
            if c >= 12:
                dst, h, dn = VT, c - 12, "VT%d" % (c - 12)
            elif c < 6:
                dst, h, dn = QT, c, "QT%d" % c
            else:
                dst, h, dn = KT, c - 6, "KT%d" % (c - 6)
            S.op("act", lambda e: e.activation(dst[:, h, :], cacc[i][:], AF.Silu), reads=B("cacc%d" % i), writes=B(dn))

        stA(0)
        stB(0)
        for c in range(18):
            if c + 1 < 18:
                stA(c + 1)
            stC(c)
            if c + 1 < 18:
                stB(c + 1)
            yield

    def front0_b(t, pg):
        sample = (t == NT - 1)
        sgT, xqT = sgT2[pg], xqT2[pg]
        SG, XQ = "sgT%d" % pg, "xqT%d" % pg
        gcol = gat[:, 24:30]
        Um, SLm = (US_, SLS_) if sample else (U_, SL_)
        eG, eGrev, eGl = cum[:, 0:6], cum[:, 6:12], cum[:, 12:18]
        eGls = cum[:, 18:114].rearrange("p (s h) -> p s h", s=16)

        def gen_norm():
            for g in range(4):
                i = g % 2
                isq = (g < 2)
                c0 = g * 3
                names = [("QT%d" % (c0 + k)) if isq else ("KT%d" % (c0 - 6 + k)) for k in range(3)]
                blk = QKT[:, c0:c0 + 3, :]
                S.op("act", lambda e: e.activation(sqb[i][:], blk, AF.Square), reads=B(*names), writes=B("sqb%d" % i))
                qv, qb = PS((2, 0)[i], 0, 3)
                S.op("pe", lambda e: e.matmul(qv, onesb[:], sqb[i][:].rearrange("p a b -> p (a b)"), start=True, stop=True), reads=B("onesb", "sqb%d" % i), writes=qb, kind="bf")
                yield
                S.op("act", lambda e: e.activation(qv, qv, AF.Ln, bias=epsc[:, 0:1], scale=1.0), reads=qb + B("epsc"), writes=qb)
                S.op("act", lambda e: e.activation(qv, qv, AF.Exp, scale=-0.5), reads=qb, writes=qb)
                qv3 = qv.rearrange("p (a b) -> p a b", a=3)
                if isq:
                    S.op("dve", lambda e: e.scalar_tensor_tensor(blk, blk, float(128.0 ** -0.5), qv3, op0=ALU.mult, op1=ALU.mult), reads=qb + B(*names), writes=B(*names))
                else:
                    S.op("dve", lambda e: e.tensor_tensor(blk, blk, qv3, op=ALU.mult), reads=qb + B(*names), writes=B(*names))
                yield

        def gen_gates():
            bav, bab = PS(3, 0)
            for kc in range(8):
                S.op("pe", lambda e, kc=kc: e.matmul(bav[:, 0:12], hT[:, kc, :], win[:, kc, 2304:2316], start=(kc == 0), stop=(kc == 7)),
                     reads=B("hT", "win%d" % kc), writes=bab, kind="bf")
            yield
            S.op("act", lambda e: e.activation(gat[:, 0:6], bav[:, 0:6], AF.Exp, scale=-1.0), reads=bab, writes=B("gat"))
            S.op("dve", lambda e: e.tensor_tensor(gat[:, 6:12], bav[:, 6:12], small[:, 6:12], op=ALU.add), reads=bab + B("small", "gat"), writes=B("gat"))
            S.op("act", lambda e: e.activation(gat[:, 6:12], gat[:, 6:12], AF.Exp), reads=B("gat"), writes=B("gat"))
            S.op("act", lambda e: e.activation(gat[:, 6:12], gat[:, 6:12], AF.Ln, bias=1.0), reads=B("gat"), writes=B("gat"))
            yield
            S.op("dve", lambda e: e.tensor_scalar(gat[:, 0:6], gat[:, 0:6], 1.0, None, op0=ALU.add), reads=B("gat"), writes=B("gat"))
            S.op("dve", lambda e: e.reciprocal(gat[:, 12:18], gat[:, 0:6]), reads=B("gat"), writes=B("gat"))
            S.op("dve", lambda e: e.tensor_scalar(gat[:, 18:24], gat[:, 12:18], -1.0, None, op0=ALU.mult), reads=B("gat"), writes=B("gat"))
            S.op("dve", lambda e: e.tensor_tensor(gat[:, 24:30], gat[:, 6:12], nA[:], op=ALU.mult), reads=B("gat", "nA"), writes=B("gat"))
            cv, cb = PS(3, 1)
            S.op("pe", lambda e: e.matmul(cv[:, 0:6], C(Um), gcol, start=True, stop=True), reads=B("cst", "gat"), writes=cb, kind="f32")
            S.op("pe", lambda e: e.matmul(cv[:, 6:12], C(SLm), gcol, start=True, stop=True), reads=B("cst", "gat"), writes=cb, kind="f32")
            if not sample:
                S.op("pe", lambda e: e.matmul(cv[:, 12:18], C(ONES_F), gcol, start=True, stop=True), reads=B("cst", "gat"), writes=cb, kind="f32")
                yield
                S.op("act", lambda e: e.activation(cum[:, 0:18], cv[:, 0:18], AF.Exp), reads=cb, writes=B("cum"))
            else:
                gm = cum[:, 18:114].rearrange("p (s h) -> p s h", s=16)
                S.op("dve", lambda e: e.tensor_tensor(gm, gcol.unsqueeze(1).to_broadcast([128, 16, 6]), smask[:].unsqueeze(2).to_broadcast([128, 16, 6]), op=ALU.mult),
                     reads=B("gat", "smask", "cum"), writes=B("cum"))
                cv2, cb2 = PS(3, 2)
                S.op("pe", lambda e: e.matmul(cv2[:, 0:96], C(ONES_F), cum[:, 18:114], start=True, stop=True), reads=B("cst", "cum"), writes=cb2, kind="f32")
                yield
                S.op("act", lambda e: e.activation(cum[:, 0:12], cv[:, 0:12], AF.Exp), reads=cb, writes=B("cum"))
                S.op("act", lambda e: e.activation(cum[:, 18:114], cv2[:, 0:96], AF.Exp), reads=cb2, writes=B("cum"))
            S.op("dve", lambda e: e.tensor_tensor(gat[:, 30:36], gat[:, 12:18], eG, op=ALU.mult), reads=B("gat", "cum"), writes=B("gat"))
            tv_, tb_ = PSB(7, 0, 3)
            tvv = tv_[:, 0:768].rearrange("p (h d) -> p h d", h=6)
            for h in range(6):
                S.op("pe", lambda e, h=h: e.transpose(tvv[:, h, :], VT[:, h, :], identb[:]), reads=B("VT%d" % h, "identb"), writes=tb_, kind="tr")
            yield
            S.op("dve", lambda e: e.tensor_tensor(VB, tvv, gat[:, 12:18].unsqueeze(2).to_broadcast([128, 6, 128]), op=ALU.mult), reads=tb_ + B("gat"), writes=B("VB"))

        run_interleaved([gen_norm(), gen_gates()])
        if _SUB == 2:
            return
        tv, tb = PSB(0, 0, 3)
        tv3 = tv.rearrange("p (h d) -> p h d", h=6)
        for h in range(6):
            S.op("pe", lambda e, h=h: e.transpose(tv3[:, h, :], KT[:, h, :], identb[:]), reads=B("KT%d" % h, "identb"), writes=tb)
        S.op("dve", lambda e: e.tensor_tensor(KBG, tv3, gat[:, 30:36].unsqueeze(2).to_broadcast([128, 6, 128]), op=ALU.mult), reads=tb + B("gat"), writes=B("KBG"))
        S.op("dve", lambda e: e.tensor_tensor(KD, tv3, eGrev.unsqueeze(2).to_broadcast([128, 6, 128]), op=ALU.mult), reads=tb + B("cum"), writes=B("KD"))
        if _SUB == 3:
            return
        if t >= NT - 2:
            for blk in range(5):
                c0 = blk * 512
                w_ = min(512, 2304 - c0)
                pv, pbf = PS(7, 0, 4)
                for kc in range(8):
                    S.op("pe", lambda e, kc=kc, pv=pv, c0=c0, w_=w_: e.matmul(pv[:, 0:w_], hT[:, kc, :], win[:, kc, c0:c0 + w_], start=(kc == 0), stop=(kc == 7)),
                         reads=B("hT", "win%d" % kc), writes=pbf)
                st = stg[blk % 2]
                S.op("act", lambda e, st=st, pv=pv, w_=w_: e.copy(st[:, 0:w_], pv[:, 0:w_]), reads=pbf, writes=B("stg%d" % (blk % 2)))
                if not sample:
                    S.dma("sp", gcp_o[0:3, c0:c0 + w_], st[125:128, 0:w_], reads=B("stg%d" % (blk % 2)))
                else:
                    for s in range(16):
                        S.dma("sp", gcs_o[s * 3:(s + 1) * 3, c0:c0 + w_], st[s * 8 + 5:s * 8 + 8, 0:w_], reads=B("stg%d" % (blk % 2)))
        if t == NT - 1:
            load_win(1)
        MLm, MUm = (MLS_, MUS_) if sample else (ML_, MU_)
        nlev = 3 if sample else 7
        have_state = sample or t > 0
        HB = ((4, 5), (0, 1), (7, 2))

        def head_gen(h, i):
            bA, bB = HB[i]
            dlv, dlb = PS(bA, 0)
            duv, dub = PS(bA, 1)
            kkv, kkb = PS(bB, 0)
            kqv, kqb = PS(bB, 1)
            UGv, SLGv = Dl[i], Du[i]
            S.op("dve", lambda e: e.tensor_scalar(UGv[:], C(Um), gat[:, 24 + h:25 + h], None, op0=ALU.mult), reads=B("cst", "gat"), writes=B("Dl%d" % i))
            S.op("dve", lambda e: e.tensor_scalar(SLGv[:], C(SLm), gat[:, 24 + h:25 + h], None, op0=ALU.mult), reads=B("cst", "gat"), writes=B("Du%d" % i))
            S.op("pe", lambda e: e.matmul(dlv, UGv[:], C(SLm), start=True, stop=True), reads=B("Dl%d" % i, "cst"), writes=dlb, kind="f32")
            S.op("pe", lambda e: e.matmul(duv, SLGv[:], C(Um), start=True, stop=True), reads=B("Du%d" % i, "cst"), writes=dub, kind="f32")
            S.op("pe", lambda e: e.matmul(kkv, KT[:, h, :], KT[:, h, :], start=True, stop=True), reads=B("KT%d" % h), writes=kkb, kind="bf")
            S.op("pe", lambda e: e.matmul(kqv, KT[:, h, :], QT[:, h, :], start=True, stop=True), reads=B("KT%d" % h, "QT%d" % h), writes=kqb, kind="bf")
            yield
            S.op("act", lambda e: e.activation(Dl[i][:], dlv, AF.Exp), reads=dlb, writes=B("Dl%d" % i))
            S.op("act", lambda e: e.activation(Du[i][:], duv, AF.Exp), reads=dub, writes=B("Du%d" % i))
            S.op("dve", lambda e: e.tensor_tensor(Dl[i][:], Dl[i][:], C(MLm), op=ALU.mult), reads=B("Dl%d" % i, "cst"), writes=B("Dl%d" % i))
            S.op("dve", lambda e: e.tensor_tensor(Du[i][:], Du[i][:], C(MUm), op=ALU.mult), reads=B("Du%d" % i, "cst"), writes=B("Du%d" % i))
            R0 = Rb[i][0]
            S.op("dve", lambda e: e.scalar_tensor_tensor(R0[:], kkv, gat[:, 18 + h:19 + h], Dl[i][:], op0=ALU.mult, op1=ALU.mult),
                 reads=kkb + B("gat", "Dl%d" % i), writes=B("R%d_0" % i))
            S.op("dve", lambda e: e.tensor_tensor(AqT[i][:], kqv, Du[i][:], op=ALU.mult), reads=kqb + B("Du%d" % i), writes=B("AqT%d" % i))
            yield
            ntv, ntb = PS(bA, 2)
            S.op("pe", lambda e: e.transpose(ntv, R0[:], C(I_F)), reads=B("R%d_0" % i, "cst"), writes=ntb, kind="tr")
            yield
            S.op("act", lambda e: e.copy(Qb[i][0][:], ntv), reads=ntb, writes=B("Q%d_0" % i))
            S.op("dve", lambda e: e.tensor_tensor(Xb[i][:], ntv, C(I_F), op=ALU.add), reads=ntb + B("cst"), writes=B("X%d" % i))
            for k in range(1, nlev):
                a, b_ = (k - 1) % 2, k % 2
                rv, rb_ = PS(bA, 0)
                qv_, qb_ = PS(bA, 1)
                xv, xb_ = PS(bB, 0)
                S.op("pe", lambda e: e.matmul(rv, Qb[i][a][:], Rb[i][a][:], start=True, stop=True),
                     reads=B("Q%d_%d" % (i, a), "R%d_%d" % (i, a)), writes=rb_, kind="f32")
                if k < nlev - 1:
                    S.op("pe", lambda e: e.matmul(qv_, Rb[i][a][:], Qb[i][a][:], start=True, stop=True),
                         reads=B("Q%d_%d" % (i, a), "R%d_%d" % (i, a)), writes=qb_, kind="f32")
                yield
                S.op("act", lambda e: e.copy(Rb[i][b_][:], rv), reads=rb_, writes=B("R%d_%d" % (i, b_)))
                if k < nlev - 1:
                    S.op("act", lambda e: e.copy(Qb[i][b_][:], qv_), reads=qb_, writes=B("Q%d_%d" % (i, b_)))
                S.op("pe", lambda e: e.matmul(xv, Rb[i][b_][:], Xb[i][:], start=True, stop=True),
                     reads=B("R%d_%d" % (i, b_), "X%d" % i), writes=xb_, kind="f32")
                yield
                S.op("dve", lambda e: e.tensor_tensor(Xb[i][:], xv, Xb[i][:], op=ALU.add), reads=xb_ + B("X%d" % i), writes=B("X%d" % i))
            S.op("act", lambda e: e.copy(Xbf[i][:], Xb[i][:]), reads=B("X%d" % i), writes=B("Xbf%d" % i))
            wv, wb = PS(bA, 2)
            S.op("pe", lambda e: e.matmul(wv, KBG[:, h, :], Xbf[i][:], start=True, stop=True), reads=B("KBG", "Xbf%d" % i), writes=wb, kind="bf")
            yield
            S.op("act", lambda e: e.activation(wTn[i][:], wv, AF.Identity, scale=-1.0), reads=wb, writes=B("wTn%d" % i))
            vv, vb = PS(bB, 1)
            qsv, qsb = PS(bB, 2)
            if not sample:
                S.op("pe", lambda e: e.matmul(vv, Xbf[i][:], VB[:, h, :], start=True, stop=(not have_state)), reads=B("Xbf%d" % i, "VB"), writes=vb, kind="bf")
                if have_state:
                    S.op("pe", lambda e: e.matmul(vv, wTn[i][:], Sbf[:, h, :], start=False, stop=True), reads=B("wTn%d" % i, "S%d" % h), writes=vb, kind="bf")
                    S.op("pe", lambda e: e.matmul(qsv, QT[:, h, :], Sbf[:, h, :], start=True, stop=True), reads=B("QT%d" % h, "S%d" % h), writes=qsb, kind="bf")
            else:
                t1v, t1b = PS(bA, 0)
                t2v, t2b = PS(bA, 1)
                for q4 in range(4):
                    j = q4 % 2
                    S.dma("sp", S0v[j], sg4[:, q4 * 4:(q4 + 1) * 4, h, :], writes=B(S0n[j]))
                    S.op("act", lambda e, j=j: e.copy(S0bf[:, j * 4:j * 4 + 4, :], S0v[j]), reads=B(S0n[j]), writes=B(*["S0bf_%d" % (j * 4 + k) for k in range(4)]) + B("ckv0"))
                    for k in range(4):
                        s, s8 = q4 * 4 + k, j * 4 + k
                        S.op("pe", lambda e, s=s, s8=s8: e.matmul(t1v[:, s * 8:(s + 1) * 8], S0bf[:, s8, :], wTn[i][:, s * 8:(s + 1) * 8], start=True, stop=True),
                             reads=B("S0bf_%d" % s8, "wTn%d" % i), writes=t1b, kind="bf")
                        S.op("pe", lambda e, s=s, s8=s8: e.matmul(t2v[:, s * 8:(s + 1) * 8], S0bf[:, s8, :], QT[:, h, s * 8:(s + 1) * 8], start=True, stop=True),
                             reads=B("S0bf_%d" % s8, "QT%d" % h), writes=t2b, kind="bf")
                S.op("act", lambda e: e.copy(tsb[0][:], t1v), reads=t1b, writes=B("tsb0"))
                S.op("act", lambda e: e.copy(tsb[1][:], t2v), reads=t2b, writes=B("tsb1"))
                S.op("pe", lambda e: e.matmul(vv, Xbf[i][:], VB[:, h, :], start=True, stop=False), reads=B("Xbf%d" % i, "VB"), writes=vb, kind="bf")
                S.op("pe", lambda e: e.matmul(vv, tsb[0][:], identb[:], start=False, stop=True), reads=B("tsb0", "identb"), writes=vb, kind="bf")
                S.op("pe", lambda e: e.matmul(qsv, tsb[1][:], identb[:], start=True, stop=True), reads=B("tsb1", "identb"), writes=qsb, kind="bf")
            yield
            S.op("act", lambda e: e.copy(vnew[i][:], vv), reads=vb, writes=B("vnew%d" % i))
            av, ab = PS(bA, 3)
            S.op("pe", lambda e: e.matmul(av, AqT[i][:], vnew[i][:], start=True, stop=True), reads=B("AqT%d" % i, "vnew%d" % i), writes=ab, kind="bf")
            if not sample:
                suv, sub = PS(6, i)
                S.op("pe", lambda e: e.matmul(suv, KD[:, h, :], vnew[i][:], start=True, stop=True), reads=B("KD", "vnew%d" % i), writes=sub, kind="bf")
            yield
            ofpv = Dl[i]
            avsv = Du[i]
            if have_state:
                S.op("act", lambda e: e.copy(avsv[:], av), reads=ab, writes=B("Du%d" % i))
                S.op("dve", lambda e: e.scalar_tensor_tensor(ofpv[:], qsv, eG[:, h:h + 1], avsv[:], op0=ALU.mult, op1=ALU.add),
                     reads=qsb + B("cum", "Du%d" % i), writes=B("Dl%d" % i))
            else:
                S.op("act", lambda e: e.copy(ofpv[:], av), reads=ab, writes=B("Dl%d" % i))
            if not sample:
                if have_state:
                    S.op("dve", lambda e: e.scalar_tensor_tensor(Sst[:, h, :], Sst[:, h, :], eGl[:, h:h + 1], suv, op0=ALU.mult, op1=ALU.add),
                         reads=sub + B("cum", "S%d" % h), writes=B("S%d" % h))
                else:
                    S.op("dve", lambda e: e.tensor_copy(Sst[:, h, :], suv), reads=sub, writes=B("S%d" % h))
                S.op("act", lambda e: e.copy(Sbf[:, h, :], Sst[:, h, :]), reads=B("S%d" % h), writes=B("S%d" % h))
                if t == NT - 2:
                    S.dma("sp", sp_o[h * 128:(h + 1) * 128, :], Sst[:, h, :], reads=B("S%d" % h))
            gn = "gatn%d" % h
            S.op("dve", lambda e: e.memset(gat[:, 36 + h:37 + h], 0.0), writes=B(gn))
            S.op("act", lambda e: e.activation(wTn[i][:], ofpv[:], AF.Square, accum_out=gat[:, 36 + h:37 + h]), reads=B("Dl%d" % i, gn), writes=B("wTn%d" % i, gn))
            S.op("act", lambda e: e.activation(gat[:, 42 + h:43 + h], gat[:, 36 + h:37 + h], AF.Ln, bias=epsc[:, 0:1], scale=1.0 / 128), reads=B(gn, "epsc"), writes=B(gn))
            S.op("act", lambda e: e.activation(gat[:, 42 + h:43 + h], gat[:, 42 + h:43 + h], AF.Exp, scale=-0.5), reads=B(gn), writes=B(gn))
            S.op("dve", lambda e: e.scalar_tensor_tensor(onb[i][:], ofpv[:], gat[:, 42 + h:43 + h], ogb[:], op0=ALU.mult, op1=ALU.mult),
                 reads=B("Dl%d" % i, gn, "ogb"), writes=B("onb%d" % i))
            otv, otb = PSB(3, i, 1)
            S.op("pe", lambda e: e.transpose(otv[:, 0:128], onb[i][:], identb[:]), reads=B("onb%d" % i, "identb"), writes=otb, kind="tr")
            yield
            S.op("dve", lambda e: e.tensor_tensor(sgT[:, h, :], otv[:, 0:128], sgT[:, h, :], op=ALU.mult), reads=otb + B(SG), writes=B(SG))
            if sample:
                for q4 in range(4):
                    j = q4 % 2
                    S.dma("sp", S0v[j], sg4[:, q4 * 4:(q4 + 1) * 4, h, :], writes=B(S0n[j]))
                    def mk_mask(k):
                        s = q4 * 4 + k
                        jj = s % 2
                        S.op("dve", lambda e: e.tensor_scalar(kdm[jj][:], KD[:, h, :], smask[:, s:s + 1], None, op0=ALU.mult),
                             reads=B("KD", "smask"), writes=B("kdm%d" % jj))
                        suv, sub = PS((bA, bB)[jj], 0)
                        S.op("pe", lambda e: e.matmul(suv, kdm[jj][:], vnew[i][:], start=True, stop=True), reads=B("kdm%d" % jj, "vnew%d" % i), writes=sub, kind="bf")
                        return suv, sub

                    def fin(k, suv, sub):
                        s = q4 * 4 + k
                        S.op("dve", lambda e: e.scalar_tensor_tensor(Snw4[j][:, k, :], S0v[j][:, k, :], eGls[:, s, h:h + 1], suv, op0=ALU.mult, op1=ALU.add),
                             reads=sub + B("cum", S0n[j]), writes=B("stg%d" % j))

                    pend = mk_mask(0)
                    for k in range(4):
                        nxt = mk_mask(k + 1) if k + 1 < 4 else None
                        fin(k, *pend)
                        pend = nxt
                    S.dma("sp", ss4[:, q4 * 4:(q4 + 1) * 4, h, :], Snw4[j], reads=B("stg%d" % j))
                    yield

        STAG = 0
        slots = [None, None, None]
        steps = [0, 0, 0]
        nxt_h = 0
        tick = 0
        while nxt_h < 6 or any(g is not None for g in slots):
            for i in range(3):
                if slots[i] is None and nxt_h < 6 and tick >= nxt_h * STAG:
                    slots[i] = head_gen(nxt_h, i)
                    nxt_h += 1
                if slots[i] is not None:
                    try:
                        next(slots[i])
                    except StopIteration:
                        slots[i] = None
            tick += 1

    def front1_a(t, pg):
        sample = (t == NT - 1)
        sgT, xqT = sgT2[pg], xqT2[pg]
        SG, XQ = "sgT%d" % pg, "xqT%d" % pg
        norm_transpose(t, NT + t, 8)
        yield
        for c in range(8):
            pv, pbf = proj_chunk(2304 + c * 128)
            S.op("act", lambda e, c=c, pv=pv: e.activation(sgT[:, c, :], pv, AF.Silu), reads=pbf, writes=B(SG))
            yield
        for c in range(2):
            pv, pbf = proj_chunk(2304 + 1024 + c * 128)
            S.op("act", lambda e, c=c, pv=pv: e.copy(xqT[:, c, :], pv), reads=pbf, writes=B(XQ))
        yield
        def l1A(c):
            i = c % 2
            pv, pbf = proj_chunk(768 + c * 128)
            S.op("act", lambda e: e.copy(sil[i][:], pv), reads=pbf, writes=B("sil%d" % i))
            pv2, pbf2 = proj_chunk(1536 + c * 128)
            conv_setup_hist(1, t, c, i, 2)
            cur, taps, last = xp_views(t, i, 2)
            S.op("dve", lambda e: e.tensor_tensor(cur, as3(pv2, t), as3(sil[i][:], t), op=ALU.mult), reads=pbf2 + B("sil%d" % i), writes=B("xp%d" % i))
            if not sample:
                S.op("pool", lambda e: e.tensor_copy(hist2[:, c, :], last), reads=B("xp%d" % i), writes=B("hist2"))
            acc = as3(cacc[i][:], t)
            S.op("act", lambda e: e.activation(acc, taps[2], AF.Identity, scale=cwb[:, c, 2:3]), reads=B("xp%d" % i, "cwb"), writes=B("cacc%d" % i))

        def l1B(c):
            i = c % 2
            cur, taps, last = xp_views(t, i, 2)
            acc = as3(cacc[i][:], t)
            for j in range(2):
                S.op("dve", lambda e, j=j: e.scalar_tensor_tensor(acc, taps[j], cwb[:, c, j:j + 1], acc, op0=ALU.mult, op1=ALU.add),
                     reads=B("xp%d" % i, "cwb", "cacc%d" % i), writes=B("cacc%d" % i))
            pv3, pbf3 = proj_chunk(c * 128)
            S.op("dve", lambda e: e.tensor_tensor(cacc[i][:], pv3, cacc[i][:], op=ALU.mult), reads=pbf3 + B("cacc%d" % i), writes=B("cacc%d" % i))
            S.op("dve", lambda e: e.tensor_tensor(sgT[:, c, :], cacc[i][:], sgT[:, c, :], op=ALU.mult), reads=B("cacc%d" % i, SG), writes=B(SG))

        l1A(0)
        for c in range(6):
            if c + 1 < 6:
                l1A(c + 1)
            l1B(c)
            yield
        if t >= NT - 2:
            for blk in range(2):
                c0 = blk * 384
                pv, pbf = PS(7, 0, 4)
                pw, pwb = PS(6, 0, 4)
                for kc in range(8):
                    S.op("pe", lambda e, kc=kc, pv=pv, c0=c0: e.matmul(pv[:, 0:384], hT[:, kc, :], win[:, kc, 768 + c0:768 + c0 + 384], start=(kc == 0), stop=(kc == 7)),
                         reads=B("hT", "win%d" % kc), writes=pbf)
                for kc in range(8):
                    S.op("pe", lambda e, kc=kc, pw=pw, c0=c0: e.matmul(pw[:, 0:384], hT[:, kc, :], win[:, kc, 1536 + c0:1536 + c0 + 384], start=(kc == 0), stop=(kc == 7)),
                         reads=B("hT", "win%d" % kc), writes=pwb)
                st = stg[blk % 2]
                S.op("act", lambda e, st=st, pv=pv: e.copy(st[:, 0:384], pv[:, 0:384]), reads=pbf, writes=B("stg%d" % (blk % 2)))
                S.op("dve", lambda e, st=st, pw=pw: e.tensor_tensor(st[:, 0:384], pw[:, 0:384], st[:, 0:384], op=ALU.mult), reads=pwb + B("stg%d" % (blk % 2)), writes=B("stg%d" % (blk % 2)))
                if not sample:
                    S.dma("sp", scp_o[0:2, c0:c0 + 384], st[126:128, 0:384], reads=B("stg%d" % (blk % 2)))
                else:
                    for s in range(16):
                        S.dma("sp", scs_o[s * 2:(s + 1) * 2, c0:c0 + 384], st[s * 8 + 6:s * 8 + 8, 0:384], reads=B("stg%d" % (blk % 2)))

    seq = [(0, t) for t in range(NT)] + [(1, t) for t in range(NT)]
    prev = None
    for k, (l, t) in enumerate(seq):
        pg = k % 2
        fa = front0_a(t, pg) if l == 0 else front1_a(t, pg)
        gens = [fa]
        if prev is not None:
            gens.insert(0, att_gen(prev[0], prev[1], prev[2]))
        run_interleaved(gens)
        if prev is not None:
            att_tail(prev[0], prev[1])
        if l == 1 and t == 0:
            load_wout(1)
            S.dma("sp", gfin, gfin_d[:, :], writes=B("KBG", "KD", "VB"))
        if l == 0:
            front0_b(t, pg)
        prev = (l, t, pg)
    run_interleaved([att_gen(prev[0], prev[1], prev[2])])
    att_tail(prev[0], prev[1])
    S.finish()
    es.close()
    return nc, S


_CACHE = {}
_STAGE = 99
_SUB = 99


def _consts():
    i = np.arange(128)
    same = (i[:, None] // 8) == (i[None, :] // 8)
    ident = np.eye(128)
    ones = np.ones((128, 128))
    U = (i[:, None] <= i[None, :])
    SL = (i[:, None] > i[None, :])
    ML = (i[:, None] > i[None, :])
    MU = (i[:, None] <= i[None, :])
    mats = [ident, ones, U, SL, U & same, SL & same]
    cst = np.stack([m.astype(np.float32) for m in mats], axis=1).reshape(128, 6 * 128)
    smask = ((i[:, None] // 8) == np.arange(16)[None, :]).astype(np.float32)
    return np.ascontiguousarray(cst), np.ascontiguousarray(smask)


def kernel(x_prompt, x_sample, mem_prompt, state_gdn, state_gdn_conv, state_sconv,
           cache_mem_k, cache_mem_v, norm_g, w_in_a, conv_w_a, a_log, dt_bias, o_norm_g,
           w_in_b, conv_w_b, mem_norm_g, w_mem_kv, w_out, final_norm_g):
    f = lambda a: np.ascontiguousarray(np.asarray(a, dtype=np.float32))
    if "nc" not in _CACHE:
        _CACHE["nc"] = build_program(_STAGE)[0]
    nc = _CACHE["nc"]
    cst, smask = _consts()
    x_prompt, x_sample, mem_prompt = f(x_prompt), f(x_sample), f(mem_prompt)
    state_gdn, state_gdn_conv, state_sconv = f(state_gdn), f(state_gdn_conv), f(state_sconv)
    cache_mem_k, cache_mem_v = f(cache_mem_k), f(cache_mem_v)
    gfm = np.concatenate([f(norm_g).reshape(2, 8, 128), f(mem_norm_g).reshape(1, 8, 128)], axis=0)
    gfm = np.ascontiguousarray(gfm.transpose(2, 0, 1).reshape(128, 24))
    cwa = np.ascontiguousarray(f(conv_w_a)[0].reshape(4, 18, 128).transpose(2, 1, 0).reshape(128, 72))
    cwb = np.ascontiguousarray(f(conv_w_b)[0].reshape(3, 6, 128).transpose(2, 1, 0).reshape(128, 18))
    small = np.ascontiguousarray(np.broadcast_to(np.concatenate([f(a_log)[0], f(dt_bias)[0]])[None, :], (128, 12)))
    ogb = np.ascontiguousarray(np.broadcast_to(f(o_norm_g)[0][None, :], (128, 128)))
    gfin = np.ascontiguousarray(np.broadcast_to(f(final_norm_g)[None, :], (128, 1024)))
    shared = {
        "wina": f(w_in_a)[0], "winb": f(w_in_b)[0], "wout": f(w_out).reshape(2048, 1024),
        "wkv": f(w_mem_kv).reshape(2048, 512), "gfm": gfm, "cwa": cwa, "cwb": cwb, "small": small,
        "ogb": ogb, "gfin": gfin, "cst": cst, "smask": smask,
    }
    in_maps = []
    for c in range(8):
        sl = slice(c * 16, (c + 1) * 16)
        m = dict(shared)
        m["x"] = np.ascontiguousarray(np.concatenate([x_prompt[c], x_sample[sl].reshape(128, 1024)], axis=0))
        m["mem"] = mem_prompt[c]
        m["sgdn"] = np.ascontiguousarray(state_gdn[0, sl].reshape(16 * 6 * 128, 128))
        m["sgc"] = np.ascontiguousarray(state_gdn_conv[0, sl].reshape(48, 2304))
        m["ssc"] = np.ascontiguousarray(state_sconv[0, sl].reshape(32, 768))
        m["cmk"] = np.ascontiguousarray(cache_mem_k[:, sl].reshape(2 * 16 * 256, 256))
        m["cmv"] = np.ascontiguousarray(cache_mem_v[:, sl].reshape(2 * 16 * 256, 256))
        in_maps.append(m)
    res = run_bass_kernel_spmd(nc, in_maps, core_ids=list(range(8)))
    R = res.results
    y = np.stack([r["y"] for r in R])
    y_prompt = np.ascontiguousarray(y[:, :2048])
    y_sample = np.ascontiguousarray(y[:, 2048:].reshape(128, 8, 1024))
    S_p = np.stack([r["sp_o"].reshape(6, 128, 128) for r in R])[None]
    gc_p = np.stack([r["gcp_o"] for r in R])[None]
    sc_p = np.stack([r["scp_o"] for r in R])[None]
    mk_p = np.stack([r["mkp_o"].reshape(2, 256, 4, 64) for r in R], axis=1)
    mv_p = np.stack([r["mvp_o"].reshape(2, 256, 4, 64) for r in R], axis=1)
    S_s = np.concatenate([r["ss_o"].reshape(16, 6, 128, 128) for r in R])[None]
    gc_s = np.concatenate([r["gcs_o"].reshape(16, 3, 2304) for r in R])[None]
    sc_s = np.concatenate([r["scs_o"].reshape(16, 2, 768) for r in R])[None]
    outs = (y_prompt, y_sample, S_p, gc_p, sc_p, mk_p, mv_p, S_s, gc_s, sc_s)
    return tuple(np.ascontiguousarray(o, dtype=np.float32) for o in outs)
```

```python
import numpy as np
from contextlib import ExitStack
import concourse.bass as bass
import concourse.mybir as mybir
from concourse.bass_utils import run_bass_kernel_spmd

F32 = mybir.dt.float32
BF16 = mybir.dt.bfloat16
AF = mybir.ActivationFunctionType
ALU = mybir.AluOpType
AX = mybir.AxisListType

NT = 17
EPS = 1e-6
CA = 3596
CB = 3584


class Buf:
    __slots__ = ("name", "last_w", "readers", "excl")

    def __init__(self, name, excl=False):
        self.name = name
        self.last_w = None
        self.readers = {}
        self.excl = excl


class Sched:
    def __init__(self, nc, es, n_dma_sems=(24, 4, 12)):
        self.nc = nc
        self.eng = {"pe": nc.tensor, "act": nc.scalar, "dve": nc.vector, "pool": nc.gpsimd, "sp": nc.sync}
        self.sem, self.cnt, self.seen = {}, {}, {}
        for k in self.eng:
            self.sem[k] = es.enter_context(nc.semaphore("s_" + k))
            self.cnt[k] = 0
            self.seen[k] = {}
        self.dma_sems, self.dma_rr = {}, {}
        for q, n in zip(("sp", "act", "pool"), n_dma_sems):
            self.dma_sems[q] = []
            self.dma_rr[q] = 0
            for i in range(n):
                key = "d%s%d" % (q, i)
                self.sem[key] = es.enter_context(nc.semaphore("s_" + key))
                self.cnt[key] = 0
                self.dma_sems[q].append(key)
        self.n_inst = 0
        self.n_wait = 0
        self.pe_last = None
        self.sep = None
        self.attach = True
        self.snap = {}

    def _wait(self, e, deps, defer=False):
        best = {}
        for (k, c) in deps:
            if c > best.get(k, 0):
                best[k] = c
        cand = sorted(best.items(), key=lambda kc: -kc[1])
        need = []
        for k, c in cand:
            if self.seen[e].get(k, 0) >= c:
                continue
            need.append((k, c))
            self._learn(e, k, c)
        last = None
        if defer and need:
            last = need.pop()
        for k, c in need:
            self.eng[e].wait_ge(self.sem[k], c)
            self.n_wait += 1
        return last

    def _learn(self, e, k, c):
        se = self.seen[e]
        if se.get(k, 0) < c:
            se[k] = c
        sn = self.snap.get((k, c))
        if sn:
            for kk, cc in sn.items():
                if se.get(kk, 0) < cc:
                    se[kk] = cc

    @staticmethod
    def _deps(reads, writes):
        deps = []
        for b in reads:
            if b.last_w is not None:
                deps.append(b.last_w)
        for b in writes:
            if b.last_w is not None:
                deps.append(b.last_w)
            deps.extend(b.readers.items())
        return deps

    def op(self, e, fn, reads=(), writes=(), kind=None):
        if e == "pe":
            if kind == "bf" and self.pe_last == "f32" and self.sep is not None:
                self.pe_last = "tr"
                self.sep()
            if kind is not None:
                self.pe_last = kind
        ex = [b for b in reads if b.excl]
        if ex:
            writes = list(writes) + [b for b in ex if b not in writes]
            reads = [b for b in reads if not b.excl]
        deps = self._deps(reads, writes)
        if e == "pe":
            deps = [d for d in deps if d[0] != "pe"]
        last = self._wait(e, deps, defer=self.attach)
        ins = fn(self.eng[e])
        if last is not None:
            ins._wait_ge(self.sem[last[0]], last[1])
        ins.then_inc(self.sem[e], 1)
        self.cnt[e] += 1
        c = self.cnt[e]
        self.snap[(e, c)] = dict(self.seen[e])
        for b in writes:
            b.last_w = (e, c)
            b.readers = {}
        for b in reads:
            if b not in writes:
                b.readers[e] = c
        self.n_inst += 1
        return ins

    def dma(self, q, out, in_, reads=(), writes=(), **kw):
        key = self.dma_sems[q][self.dma_rr[q] % len(self.dma_sems[q])]
        self.dma_rr[q] += 1
        deps = self._deps(reads, writes)
        if self.cnt[key] > 0:
            deps.append((key, self.cnt[key]))
        last = self._wait(q, deps, defer=self.attach)
        ins = self.eng[q].dma_start(out=out, in_=in_, **kw)
        if last is not None:
            ins._wait_ge(self.sem[last[0]], last[1])
        ins.then_inc(self.sem[key], 16)
        self.cnt[key] += 16
        c = self.cnt[key]
        self.snap[(key, c)] = dict(self.seen[q])
        for b in writes:
            b.last_w = (key, c)
            b.readers = {}
        for b in reads:
            b.readers[key] = c
        self.n_inst += 1
        return ins

    def finish(self):
        deps = [(k, c) for k, c in self.cnt.items() if c > 0]
        for e in ("sp", "act", "pool", "dve", "pe"):
            self._wait(e, [d for d in deps if d[0] != e])


def build_program(stage=99):
    nc = bass.Bass("TRN2", target_bir_lowering=False)

    def din(name, shape):
        return nc.dram_tensor(name, list(shape), F32, kind="ExternalInput").ap()

    def dout(name, shape):
        return nc.dram_tensor(name, list(shape), F32, kind="ExternalOutput").ap()

    x_d = din("x", [NT * 128, 1024])
    mem_d = din("mem", [256, 1024])
    sgdn_d = din("sgdn", [16 * 6 * 128, 128])
    sgc_d = din("sgc", [48, 2304])
    ssc_d = din("ssc", [32, 768])
    cmk_d = din("cmk", [2 * 16 * 256, 256])
    cmv_d = din("cmv", [2 * 16 * 256, 256])
    wina_d = din("wina", [1024, CA])
    winb_d = din("winb", [1024, CB])
    wout_d = din("wout", [2048, 1024])
    wkv_d = din("wkv", [2048, 512])
    gfm_d = din("gfm", [128, 24])
    cwa_d = din("cwa", [128, 18 * 4])
    cwb_d = din("cwb", [128, 6 * 3])
    small_d = din("small", [128, 12])
    ogb_d = din("ogb", [128, 128])
    gfin_d = din("gfin", [128, 1024])
    cst_d = din("cst", [128, 6 * 128])
    smask_d = din("smask", [128, 16])

    y_o = dout("y", [NT * 128, 1024])
    sp_o = dout("sp_o", [6 * 128, 128])
    gcp_o = dout("gcp_o", [3, 2304])
    scp_o = dout("scp_o", [2, 768])
    mkp_o = dout("mkp_o", [512, 256])
    mvp_o = dout("mvp_o", [512, 256])
    ss_o = dout("ss_o", [16 * 6 * 128, 128])
    gcs_o = dout("gcs_o", [48, 2304])
    scs_o = dout("scs_o", [32, 768])

    es = ExitStack()
    S = Sched(nc, es)
    bufs = {}

    def sb(name, shape, dt=F32):
        t = es.enter_context(nc.sbuf_tensor("sb_" + name, list(shape), dt))
        bufs[name] = Buf(name)
        return t

    def B(*names):
        return [bufs[n] for n in names]

    def nb(name):
        bufs[name] = Buf(name)
        return bufs[name]

    xres = sb("xres", [128, NT, 1024])
    for t in range(NT):
        nb("x%d" % t)
    win = sb("win", [128, 8, CA], BF16)
    for k in range(8):
        nb("win%d" % k)
    wout = sb("wout", [128, 8, 1024], BF16)
    cst = sb("cst", [128, 6, 128])
    identb = sb("identb", [128, 128], BF16)
    onesb = sb("onesb", [128, 128], BF16)
    smask = sb("smask", [128, 16])
    gfm = sb("gfm", [128, 24])
    cwa = sb("cwa", [128, 18, 4])
    cwb = sb("cwb", [128, 6, 3])
    small = sb("small", [128, 12])
    nA = sb("nA", [128, 6])
    ogb = sb("ogb", [128, 128])
    epsc = sb("epsc", [128, 2])
    rstd = sb("rstd", [128, 3 * NT + 2])
    for i in range(3 * NT + 2):
        nb("rstd_%d" % i)
    ssq = sb("ssq", [128, 3 * NT + 2])
    mkT = sb("mkT", [128, 2, 2, 256], BF16)
    mvp = sb("mvp", [128, 2, 2, 256], BF16)
    hb = sb("hb", [128, 1024], BF16)
    hT = sb("hT", [128, 8, 128], BF16)
    xp = [sb("xp%d" % i, [128, 16 * 11]) for i in range(2)]
    hist = sb("hist", [128, 18, 3])
    hist2 = sb("hist2", [128, 6, 2])
    cacc = [sb("cacc%d" % i, [128, 128]) for i in range(2)]
    sil = [sb("sil%d" % i, [128, 128]) for i in range(2)]
    sqb = [sb("sqb%d" % i, [128, 3, 128], BF16) for i in range(2)]
    QKT = sb("QKT", [128, 12, 128], BF16)
    QT, KT = QKT[:, 0:6], QKT[:, 6:12]
    VT = sb("VT", [128, 6, 128], BF16)
    for h in range(6):
        nb("QT%d" % h), nb("KT%d" % h), nb("VT%d" % h)
    sgT2 = [sb("sgT%d" % i, [128, 8, 128], BF16) for i in range(2)]
    xqT2 = [sb("xqT%d" % i, [128, 2, 128], BF16) for i in range(2)]
    gat = sb("gat", [128, 48])
    cum = sb("cum", [128, 18 + 96])
    kkv3 = sb("kkv3", [128, 3, 6, 128], BF16)
    KBG, KD, VB = kkv3[:, 0], kkv3[:, 1], kkv3[:, 2]
    nb("KBG"), nb("KD"), nb("VB")
    for h6 in range(6):
        nb("gatn%d" % h6)
    gfin = kkv3[:].rearrange("p a b c -> p (a b c)")[:, 0:2048].bitcast(F32)
    Dl = [sb("Dl%d" % i, [128, 128]) for i in range(3)]
    Du = [sb("Du%d" % i, [128, 128]) for i in range(3)]
    AqT = [sb("AqT%d" % i, [128, 128], BF16) for i in range(3)]
    Rb = [[sb("R%d_%d" % (i, j), [128, 128]) for j in range(2)] for i in range(3)]
    Qb = [[sb("Q%d_%d" % (i, j), [128, 128]) for j in range(2)] for i in range(3)]
    Xb = [sb("X%d" % i, [128, 128]) for i in range(3)]
    Xbf = [sb("Xbf%d" % i, [128, 128], BF16) for i in range(3)]
    wTn = [sb("wTn%d" % i, [128, 128], BF16) for i in range(3)]
    vnew = [sb("vnew%d" % i, [128, 128], BF16) for i in range(3)]
    onb = [sb("onb%d" % i, [128, 128], BF16) for i in range(3)]
    Sst = sb("Sst", [128, 6, 128])
    Sbf = sb("Sbf", [128, 6, 128], BF16)
    for h in range(6):
        nb("S%d" % h)
    Pb = sb("Pb", [128, 4, 256], BF16)
    PTb = hb[:].rearrange("p (h m t) -> p h m t", h=4, m=2)
    sm = sb("sm", [128, 16])
    stg = [sb("stg%d" % i, [128, 512]) for i in range(2)]
    for k8 in range(8):
        nb("S0bf_%d" % k8)
    tsb = [sb("tsb%d" % i, [128, 128], BF16) for i in range(2)]
    kdm = [sb("kdm%d" % i, [128, 128], BF16) for i in range(2)]
    xqm = [sb("xqm%d" % i, [128, 2, 128], BF16) for i in range(2)]
    ckv = [sb("ckv%d" % i, [128, 2, 2, 256], BF16) for i in range(2)]
    S0bf = ckv[0][:].rearrange("p a b c -> p (a b c)").rearrange("p (s e) -> p s e", s=8)
    S0BN = ["S0bf_%d" % k8 for k8 in range(8)]
    mkTs = [sb("mkTs%d" % i, [128, 2, 256], BF16) for i in range(2)]
    hst = [sb("hst%d" % i, [48, 128]) for i in range(2)]

    S0v = [hb[:].bitcast(F32).rearrange("p (s e) -> p s e", s=4),
           Pb[:].rearrange("p a b -> p (a b)").bitcast(F32).rearrange("p (s e) -> p s e", s=4)]
    S0n = ["hb", "Pb"]
    Snw4 = [stg[0][:].rearrange("p (s e) -> p s e", s=4), stg[1][:].rearrange("p (s e) -> p s e", s=4)]
    stgKV = [stg[0][:].rearrange("p (m c) -> p m c", m=2), stg[1][:].rearrange("p (m c) -> p m c", m=2)]
    sg4 = sgdn_d.rearrange("(s h d) e -> d s h e", s=16, h=6)
    ss4 = ss_o.rearrange("(s h d) e -> d s h e", s=16, h=6)

    pbank = [es.enter_context(nc.psum_tensor("pb%d" % i, [128, 512], F32)) for i in range(8)]
    for i in range(8):
        bk = nb("pbank%d" % i)
        bk.excl = True
        for j in range(4):
            bufs["p%d_%d" % (i, j)] = bk

    def PS(bank, slot, n=1):
        return pbank[bank][:, slot * 128:(slot + n) * 128], B("p%d_0" % bank)

    def PSB(bank, slot, n=1):
        v = pbank[bank][:, slot * 128:(slot + n) * 128].bitcast(BF16)
        return v, B("p%d_0" % bank)

    I_F, ONES_F, U_, SL_, US_, SLS_ = range(6)
    ML_, MU_, MLS_, MUS_ = SL_, U_, SLS_, US_

    def C(i):
        return cst[:, i, :]

    S.dma("sp", cst[:].rearrange("p a b -> p (a b)"), cst_d[:, :], writes=B("cst"))
    S.dma("sp", smask[:], smask_d[:, :], writes=B("smask"))
    S.dma("sp", gfm[:], gfm_d[:, :], writes=B("gfm"))
    S.dma("sp", cwa[:].rearrange("p a b -> p (a b)"), cwa_d[:, :], writes=B("cwa"))
    S.dma("sp", cwb[:].rearrange("p a b -> p (a b)"), cwb_d[:, :], writes=B("cwb"))
    S.dma("sp", small[:], small_d[:, :], writes=B("small"))
    S.dma("sp", ogb[:], ogb_d[:, :], writes=B("ogb"))
    S.op("dve", lambda e: e.tensor_copy(identb[:], C(I_F)), reads=B("cst"), writes=B("identb"))
    S.op("dve", lambda e: e.tensor_copy(onesb[:], C(ONES_F)), reads=B("cst"), writes=B("onesb"))
    S.op("dve", lambda e: e.memset(epsc[:, 0:1], EPS), writes=B("epsc"))
    S.op("dve", lambda e: e.memset(epsc[:, 1:2], 128.0 * EPS), writes=B("epsc"))
    S.op("dve", lambda e: e.memset(hist[:], 0.0), writes=B("hist"))
    S.op("dve", lambda e: e.memset(hist2[:], 0.0), writes=B("hist2"))
    S.op("dve", lambda e: e.memset(ssq[:], 0.0), writes=B("ssq"))
    S.op("dve", lambda e: e.memset(sm[:], 0.0), writes=B("sm"))
    S.op("dve", lambda e: e.memset(gat[:], 0.0), writes=B("gat"))
    S.op("dve", lambda e: e.memset(rstd[:], 0.0), writes=B(*["rstd_%d" % i for i in range(3 * NT + 2)]))
    S.op("act", lambda e: e.activation(nA[:], small[:, 0:6], AF.Exp), reads=B("small"), writes=B("nA"))
    S.op("dve", lambda e: e.tensor_scalar(nA[:], nA[:], -1.0, None, op0=ALU.mult), reads=B("nA"), writes=B("nA"))

    for l in range(2):
        for kc in range(8):
            S.dma("pool", wout[:, kc, l * 512:(l + 1) * 512], wkv_d[l * 1024 + kc * 128:l * 1024 + (kc + 1) * 128, :],
                  writes=B("wout"))
    for kc in range(8):
        S.dma("pool", win[:, kc, 0:CA], wina_d[kc * 128:(kc + 1) * 128, :], writes=B("win%d" % kc))
    for mt in range(2):
        S.dma("sp", xres[:, mt, :], mem_d[mt * 128:(mt + 1) * 128, :], writes=B("x%d" % mt))

    def rstd_from_ssq(col, n_feat):
        S.op("act", lambda e: e.activation(rstd[:, col:col + 1], ssq[:, col:col + 1], AF.Ln, bias=epsc[:, 0:1], scale=1.0 / n_feat),
             reads=B("ssq", "epsc"), writes=B("rstd_%d" % col))
        S.op("act", lambda e: e.activation(rstd[:, col:col + 1], rstd[:, col:col + 1], AF.Exp, scale=-0.5),
             reads=B("rstd_%d" % col), writes=B("rstd_%d" % col))

    def ssq_of_tile(t, col):
        S.op("act", lambda e: e.activation(hb[:], xres[:, t, :], AF.Square, accum_out=ssq[:, col:col + 1]),
             reads=B("x%d" % t, "ssq"), writes=B("hb", "ssq"))

    def norm_transpose(t, rcol, gcol0):
        hbx = QKT[:].rearrange("p a b -> p (a b)")[:, 0:1024]
        HBN = B("QT0", "QT1", "QT2", "QT3", "QT4", "QT5", "KT0", "KT1")
        S.op("dve", lambda e: e.tensor_scalar(hbx, xres[:, t, :], rstd[:, rcol:rcol + 1], None, op0=ALU.mult),
             reads=B("x%d" % t, "rstd_%d" % rcol), writes=HBN)
        pv, pbf = PSB(7, 0, 4)
        pv3 = pv.rearrange("p (k t) -> p k t", k=8)
        for kc in range(8):
            S.op("pe", lambda e, kc=kc: e.transpose(pv3[:, kc, :], hbx[:, kc * 128:(kc + 1) * 128], identb[:]),
                 reads=HBN + B("identb"), writes=pbf, kind="tr")
        g3 = gfm[:, gcol0:gcol0 + 8].unsqueeze(2).to_broadcast([128, 8, 128])
        S.op("dve", lambda e: e.tensor_tensor(hT[:], pv3, g3, op=ALU.mult), reads=pbf + B("gfm"), writes=B("hT"))

    def pe_sep():
        sv, sbk = PSB(3, 3, 1)
        S.op("pe", lambda e: e.transpose(sv[:, 0:128], identb[:], identb[:]), reads=B("identb"), writes=sbk, kind="tr")

    S.sep = pe_sep
    def run_interleaved(gens):
        gens = list(gens)
        while gens:
            for g in list(gens):
                try:
                    next(g)
                except StopIteration:
                    gens.remove(g)

    proj_rr = [0]

    def proj_chunk(col0, ncols=128):
        bank = (1, 7)[proj_rr[0] % 2]
        proj_rr[0] += 1
        pv, pbf = PS(bank, 0)
        for kc in range(8):
            S.op("pe", lambda e, kc=kc: e.matmul(pv[0:ncols, :], win[:, kc, col0:col0 + ncols], hT[:, kc, :], start=(kc == 0), stop=(kc == 7)),
                 reads=B("win%d" % kc, "hT"), writes=pbf)
        return pv, pbf

    for mt in range(2):
        ssq_of_tile(mt, 3 * NT + mt)
        rstd_from_ssq(3 * NT + mt, 1024)
    memT = kkv3[:].rearrange("p a b c -> p (a b c)")[:, 0:2048].rearrange("p (k m) -> p k m", k=8)
    MEMB = B("KBG", "KD", "VB")
    for mt in range(2):
        norm_transpose(mt, 3 * NT + mt, 16)
        S.op("act", lambda e, mt=mt: e.copy(memT[:, :, mt * 128:(mt + 1) * 128], hT[:]), reads=B("hT"), writes=MEMB)
    for l in range(2):
        for mt in range(2):
            pv, pbf = PS(2, 0, 4)
            for kc in range(8):
                S.op("pe", lambda e, kc=kc: e.matmul(pv, memT[:, kc, mt * 128:(mt + 1) * 128], wout[:, kc, l * 512:(l + 1) * 512], start=(kc == 0), stop=(kc == 7)),
                     reads=MEMB + B("wout"), writes=pbf)
            st = stg[(l * 2 + mt) % 2]
            sn = "stg%d" % ((l * 2 + mt) % 2)
            S.op("act", lambda e, st=st: e.copy(st[:], pv), reads=pbf, writes=B(sn))
            S.dma("sp", mkp_o[l * 256 + mt * 128:l * 256 + (mt + 1) * 128, :], st[:, 0:256], reads=B(sn))
            S.dma("sp", mvp_o[l * 256 + mt * 128:l * 256 + (mt + 1) * 128, :], st[:, 256:512], reads=B(sn))
            S.op("dve", lambda e, st=st: e.tensor_copy(mvp[:, l, mt, :], st[:, 256:512]), reads=B(sn), writes=B("mvp"))
        for c in range(2):
            pv, pbf = PS(3, 0, 2)
            for kc in range(8):
                S.op("pe", lambda e, kc=kc: e.matmul(pv, wout[:, kc, l * 512 + c * 128:l * 512 + (c + 1) * 128], memT[:, kc, :], start=(kc == 0), stop=(kc == 7)),
                     reads=MEMB + B("wout"), writes=pbf)
            S.op("act", lambda e, c=c: e.copy(mkT[:, l, c, :], pv), reads=pbf, writes=B("mkT"))

    for t in range(NT):
        S.dma("sp", xres[:, t, :], x_d[t * 128:(t + 1) * 128, :], writes=B("x%d" % t))

    def load_win(l):
        wd = wina_d if l == 0 else winb_d
        ncol = CA if l == 0 else CB
        for kc in range(8):
            S.dma("pool", win[:, kc, 0:ncol], wd[kc * 128:(kc + 1) * 128, :], writes=B("win%d" % kc))

    def load_wout(l):
        for kc in range(8):
            S.dma("pool", wout[:, kc, :], wout_d[l * 1024 + kc * 128:l * 1024 + (kc + 1) * 128, :], writes=B("wout"))

    def load_layer_weights(l):
        load_win(l)
        load_wout(l)

    load_wout(0)
    for t in range(NT):
        ssq_of_tile(t, t)
        rstd_from_ssq(t, 1024)

    def att_gen(l, t, pg):
        sample = (t == NT - 1)
        sgT, xqT = sgT2[pg], xqT2[pg]
        SG, XQ = "sgT%d" % pg, "xqT%d" % pg
        scv, scb = [], []
        for bk in (3, 4, 5, 6):
            v_, b_ = PS(bk, 0, 2)
            scv.append(v_)
            scb.append(b_)
        if not sample:
            for h in range(4):
                po = (h % 2) * 64
                if _SUB == 700 and h != 0:
                    continue
                if _SUB == 701 and h != 1:
                    continue
                S.op("pe", lambda e, h=h, po=po: e.matmul(scv[h], xqT[po:po + 64, h // 2, :], mkT[po:po + 64, l, h // 2, :], start=True, stop=True),
                     reads=B(XQ, "mkT"), writes=scb[h])
        else:
            for s in range(16):
                i = s % 2
                r0 = (l * 16 + s) * 256
                S.dma("sp", stgKV[i], cmk_d[r0:r0 + 256, :].rearrange("(mt p) c -> p mt c", p=128), writes=B("stg%d" % i))
                S.op("act", lambda e, i=i: e.copy(ckv[i][:, 0, :, :], stgKV[i]), reads=B("stg%d" % i), writes=B("ckv%d" % i) + (B(*S0BN) if i == 0 else []))
                tv, tb = PSB(0, 0, 2)
                tv3 = tv.rearrange("p (c m) -> p c m", c=2)
                for c in range(2):
                    for mt in range(2):
                        S.op("pe", lambda e, c=c, mt=mt, i=i: e.transpose(tv3[:, c, mt * 128:(mt + 1) * 128], ckv[i][:, 0, mt, c * 128:(c + 1) * 128], identb[:]),
                             reads=B("ckv%d" % i, "identb"), writes=tb)
                S.op("act", lambda e, i=i: e.copy(mkTs[i][:], tv3), reads=tb, writes=B("mkTs%d" % i))
                S.op("pool", lambda e, i=i: e.memset(xqm[i][:].rearrange("p a b -> p (a b)"), 0.0), writes=B("xqm%d" % i))
                S.op("pool", lambda e, i=i, s=s: e.tensor_copy(xqm[i][:, :, s * 8:(s + 1) * 8], xqT[:, :, s * 8:(s + 1) * 8]),
                     reads=B(XQ), writes=B("xqm%d" % i))
                for h in range(4):
                    po = (h % 2) * 64
                    S.op("pe", lambda e, h=h, po=po, i=i: e.matmul(scv[h], xqm[i][po:po + 64, h // 2, :], mkTs[i][po:po + 64, h // 2, :], start=(s == 0), stop=(s == 15)),
                         reads=B("xqm%d" % i, "mkTs%d" % i), writes=scb[h])
                yield
        yield
        ovs = [PS(0, 0, 4), PS(2, 0, 4)]
        for half in range(2):
            for c in range(6):
                S.op("pe", lambda e, c=c, half=half: e.matmul(ovs[half][0], sgT[:, c, :], wout[:, c, half * 512:(half + 1) * 512], start=(c == 0), stop=False),
                     reads=B(SG, "wout"), writes=ovs[half][1], kind="bf")
        yield
        for h in range(4):
            S.op("dve", lambda e, h=h: e.tensor_reduce(sm[:, h:h + 1], scv[h], axis=AX.X, op=ALU.max), reads=scb[h], writes=B("sm"))
        S.op("dve", lambda e: e.tensor_scalar(sm[:, 4:8], sm[:, 0:4], -0.125, None, op0=ALU.mult), reads=B("sm"), writes=B("sm"))
        S.op("dve", lambda e: e.memset(sm[:, 8:12], 0.0), writes=B("sm"))
        yield
        for h in range(4):
            S.op("act", lambda e, h=h: e.activation(Pb[:, h, :], scv[h], AF.Exp, bias=sm[:, 4 + h:5 + h], scale=0.125, accum_out=sm[:, 8 + h:9 + h]),
                 reads=scb[h] + B("sm"), writes=B("Pb", "sm"))
        yield
        S.op("dve", lambda e: e.reciprocal(sm[:, 12:16], sm[:, 8:12]), reads=B("sm"), writes=B("sm"))
        S.op("dve", lambda e: e.tensor_tensor(Pb[:], Pb[:], sm[:, 12:16].unsqueeze(2).to_broadcast([128, 4, 256]), op=ALU.mult),
             reads=B("Pb", "sm"), writes=B("Pb"))
        yield
        if _SUB == 71:
            return
        tv, tb = PSB(5, 0, 4)
        tv4 = tv.rearrange("p (h m t) -> p h m t", h=4, m=2)
        for h in range(4):
            for mt in range(2):
                S.op("pe", lambda e, h=h, mt=mt: e.transpose(tv4[:, h, mt, :], Pb[:, h, mt * 128:(mt + 1) * 128], identb[:]),
                     reads=B("Pb", "identb"), writes=tb)
        yield
        S.op("act", lambda e: e.copy(PTb, tv4), reads=tb, writes=B("hb"))
        yield
        if _SUB == 72:
            return
        xo_v, xo_b = PS(6, 0, 4)
        xo4 = xo_v.rearrange("p (c k t) -> p c k t", c=2, k=2)
        if not sample:
            for c in range(2):
                for hh in range(2):
                    h = 2 * c + hh
                    for mt in range(2):
                        S.op("pe", lambda e, h=h, mt=mt, c=c, hh=hh: e.matmul(xo4[:, c, hh, :], mvp[:, l, mt, c * 128:(c + 1) * 128], PTb[:, h, mt, :], start=(mt == 0), stop=(mt == 1)),
                             reads=B("mvp", "hb"), writes=xo_b)
        else:
            for s in range(16):
                i = s % 2
                r0 = (l * 16 + s) * 256
                S.dma("sp", stgKV[i], cmv_d[r0:r0 + 256, :].rearrange("(mt p) c -> p mt c", p=128), writes=B("stg%d" % i))
                S.op("act", lambda e, i=i: e.copy(ckv[i][:, 1, :, :], stgKV[i]), reads=B("stg%d" % i), writes=B("ckv%d" % i) + (B(*S0BN) if i == 0 else []))
                for c in range(2):
                    for hh in range(2):
                        h = 2 * c + hh
                        for mt in range(2):
                            S.op("pe", lambda e, h=h, mt=mt, c=c, hh=hh, i=i, s=s: e.matmul(xo4[:, c, hh, s * 8:(s + 1) * 8], ckv[i][:, 1, mt, c * 128:(c + 1) * 128], PTb[:, h, mt, s * 8:(s + 1) * 8], start=(mt == 0), stop=(mt == 1)),
                                 reads=B("ckv%d" % i, "hb"), writes=xo_b)
                yield
        yield
        for c in range(2):
            for hh in range(2):
                po = hh * 64
                S.op("dve", lambda e, c=c, hh=hh, po=po: e.tensor_tensor(sgT[po:po + 64, 6 + c, :], xo4[po:po + 64, c, hh, :], sgT[po:po + 64, 6 + c, :], op=ALU.mult),
                     reads=xo_b + B(SG), writes=B(SG))
        yield
        for half in range(2):
            for c in (6, 7):
                S.op("pe", lambda e, c=c, half=half: e.matmul(ovs[half][0], sgT[:, c, :], wout[:, c, half * 512:(half + 1) * 512], start=False, stop=(c == 7)),
                     reads=B(SG, "wout"), writes=ovs[half][1], kind="bf")
            S.op("dve", lambda e, half=half: e.tensor_tensor(xres[:, t, half * 512:(half + 1) * 512], xres[:, t, half * 512:(half + 1) * 512], ovs[half][0], op=ALU.add),
                 reads=ovs[half][1] + B("x%d" % t), writes=B("x%d" % t))
    def att_tail(l, t):
        col = (l + 1) * NT + t
        ssq_of_tile(t, col)
        rstd_from_ssq(col, 1024)
        if l == 1:
            st = stg[t % 2]
            for half in range(2):
                S.op("dve", lambda e, half=half, st=st: e.scalar_tensor_tensor(st[:], xres[:, t, half * 512:(half + 1) * 512], rstd[:, col:col + 1], gfin[:, half * 512:(half + 1) * 512], op0=ALU.mult, op1=ALU.mult),
                     reads=B("x%d" % t, "rstd_%d" % col, "KBG", "KD", "VB"), writes=B("stg%d" % (t % 2)))
                S.dma("sp", y_o[t * 128:(t + 1) * 128, half * 512:(half + 1) * 512], st[:], reads=B("stg%d" % (t % 2)))

    def conv_setup_hist(l, t, c, i, nh):
        sample = (t == NT - 1)
        hbuf = hist if l == 0 else hist2
        hname = "hist" if l == 0 else "hist2"
        if not sample:
            S.op("pool", lambda e: e.tensor_copy(xp[i][:, 0:nh], hbuf[:, c, :]), reads=B(hname), writes=B("xp%d" % i))
        else:
            srcd = sgc_d if l == 0 else ssc_d
            nr = 16 * nh
            S.dma("sp", hst[i][0:nr, :], srcd[0:nr, c * 128:(c + 1) * 128], writes=B("hst%d" % i))
            pv, pbf = PS(2, 2, 1)
            S.op("pe", lambda e: e.transpose(pv[:, 0:nr], hst[i][0:nr, :], C(I_F)[0:nr, 0:nr]),
                 reads=B("hst%d" % i, "cst"), writes=pbf)
            L = 8
            xp3 = xp[i][:, 0:16 * (nh + L)].rearrange("p (s k) -> p s k", s=16)
            S.op("act", lambda e: e.copy(xp3[:, :, 0:nh], pv[:, 0:nr].rearrange("p (s j) -> p s j", s=16)), reads=pbf, writes=B("xp%d" % i))

    def xp_views(t, i, nh):
        sample = (t == NT - 1)
        if not sample:
            cur = xp[i][:, nh:nh + 128]
            taps = [xp[i][:, j:j + 128] for j in range(nh + 1)]
            last = xp[i][:, 128:128 + nh]
            return cur, taps, last
        L = 8
        xp3 = xp[i][:, 0:16 * (nh + L)].rearrange("p (s k) -> p s k", s=16)
        cur = xp3[:, :, nh:nh + L]
        taps = [xp3[:, :, j:j + L] for j in range(nh + 1)]
        return cur, taps, None

    def as3(ap, t):
        if t == NT - 1:
            return ap.rearrange("p (s k) -> p s k", s=16)
        return ap

    def front0_a(t, pg):
        sample = (t == NT - 1)
        sgT, xqT = sgT2[pg], xqT2[pg]
        SG, XQ = "sgT%d" % pg, "xqT%d" % pg
        norm_transpose(t, t, 0)
        yield
        for c in range(8):
            pv, pbf = proj_chunk(2316 + c * 128)
            S.op("act", lambda e, c=c, pv=pv: e.activation(sgT[:, c, :], pv, AF.Silu), reads=pbf, writes=B(SG))
            yield
        for c in range(2):
            pv, pbf = proj_chunk(2316 + 1024 + c * 128)
            S.op("act", lambda e, c=c, pv=pv: e.copy(xqT[:, c, :], pv), reads=pbf, writes=B(XQ))
        yield
        def stA(c):
            i = c % 2
            pv, pbf = proj_chunk(c * 128)
            conv_setup_hist(0, t, c, i, 3)
            cur, taps, last = xp_views(t, i, 3)
            S.op("act", lambda e: e.copy(cur, as3(pv, t)), reads=pbf, writes=B("xp%d" % i))
            if not sample:
                S.op("pool", lambda e: e.tensor_copy(hist[:, c, :], last), reads=B("xp%d" % i), writes=B("hist"))
            acc = as3(cacc[i][:], t)
            S.op("act", lambda e: e.activation(acc, taps[3], AF.Identity, scale=cwa[:, c, 3:4]),
                 reads=B("xp%d" % i, "cwa"), writes=B("cacc%d" % i))

        def stB(c):
            i = c % 2
            cur, taps, last = xp_views(t, i, 3)
            acc = as3(cacc[i][:], t)
            for j in range(3):
                S.op("dve", lambda e, j=j: e.scalar_tensor_tensor(acc, taps[j], cwa[:, c, j:j + 1], acc, op0=ALU.mult, op1=ALU.add),
                     reads=B("xp%d" % i, "cwa", "cacc%d" % i), writes=B("cacc%d" % i))

        def stC(c):
            i = c % 2
            if c >= 12:
                dst, h, dn = VT, c - 12, "VT%d" % (c - 12)
            elif c < 6:
                dst, h, dn = QT, c, "QT%d" % c
            else:
                dst, h, dn = KT, c - 6, "KT%d" % (c - 6)
            S.op("act", lambda e: e.activation(dst[:, h, :], cacc[i][:], AF.Silu), reads=B("cacc%d" % i), writes=B(dn))

        stA(0)
        stB(0)
        for c in range(18):
            if c + 1 < 18:
                stA(c + 1)
            stC(c)
            if c + 1 < 18:
                stB(c + 1)
            yield

    def front0_b(t, pg):
        sample = (t == NT - 1)
        sgT, xqT = sgT2[pg], xqT2[pg]
        SG, XQ = "sgT%d" % pg, "xqT%d" % pg
        gcol = gat[:, 24:30]
        Um, SLm = (US_, SLS_) if sample else (U_, SL_)
        eG, eGrev, eGl = cum[:, 0:6], cum[:, 6:12], cum[:, 12:18]
        eGls = cum[:, 18:114].rearrange("p (s h) -> p s h", s=16)

        def gen_norm():
            for g in range(4):
                i = g % 2
                isq = (g < 2)
                c0 = g * 3
                names = [("QT%d" % (c0 + k)) if isq else ("KT%d" % (c0 - 6 + k)) for k in range(3)]
                blk = QKT[:, c0:c0 + 3, :]
                S.op("act", lambda e: e.activation(sqb[i][:], blk, AF.Square), reads=B(*names), writes=B("sqb%d" % i))
                qv, qb = PS((2, 0)[i], 0, 3)
                S.op("pe", lambda e: e.matmul(qv, onesb[:], sqb[i][:].rearrange("p a b -> p (a b)"), start=True, stop=True), reads=B("onesb", "sqb%d" % i), writes=qb, kind="bf")
                yield
                S.op("act", lambda e: e.activation(qv, qv, AF.Ln, bias=epsc[:, 0:1], scale=1.0), reads=qb + B("epsc"), writes=qb)
                S.op("act", lambda e: e.activation(qv, qv, AF.Exp, scale=-0.5), reads=qb, writes=qb)
                qv3 = qv.rearrange("p (a b) -> p a b", a=3)
                if isq:
                    S.op("dve", lambda e: e.scalar_tensor_tensor(blk, blk, float(128.0 ** -0.5), qv3, op0=ALU.mult, op1=ALU.mult), reads=qb + B(*names), writes=B(*names))
                else:
                    S.op("dve", lambda e: e.tensor_tensor(blk, blk, qv3, op=ALU.mult), reads=qb + B(*names), writes=B(*names))
                yield

        def gen_gates():
            bav, bab = PS(3, 0)
            for kc in range(8):
                S.op("pe", lambda e, kc=kc: e.matmul(bav[:, 0:12], hT[:, kc, :], win[:, kc, 2304:2316], start=(kc == 0), stop=(kc == 7)),
                     reads=B("hT", "win%d" % kc), writes=bab, kind="bf")
            yield
            S.op("act", lambda e: e.activation(gat[:, 0:6], bav[:, 0:6], AF.Exp, scale=-1.0), reads=bab, writes=B("gat"))
            S.op("dve", lambda e: e.tensor_tensor(gat[:, 6:12], bav[:, 6:12], small[:, 6:12], op=ALU.add), reads=bab + B("small", "gat"), writes=B("gat"))
            S.op("act", lambda e: e.activation(gat[:, 6:12], gat[:, 6:12], AF.Exp), reads=B("gat"), writes=B("gat"))
            S.op("act", lambda e: e.activation(gat[:, 6:12], gat[:, 6:12], AF.Ln, bias=1.0), reads=B("gat"), writes=B("gat"))
            yield
            S.op("dve", lambda e: e.tensor_scalar(gat[:, 0:6], gat[:, 0:6], 1.0, None, op0=ALU.add), reads=B("gat"), writes=B("gat"))
            S.op("dve", lambda e: e.reciprocal(gat[:, 12:18], gat[:, 0:6]), reads=B("gat"), writes=B("gat"))
            S.op("dve", lambda e: e.tensor_scalar(gat[:, 18:24], gat[:, 12:18], -1.0, None, op0=ALU.mult), reads=B("gat"), writes=B("gat"))
            S.op("dve", lambda e: e.tensor_tensor(gat[:, 24:30], gat[:, 6:12], nA[:], op=ALU.mult), reads=B("gat", "nA"), writes=B("gat"))
            cv, cb = PS(3, 1)
            S.op("pe", lambda e: e.matmul(cv[:, 0:6], C(Um), gcol, start=True, stop=True), reads=B("cst", "gat"), writes=cb, kind="f32")
            S.op("pe", lambda e: e.matmul(cv[:, 6:12], C(SLm), gcol, start=True, stop=True), reads=B("cst", "gat"), writes=cb, kind="f32")
            if not sample:
                S.op("pe", lambda e: e.matmul(cv[:, 12:18], C(ONES_F), gcol, start=True, stop=True), reads=B("cst", "gat"), writes=cb, kind="f32")
                yield
                S.op("act", lambda e: e.activation(cum[:, 0:18], cv[:, 0:18], AF.Exp), reads=cb, writes=B("cum"))
            else:
                gm = cum[:, 18:114].rearrange("p (s h) -> p s h", s=16)
                S.op("dve", lambda e: e.tensor_tensor(gm, gcol.unsqueeze(1).to_broadcast([128, 16, 6]), smask[:].unsqueeze(2).to_broadcast([128, 16, 6]), op=ALU.mult),
                     reads=B("gat", "smask", "cum"), writes=B("cum"))
                cv2, cb2 = PS(3, 2)
                S.op("pe", lambda e: e.matmul(cv2[:, 0:96], C(ONES_F), cum[:, 18:114], start=True, stop=True), reads=B("cst", "cum"), writes=cb2, kind="f32")
                yield
                S.op("act", lambda e: e.activation(cum[:, 0:12], cv[:, 0:12], AF.Exp), reads=cb, writes=B("cum"))
                S.op("act", lambda e: e.activation(cum[:, 18:114], cv2[:, 0:96], AF.Exp), reads=cb2, writes=B("cum"))
            S.op("dve", lambda e: e.tensor_tensor(gat[:, 30:36], gat[:, 12:18], eG, op=ALU.mult), reads=B("gat", "cum"), writes=B("gat"))
            tv_, tb_ = PSB(7, 0, 3)
            tvv = tv_[:, 0:768].rearrange("p (h d) -> p h d", h=6)
            for h in range(6):
                S.op("pe", lambda e, h=h: e.transpose(tvv[:, h, :], VT[:, h, :], identb[:]), reads=B("VT%d" % h, "identb"), writes=tb_, kind="tr")
            yield
            S.op("dve", lambda e: e.tensor_tensor(VB, tvv, gat[:, 12:18].unsqueeze(2).to_broadcast([128, 6, 128]), op=ALU.mult), reads=tb_ + B("gat"), writes=B("VB"))

        run_interleaved([gen_norm(), gen_gates()])
        if _SUB == 2:
            return
        tv, tb = PSB(0, 0, 3)
        tv3 = tv.rearrange("p (h d) -> p h d", h=6)
        for h in range(6):
            S.op("pe", lambda e, h=h: e.transpose(tv3[:, h, :], KT[:, h, :], identb[:]), reads=B("KT%d" % h, "identb"), writes=tb)
        S.op("dve", lambda e: e.tensor_tensor(KBG, tv3, gat[:, 30:36].unsqueeze(2).to_broadcast([128, 6, 128]), op=ALU.mult), reads=tb + B("gat"), writes=B("KBG"))
        S.op("dve", lambda e: e.tensor_tensor(KD, tv3, eGrev.unsqueeze(2).to_broadcast([128, 6, 128]), op=ALU.mult), reads=tb + B("cum"), writes=B("KD"))
        if _SUB == 3:
            return
        if t >= NT - 2:
            for blk in range(5):
                c0 = blk * 512
                w_ = min(512, 2304 - c0)
                pv, pbf = PS(7, 0, 4)
                for kc in range(8):
                    S.op("pe", lambda e, kc=kc, pv=pv, c0=c0, w_=w_: e.matmul(pv[:, 0:w_], hT[:, kc, :], win[:, kc, c0:c0 + w_], start=(kc == 0), stop=(kc == 7)),
                         reads=B("hT", "win%d" % kc), writes=pbf)
                st = stg[blk % 2]
                S.op("act", lambda e, st=st, pv=pv, w_=w_: e.copy(st[:, 0:w_], pv[:, 0:w_]), reads=pbf, writes=B("stg%d" % (blk % 2)))
                if not sample:
                    S.dma("sp", gcp_o[0:3, c0:c0 + w_], st[125:128, 0:w_], reads=B("stg%d" % (blk % 2)))
                else:
                    for s in range(16):
                        S.dma("sp", gcs_o[s * 3:(s + 1) * 3, c0:c0 + w_], st[s * 8 + 5:s * 8 + 8, 0:w_], reads=B("stg%d" % (blk % 2)))
        if t == NT - 1:
            load_win(1)
        MLm, MUm = (MLS_, MUS_) if sample else (ML_, MU_)
        nlev = 3 if sample else 7
        have_state = sample or t > 0
        HB = ((4, 5), (0, 1), (7, 2))

        def head_gen(h, i):
            bA, bB = HB[i]
            dlv, dlb = PS(bA, 0)
            duv, dub = PS(bA, 1)
            kkv, kkb = PS(bB, 0)
            kqv, kqb = PS(bB, 1)
            UGv, SLGv = Dl[i], Du[i]
            S.op("dve", lambda e: e.tensor_scalar(UGv[:], C(Um), gat[:, 24 + h:25 + h], None, op0=ALU.mult), reads=B("cst", "gat"), writes=B("Dl%d" % i))
            S.op("dve", lambda e: e.tensor_scalar(SLGv[:], C(SLm), gat[:, 24 + h:25 + h], None, op0=ALU.mult), reads=B("cst", "gat"), writes=B("Du%d" % i))
            S.op("pe", lambda e: e.matmul(dlv, UGv[:], C(SLm), start=True, stop=True), reads=B("Dl%d" % i, "cst"), writes=dlb, kind="f32")
            S.op("pe", lambda e: e.matmul(duv, SLGv[:], C(Um), start=True, stop=True), reads=B("Du%d" % i, "cst"), writes=dub, kind="f32")
            S.op("pe", lambda e: e.matmul(kkv, KT[:, h, :], KT[:, h, :], start=True, stop=True), reads=B("KT%d" % h), writes=kkb, kind="bf")
            S.op("pe", lambda e: e.matmul(kqv, KT[:, h, :], QT[:, h, :], start=True, stop=True), reads=B("KT%d" % h, "QT%d" % h), writes=kqb, kind="bf")
            yield
            S.op("act", lambda e: e.activation(Dl[i][:], dlv, AF.Exp), reads=dlb, writes=B("Dl%d" % i))
            S.op("act", lambda e: e.activation(Du[i][:], duv, AF.Exp), reads=dub, writes=B("Du%d" % i))
            S.op("dve", lambda e: e.tensor_tensor(Dl[i][:], Dl[i][:], C(MLm), op=ALU.mult), reads=B("Dl%d" % i, "cst"), writes=B("Dl%d" % i))
            S.op("dve", lambda e: e.tensor_tensor(Du[i][:], Du[i][:], C(MUm), op=ALU.mult), reads=B("Du%d" % i, "cst"), writes=B("Du%d" % i))
            R0 = Rb[i][0]
            S.op("dve", lambda e: e.scalar_tensor_tensor(R0[:], kkv, gat[:, 18 + h:19 + h], Dl[i][:], op0=ALU.mult, op1=ALU.mult),
                 reads=kkb + B("gat", "Dl%d" % i), writes=B("R%d_0" % i))
            S.op("dve", lambda e: e.tensor_tensor(AqT[i][:], kqv, Du[i][:], op=ALU.mult), reads=kqb + B("Du%d" % i), writes=B("AqT%d" % i))
            yield
            ntv, ntb = PS(bA, 2)
            S.op("pe", lambda e: e.transpose(ntv, R0[:], C(I_F)), reads=B("R%d_0" % i, "cst"), writes=ntb, kind="tr")
            yield
            S.op("act", lambda e: e.copy(Qb[i][0][:], ntv), reads=ntb, writes=B("Q%d_0" % i))
            S.op("dve", lambda e: e.tensor_tensor(Xb[i][:], ntv, C(I_F), op=ALU.add), reads=ntb + B("cst"), writes=B("X%d" % i))
            for k in range(1, nlev):
                a, b_ = (k - 1) % 2, k % 2
                rv, rb_ = PS(bA, 0)
                qv_, qb_ = PS(bA, 1)
                xv, xb_ = PS(bB, 0)
                S.op("pe", lambda e: e.matmul(rv, Qb[i][a][:], Rb[i][a][:], start=True, stop=True),
                     reads=B("Q%d_%d" % (i, a), "R%d_%d" % (i, a)), writes=rb_, kind="f32")
                if k < nlev - 1:
                    S.op("pe", lambda e: e.matmul(qv_, Rb[i][a][:], Qb[i][a][:], start=True, stop=True),
                         reads=B("Q%d_%d" % (i, a), "R%d_%d" % (i, a)), writes=qb_, kind="f32")
                yield
                S.op("act", lambda e: e.copy(Rb[i][b_][:], rv), reads=rb_, writes=B("R%d_%d" % (i, b_)))
                if k < nlev - 1:
                    S.op("act", lambda e: e.copy(Qb[i][b_][:], qv_), reads=qb_, writes=B("Q%d_%d" % (i, b_)))
                S.op("pe", lambda e: e.matmul(xv, Rb[i][b_][:], Xb[i][:], start=True, stop=True),
                     reads=B("R%d_%d" % (i, b_), "X%d" % i), writes=xb_, kind="f32")
                yield
                S.op("dve", lambda e: e.tensor_tensor(Xb[i][:], xv, Xb[i][:], op=ALU.add), reads=xb_ + B("X%d" % i), writes=B("X%d" % i))
            S.op("act", lambda e: e.copy(Xbf[i][:], Xb[i][:]), reads=B("X%d" % i), writes=B("Xbf%d" % i))
            wv, wb = PS(bA, 2)
            S.op("pe", lambda e: e.matmul(wv, KBG[:, h, :], Xbf[i][:], start=True, stop=True), reads=B("KBG", "Xbf%d" % i), writes=wb, kind="bf")
            yield
            S.op("act", lambda e: e.activation(wTn[i][:], wv, AF.Identity, scale=-1.0), reads=wb, writes=B("wTn%d" % i))
            vv, vb = PS(bB, 1)
            qsv, qsb = PS(bB, 2)
            if not sample:
                S.op("pe", lambda e: e.matmul(vv, Xbf[i][:], VB[:, h, :], start=True, stop=(not have_state)), reads=B("Xbf%d" % i, "VB"), writes=vb, kind="bf")
                if have_state:
                    S.op("pe", lambda e: e.matmul(vv, wTn[i][:], Sbf[:, h, :], start=False, stop=True), reads=B("wTn%d" % i, "S%d" % h), writes=vb, kind="bf")
                    S.op("pe", lambda e: e.matmul(qsv, QT[:, h, :], Sbf[:, h, :], start=True, stop=True), reads=B("QT%d" % h, "S%d" % h), writes=qsb, kind="bf")
            else:
                t1v, t1b = PS(bA, 0)
                t2v, t2b = PS(bA, 1)
                for q4 in range(4):
                    j = q4 % 2
                    S.dma("sp", S0v[j], sg4[:, q4 * 4:(q4 + 1) * 4, h, :], writes=B(S0n[j]))
                    S.op("act", lambda e, j=j: e.copy(S0bf[:, j * 4:j * 4 + 4, :], S0v[j]), reads=B(S0n[j]), writes=B(*["S0bf_%d" % (j * 4 + k) for k in range(4)]) + B("ckv0"))
                    for k in range(4):
                        s, s8 = q4 * 4 + k, j * 4 + k
                        S.op("pe", lambda e, s=s, s8=s8: e.matmul(t1v[:, s * 8:(s + 1) * 8], S0bf[:, s8, :], wTn[i][:, s * 8:(s + 1) * 8], start=True, stop=True),
                             reads=B("S0bf_%d" % s8, "wTn%d" % i), writes=t1b, kind="bf")
                        S.op("pe", lambda e, s=s, s8=s8: e.matmul(t2v[:, s * 8:(s + 1) * 8], S0bf[:, s8, :], QT[:, h, s * 8:(s + 1) * 8], start=True, stop=True),
                             reads=B("S0bf_%d" % s8, "QT%d" % h), writes=t2b, kind="bf")
                S.op("act", lambda e: e.copy(tsb[0][:], t1v), reads=t1b, writes=B("tsb0"))
                S.op("act", lambda e: e.copy(tsb[1][:], t2v), reads=t2b, writes=B("tsb1"))
                S.op("pe", lambda e: e.matmul(vv, Xbf[i][:], VB[:, h, :], start=True, stop=False), reads=B("Xbf%d" % i, "VB"), writes=vb, kind="bf")
                S.op("pe", lambda e: e.matmul(vv, tsb[0][:], identb[:], start=False, stop=True), reads=B("tsb0", "identb"), writes=vb, kind="bf")
                S.op("pe", lambda e: e.matmul(qsv, tsb[1][:], identb[:], start=True, stop=True), reads=B("tsb1", "identb"), writes=qsb, kind="bf")
            yield
            S.op("act", lambda e: e.copy(vnew[i][:], vv), reads=vb, writes=B("vnew%d" % i))
            av, ab = PS(bA, 3)
            S.op("pe", lambda e: e.matmul(av, AqT[i][:], vnew[i][:], start=True, stop=True), reads=B("AqT%d" % i, "vnew%d" % i), writes=ab, kind="bf")
            if not sample:
                suv, sub = PS(6, i)
                S.op("pe", lambda e: e.matmul(suv, KD[:, h, :], vnew[i][:], start=True, stop=True), reads=B("KD", "vnew%d" % i), writes=sub, kind="bf")
            yield
            ofpv = Dl[i]
            avsv = Du[i]
            if have_state:
                S.op("act", lambda e: e.copy(avsv[:], av), reads=ab, writes=B("Du%d" % i))
                S.op("dve", lambda e: e.scalar_tensor_tensor(ofpv[:], qsv, eG[:, h:h + 1], avsv[:], op0=ALU.mult, op1=ALU.add),
                     reads=qsb + B("cum", "Du%d" % i), writes=B("Dl%d" % i))
            else:
                S.op("act", lambda e: e.copy(ofpv[:], av), reads=ab, writes=B("Dl%d" % i))
            if not sample:
                if have_state:
                    S.op("dve", lambda e: e.scalar_tensor_tensor(Sst[:, h, :], Sst[:, h, :], eGl[:, h:h + 1], suv, op0=ALU.mult, op1=ALU.add),
                         reads=sub + B("cum", "S%d" % h), writes=B("S%d" % h))
                else:
                    S.op("dve", lambda e: e.tensor_copy(Sst[:, h, :], suv), reads=sub, writes=B("S%d" % h))
                S.op("act", lambda e: e.copy(Sbf[:, h, :], Sst[:, h, :]), reads=B("S%d" % h), writes=B("S%d" % h))
                if t == NT - 2:
                    S.dma("sp", sp_o[h * 128:(h + 1) * 128, :], Sst[:, h, :], reads=B("S%d" % h))
            gn = "gatn%d" % h
            S.op("dve", lambda e: e.memset(gat[:, 36 + h:37 + h], 0.0), writes=B(gn))
            S.op("act", lambda e: e.activation(wTn[i][:], ofpv[:], AF.Square, accum_out=gat[:, 36 + h:37 + h]), reads=B("Dl%d" % i, gn), writes=B("wTn%d" % i, gn))
            S.op("act", lambda e: e.activation(gat[:, 42 + h:43 + h], gat[:, 36 + h:37 + h], AF.Ln, bias=epsc[:, 0:1], scale=1.0 / 128), reads=B(gn, "epsc"), writes=B(gn))
            S.op("act", lambda e: e.activation(gat[:, 42 + h:43 + h], gat[:, 42 + h:43 + h], AF.Exp, scale=-0.5), reads=B(gn), writes=B(gn))
            S.op("dve", lambda e: e.scalar_tensor_tensor(onb[i][:], ofpv[:], gat[:, 42 + h:43 + h], ogb[:], op0=ALU.mult, op1=ALU.mult),
                 reads=B("Dl%d" % i, gn, "ogb"), writes=B("onb%d" % i))
            otv, otb = PSB(3, i, 1)
            S.op("pe", lambda e: e.transpose(otv[:, 0:128], onb[i][:], identb[:]), reads=B("onb%d" % i, "identb"), writes=otb, kind="tr")
            yield
            S.op("dve", lambda e: e.tensor_tensor(sgT[:, h, :], otv[:, 0:128], sgT[:, h, :], op=ALU.mult), reads=otb + B(SG), writes=B(SG))
            if sample:
                for q4 in range(4):
                    j = q4 % 2
                    S.dma("sp", S0v[j], sg4[:, q4 * 4:(q4 + 1) * 4, h, :], writes=B(S0n[j]))
                    def mk_mask(k):
                        s = q4 * 4 + k
                        jj = s % 2
                        S.op("dve", lambda e: e.tensor_scalar(kdm[jj][:], KD[:, h, :], smask[:, s:s + 1], None, op0=ALU.mult),
                             reads=B("KD", "smask"), writes=B("kdm%d" % jj))
                        suv, sub = PS((bA, bB)[jj], 0)
                        S.op("pe", lambda e: e.matmul(suv, kdm[jj][:], vnew[i][:], start=True, stop=True), reads=B("kdm%d" % jj, "vnew%d" % i), writes=sub, kind="bf")
                        return suv, sub

                    def fin(k, suv, sub):
                        s = q4 * 4 + k
                        S.op("dve", lambda e: e.scalar_tensor_tensor(Snw4[j][:, k, :], S0v[j][:, k, :], eGls[:, s, h:h + 1], suv, op0=ALU.mult, op1=ALU.add),
                             reads=sub + B("cum", S0n[j]), writes=B("stg%d" % j))

                    pend = mk_mask(0)
                    for k in range(4):
                        nxt = mk_mask(k + 1) if k + 1 < 4 else None
                        fin(k, *pend)
                        pend = nxt
                    S.dma("sp", ss4[:, q4 * 4:(q4 + 1) * 4, h, :], Snw4[j], reads=B("stg%d" % j))
                    yield

        STAG = 0
        slots = [None, None, None]
        steps = [0, 0, 0]
        nxt_h = 0
        tick = 0
        while nxt_h < 6 or any(g is not None for g in slots):
            for i in range(3):
                if slots[i] is None and nxt_h < 6 and tick >= nxt_h * STAG:
                    slots[i] = head_gen(nxt_h, i)
                    nxt_h += 1
                if slots[i] is not None:
                    try:
                        next(slots[i])
                    except StopIteration:
                        slots[i] = None
            tick += 1

    def front1_a(t, pg):
        sample = (t == NT - 1)
        sgT, xqT = sgT2[pg], xqT2[pg]
        SG, XQ = "sgT%d" % pg, "xqT%d" % pg
        norm_transpose(t, NT + t, 8)
        yield
        for c in range(8):
            pv, pbf = proj_chunk(2304 + c * 128)
            S.op("act", lambda e, c=c, pv=pv: e.activation(sgT[:, c, :], pv, AF.Silu), reads=pbf, writes=B(SG))
            yield
        for c in range(2):
            pv, pbf = proj_chunk(2304 + 1024 + c * 128)
            S.op("act", lambda e, c=c, pv=pv: e.copy(xqT[:, c, :], pv), reads=pbf, writes=B(XQ))
        yield
        def l1A(c):
            i = c % 2
            pv, pbf = proj_chunk(768 + c * 128)
            S.op("act", lambda e: e.copy(sil[i][:], pv), reads=pbf, writes=B("sil%d" % i))
            pv2, pbf2 = proj_chunk(1536 + c * 128)
            conv_setup_hist(1, t, c, i, 2)
            cur, taps, last = xp_views(t, i, 2)
            S.op("dve", lambda e: e.tensor_tensor(cur, as3(pv2, t), as3(sil[i][:], t), op=ALU.mult), reads=pbf2 + B("sil%d" % i), writes=B("xp%d" % i))
            if not sample:
                S.op("pool", lambda e: e.tensor_copy(hist2[:, c, :], last), reads=B("xp%d" % i), writes=B("hist2"))
            acc = as3(cacc[i][:], t)
            S.op("act", lambda e: e.activation(acc, taps[2], AF.Identity, scale=cwb[:, c, 2:3]), reads=B("xp%d" % i, "cwb"), writes=B("cacc%d" % i))

        def l1B(c):
            i = c % 2
            cur, taps, last = xp_views(t, i, 2)
            acc = as3(cacc[i][:], t)
            for j in range(2):
                S.op("dve", lambda e, j=j: e.scalar_tensor_tensor(acc, taps[j], cwb[:, c, j:j + 1], acc, op0=ALU.mult, op1=ALU.add),
                     reads=B("xp%d" % i, "cwb", "cacc%d" % i), writes=B("cacc%d" % i))
            pv3, pbf3 = proj_chunk(c * 128)
            S.op("dve", lambda e: e.tensor_tensor(cacc[i][:], pv3, cacc[i][:], op=ALU.mult), reads=pbf3 + B("cacc%d" % i), writes=B("cacc%d" % i))
            S.op("dve", lambda e: e.tensor_tensor(sgT[:, c, :], cacc[i][:], sgT[:, c, :], op=ALU.mult), reads=B("cacc%d" % i, SG), writes=B(SG))

        l1A(0)
        for c in range(6):
            if c + 1 < 6:
                l1A(c + 1)
            l1B(c)
            yield
        if t >= NT - 2:
            for blk in range(2):
                c0 = blk * 384
                pv, pbf = PS(7, 0, 4)
                pw, pwb = PS(6, 0, 4)
                for kc in range(8):
                    S.op("pe", lambda e, kc=kc, pv=pv, c0=c0: e.matmul(pv[:, 0:384], hT[:, kc, :], win[:, kc, 768 + c0:768 + c0 + 384], start=(kc == 0), stop=(kc == 7)),
                         reads=B("hT", "win%d" % kc), writes=pbf)
                for kc in range(8):
                    S.op("pe", lambda e, kc=kc, pw=pw, c0=c0: e.matmul(pw[:, 0:384], hT[:, kc, :], win[:, kc, 1536 + c0:1536 + c0 + 384], start=(kc == 0), stop=(kc == 7)),
                         reads=B("hT", "win%d" % kc), writes=pwb)
                st = stg[blk % 2]
                S.op("act", lambda e, st=st, pv=pv: e.copy(st[:, 0:384], pv[:, 0:384]), reads=pbf, writes=B("stg%d" % (blk % 2)))
                S.op("dve", lambda e, st=st, pw=pw: e.tensor_tensor(st[:, 0:384], pw[:, 0:384], st[:, 0:384], op=ALU.mult), reads=pwb + B("stg%d" % (blk % 2)), writes=B("stg%d" % (blk % 2)))
                if not sample:
                    S.dma("sp", scp_o[0:2, c0:c0 + 384], st[126:128, 0:384], reads=B("stg%d" % (blk % 2)))
                else:
                    for s in range(16):
                        S.dma("sp", scs_o[s * 2:(s + 1) * 2, c0:c0 + 384], st[s * 8 + 6:s * 8 + 8, 0:384], reads=B("stg%d" % (blk % 2)))

    seq = [(0, t) for t in range(NT)] + [(1, t) for t in range(NT)]
    prev = None
    for k, (l, t) in enumerate(seq):
        pg = k % 2
        fa = front0_a(t, pg) if l == 0 else front1_a(t, pg)
        gens = [fa]
        if prev is not None:
            gens.insert(0, att_gen(prev[0], prev[1], prev[2]))
        run_interleaved(gens)
        if prev is not None:
            att_tail(prev[0], prev[1])
        if l == 1 and t == 0:
            load_wout(1)
            S.dma("sp", gfin, gfin_d[:, :], writes=B("KBG", "KD", "VB"))
        if l == 0:
            front0_b(t, pg)
        prev = (l, t, pg)
    run_interleaved([att_gen(prev[0], prev[1], prev[2])])
    att_tail(prev[0], prev[1])
    S.finish()
    es.close()
    return nc, S


_CACHE = {}
_STAGE = 99
_SUB = 99


def _consts():
    i = np.arange(128)
    same = (i[:, None] // 8) == (i[None, :] // 8)
    ident = np.eye(128)
    ones = np.ones((128, 128))
    U = (i[:, None] <= i[None, :])
    SL = (i[:, None] > i[None, :])
    ML = (i[:, None] > i[None, :])
    MU = (i[:, None] <= i[None, :])
    mats = [ident, ones, U, SL, U & same, SL & same]
    cst = np.stack([m.astype(np.float32) for m in mats], axis=1).reshape(128, 6 * 128)
    smask = ((i[:, None] // 8) == np.arange(16)[None, :]).astype(np.float32)
    return np.ascontiguousarray(cst), np.ascontiguousarray(smask)


def kernel(x_prompt, x_sample, mem_prompt, state_gdn, state_gdn_conv, state_sconv,
           cache_mem_k, cache_mem_v, norm_g, w_in_a, conv_w_a, a_log, dt_bias, o_norm_g,
           w_in_b, conv_w_b, mem_norm_g, w_mem_kv, w_out, final_norm_g):
    f = lambda a: np.ascontiguousarray(np.asarray(a, dtype=np.float32))
    if "nc" not in _CACHE:
        _CACHE["nc"] = build_program(_STAGE)[0]
    nc = _CACHE["nc"]
    cst, smask = _consts()
    x_prompt, x_sample, mem_prompt = f(x_prompt), f(x_sample), f(mem_prompt)
    state_gdn, state_gdn_conv, state_sconv = f(state_gdn), f(state_gdn_conv), f(state_sconv)
    cache_mem_k, cache_mem_v = f(cache_mem_k), f(cache_mem_v)
    gfm = np.concatenate([f(norm_g).reshape(2, 8, 128), f(mem_norm_g).reshape(1, 8, 128)], axis=0)
    gfm = np.ascontiguousarray(gfm.transpose(2, 0, 1).reshape(128, 24))
    cwa = np.ascontiguousarray(f(conv_w_a)[0].reshape(4, 18, 128).transpose(2, 1, 0).reshape(128, 72))
    cwb = np.ascontiguousarray(f(conv_w_b)[0].reshape(3, 6, 128).transpose(2, 1, 0).reshape(128, 18))
    small = np.ascontiguousarray(np.broadcast_to(np.concatenate([f(a_log)[0], f(dt_bias)[0]])[None, :], (128, 12)))
    ogb = np.ascontiguousarray(np.broadcast_to(f(o_norm_g)[0][None, :], (128, 128)))
    gfin = np.ascontiguousarray(np.broadcast_to(f(final_norm_g)[None, :], (128, 1024)))
    shared = {
        "wina": f(w_in_a)[0], "winb": f(w_in_b)[0], "wout": f(w_out).reshape(2048, 1024),
        "wkv": f(w_mem_kv).reshape(2048, 512), "gfm": gfm, "cwa": cwa, "cwb": cwb, "small": small,
        "ogb": ogb, "gfin": gfin, "cst": cst, "smask": smask,
    }
    in_maps = []
    for c in range(8):
        sl = slice(c * 16, (c + 1) * 16)
        m = dict(shared)
        m["x"] = np.ascontiguousarray(np.concatenate([x_prompt[c], x_sample[sl].reshape(128, 1024)], axis=0))
        m["mem"] = mem_prompt[c]
        m["sgdn"] = np.ascontiguousarray(state_gdn[0, sl].reshape(16 * 6 * 128, 128))
        m["sgc"] = np.ascontiguousarray(state_gdn_conv[0, sl].reshape(48, 2304))
        m["ssc"] = np.ascontiguousarray(state_sconv[0, sl].reshape(32, 768))
        m["cmk"] = np.ascontiguousarray(cache_mem_k[:, sl].reshape(2 * 16 * 256, 256))
        m["cmv"] = np.ascontiguousarray(cache_mem_v[:, sl].reshape(2 * 16 * 256, 256))
        in_maps.append(m)
    res = run_bass_kernel_spmd(nc, in_maps, core_ids=list(range(8)))
    R = res.results
    y = np.stack([r["y"] for r in R])
    y_prompt = np.ascontiguousarray(y[:, :2048])
    y_sample = np.ascontiguousarray(y[:, 2048:].reshape(128, 8, 1024))
    S_p = np.stack([r["sp_o"].reshape(6, 128, 128) for r in R])[None]
    gc_p = np.stack([r["gcp_o"] for r in R])[None]
    sc_p = np.stack([r["scp_o"] for r in R])[None]
    mk_p = np.stack([r["mkp_o"].reshape(2, 256, 4, 64) for r in R], axis=1)
    mv_p = np.stack([r["mvp_o"].reshape(2, 256, 4, 64) for r in R], axis=1)
    S_s = np.concatenate([r["ss_o"].reshape(16, 6, 128, 128) for r in R])[None]
    gc_s = np.concatenate([r["gcs_o"].reshape(16, 3, 2304) for r in R])[None]
    sc_s = np.concatenate([r["scs_o"].reshape(16, 2, 768) for r in R])[None]
    outs = (y_prompt, y_sample, S_p, gc_p, sc_p, mk_p, mv_p, S_s, gc_s, sc_s)
    return tuple(np.ascontiguousarray(o, dtype=np.float32) for o in outs)
```

```python
import numpy as np
from contextlib import ExitStack
import concourse.bass as bass
import concourse.mybir as mybir
from concourse.bass_utils import run_bass_kernel_spmd

F32 = mybir.dt.float32
BF16 = mybir.dt.bfloat16
AF = mybir.ActivationFunctionType
ALU = mybir.AluOpType
AX = mybir.AxisListType

NT = 17
EPS = 1e-6
CA = 3596
CB = 3584


class Buf:
    __slots__ = ("name", "last_w", "readers", "excl")

    def __init__(self, name, excl=False):
        self.name = name
        self.last_w = None
        self.readers = {}
        self.excl = excl


class Sched:
    def __init__(self, nc, es, n_dma_sems=(24, 4, 12)):
        self.nc = nc
        self.eng = {"pe": nc.tensor, "act": nc.scalar, "dve": nc.vector, "pool": nc.gpsimd, "sp": nc.sync}
        self.sem, self.cnt, self.seen = {}, {}, {}
        for k in self.eng:
            self.sem[k] = es.enter_context(nc.semaphore("s_" + k))
            self.cnt[k] = 0
            self.seen[k] = {}
        self.dma_sems, self.dma_rr = {}, {}
        for q, n in zip(("sp", "act", "pool"), n_dma_sems):
            self.dma_sems[q] = []
            self.dma_rr[q] = 0
            for i in range(n):
                key = "d%s%d" % (q, i)
                self.sem[key] = es.enter_context(nc.semaphore("s_" + key))
                self.cnt[key] = 0
                self.dma_sems[q].append(key)
        self.n_inst = 0
        self.n_wait = 0
        self.pe_last = None
        self.sep = None
        self.attach = True
        self.max_attach = 1
        self.snap = {}

    def _wait(self, e, deps, defer=False):
        best = {}
        for (k, c) in deps:
            if c > best.get(k, 0):
                best[k] = c
        cand = sorted(best.items(), key=lambda kc: -kc[1])
        need = []
        for k, c in cand:
            if self.seen[e].get(k, 0) >= c:
                continue
            need.append((k, c))
            self._learn(e, k, c)
        last = []
        while defer and need and len(last) < self.max_attach:
            last.append(need.pop())
        for k, c in need:
            self.eng[e].wait_ge(self.sem[k], c)
            self.n_wait += 1
        return last

    def _learn(self, e, k, c):
        se = self.seen[e]
        if se.get(k, 0) < c:
            se[k] = c
        sn = self.snap.get((k, c))
        if sn:
            for kk, cc in sn.items():
                if se.get(kk, 0) < cc:
                    se[kk] = cc

    @staticmethod
    def _deps(reads, writes):
        deps = []
        for b in reads:
            if b.last_w is not None:
                deps.append(b.last_w)
        for b in writes:
            if b.last_w is not None:
                deps.append(b.last_w)
            deps.extend(b.readers.items())
        return deps

    def op(self, e, fn, reads=(), writes=(), kind=None, inc=True):
        if e == "pe":
            if kind == "bf" and self.pe_last == "f32" and self.sep is not None:
                self.pe_last = "tr"
                self.sep()
            if kind is not None:
                self.pe_last = kind
        ex = [b for b in reads if b.excl]
        if ex:
            writes = list(writes) + [b for b in ex if b not in writes]
            reads = [b for b in reads if not b.excl]
        deps = self._deps(reads, writes)
        if e == "pe":
            deps = [d for d in deps if d[0] != "pe"]
        last = self._wait(e, deps, defer=self.attach)
        ins = fn(self.eng[e])
        for (lk, lc) in (last or []):
            ins._wait_ge(self.sem[lk], lc)
        if inc:
            ins.then_inc(self.sem[e], 1)
            self.cnt[e] += 1
            c = self.cnt[e]
            self.snap[(e, c)] = dict(self.seen[e])
        else:
            c = self.cnt[e] + 1
        for b in writes:
            b.last_w = (e, c)
            b.readers = {}
        for b in reads:
            if b not in writes:
                b.readers[e] = c
        self.n_inst += 1
        return ins

    def dma(self, q, out, in_, reads=(), writes=(), **kw):
        key = self.dma_sems[q][self.dma_rr[q] % len(self.dma_sems[q])]
        self.dma_rr[q] += 1
        deps = self._deps(reads, writes)
        if self.cnt[key] > 0:
            deps.append((key, self.cnt[key]))
        last = self._wait(q, deps, defer=self.attach)
        ins = self.eng[q].dma_start(out=out, in_=in_, **kw)
        for (lk, lc) in (last or []):
            ins._wait_ge(self.sem[lk], lc)
        ins.then_inc(self.sem[key], 16)
        self.cnt[key] += 16
        c = self.cnt[key]
        self.snap[(key, c)] = dict(self.seen[q])
        for b in writes:
            b.last_w = (key, c)
            b.readers = {}
        for b in reads:
            b.readers[key] = c
        self.n_inst += 1
        return ins

    def finish(self):
        deps = [(k, c) for k, c in self.cnt.items() if c > 0]
        for e in ("sp", "act", "pool", "dve", "pe"):
            self._wait(e, [d for d in deps if d[0] != e])


def build_program(stage=99):
    nc = bass.Bass("TRN2", target_bir_lowering=False)

    def din(name, shape):
        return nc.dram_tensor(name, list(shape), F32, kind="ExternalInput").ap()

    def dout(name, shape):
        return nc.dram_tensor(name, list(shape), F32, kind="ExternalOutput").ap()

    x_d = din("x", [NT * 128, 1024])
    mem_d = din("mem", [256, 1024])
    sgdn_d = din("sgdn", [16 * 6 * 128, 128])
    sgc_d = din("sgc", [48, 2304])
    ssc_d = din("ssc", [32, 768])
    cmk_d = din("cmk", [2 * 16 * 256, 256])
    cmv_d = din("cmv", [2 * 16 * 256, 256])
    wina_d = din("wina", [1024, CA])
    winb_d = din("winb", [1024, CB])
    wout_d = din("wout", [2048, 1024])
    wkv_d = din("wkv", [2048, 512])
    gfm_d = din("gfm", [128, 24])
    cwa_d = din("cwa", [128, 18 * 4])
    cwb_d = din("cwb", [128, 6 * 3])
    small_d = din("small", [128, 12])
    ogb_d = din("ogb", [128, 128])
    gfin_d = din("gfin", [128, 1024])
    cst_d = din("cst", [128, 6 * 128])
    smask_d = din("smask", [128, 16])

    y_o = dout("y", [NT * 128, 1024])
    sp_o = dout("sp_o", [6 * 128, 128])
    gcp_o = dout("gcp_o", [3, 2304])
    scp_o = dout("scp_o", [2, 768])
    mkp_o = dout("mkp_o", [512, 256])
    mvp_o = dout("mvp_o", [512, 256])
    ss_o = dout("ss_o", [16 * 6 * 128, 128])
    gcs_o = dout("gcs_o", [48, 2304])
    scs_o = dout("scs_o", [32, 768])

    es = ExitStack()
    S = Sched(nc, es)
    bufs = {}

    def sb(name, shape, dt=F32):
        t = es.enter_context(nc.sbuf_tensor("sb_" + name, list(shape), dt))
        bufs[name] = Buf(name)
        return t

    def B(*names):
        return [bufs[n] for n in names]

    def nb(name):
        bufs[name] = Buf(name)
        return bufs[name]

    xres = sb("xres", [128, NT, 1024])
    for t in range(NT):
        nb("x%d" % t)
    win = sb("win", [128, 8, CA], BF16)
    for k in range(8):
        nb("win%d" % k)
    wout = sb("wout", [128, 8, 1024], BF16)
    cst = sb("cst", [128, 6, 128])
    identb = sb("identb", [128, 128], BF16)
    onesb = sb("onesb", [128, 128], BF16)
    smask = sb("smask", [128, 16])
    gfm = sb("gfm", [128, 24])
    cwa = sb("cwa", [128, 18, 4])
    cwb = sb("cwb", [128, 6, 3])
    small = sb("small", [128, 12])
    nA = sb("nA", [128, 6])
    ogb = sb("ogb", [128, 128])
    epsc = sb("epsc", [128, 2])
    rstd = sb("rstd", [128, 3 * NT + 2])
    for i in range(3 * NT + 2):
        nb("rstd_%d" % i)
    ssq = sb("ssq", [128, 3 * NT + 2])
    mkT = sb("mkT", [128, 2, 2, 256], BF16)
    mvp = sb("mvp", [128, 2, 2, 256], BF16)
    hb = sb("hb", [128, 1024], BF16)
    hT = sb("hT", [128, 8, 128], BF16)
    xp = [sb("xp%d" % i, [128, 16 * 11]) for i in range(2)]
    hist = sb("hist", [128, 18, 3])
    hist2 = sb("hist2", [128, 6, 2])
    cacc = [sb("cacc%d" % i, [128, 128]) for i in range(2)]
    sil = [sb("sil%d" % i, [128, 128]) for i in range(2)]
    sqb = [sb("sqb%d" % i, [128, 3, 128], BF16) for i in range(2)]
    QKT = sb("QKT", [128, 12, 128], BF16)
    QT, KT = QKT[:, 0:6], QKT[:, 6:12]
    VT = sb("VT", [128, 6, 128], BF16)
    for h in range(6):
        nb("QT%d" % h), nb("KT%d" % h), nb("VT%d" % h)
    sgT2 = [sb("sgT%d" % i, [128, 8, 128], BF16) for i in range(2)]
    xqT2 = [sb("xqT%d" % i, [128, 2, 128], BF16) for i in range(2)]
    gat = sb("gat", [128, 48])
    cum = sb("cum", [128, 18 + 96])
    kkv3 = sb("kkv3", [128, 3, 6, 128], BF16)
    KBG, KD, VB = kkv3[:, 0], kkv3[:, 1], kkv3[:, 2]
    nb("KBG"), nb("KD"), nb("VB")
    for h6 in range(6):
        nb("gatn%d" % h6)
    gfin = kkv3[:].rearrange("p a b c -> p (a b c)")[:, 0:2048].bitcast(F32)
    Dl = [sb("Dl%d" % i, [128, 128]) for i in range(3)]
    Du = [sb("Du%d" % i, [128, 128]) for i in range(3)]
    AqT = [sb("AqT%d" % i, [128, 128], BF16) for i in range(3)]
    Rb = [[sb("R%d_%d" % (i, j), [128, 128]) for j in range(2)] for i in range(3)]
    Qb = [[sb("Q%d_%d" % (i, j), [128, 128]) for j in range(2)] for i in range(3)]
    Xb = [sb("X%d" % i, [128, 128]) for i in range(3)]
    Xbf = [sb("Xbf%d" % i, [128, 128], BF16) for i in range(3)]
    wTn = [sb("wTn%d" % i, [128, 128], BF16) for i in range(3)]
    vnew = [sb("vnew%d" % i, [128, 128], BF16) for i in range(3)]
    onb = [sb("onb%d" % i, [128, 128], BF16) for i in range(3)]
    Sst = sb("Sst", [128, 6, 128])
    Sbf = sb("Sbf", [128, 6, 128], BF16)
    for h in range(6):
        nb("S%d" % h)
    Pb = sb("Pb", [128, 4, 256], BF16)
    PTb = hb[:].rearrange("p (h m t) -> p h m t", h=4, m=2)
    sm = sb("sm", [128, 16])
    stg = [sb("stg%d" % i, [128, 512]) for i in range(2)]
    for k8 in range(8):
        nb("S0bf_%d" % k8)
    tsb = [sb("tsb%d" % i, [128, 128], BF16) for i in range(2)]
    kdm = [sb("kdm%d" % i, [128, 128], BF16) for i in range(2)]
    xqm = [sb("xqm%d" % i, [128, 2, 128], BF16) for i in range(2)]
    ckv = [sb("ckv%d" % i, [128, 2, 2, 256], BF16) for i in range(2)]
    S0bf = ckv[0][:].rearrange("p a b c -> p (a b c)").rearrange("p (s e) -> p s e", s=8)
    S0BN = ["S0bf_%d" % k8 for k8 in range(8)]
    mkTs = [sb("mkTs%d" % i, [128, 2, 256], BF16) for i in range(2)]
    hst = [sb("hst%d" % i, [48, 128]) for i in range(2)]

    S0v = [hb[:].bitcast(F32).rearrange("p (s e) -> p s e", s=4),
           Pb[:].rearrange("p a b -> p (a b)").bitcast(F32).rearrange("p (s e) -> p s e", s=4)]
    S0n = ["hb", "Pb"]
    Snw4 = [stg[0][:].rearrange("p (s e) -> p s e", s=4), stg[1][:].rearrange("p (s e) -> p s e", s=4)]
    stgKV = [stg[0][:].rearrange("p (m c) -> p m c", m=2), stg[1][:].rearrange("p (m c) -> p m c", m=2)]
    sg4 = sgdn_d.rearrange("(s h d) e -> d s h e", s=16, h=6)
    ss4 = ss_o.rearrange("(s h d) e -> d s h e", s=16, h=6)

    pbank = [es.enter_context(nc.psum_tensor("pb%d" % i, [128, 512], F32)) for i in range(8)]
    for i in range(8):
        bk = nb("pbank%d" % i)
        bk.excl = True
        for j in range(4):
            bufs["p%d_%d" % (i, j)] = bk

    def PS(bank, slot, n=1):
        return pbank[bank][:, slot * 128:(slot + n) * 128], B("p%d_0" % bank)

    def PSB(bank, slot, n=1):
        v = pbank[bank][:, slot * 128:(slot + n) * 128].bitcast(BF16)
        return v, B("p%d_0" % bank)

    I_F, ONES_F, U_, SL_, US_, SLS_ = range(6)
    ML_, MU_, MLS_, MUS_ = SL_, U_, SLS_, US_

    def C(i):
        return cst[:, i, :]

    S.dma("sp", cst[:].rearrange("p a b -> p (a b)"), cst_d[:, :], writes=B("cst"))
    S.dma("sp", smask[:], smask_d[:, :], writes=B("smask"))
    S.dma("sp", gfm[:], gfm_d[:, :], writes=B("gfm"))
    S.dma("sp", cwa[:].rearrange("p a b -> p (a b)"), cwa_d[:, :], writes=B("cwa"))
    S.dma("sp", cwb[:].rearrange("p a b -> p (a b)"), cwb_d[:, :], writes=B("cwb"))
    S.dma("sp", small[:], small_d[:, :], writes=B("small"))
    S.dma("sp", ogb[:], ogb_d[:, :], writes=B("ogb"))
    S.op("dve", lambda e: e.tensor_copy(identb[:], C(I_F)), reads=B("cst"), writes=B("identb"))
    S.op("dve", lambda e: e.tensor_copy(onesb[:], C(ONES_F)), reads=B("cst"), writes=B("onesb"))
    S.op("dve", lambda e: e.memset(epsc[:, 0:1], EPS), writes=B("epsc"))
    S.op("dve", lambda e: e.memset(epsc[:, 1:2], 128.0 * EPS), writes=B("epsc"))
    S.op("dve", lambda e: e.memset(hist[:], 0.0), writes=B("hist"))
    S.op("dve", lambda e: e.memset(hist2[:], 0.0), writes=B("hist2"))
    S.op("dve", lambda e: e.memset(ssq[:], 0.0), writes=B("ssq"))
    S.op("dve", lambda e: e.memset(sm[:], 0.0), writes=B("sm"))
    S.op("dve", lambda e: e.memset(gat[:], 0.0), writes=B("gat"))
    S.op("dve", lambda e: e.memset(rstd[:], 0.0), writes=B(*["rstd_%d" % i for i in range(3 * NT + 2)]))
    S.op("act", lambda e: e.activation(nA[:], small[:, 0:6], AF.Exp), reads=B("small"), writes=B("nA"))
    S.op("dve", lambda e: e.tensor_scalar(nA[:], nA[:], -1.0, None, op0=ALU.mult), reads=B("nA"), writes=B("nA"))

    for l in range(2):
        for kc in range(8):
            S.dma("pool", wout[:, kc, l * 512:(l + 1) * 512], wkv_d[l * 1024 + kc * 128:l * 1024 + (kc + 1) * 128, :],
                  writes=B("wout"))
    for kc in range(8):
        S.dma("pool", win[:, kc, 0:CA], wina_d[kc * 128:(kc + 1) * 128, :], writes=B("win%d" % kc))
    for mt in range(2):
        S.dma("sp", xres[:, mt, :], mem_d[mt * 128:(mt + 1) * 128, :], writes=B("x%d" % mt))

    def rstd_from_ssq(col, n_feat):
        S.op("act", lambda e: e.activation(rstd[:, col:col + 1], ssq[:, col:col + 1], AF.Ln, bias=epsc[:, 0:1], scale=1.0 / n_feat),
             reads=B("ssq", "epsc"), writes=B("rstd_%d" % col))
        S.op("act", lambda e: e.activation(rstd[:, col:col + 1], rstd[:, col:col + 1], AF.Exp, scale=-0.5),
             reads=B("rstd_%d" % col), writes=B("rstd_%d" % col))

    def ssq_of_tile(t, col):
        S.op("act", lambda e: e.activation(hb[:], xres[:, t, :], AF.Square, accum_out=ssq[:, col:col + 1]),
             reads=B("x%d" % t, "ssq"), writes=B("hb", "ssq"))

    def norm_transpose(t, rcol, gcol0):
        hbx = QKT[:].rearrange("p a b -> p (a b)")[:, 0:1024]
        HBN = B("QT0", "QT1", "QT2", "QT3", "QT4", "QT5", "KT0", "KT1")
        S.op("dve", lambda e: e.tensor_scalar(hbx, xres[:, t, :], rstd[:, rcol:rcol + 1], None, op0=ALU.mult),
             reads=B("x%d" % t, "rstd_%d" % rcol), writes=HBN)
        pv, pbf = PSB(7, 0, 4)
        pv3 = pv.rearrange("p (k t) -> p k t", k=8)
        for kc in range(8):
            S.op("pe", lambda e, kc=kc: e.transpose(pv3[:, kc, :], hbx[:, kc * 128:(kc + 1) * 128], identb[:]),
                 reads=HBN + B("identb"), writes=pbf, kind="tr")
        g3 = gfm[:, gcol0:gcol0 + 8].unsqueeze(2).to_broadcast([128, 8, 128])
        S.op("dve", lambda e: e.tensor_tensor(hT[:], pv3, g3, op=ALU.mult), reads=pbf + B("gfm"), writes=B("hT"))

    def pe_sep():
        sv, sbk = PSB(3, 3, 1)
        S.op("pe", lambda e: e.transpose(sv[:, 0:128], identb[:], identb[:]), reads=B("identb"), writes=sbk, kind="tr")

    S.sep = pe_sep
    def run_interleaved(gens):
        gens = list(gens)
        while gens:
            for g in list(gens):
                try:
                    next(g)
                except StopIteration:
                    gens.remove(g)

    proj_rr = [0]

    def proj_chunk(col0, ncols=128):
        bank = (1, 7)[proj_rr[0] % 2]
        proj_rr[0] += 1
        pv, pbf = PS(bank, 0)
        for kc in range(8):
            S.op("pe", lambda e, kc=kc: e.matmul(pv[0:ncols, :], win[:, kc, col0:col0 + ncols], hT[:, kc, :], start=(kc == 0), stop=(kc == 7)),
                 reads=B("win%d" % kc, "hT"), writes=pbf, kind="bf", inc=(kc == 7))
        return pv, pbf

    for mt in range(2):
        ssq_of_tile(mt, 3 * NT + mt)
        rstd_from_ssq(3 * NT + mt, 1024)
    memT = kkv3[:].rearrange("p a b c -> p (a b c)")[:, 0:2048].rearrange("p (k m) -> p k m", k=8)
    MEMB = B("KBG", "KD", "VB")
    for mt in range(2):
        norm_transpose(mt, 3 * NT + mt, 16)
        S.op("act", lambda e, mt=mt: e.copy(memT[:, :, mt * 128:(mt + 1) * 128], hT[:]), reads=B("hT"), writes=MEMB)
    for l in range(2):
        for mt in range(2):
            pv, pbf = PS(2, 0, 4)
            for kc in range(8):
                S.op("pe", lambda e, kc=kc: e.matmul(pv, memT[:, kc, mt * 128:(mt + 1) * 128], wout[:, kc, l * 512:(l + 1) * 512], start=(kc == 0), stop=(kc == 7)),
                     reads=MEMB + B("wout"), writes=pbf)
            st = stg[(l * 2 + mt) % 2]
            sn = "stg%d" % ((l * 2 + mt) % 2)
            S.op("act", lambda e, st=st: e.copy(st[:], pv), reads=pbf, writes=B(sn))
            S.dma("sp", mkp_o[l * 256 + mt * 128:l * 256 + (mt + 1) * 128, :], st[:, 0:256], reads=B(sn))
            S.dma("sp", mvp_o[l * 256 + mt * 128:l * 256 + (mt + 1) * 128, :], st[:, 256:512], reads=B(sn))
            S.op("dve", lambda e, st=st: e.tensor_copy(mvp[:, l, mt, :], st[:, 256:512]), reads=B(sn), writes=B("mvp"))
        for c in range(2):
            pv, pbf = PS(3, 0, 2)
            for kc in range(8):
                S.op("pe", lambda e, kc=kc: e.matmul(pv, wout[:, kc, l * 512 + c * 128:l * 512 + (c + 1) * 128], memT[:, kc, :], start=(kc == 0), stop=(kc == 7)),
                     reads=MEMB + B("wout"), writes=pbf)
            S.op("act", lambda e, c=c: e.copy(mkT[:, l, c, :], pv), reads=pbf, writes=B("mkT"))

    for t in range(NT):
        S.dma("sp", xres[:, t, :], x_d[t * 128:(t + 1) * 128, :], writes=B("x%d" % t))

    def load_win(l):
        wd = wina_d if l == 0 else winb_d
        ncol = CA if l == 0 else CB
        for kc in range(8):
            S.dma("pool", win[:, kc, 0:ncol], wd[kc * 128:(kc + 1) * 128, :], writes=B("win%d" % kc))

    def load_wout(l):
        for kc in range(8):
            S.dma("pool", wout[:, kc, :], wout_d[l * 1024 + kc * 128:l * 1024 + (kc + 1) * 128, :], writes=B("wout"))

    def load_layer_weights(l):
        load_win(l)
        load_wout(l)

    load_wout(0)
    for t in range(NT):
        ssq_of_tile(t, t)
        rstd_from_ssq(t, 1024)

    def att_gen(l, t, pg):
        sample = (t == NT - 1)
        sgT, xqT = sgT2[pg], xqT2[pg]
        SG, XQ = "sgT%d" % pg, "xqT%d" % pg
        scv, scb = [], []
        for bk in (3, 4, 5, 6):
            v_, b_ = PS(bk, 0, 2)
            scv.append(v_)
            scb.append(b_)
        if not sample:
            for h in range(4):
                po = (h % 2) * 64
                if _SUB == 700 and h != 0:
                    continue
                if _SUB == 701 and h != 1:
                    continue
                S.op("pe", lambda e, h=h, po=po: e.matmul(scv[h], xqT[po:po + 64, h // 2, :], mkT[po:po + 64, l, h // 2, :], start=True, stop=True),
                     reads=B(XQ, "mkT"), writes=scb[h])
        else:
            for s in range(16):
                i = s % 2
                r0 = (l * 16 + s) * 256
                S.dma("sp", stgKV[i], cmk_d[r0:r0 + 256, :].rearrange("(mt p) c -> p mt c", p=128), writes=B("stg%d" % i))
                S.op("act", lambda e, i=i: e.copy(ckv[i][:, 0, :, :], stgKV[i]), reads=B("stg%d" % i), writes=B("ckv%d" % i) + (B(*S0BN) if i == 0 else []))
                tv, tb = PSB(0, 0, 2)
                tv3 = tv.rearrange("p (c m) -> p c m", c=2)
                for c in range(2):
                    for mt in range(2):
                        S.op("pe", lambda e, c=c, mt=mt, i=i: e.transpose(tv3[:, c, mt * 128:(mt + 1) * 128], ckv[i][:, 0, mt, c * 128:(c + 1) * 128], identb[:]),
                             reads=B("ckv%d" % i, "identb"), writes=tb)
                S.op("act", lambda e, i=i: e.copy(mkTs[i][:], tv3), reads=tb, writes=B("mkTs%d" % i))
                S.op("pool", lambda e, i=i: e.memset(xqm[i][:].rearrange("p a b -> p (a b)"), 0.0), writes=B("xqm%d" % i))
                S.op("pool", lambda e, i=i, s=s: e.tensor_copy(xqm[i][:, :, s * 8:(s + 1) * 8], xqT[:, :, s * 8:(s + 1) * 8]),
                     reads=B(XQ), writes=B("xqm%d" % i))
                for h in range(4):
                    po = (h % 2) * 64
                    S.op("pe", lambda e, h=h, po=po, i=i: e.matmul(scv[h], xqm[i][po:po + 64, h // 2, :], mkTs[i][po:po + 64, h // 2, :], start=(s == 0), stop=(s == 15)),
                         reads=B("xqm%d" % i, "mkTs%d" % i), writes=scb[h])
                yield
        yield
        ovs = [PS(0, 0, 4), PS(2, 0, 4)]
        for half in range(2):
            for c in range(6):
                S.op("pe", lambda e, c=c, half=half: e.matmul(ovs[half][0], sgT[:, c, :], wout[:, c, half * 512:(half + 1) * 512], start=(c == 0), stop=False),
                     reads=B(SG, "wout"), writes=ovs[half][1], kind="bf", inc=(c == 5))
        yield
        for h in range(4):
            S.op("dve", lambda e, h=h: e.tensor_reduce(sm[:, h:h + 1], scv[h], axis=AX.X, op=ALU.max), reads=scb[h], writes=B("sm"))
        S.op("dve", lambda e: e.tensor_scalar(sm[:, 4:8], sm[:, 0:4], -0.125, None, op0=ALU.mult), reads=B("sm"), writes=B("sm"))
        S.op("dve", lambda e: e.memset(sm[:, 8:12], 0.0), writes=B("sm"))
        yield
        for h in range(4):
            S.op("act", lambda e, h=h: e.activation(Pb[:, h, :], scv[h], AF.Exp, bias=sm[:, 4 + h:5 + h], scale=0.125, accum_out=sm[:, 8 + h:9 + h]),
                 reads=scb[h] + B("sm"), writes=B("Pb", "sm"))
        yield
        S.op("dve", lambda e: e.reciprocal(sm[:, 12:16], sm[:, 8:12]), reads=B("sm"), writes=B("sm"))
        S.op("dve", lambda e: e.tensor_tensor(Pb[:], Pb[:], sm[:, 12:16].unsqueeze(2).to_broadcast([128, 4, 256]), op=ALU.mult),
             reads=B("Pb", "sm"), writes=B("Pb"))
        yield
        if _SUB == 71:
            return
        tv, tb = PSB(5, 0, 4)
        tv4 = tv.rearrange("p (h m t) -> p h m t", h=4, m=2)
        for h in range(4):
            for mt in range(2):
                S.op("pe", lambda e, h=h, mt=mt: e.transpose(tv4[:, h, mt, :], Pb[:, h, mt * 128:(mt + 1) * 128], identb[:]),
                     reads=B("Pb", "identb"), writes=tb)
        yield
        S.op("act", lambda e: e.copy(PTb, tv4), reads=tb, writes=B("hb"))
        yield
        if _SUB == 72:
            return
        xo_v, xo_b = PS(6, 0, 4)
        xo4 = xo_v.rearrange("p (c k t) -> p c k t", c=2, k=2)
        if not sample:
            for c in range(2):
                for hh in range(2):
                    h = 2 * c + hh
                    for mt in range(2):
                        S.op("pe", lambda e, h=h, mt=mt, c=c, hh=hh: e.matmul(xo4[:, c, hh, :], mvp[:, l, mt, c * 128:(c + 1) * 128], PTb[:, h, mt, :], start=(mt == 0), stop=(mt == 1)),
                             reads=B("mvp", "hb"), writes=xo_b)
        else:
            for s in range(16):
                i = s % 2
                r0 = (l * 16 + s) * 256
                S.dma("sp", stgKV[i], cmv_d[r0:r0 + 256, :].rearrange("(mt p) c -> p mt c", p=128), writes=B("stg%d" % i))
                S.op("act", lambda e, i=i: e.copy(ckv[i][:, 1, :, :], stgKV[i]), reads=B("stg%d" % i), writes=B("ckv%d" % i) + (B(*S0BN) if i == 0 else []))
                for c in range(2):
                    for hh in range(2):
                        h = 2 * c + hh
                        for mt in range(2):
                            S.op("pe", lambda e, h=h, mt=mt, c=c, hh=hh, i=i, s=s: e.matmul(xo4[:, c, hh, s * 8:(s + 1) * 8], ckv[i][:, 1, mt, c * 128:(c + 1) * 128], PTb[:, h, mt, s * 8:(s + 1) * 8], start=(mt == 0), stop=(mt == 1)),
                                 reads=B("ckv%d" % i, "hb"), writes=xo_b)
                yield
        yield
        for c in range(2):
            for hh in range(2):
                po = hh * 64
                S.op("dve", lambda e, c=c, hh=hh, po=po: e.tensor_tensor(sgT[po:po + 64, 6 + c, :], xo4[po:po + 64, c, hh, :], sgT[po:po + 64, 6 + c, :], op=ALU.mult),
                     reads=xo_b + B(SG), writes=B(SG))
        yield
        for half in range(2):
            for c in (6, 7):
                S.op("pe", lambda e, c=c, half=half: e.matmul(ovs[half][0], sgT[:, c, :], wout[:, c, half * 512:(half + 1) * 512], start=False, stop=(c == 7)),
                     reads=B(SG, "wout"), writes=ovs[half][1], kind="bf", inc=(c == 7))
            S.op("dve", lambda e, half=half: e.tensor_tensor(xres[:, t, half * 512:(half + 1) * 512], xres[:, t, half * 512:(half + 1) * 512], ovs[half][0], op=ALU.add),
                 reads=ovs[half][1] + B("x%d" % t), writes=B("x%d" % t))
    def att_tail(l, t):
        col = (l + 1) * NT + t
        ssq_of_tile(t, col)
        rstd_from_ssq(col, 1024)
        if l == 1:
            st = stg[t % 2]
            for half in range(2):
                S.op("dve", lambda e, half=half, st=st: e.scalar_tensor_tensor(st[:], xres[:, t, half * 512:(half + 1) * 512], rstd[:, col:col + 1], gfin[:, half * 512:(half + 1) * 512], op0=ALU.mult, op1=ALU.mult),
                     reads=B("x%d" % t, "rstd_%d" % col, "KBG", "KD", "VB"), writes=B("stg%d" % (t % 2)))
                S.dma("sp", y_o[t * 128:(t + 1) * 128, half * 512:(half + 1) * 512], st[:], reads=B("stg%d" % (t % 2)))

    def conv_setup_hist(l, t, c, i, nh):
        sample = (t == NT - 1)
        hbuf = hist if l == 0 else hist2
        hname = "hist" if l == 0 else "hist2"
        if not sample:
            S.op("pool", lambda e: e.tensor_copy(xp[i][:, 0:nh], hbuf[:, c, :]), reads=B(hname), writes=B("xp%d" % i))
        else:
            srcd = sgc_d if l == 0 else ssc_d
            nr = 16 * nh
            S.dma("sp", hst[i][0:nr, :], srcd[0:nr, c * 128:(c + 1) * 128], writes=B("hst%d" % i))
            pv, pbf = PS(2, 2, 1)
            S.op("pe", lambda e: e.transpose(pv[:, 0:nr], hst[i][0:nr, :], C(I_F)[0:nr, 0:nr]),
                 reads=B("hst%d" % i, "cst"), writes=pbf)
            L = 8
            xp3 = xp[i][:, 0:16 * (nh + L)].rearrange("p (s k) -> p s k", s=16)
            S.op("act", lambda e: e.copy(xp3[:, :, 0:nh], pv[:, 0:nr].rearrange("p (s j) -> p s j", s=16)), reads=pbf, writes=B("xp%d" % i))

    def xp_views(t, i, nh):
        sample = (t == NT - 1)
        if not sample:
            cur = xp[i][:, nh:nh + 128]
            taps = [xp[i][:, j:j + 128] for j in range(nh + 1)]
            last = xp[i][:, 128:128 + nh]
            return cur, taps, last
        L = 8
        xp3 = xp[i][:, 0:16 * (nh + L)].rearrange("p (s k) -> p s k", s=16)
        cur = xp3[:, :, nh:nh + L]
        taps = [xp3[:, :, j:j + L] for j in range(nh + 1)]
        return cur, taps, None

    def as3(ap, t):
        if t == NT - 1:
            return ap.rearrange("p (s k) -> p s k", s=16)
        return ap

    def front0_a(t, pg):
        sample = (t == NT - 1)
        sgT, xqT = sgT2[pg], xqT2[pg]
        SG, XQ = "sgT%d" % pg, "xqT%d" % pg
        norm_transpose(t, t, 0)
        yield
        for c in range(8):
            pv, pbf = proj_chunk(2316 + c * 128)
            S.op("act", lambda e, c=c, pv=pv: e.activation(sgT[:, c, :], pv, AF.Silu), reads=pbf, writes=B(SG))
            yield
        for c in range(2):
            pv, pbf = proj_chunk(2316 + 1024 + c * 128)
            S.op("act", lambda e, c=c, pv=pv: e.copy(xqT[:, c, :], pv), reads=pbf, writes=B(XQ))
        yield
        def stA(c):
            i = c % 2
            pv, pbf = proj_chunk(c * 128)
            conv_setup_hist(0, t, c, i, 3)
            cur, taps, last = xp_views(t, i, 3)
            S.op("act", lambda e: e.copy(cur, as3(pv, t)), reads=pbf, writes=B("xp%d" % i))
            if not sample:
                S.op("pool", lambda e: e.tensor_copy(hist[:, c, :], last), reads=B("xp%d" % i), writes=B("hist"))
            acc = as3(cacc[i][:], t)
            S.op("act", lambda e: e.activation(acc, taps[3], AF.Identity, scale=cwa[:, c, 3:4]),
                 reads=B("xp%d" % i, "cwa"), writes=B("cacc%d" % i))

        def stB(c):
            i = c % 2
            cur, taps, last = xp_views(t, i, 3)
            acc = as3(cacc[i][:], t)
            for j in range(3):
                S.op("dve", lambda e, j=j: e.scalar_tensor_tensor(acc, taps[j], cwa[:, c, j:j + 1], acc, op0=ALU.mult, op1=ALU.add),
                     reads=B("xp%d" % i, "cwa", "cacc%d" % i), writes=B("cacc%d" % i))

        def stC(c):
            i = c % 2
            if c >= 12:
                dst, h, dn = VT, c - 12, "VT%d" % (c - 12)
            elif c < 6:
                dst, h, dn = QT, c, "QT%d" % c
            else:
                dst, h, dn = KT, c - 6, "KT%d" % (c - 6)
            S.op("act", lambda e: e.activation(dst[:, h, :], cacc[i][:], AF.Silu), reads=B("cacc%d" % i), writes=B(dn))

        stA(0)
        stB(0)
        for c in range(18):
            if c + 1 < 18:
                stA(c + 1)
            stC(c)
            if c + 1 < 18:
                stB(c + 1)
            yield

    def front0_b(t, pg):
        sample = (t == NT - 1)
        sgT, xqT = sgT2[pg], xqT2[pg]
        SG, XQ = "sgT%d" % pg, "xqT%d" % pg
        gcol = gat[:, 24:30]
        Um, SLm = (US_, SLS_) if sample else (U_, SL_)
        eG, eGrev, eGl = cum[:, 0:6], cum[:, 6:12], cum[:, 12:18]
        eGls = cum[:, 18:114].rearrange("p (s h) -> p s h", s=16)

        def gen_norm():
            for g in range(4):
                i = g % 2
                isq = (g < 2)
                c0 = g * 3
                names = [("QT%d" % (c0 + k)) if isq else ("KT%d" % (c0 - 6 + k)) for k in range(3)]
                blk = QKT[:, c0:c0 + 3, :]
                S.op("act", lambda e: e.activation(sqb[i][:], blk, AF.Square), reads=B(*names), writes=B("sqb%d" % i))
                qv, qb = PS((2, 0)[i], 0, 3)
                S.op("pe", lambda e: e.matmul(qv, onesb[:], sqb[i][:].rearrange("p a b -> p (a b)"), start=True, stop=True), reads=B("onesb", "sqb%d" % i), writes=qb, kind="bf")
                yield
                S.op("act", lambda e: e.activation(qv, qv, AF.Ln, bias=epsc[:, 0:1], scale=1.0), reads=qb + B("epsc"), writes=qb)
                S.op("act", lambda e: e.activation(qv, qv, AF.Exp, scale=-0.5), reads=qb, writes=qb)
                qv3 = qv.rearrange("p (a b) -> p a b", a=3)
                if isq:
                    S.op("dve", lambda e: e.scalar_tensor_tensor(blk, blk, float(128.0 ** -0.5), qv3, op0=ALU.mult, op1=ALU.mult), reads=qb + B(*names), writes=B(*names))
                else:
                    S.op("dve", lambda e: e.tensor_tensor(blk, blk, qv3, op=ALU.mult), reads=qb + B(*names), writes=B(*names))
                yield

        def gen_gates():
            bav, bab = PS(3, 0)
            for kc in range(8):
                S.op("pe", lambda e, kc=kc: e.matmul(bav[:, 0:12], hT[:, kc, :], win[:, kc, 2304:2316], start=(kc == 0), stop=(kc == 7)),
                     reads=B("hT", "win%d" % kc), writes=bab, kind="bf")
            yield
            S.op("act", lambda e: e.activation(gat[:, 0:6], bav[:, 0:6], AF.Exp, scale=-1.0), reads=bab, writes=B("gat"))
            S.op("dve", lambda e: e.tensor_tensor(gat[:, 6:12], bav[:, 6:12], small[:, 6:12], op=ALU.add), reads=bab + B("small", "gat"), writes=B("gat"))
            S.op("act", lambda e: e.activation(gat[:, 6:12], gat[:, 6:12], AF.Exp), reads=B("gat"), writes=B("gat"))
            S.op("act", lambda e: e.activation(gat[:, 6:12], gat[:, 6:12], AF.Ln, bias=1.0), reads=B("gat"), writes=B("gat"))
            yield
            S.op("dve", lambda e: e.tensor_scalar(gat[:, 0:6], gat[:, 0:6], 1.0, None, op0=ALU.add), reads=B("gat"), writes=B("gat"))
            S.op("dve", lambda e: e.reciprocal(gat[:, 12:18], gat[:, 0:6]), reads=B("gat"), writes=B("gat"))
            S.op("dve", lambda e: e.tensor_scalar(gat[:, 18:24], gat[:, 12:18], -1.0, None, op0=ALU.mult), reads=B("gat"), writes=B("gat"))
            S.op("dve", lambda e: e.tensor_tensor(gat[:, 24:30], gat[:, 6:12], nA[:], op=ALU.mult), reads=B("gat", "nA"), writes=B("gat"))
            cv, cb = PS(3, 1)
            S.op("pe", lambda e: e.matmul(cv[:, 0:6], C(Um), gcol, start=True, stop=True), reads=B("cst", "gat"), writes=cb, kind="f32")
            S.op("pe", lambda e: e.matmul(cv[:, 6:12], C(SLm), gcol, start=True, stop=True), reads=B("cst", "gat"), writes=cb, kind="f32")
            if not sample:
                S.op("pe", lambda e: e.matmul(cv[:, 12:18], C(ONES_F), gcol, start=True, stop=True), reads=B("cst", "gat"), writes=cb, kind="f32")
                yield
                S.op("act", lambda e: e.activation(cum[:, 0:18], cv[:, 0:18], AF.Exp), reads=cb, writes=B("cum"))
            else:
                gm = cum[:, 18:114].rearrange("p (s h) -> p s h", s=16)
                S.op("dve", lambda e: e.tensor_tensor(gm, gcol.unsqueeze(1).to_broadcast([128, 16, 6]), smask[:].unsqueeze(2).to_broadcast([128, 16, 6]), op=ALU.mult),
                     reads=B("gat", "smask", "cum"), writes=B("cum"))
                cv2, cb2 = PS(3, 2)
                S.op("pe", lambda e: e.matmul(cv2[:, 0:96], C(ONES_F), cum[:, 18:114], start=True, stop=True), reads=B("cst", "cum"), writes=cb2, kind="f32")
                yield
                S.op("act", lambda e: e.activation(cum[:, 0:12], cv[:, 0:12], AF.Exp), reads=cb, writes=B("cum"))
                S.op("act", lambda e: e.activation(cum[:, 18:114], cv2[:, 0:96], AF.Exp), reads=cb2, writes=B("cum"))
            S.op("dve", lambda e: e.tensor_tensor(gat[:, 30:36], gat[:, 12:18], eG, op=ALU.mult), reads=B("gat", "cum"), writes=B("gat"))
            tv_, tb_ = PSB(7, 0, 3)
            tvv = tv_[:, 0:768].rearrange("p (h d) -> p h d", h=6)
            for h in range(6):
                S.op("pe", lambda e, h=h: e.transpose(tvv[:, h, :], VT[:, h, :], identb[:]), reads=B("VT%d" % h, "identb"), writes=tb_, kind="tr")
            yield
            S.op("dve", lambda e: e.tensor_tensor(VB, tvv, gat[:, 12:18].unsqueeze(2).to_broadcast([128, 6, 128]), op=ALU.mult), reads=tb_ + B("gat"), writes=B("VB"))

        run_interleaved([gen_norm(), gen_gates()])
        if _SUB == 2:
            return
        tv, tb = PSB(0, 0, 3)
        tv3 = tv.rearrange("p (h d) -> p h d", h=6)
        for h in range(6):
            S.op("pe", lambda e, h=h: e.transpose(tv3[:, h, :], KT[:, h, :], identb[:]), reads=B("KT%d" % h, "identb"), writes=tb)
        S.op("dve", lambda e: e.tensor_tensor(KBG, tv3, gat[:, 30:36].unsqueeze(2).to_broadcast([128, 6, 128]), op=ALU.mult), reads=tb + B("gat"), writes=B("KBG"))
        S.op("dve", lambda e: e.tensor_tensor(KD, tv3, eGrev.unsqueeze(2).to_broadcast([128, 6, 128]), op=ALU.mult), reads=tb + B("cum"), writes=B("KD"))
        if _SUB == 3:
            return
        if t >= NT - 2:
            for blk in range(5):
                c0 = blk * 512
                w_ = min(512, 2304 - c0)
                pv, pbf = PS(7, 0, 4)
                for kc in range(8):
                    S.op("pe", lambda e, kc=kc, pv=pv, c0=c0, w_=w_: e.matmul(pv[:, 0:w_], hT[:, kc, :], win[:, kc, c0:c0 + w_], start=(kc == 0), stop=(kc == 7)),
                         reads=B("hT", "win%d" % kc), writes=pbf)
                st = stg[blk % 2]
                S.op("act", lambda e, st=st, pv=pv, w_=w_: e.copy(st[:, 0:w_], pv[:, 0:w_]), reads=pbf, writes=B("stg%d" % (blk % 2)))
                if not sample:
                    S.dma("sp", gcp_o[0:3, c0:c0 + w_], st[125:128, 0:w_], reads=B("stg%d" % (blk % 2)))
                else:
                    for s in range(16):
                        S.dma("sp", gcs_o[s * 3:(s + 1) * 3, c0:c0 + w_], st[s * 8 + 5:s * 8 + 8, 0:w_], reads=B("stg%d" % (blk % 2)))
        if t == NT - 1:
            load_win(1)
        MLm, MUm = (MLS_, MUS_) if sample else (ML_, MU_)
        nlev = 3 if sample else 7
        have_state = sample or t > 0
        HB = ((4, 5), (0, 1), (7, 2))

        def head_gen(h, i):
            bA, bB = HB[i]
            dlv, dlb = PS(bA, 0)
            duv, dub = PS(bA, 1)
            kkv, kkb = PS(bB, 0)
            kqv, kqb = PS(bB, 1)
            UGv, SLGv = Dl[i], Du[i]
            S.op("dve", lambda e: e.tensor_scalar(UGv[:], C(Um), gat[:, 24 + h:25 + h], None, op0=ALU.mult), reads=B("cst", "gat"), writes=B("Dl%d" % i))
            S.op("dve", lambda e: e.tensor_scalar(SLGv[:], C(SLm), gat[:, 24 + h:25 + h], None, op0=ALU.mult), reads=B("cst", "gat"), writes=B("Du%d" % i))
            S.op("pe", lambda e: e.matmul(dlv, UGv[:], C(SLm), start=True, stop=True), reads=B("Dl%d" % i, "cst"), writes=dlb, kind="f32")
            S.op("pe", lambda e: e.matmul(duv, SLGv[:], C(Um), start=True, stop=True), reads=B("Du%d" % i, "cst"), writes=dub, kind="f32")
            S.op("pe", lambda e: e.matmul(kkv, KT[:, h, :], KT[:, h, :], start=True, stop=True), reads=B("KT%d" % h), writes=kkb, kind="bf")
            S.op("pe", lambda e: e.matmul(kqv, KT[:, h, :], QT[:, h, :], start=True, stop=True), reads=B("KT%d" % h, "QT%d" % h), writes=kqb, kind="bf")
            yield
            S.op("act", lambda e: e.activation(Dl[i][:], dlv, AF.Exp), reads=dlb, writes=B("Dl%d" % i))
            S.op("act", lambda e: e.activation(Du[i][:], duv, AF.Exp), reads=dub, writes=B("Du%d" % i))
            S.op("dve", lambda e: e.tensor_tensor(Dl[i][:], Dl[i][:], C(MLm), op=ALU.mult), reads=B("Dl%d" % i, "cst"), writes=B("Dl%d" % i))
            S.op("dve", lambda e: e.tensor_tensor(Du[i][:], Du[i][:], C(MUm), op=ALU.mult), reads=B("Du%d" % i, "cst"), writes=B("Du%d" % i))
            R0 = Rb[i][0]
            S.op("dve", lambda e: e.scalar_tensor_tensor(R0[:], kkv, gat[:, 18 + h:19 + h], Dl[i][:], op0=ALU.mult, op1=ALU.mult),
                 reads=kkb + B("gat", "Dl%d" % i), writes=B("R%d_0" % i))
            S.op("dve", lambda e: e.tensor_tensor(AqT[i][:], kqv, Du[i][:], op=ALU.mult), reads=kqb + B("Du%d" % i), writes=B("AqT%d" % i))
            yield
            ntv, ntb = PS(bA, 2)
            S.op("pe", lambda e: e.transpose(ntv, R0[:], C(I_F)), reads=B("R%d_0" % i, "cst"), writes=ntb, kind="tr")
            yield
            S.op("act", lambda e: e.copy(Qb[i][0][:], ntv), reads=ntb, writes=B("Q%d_0" % i))
            S.op("dve", lambda e: e.tensor_tensor(Xb[i][:], ntv, C(I_F), op=ALU.add), reads=ntb + B("cst"), writes=B("X%d" % i))
            for k in range(1, nlev):
                a, b_ = (k - 1) % 2, k % 2
                rv, rb_ = PS(bA, 0)
                qv_, qb_ = PS(bA, 1)
                xv, xb_ = PS(bB, 0)
                S.op("pe", lambda e: e.matmul(rv, Qb[i][a][:], Rb[i][a][:], start=True, stop=True),
                     reads=B("Q%d_%d" % (i, a), "R%d_%d" % (i, a)), writes=rb_, kind="f32")
                if k < nlev - 1:
                    S.op("pe", lambda e: e.matmul(qv_, Rb[i][a][:], Qb[i][a][:], start=True, stop=True),
                         reads=B("Q%d_%d" % (i, a), "R%d_%d" % (i, a)), writes=qb_, kind="f32")
                yield
                S.op("act", lambda e: e.copy(Rb[i][b_][:], rv), reads=rb_, writes=B("R%d_%d" % (i, b_)))
                if k < nlev - 1:
                    S.op("act", lambda e: e.copy(Qb[i][b_][:], qv_), reads=qb_, writes=B("Q%d_%d" % (i, b_)))
                S.op("pe", lambda e: e.matmul(xv, Rb[i][b_][:], Xb[i][:], start=True, stop=True),
                     reads=B("R%d_%d" % (i, b_), "X%d" % i), writes=xb_, kind="f32")
                yield
                S.op("dve", lambda e: e.tensor_tensor(Xb[i][:], xv, Xb[i][:], op=ALU.add), reads=xb_ + B("X%d" % i), writes=B("X%d" % i))
            S.op("act", lambda e: e.copy(Xbf[i][:], Xb[i][:]), reads=B("X%d" % i), writes=B("Xbf%d" % i))
            wv, wb = PS(bA, 2)
            S.op("pe", lambda e: e.matmul(wv, KBG[:, h, :], Xbf[i][:], start=True, stop=True), reads=B("KBG", "Xbf%d" % i), writes=wb, kind="bf")
            yield
            S.op("act", lambda e: e.activation(wTn[i][:], wv, AF.Identity, scale=-1.0), reads=wb, writes=B("wTn%d" % i))
            vv, vb = PS(bB, 1)
            qsv, qsb = PS(bB, 2)
            if not sample:
                S.op("pe", lambda e: e.matmul(vv, Xbf[i][:], VB[:, h, :], start=True, stop=(not have_state)), reads=B("Xbf%d" % i, "VB"), writes=vb, kind="bf")
                if have_state:
                    S.op("pe", lambda e: e.matmul(vv, wTn[i][:], Sbf[:, h, :], start=False, stop=True), reads=B("wTn%d" % i, "S%d" % h), writes=vb, kind="bf")
                    S.op("pe", lambda e: e.matmul(qsv, QT[:, h, :], Sbf[:, h, :], start=True, stop=True), reads=B("QT%d" % h, "S%d" % h), writes=qsb, kind="bf")
            else:
                t1v, t1b = PS(bA, 0)
                t2v, t2b = PS(bA, 1)
                for q4 in range(4):
                    j = q4 % 2
                    S.dma("sp", S0v[j], sg4[:, q4 * 4:(q4 + 1) * 4, h, :], writes=B(S0n[j]))
                    S.op("act", lambda e, j=j: e.copy(S0bf[:, j * 4:j * 4 + 4, :], S0v[j]), reads=B(S0n[j]), writes=B(*["S0bf_%d" % (j * 4 + k) for k in range(4)]) + B("ckv0"))
                    for k in range(4):
                        s, s8 = q4 * 4 + k, j * 4 + k
                        S.op("pe", lambda e, s=s, s8=s8: e.matmul(t1v[:, s * 8:(s + 1) * 8], S0bf[:, s8, :], wTn[i][:, s * 8:(s + 1) * 8], start=True, stop=True),
                             reads=B("S0bf_%d" % s8, "wTn%d" % i), writes=t1b, kind="bf")
                        S.op("pe", lambda e, s=s, s8=s8: e.matmul(t2v[:, s * 8:(s + 1) * 8], S0bf[:, s8, :], QT[:, h, s * 8:(s + 1) * 8], start=True, stop=True),
                             reads=B("S0bf_%d" % s8, "QT%d" % h), writes=t2b, kind="bf")
                S.op("act", lambda e: e.copy(tsb[0][:], t1v), reads=t1b, writes=B("tsb0"))
                S.op("act", lambda e: e.copy(tsb[1][:], t2v), reads=t2b, writes=B("tsb1"))
                S.op("pe", lambda e: e.matmul(vv, Xbf[i][:], VB[:, h, :], start=True, stop=False), reads=B("Xbf%d" % i, "VB"), writes=vb, kind="bf")
                S.op("pe", lambda e: e.matmul(vv, tsb[0][:], identb[:], start=False, stop=True), reads=B("tsb0", "identb"), writes=vb, kind="bf")
                S.op("pe", lambda e: e.matmul(qsv, tsb[1][:], identb[:], start=True, stop=True), reads=B("tsb1", "identb"), writes=qsb, kind="bf")
            yield
            S.op("act", lambda e: e.copy(vnew[i][:], vv), reads=vb, writes=B("vnew%d" % i))
            av, ab = PS(bA, 3)
            S.op("pe", lambda e: e.matmul(av, AqT[i][:], vnew[i][:], start=True, stop=True), reads=B("AqT%d" % i, "vnew%d" % i), writes=ab, kind="bf")
            if not sample:
                suv, sub = PS(6, i)
                S.op("pe", lambda e: e.matmul(suv, KD[:, h, :], vnew[i][:], start=True, stop=True), reads=B("KD", "vnew%d" % i), writes=sub, kind="bf")
            yield
            ofpv = Dl[i]
            avsv = Du[i]
            if have_state:
                S.op("act", lambda e: e.copy(avsv[:], av), reads=ab, writes=B("Du%d" % i))
                S.op("dve", lambda e: e.scalar_tensor_tensor(ofpv[:], qsv, eG[:, h:h + 1], avsv[:], op0=ALU.mult, op1=ALU.add),
                     reads=qsb + B("cum", "Du%d" % i), writes=B("Dl%d" % i))
            else:
                S.op("act", lambda e: e.copy(ofpv[:], av), reads=ab, writes=B("Dl%d" % i))
            if not sample:
                if have_state:
                    S.op("dve", lambda e: e.scalar_tensor_tensor(Sst[:, h, :], Sst[:, h, :], eGl[:, h:h + 1], suv, op0=ALU.mult, op1=ALU.add),
                         reads=sub + B("cum", "S%d" % h), writes=B("S%d" % h))
                else:
                    S.op("dve", lambda e: e.tensor_copy(Sst[:, h, :], suv), reads=sub, writes=B("S%d" % h))
                S.op("act", lambda e: e.copy(Sbf[:, h, :], Sst[:, h, :]), reads=B("S%d" % h), writes=B("S%d" % h))
                if t == NT - 2:
                    S.dma("sp", sp_o[h * 128:(h + 1) * 128, :], Sst[:, h, :], reads=B("S%d" % h))
            gn = "gatn%d" % h
            S.op("dve", lambda e: e.memset(gat[:, 36 + h:37 + h], 0.0), writes=B(gn))
            S.op("act", lambda e: e.activation(wTn[i][:], ofpv[:], AF.Square, accum_out=gat[:, 36 + h:37 + h]), reads=B("Dl%d" % i, gn), writes=B("wTn%d" % i, gn))
            S.op("act", lambda e: e.activation(gat[:, 42 + h:43 + h], gat[:, 36 + h:37 + h], AF.Ln, bias=epsc[:, 0:1], scale=1.0 / 128), reads=B(gn, "epsc"), writes=B(gn))
            S.op("act", lambda e: e.activation(gat[:, 42 + h:43 + h], gat[:, 42 + h:43 + h], AF.Exp, scale=-0.5), reads=B(gn), writes=B(gn))
            S.op("dve", lambda e: e.scalar_tensor_tensor(onb[i][:], ofpv[:], gat[:, 42 + h:43 + h], ogb[:], op0=ALU.mult, op1=ALU.mult),
                 reads=B("Dl%d" % i, gn, "ogb"), writes=B("onb%d" % i))
            otv, otb = PSB(3, i, 1)
            S.op("pe", lambda e: e.transpose(otv[:, 0:128], onb[i][:], identb[:]), reads=B("onb%d" % i, "identb"), writes=otb, kind="tr")
            yield
            S.op("dve", lambda e: e.tensor_tensor(sgT[:, h, :], otv[:, 0:128], sgT[:, h, :], op=ALU.mult), reads=otb + B(SG), writes=B(SG))
            if sample:
                for q4 in range(4):
                    j = q4 % 2
                    S.dma("sp", S0v[j], sg4[:, q4 * 4:(q4 + 1) * 4, h, :], writes=B(S0n[j]))
                    def mk_mask(k):
                        s = q4 * 4 + k
                        jj = s % 2
                        S.op("dve", lambda e: e.tensor_scalar(kdm[jj][:], KD[:, h, :], smask[:, s:s + 1], None, op0=ALU.mult),
                             reads=B("KD", "smask"), writes=B("kdm%d" % jj))
                        suv, sub = PS((bA, bB)[jj], 0)
                        S.op("pe", lambda e: e.matmul(suv, kdm[jj][:], vnew[i][:], start=True, stop=True), reads=B("kdm%d" % jj, "vnew%d" % i), writes=sub, kind="bf")
                        return suv, sub

                    def fin(k, suv, sub):
                        s = q4 * 4 + k
                        S.op("dve", lambda e: e.scalar_tensor_tensor(Snw4[j][:, k, :], S0v[j][:, k, :], eGls[:, s, h:h + 1], suv, op0=ALU.mult, op1=ALU.add),
                             reads=sub + B("cum", S0n[j]), writes=B("stg%d" % j))

                    pend = mk_mask(0)
                    for k in range(4):
                        nxt = mk_mask(k + 1) if k + 1 < 4 else None
                        fin(k, *pend)
                        pend = nxt
                    S.dma("sp", ss4[:, q4 * 4:(q4 + 1) * 4, h, :], Snw4[j], reads=B("stg%d" % j))
                    yield

        STAG = 0
        slots = [None, None, None]
        steps = [0, 0, 0]
        nxt_h = 0
        tick = 0
        while nxt_h < 6 or any(g is not None for g in slots):
            for i in range(3):
                if slots[i] is None and nxt_h < 6 and tick >= nxt_h * STAG:
                    slots[i] = head_gen(nxt_h, i)
                    nxt_h += 1
                if slots[i] is not None:
                    try:
                        next(slots[i])
                    except StopIteration:
                        slots[i] = None
            tick += 1

    def front1_a(t, pg):
        sample = (t == NT - 1)
        sgT, xqT = sgT2[pg], xqT2[pg]
        SG, XQ = "sgT%d" % pg, "xqT%d" % pg
        norm_transpose(t, NT + t, 8)
        yield
        for c in range(8):
            pv, pbf = proj_chunk(2304 + c * 128)
            S.op("act", lambda e, c=c, pv=pv: e.activation(sgT[:, c, :], pv, AF.Silu), reads=pbf, writes=B(SG))
            yield
        for c in range(2):
            pv, pbf = proj_chunk(2304 + 1024 + c * 128)
            S.op("act", lambda e, c=c, pv=pv: e.copy(xqT[:, c, :], pv), reads=pbf, writes=B(XQ))
        yield
        def l1A(c):
            i = c % 2
            pv, pbf = proj_chunk(768 + c * 128)
            S.op("act", lambda e: e.copy(sil[i][:], pv), reads=pbf, writes=B("sil%d" % i))
            pv2, pbf2 = proj_chunk(1536 + c * 128)
            conv_setup_hist(1, t, c, i, 2)
            cur, taps, last = xp_views(t, i, 2)
            S.op("dve", lambda e: e.tensor_tensor(cur, as3(pv2, t), as3(sil[i][:], t), op=ALU.mult), reads=pbf2 + B("sil%d" % i), writes=B("xp%d" % i))
            if not sample:
                S.op("pool", lambda e: e.tensor_copy(hist2[:, c, :], last), reads=B("xp%d" % i), writes=B("hist2"))
            acc = as3(cacc[i][:], t)
            S.op("act", lambda e: e.activation(acc, taps[2], AF.Identity, scale=cwb[:, c, 2:3]), reads=B("xp%d" % i, "cwb"), writes=B("cacc%d" % i))

        def l1B(c):
            i = c % 2
            cur, taps, last = xp_views(t, i, 2)
            acc = as3(cacc[i][:], t)
            for j in range(2):
                S.op("dve", lambda e, j=j: e.scalar_tensor_tensor(acc, taps[j], cwb[:, c, j:j + 1], acc, op0=ALU.mult, op1=ALU.add),
                     reads=B("xp%d" % i, "cwb", "cacc%d" % i), writes=B("cacc%d" % i))
            pv3, pbf3 = proj_chunk(c * 128)
            S.op("dve", lambda e: e.tensor_tensor(cacc[i][:], pv3, cacc[i][:], op=ALU.mult), reads=pbf3 + B("cacc%d" % i), writes=B("cacc%d" % i))
            S.op("dve", lambda e: e.tensor_tensor(sgT[:, c, :], cacc[i][:], sgT[:, c, :], op=ALU.mult), reads=B("cacc%d" % i, SG), writes=B(SG))

        l1A(0)
        for c in range(6):
            if c + 1 < 6:
                l1A(c + 1)
            l1B(c)
            yield
        if t >= NT - 2:
            for blk in range(2):
                c0 = blk * 384
                pv, pbf = PS(7, 0, 4)
                pw, pwb = PS(6, 0, 4)
                for kc in range(8):
                    S.op("pe", lambda e, kc=kc, pv=pv, c0=c0: e.matmul(pv[:, 0:384], hT[:, kc, :], win[:, kc, 768 + c0:768 + c0 + 384], start=(kc == 0), stop=(kc == 7)),
                         reads=B("hT", "win%d" % kc), writes=pbf)
                for kc in range(8):
                    S.op("pe", lambda e, kc=kc, pw=pw, c0=c0: e.matmul(pw[:, 0:384], hT[:, kc, :], win[:, kc, 1536 + c0:1536 + c0 + 384], start=(kc == 0), stop=(kc == 7)),
                         reads=B("hT", "win%d" % kc), writes=pwb)
                st = stg[blk % 2]
                S.op("act", lambda e, st=st, pv=pv: e.copy(st[:, 0:384], pv[:, 0:384]), reads=pbf, writes=B("stg%d" % (blk % 2)))
                S.op("dve", lambda e, st=st, pw=pw: e.tensor_tensor(st[:, 0:384], pw[:, 0:384], st[:, 0:384], op=ALU.mult), reads=pwb + B("stg%d" % (blk % 2)), writes=B("stg%d" % (blk % 2)))
                if not sample:
                    S.dma("sp", scp_o[0:2, c0:c0 + 384], st[126:128, 0:384], reads=B("stg%d" % (blk % 2)))
                else:
                    for s in range(16):
                        S.dma("sp", scs_o[s * 2:(s + 1) * 2, c0:c0 + 384], st[s * 8 + 6:s * 8 + 8, 0:384], reads=B("stg%d" % (blk % 2)))

    seq = [(0, t) for t in range(NT)] + [(1, t) for t in range(NT)]
    prev = None
    for k, (l, t) in enumerate(seq):
        pg = k % 2
        fa = front0_a(t, pg) if l == 0 else front1_a(t, pg)
        gens = [fa]
        if prev is not None:
            gens.insert(0, att_gen(prev[0], prev[1], prev[2]))
        run_interleaved(gens)
        if prev is not None:
            att_tail(prev[0], prev[1])
        if l == 1 and t == 0:
            load_wout(1)
            S.dma("sp", gfin, gfin_d[:, :], writes=B("KBG", "KD", "VB"))
        if l == 0:
            front0_b(t, pg)
        prev = (l, t, pg)
    run_interleaved([att_gen(prev[0], prev[1], prev[2])])
    att_tail(prev[0], prev[1])
    S.finish()
    es.close()
    return nc, S


_CACHE = {}
_STAGE = 99
_SUB = 99


def _consts():
    i = np.arange(128)
    same = (i[:, None] // 8) == (i[None, :] // 8)
    ident = np.eye(128)
    ones = np.ones((128, 128))
    U = (i[:, None] <= i[None, :])
    SL = (i[:, None] > i[None, :])
    ML = (i[:, None] > i[None, :])
    MU = (i[:, None] <= i[None, :])
    mats = [ident, ones, U, SL, U & same, SL & same]
    cst = np.stack([m.astype(np.float32) for m in mats], axis=1).reshape(128, 6 * 128)
    smask = ((i[:, None] // 8) == np.arange(16)[None, :]).astype(np.float32)
    return np.ascontiguousarray(cst), np.ascontiguousarray(smask)


def kernel(x_prompt, x_sample, mem_prompt, state_gdn, state_gdn_conv, state_sconv,
           cache_mem_k, cache_mem_v, norm_g, w_in_a, conv_w_a, a_log, dt_bias, o_norm_g,
           w_in_b, conv_w_b, mem_norm_g, w_mem_kv, w_out, final_norm_g):
    f = lambda a: np.ascontiguousarray(np.asarray(a, dtype=np.float32))
    if "nc" not in _CACHE:
        _CACHE["nc"] = build_program(_STAGE)[0]
    nc = _CACHE["nc"]
    cst, smask = _consts()
    x_prompt, x_sample, mem_prompt = f(x_prompt), f(x_sample), f(mem_prompt)
    state_gdn, state_gdn_conv, state_sconv = f(state_gdn), f(state_gdn_conv), f(state_sconv)
    cache_mem_k, cache_mem_v = f(cache_mem_k), f(cache_mem_v)
    gfm = np.concatenate([f(norm_g).reshape(2, 8, 128), f(mem_norm_g).reshape(1, 8, 128)], axis=0)
    gfm = np.ascontiguousarray(gfm.transpose(2, 0, 1).reshape(128, 24))
    cwa = np.ascontiguousarray(f(conv_w_a)[0].reshape(4, 18, 128).transpose(2, 1, 0).reshape(128, 72))
    cwb = np.ascontiguousarray(f(conv_w_b)[0].reshape(3, 6, 128).transpose(2, 1, 0).reshape(128, 18))
    small = np.ascontiguousarray(np.broadcast_to(np.concatenate([f(a_log)[0], f(dt_bias)[0]])[None, :], (128, 12)))
    ogb = np.ascontiguousarray(np.broadcast_to(f(o_norm_g)[0][None, :], (128, 128)))
    gfin = np.ascontiguousarray(np.broadcast_to(f(final_norm_g)[None, :], (128, 1024)))
    shared = {
        "wina": f(w_in_a)[0], "winb": f(w_in_b)[0], "wout": f(w_out).reshape(2048, 1024),
        "wkv": f(w_mem_kv).reshape(2048, 512), "gfm": gfm, "cwa": cwa, "cwb": cwb, "small": small,
        "ogb": ogb, "gfin": gfin, "cst": cst, "smask": smask,
    }
    in_maps = []
    for c in range(8):
        sl = slice(c * 16, (c + 1) * 16)
        m = dict(shared)
        m["x"] = np.ascontiguousarray(np.concatenate([x_prompt[c], x_sample[sl].reshape(128, 1024)], axis=0))
        m["mem"] = mem_prompt[c]
        m["sgdn"] = np.ascontiguousarray(state_gdn[0, sl].reshape(16 * 6 * 128, 128))
        m["sgc"] = np.ascontiguousarray(state_gdn_conv[0, sl].reshape(48, 2304))
        m["ssc"] = np.ascontiguousarray(state_sconv[0, sl].reshape(32, 768))
        m["cmk"] = np.ascontiguousarray(cache_mem_k[:, sl].reshape(2 * 16 * 256, 256))
        m["cmv"] = np.ascontiguousarray(cache_mem_v[:, sl].reshape(2 * 16 * 256, 256))
        in_maps.append(m)
    res = run_bass_kernel_spmd(nc, in_maps, core_ids=list(range(8)))
    R = res.results
    y = np.stack([r["y"] for r in R])
    y_prompt = np.ascontiguousarray(y[:, :2048])
    y_sample = np.ascontiguousarray(y[:, 2048:].reshape(128, 8, 1024))
    S_p = np.stack([r["sp_o"].reshape(6, 128, 128) for r in R])[None]
    gc_p = np.stack([r["gcp_o"] for r in R])[None]
    sc_p = np.stack([r["scp_o"] for r in R])[None]
    mk_p = np.stack([r["mkp_o"].reshape(2, 256, 4, 64) for r in R], axis=1)
    mv_p = np.stack([r["mvp_o"].reshape(2, 256, 4, 64) for r in R], axis=1)
    S_s = np.concatenate([r["ss_o"].reshape(16, 6, 128, 128) for r in R])[None]
    gc_s = np.concatenate([r["gcs_o"].reshape(16, 3, 2304) for r in R])[None]
    sc_s = np.concatenate([r["scs_o"].reshape(16, 2, 768) for r in R])[None]
    outs = (y_prompt, y_sample, S_p, gc_p, sc_p, mk_p, mv_p, S_s, gc_s, sc_s)
    return tuple(np.ascontiguousarray(o, dtype=np.float32) for o in outs)
```

```python
import numpy as np
from contextlib import ExitStack
import concourse.bass as bass
import concourse.mybir as mybir
from concourse.bass_utils import run_bass_kernel_spmd

F32 = mybir.dt.float32
BF16 = mybir.dt.bfloat16
AF = mybir.ActivationFunctionType
ALU = mybir.AluOpType
AX = mybir.AxisListType

NT = 17
EPS = 1e-6
CA = 3596
CB = 3584


class Buf:
    __slots__ = ("name", "last_w", "readers", "excl")

    def __init__(self, name, excl=False):
        self.name = name
        self.last_w = None
        self.readers = {}
        self.excl = excl


class Sched:
    def __init__(self, nc, es, n_dma_sems=(24, 4, 12)):
        self.nc = nc
        self.eng = {"pe": nc.tensor, "act": nc.scalar, "dve": nc.vector, "pool": nc.gpsimd, "sp": nc.sync}
        self.sem, self.cnt, self.seen = {}, {}, {}
        for k in self.eng:
            self.sem[k] = es.enter_context(nc.semaphore("s_" + k))
            self.cnt[k] = 0
            self.seen[k] = {}
        self.dma_sems, self.dma_rr = {}, {}
        for q, n in zip(("sp", "act", "pool"), n_dma_sems):
            self.dma_sems[q] = []
            self.dma_rr[q] = 0
            for i in range(n):
                key = "d%s%d" % (q, i)
                self.sem[key] = es.enter_context(nc.semaphore("s_" + key))
                self.cnt[key] = 0
                self.dma_sems[q].append(key)
        self.n_inst = 0
        self.n_wait = 0
        self.pe_last = None
        self.sep = None
        self.attach = True
        self.max_attach = 1
        self.snap = {}

    def _wait(self, e, deps, defer=False):
        best = {}
        for (k, c) in deps:
            if c > best.get(k, 0):
                best[k] = c
        cand = sorted(best.items(), key=lambda kc: -kc[1])
        need = []
        for k, c in cand:
            if self.seen[e].get(k, 0) >= c:
                continue
            need.append((k, c))
            self._learn(e, k, c)
        last = []
        while defer and need and len(last) < self.max_attach:
            last.append(need.pop())
        for k, c in need:
            self.eng[e].wait_ge(self.sem[k], c)
            self.n_wait += 1
        return last

    def _learn(self, e, k, c):
        se = self.seen[e]
        if se.get(k, 0) < c:
            se[k] = c
        sn = self.snap.get((k, c))
        if sn:
            for kk, cc in sn.items():
                if se.get(kk, 0) < cc:
                    se[kk] = cc

    @staticmethod
    def _deps(reads, writes):
        deps = []
        for b in reads:
            if b.last_w is not None:
                deps.append(b.last_w)
        for b in writes:
            if b.last_w is not None:
                deps.append(b.last_w)
            deps.extend(b.readers.items())
        return deps

    def op(self, e, fn, reads=(), writes=(), kind=None, inc=True):
        if e == "pe":
            if kind == "bf" and self.pe_last == "f32" and self.sep is not None:
                self.pe_last = "tr"
                self.sep()
            if kind is not None:
                self.pe_last = kind
        ex = [b for b in reads if b.excl]
        if ex:
            writes = list(writes) + [b for b in ex if b not in writes]
            reads = [b for b in reads if not b.excl]
        deps = self._deps(reads, writes)
        if e == "pe":
            deps = [d for d in deps if d[0] != "pe"]
        last = self._wait(e, deps, defer=self.attach)
        ins = fn(self.eng[e])
        for (lk, lc) in (last or []):
            ins._wait_ge(self.sem[lk], lc)
        if inc:
            ins.then_inc(self.sem[e], 1)
            self.cnt[e] += 1
            c = self.cnt[e]
            self.snap[(e, c)] = dict(self.seen[e])
        else:
            c = self.cnt[e] + 1
        for b in writes:
            b.last_w = (e, c)
            b.readers = {}
        for b in reads:
            if b not in writes:
                b.readers[e] = c
        self.n_inst += 1
        return ins

    def dma(self, q, out, in_, reads=(), writes=(), **kw):
        key = self.dma_sems[q][self.dma_rr[q] % len(self.dma_sems[q])]
        self.dma_rr[q] += 1
        deps = self._deps(reads, writes)
        if self.cnt[key] > 0:
            deps.append((key, self.cnt[key]))
        last = self._wait(q, deps, defer=self.attach)
        ins = self.eng[q].dma_start(out=out, in_=in_, **kw)
        for (lk, lc) in (last or []):
            ins._wait_ge(self.sem[lk], lc)
        ins.then_inc(self.sem[key], 16)
        self.cnt[key] += 16
        c = self.cnt[key]
        self.snap[(key, c)] = dict(self.seen[q])
        for b in writes:
            b.last_w = (key, c)
            b.readers = {}
        for b in reads:
            b.readers[key] = c
        self.n_inst += 1
        return ins

    def finish(self):
        deps = [(k, c) for k, c in self.cnt.items() if c > 0]
        for e in ("sp", "act", "pool", "dve", "pe"):
            self._wait(e, [d for d in deps if d[0] != e])


def build_program(stage=99):
    nc = bass.Bass("TRN2", target_bir_lowering=False)

    def din(name, shape):
        return nc.dram_tensor(name, list(shape), F32, kind="ExternalInput").ap()

    def dout(name, shape):
        return nc.dram_tensor(name, list(shape), F32, kind="ExternalOutput").ap()

    x_d = din("x", [NT * 128, 1024])
    mem_d = din("mem", [256, 1024])
    sgdn_d = din("sgdn", [16 * 6 * 128, 128])
    sgc_d = din("sgc", [48, 2304])
    ssc_d = din("ssc", [32, 768])
    cmk_d = din("cmk", [2 * 16 * 256, 256])
    cmv_d = din("cmv", [2 * 16 * 256, 256])
    wina_d = din("wina", [1024, CA])
    winb_d = din("winb", [1024, CB])
    wout_d = din("wout", [2048, 1024])
    wkv_d = din("wkv", [2048, 512])
    gfm_d = din("gfm", [128, 24])
    cwa_d = din("cwa", [128, 18 * 4])
    cwb_d = din("cwb", [128, 6 * 3])
    small_d = din("small", [128, 12])
    ogb_d = din("ogb", [128, 128])
    gfin_d = din("gfin", [128, 1024])
    cst_d = din("cst", [128, 6 * 128])
    smask_d = din("smask", [128, 16])

    y_o = dout("y", [NT * 128, 1024])
    sp_o = dout("sp_o", [6 * 128, 128])
    gcp_o = dout("gcp_o", [3, 2304])
    scp_o = dout("scp_o", [2, 768])
    mkp_o = dout("mkp_o", [512, 256])
    mvp_o = dout("mvp_o", [512, 256])
    ss_o = dout("ss_o", [16 * 6 * 128, 128])
    gcs_o = dout("gcs_o", [48, 2304])
    scs_o = dout("scs_o", [32, 768])

    es = ExitStack()
    S = Sched(nc, es)
    bufs = {}

    def sb(name, shape, dt=F32):
        t = es.enter_context(nc.sbuf_tensor("sb_" + name, list(shape), dt))
        bufs[name] = Buf(name)
        return t

    def B(*names):
        return [bufs[n] for n in names]

    def nb(name):
        bufs[name] = Buf(name)
        return bufs[name]

    xres = sb("xres", [128, NT, 1024])
    for t in range(NT):
        nb("x%d" % t)
    win = sb("win", [128, 8, CA], BF16)
    for k in range(8):
        nb("win%d" % k)
    wout = sb("wout", [128, 8, 1024], BF16)
    cst = sb("cst", [128, 6, 128])
    identb = sb("identb", [128, 128], BF16)
    onesb = sb("onesb", [128, 128], BF16)
    smask = sb("smask", [128, 16])
    gfm = sb("gfm", [128, 24])
    cwa = sb("cwa", [128, 18, 4])
    cwb = sb("cwb", [128, 6, 3])
    small = sb("small", [128, 12])
    nA = sb("nA", [128, 6])
    ogb = sb("ogb", [128, 128])
    epsc = sb("epsc", [128, 2])
    rstd = sb("rstd", [128, 3 * NT + 2])
    for i in range(3 * NT + 2):
        nb("rstd_%d" % i)
    ssq = sb("ssq", [128, 3 * NT + 2])
    mkT = sb("mkT", [128, 2, 2, 256], BF16)
    mvp = sb("mvp", [128, 2, 2, 256], BF16)
    hb = sb("hb", [128, 1024], BF16)
    hT = sb("hT", [128, 8, 128], BF16)
    xp = [sb("xp%d" % i, [128, 16 * 11]) for i in range(2)]
    hist = sb("hist", [128, 18, 3])
    hist2 = sb("hist2", [128, 6, 2])
    cacc = [sb("cacc%d" % i, [128, 128]) for i in range(2)]
    sil = [sb("sil%d" % i, [128, 128]) for i in range(2)]
    sqb = [sb("sqb%d" % i, [128, 3, 128], BF16) for i in range(2)]
    QKT = sb("QKT", [128, 12, 128], BF16)
    QT, KT = QKT[:, 0:6], QKT[:, 6:12]
    VT = sb("VT", [128, 6, 128], BF16)
    for h in range(6):
        nb("QT%d" % h), nb("KT%d" % h), nb("VT%d" % h)
    sgT2 = [sb("sgT%d" % i, [128, 8, 128], BF16) for i in range(2)]
    xqT2 = [sb("xqT%d" % i, [128, 2, 128], BF16) for i in range(2)]
    gat = sb("gat", [128, 48])
    cum = sb("cum", [128, 18 + 96])
    kkv3 = sb("kkv3", [128, 3, 6, 128], BF16)
    KBG, KD, VB = kkv3[:, 0], kkv3[:, 1], kkv3[:, 2]
    nb("KBG"), nb("KD"), nb("VB")
    for h6 in range(6):
        nb("gatn%d" % h6)
    gfin = kkv3[:].rearrange("p a b c -> p (a b c)")[:, 0:2048].bitcast(F32)
    Dl = [sb("Dl%d" % i, [128, 128]) for i in range(3)]
    Du = [sb("Du%d" % i, [128, 128]) for i in range(3)]
    AqT = [sb("AqT%d" % i, [128, 128], BF16) for i in range(3)]
    Rb = [[sb("R%d_%d" % (i, j), [128, 128]) for j in range(2)] for i in range(3)]
    Qb = [[sb("Q%d_%d" % (i, j), [128, 128]) for j in range(2)] for i in range(3)]
    Xb = [sb("X%d" % i, [128, 128]) for i in range(3)]
    Xbf = [sb("Xbf%d" % i, [128, 128], BF16) for i in range(3)]
    wTn = [sb("wTn%d" % i, [128, 128], BF16) for i in range(3)]
    vnew = [sb("vnew%d" % i, [128, 128], BF16) for i in range(3)]
    onb = [sb("onb%d" % i, [128, 128], BF16) for i in range(3)]
    Sst = sb("Sst", [128, 6, 128])
    Sbf = sb("Sbf", [128, 6, 128], BF16)
    for h in range(6):
        nb("S%d" % h)
    Pb = sb("Pb", [128, 4, 256], BF16)
    PTb = hb[:].rearrange("p (h m t) -> p h m t", h=4, m=2)
    sm = sb("sm", [128, 16])
    stg = [sb("stg%d" % i, [128, 512]) for i in range(2)]
    for k8 in range(8):
        nb("S0bf_%d" % k8)
    tsb = [sb("tsb%d" % i, [128, 128], BF16) for i in range(2)]
    kdm = [sb("kdm%d" % i, [128, 128], BF16) for i in range(2)]
    xqm = [sb("xqm%d" % i, [128, 2, 128], BF16) for i in range(2)]
    ckv = [sb("ckv%d" % i, [128, 2, 2, 256], BF16) for i in range(2)]
    S0bf = ckv[0][:].rearrange("p a b c -> p (a b c)").rearrange("p (s e) -> p s e", s=8)
    S0BN = ["S0bf_%d" % k8 for k8 in range(8)]
    mkTs = [sb("mkTs%d" % i, [128, 2, 256], BF16) for i in range(2)]
    hst = [sb("hst%d" % i, [48, 128]) for i in range(2)]

    S0v = [hb[:].bitcast(F32).rearrange("p (s e) -> p s e", s=4),
           Pb[:].rearrange("p a b -> p (a b)").bitcast(F32).rearrange("p (s e) -> p s e", s=4)]
    S0n = ["hb", "Pb"]
    Snw4 = [stg[0][:].rearrange("p (s e) -> p s e", s=4), stg[1][:].rearrange("p (s e) -> p s e", s=4)]
    stgKV = [stg[0][:].rearrange("p (m c) -> p m c", m=2), stg[1][:].rearrange("p (m c) -> p m c", m=2)]
    sg4 = sgdn_d.rearrange("(s h d) e -> d s h e", s=16, h=6)
    ss4 = ss_o.rearrange("(s h d) e -> d s h e", s=16, h=6)

    pbank = [es.enter_context(nc.psum_tensor("pb%d" % i, [128, 512], F32)) for i in range(8)]
    for i in range(8):
        bk = nb("pbank%d" % i)
        bk.excl = True
        for j in range(4):
            bufs["p%d_%d" % (i, j)] = bk

    def PS(bank, slot, n=1):
        return pbank[bank][:, slot * 128:(slot + n) * 128], B("p%d_0" % bank)

    def PSB(bank, slot, n=1):
        v = pbank[bank][:, slot * 128:(slot + n) * 128].bitcast(BF16)
        return v, B("p%d_0" % bank)

    I_F, ONES_F, U_, SL_, US_, SLS_ = range(6)
    ML_, MU_, MLS_, MUS_ = SL_, U_, SLS_, US_

    def C(i):
        return cst[:, i, :]

    S.dma("sp", cst[:].rearrange("p a b -> p (a b)"), cst_d[:, :], writes=B("cst"))
    S.dma("sp", smask[:], smask_d[:, :], writes=B("smask"))
    S.dma("sp", gfm[:], gfm_d[:, :], writes=B("gfm"))
    S.dma("sp", cwa[:].rearrange("p a b -> p (a b)"), cwa_d[:, :], writes=B("cwa"))
    S.dma("sp", cwb[:].rearrange("p a b -> p (a b)"), cwb_d[:, :], writes=B("cwb"))
    S.dma("sp", small[:], small_d[:, :], writes=B("small"))
    S.dma("sp", ogb[:], ogb_d[:, :], writes=B("ogb"))
    S.op("dve", lambda e: e.tensor_copy(identb[:], C(I_F)), reads=B("cst"), writes=B("identb"))
    S.op("dve", lambda e: e.tensor_copy(onesb[:], C(ONES_F)), reads=B("cst"), writes=B("onesb"))
    S.op("dve", lambda e: e.memset(epsc[:, 0:1], EPS), writes=B("epsc"))
    S.op("dve", lambda e: e.memset(epsc[:, 1:2], 128.0 * EPS), writes=B("epsc"))
    S.op("dve", lambda e: e.memset(hist[:], 0.0), writes=B("hist"))
    S.op("dve", lambda e: e.memset(hist2[:], 0.0), writes=B("hist2"))
    S.op("dve", lambda e: e.memset(ssq[:], 0.0), writes=B("ssq"))
    S.op("dve", lambda e: e.memset(sm[:], 0.0), writes=B("sm"))
    S.op("dve", lambda e: e.memset(gat[:], 0.0), writes=B("gat"))
    S.op("dve", lambda e: e.memset(rstd[:], 0.0), writes=B(*["rstd_%d" % i for i in range(3 * NT + 2)]))
    S.op("act", lambda e: e.activation(nA[:], small[:, 0:6], AF.Exp), reads=B("small"), writes=B("nA"))
    S.op("dve", lambda e: e.tensor_scalar(nA[:], nA[:], -1.0, None, op0=ALU.mult), reads=B("nA"), writes=B("nA"))

    for l in range(2):
        for kc in range(8):
            S.dma("pool", wout[:, kc, l * 512:(l + 1) * 512], wkv_d[l * 1024 + kc * 128:l * 1024 + (kc + 1) * 128, :],
                  writes=B("wout"))
    for kc in range(8):
        S.dma("pool", win[:, kc, 0:CA], wina_d[kc * 128:(kc + 1) * 128, :], writes=B("win%d" % kc))
    for mt in range(2):
        S.dma("sp", xres[:, mt, :], mem_d[mt * 128:(mt + 1) * 128, :], writes=B("x%d" % mt))

    def rstd_from_ssq(col, n_feat):
        S.op("act", lambda e: e.activation(rstd[:, col:col + 1], ssq[:, col:col + 1], AF.Ln, bias=epsc[:, 0:1], scale=1.0 / n_feat),
             reads=B("ssq", "epsc"), writes=B("rstd_%d" % col))
        S.op("act", lambda e: e.activation(rstd[:, col:col + 1], rstd[:, col:col + 1], AF.Exp, scale=-0.5),
             reads=B("rstd_%d" % col), writes=B("rstd_%d" % col))

    def ssq_of_tile(t, col):
        S.op("act", lambda e: e.activation(hb[:], xres[:, t, :], AF.Square, accum_out=ssq[:, col:col + 1]),
             reads=B("x%d" % t, "ssq"), writes=B("hb", "ssq"))

    def norm_transpose(t, rcol, gcol0):
        hbx = QKT[:].rearrange("p a b -> p (a b)")[:, 0:1024]
        HBN = B("QT0", "QT1", "QT2", "QT3", "QT4", "QT5", "KT0", "KT1")
        S.op("dve", lambda e: e.tensor_scalar(hbx, xres[:, t, :], rstd[:, rcol:rcol + 1], None, op0=ALU.mult),
             reads=B("x%d" % t, "rstd_%d" % rcol), writes=HBN)
        pv, pbf = PSB(7, 0, 4)
        pv3 = pv.rearrange("p (k t) -> p k t", k=8)
        for kc in range(8):
            S.op("pe", lambda e, kc=kc: e.transpose(pv3[:, kc, :], hbx[:, kc * 128:(kc + 1) * 128], identb[:]),
                 reads=HBN + B("identb"), writes=pbf, kind="tr")
        g3 = gfm[:, gcol0:gcol0 + 8].unsqueeze(2).to_broadcast([128, 8, 128])
        S.op("dve", lambda e: e.tensor_tensor(hT[:], pv3, g3, op=ALU.mult), reads=pbf + B("gfm"), writes=B("hT"))

    def pe_sep():
        sv, sbk = PSB(3, 3, 1)
        S.op("pe", lambda e: e.transpose(sv[:, 0:128], identb[:], identb[:]), reads=B("identb"), writes=sbk, kind="tr")

    S.sep = pe_sep
    def run_interleaved(gens):
        gens = list(gens)
        while gens:
            for g in list(gens):
                try:
                    next(g)
                except StopIteration:
                    gens.remove(g)

    proj_rr = [0]

    def proj_chunk(col0, ncols=128):
        bank = (1, 7)[proj_rr[0] % 2]
        proj_rr[0] += 1
        pv, pbf = PS(bank, 0)
        for kc in range(8):
            S.op("pe", lambda e, kc=kc: e.matmul(pv[0:ncols, :], win[:, kc, col0:col0 + ncols], hT[:, kc, :], start=(kc == 0), stop=(kc == 7)),
                 reads=B("win%d" % kc, "hT"), writes=pbf, kind="bf", inc=(kc == 7))
        return pv, pbf

    for mt in range(2):
        ssq_of_tile(mt, 3 * NT + mt)
        rstd_from_ssq(3 * NT + mt, 1024)
    memT = kkv3[:].rearrange("p a b c -> p (a b c)")[:, 0:2048].rearrange("p (k m) -> p k m", k=8)
    MEMB = B("KBG", "KD", "VB")
    for mt in range(2):
        norm_transpose(mt, 3 * NT + mt, 16)
        S.op("act", lambda e, mt=mt: e.copy(memT[:, :, mt * 128:(mt + 1) * 128], hT[:]), reads=B("hT"), writes=MEMB)
    for l in range(2):
        for mt in range(2):
            pv, pbf = PS(2, 0, 4)
            for kc in range(8):
                S.op("pe", lambda e, kc=kc: e.matmul(pv, memT[:, kc, mt * 128:(mt + 1) * 128], wout[:, kc, l * 512:(l + 1) * 512], start=(kc == 0), stop=(kc == 7)),
                     reads=MEMB + B("wout"), writes=pbf)
            st = stg[(l * 2 + mt) % 2]
            sn = "stg%d" % ((l * 2 + mt) % 2)
            S.op("act", lambda e, st=st: e.copy(st[:], pv), reads=pbf, writes=B(sn))
            S.dma("sp", mkp_o[l * 256 + mt * 128:l * 256 + (mt + 1) * 128, :], st[:, 0:256], reads=B(sn))
            S.dma("sp", mvp_o[l * 256 + mt * 128:l * 256 + (mt + 1) * 128, :], st[:, 256:512], reads=B(sn))
            S.op("dve", lambda e, st=st: e.tensor_copy(mvp[:, l, mt, :], st[:, 256:512]), reads=B(sn), writes=B("mvp"))
        for c in range(2):
            pv, pbf = PS(3, 0, 2)
            for kc in range(8):
                S.op("pe", lambda e, kc=kc: e.matmul(pv, wout[:, kc, l * 512 + c * 128:l * 512 + (c + 1) * 128], memT[:, kc, :], start=(kc == 0), stop=(kc == 7)),
                     reads=MEMB + B("wout"), writes=pbf)
            S.op("act", lambda e, c=c: e.copy(mkT[:, l, c, :], pv), reads=pbf, writes=B("mkT"))

    for t in range(NT):
        S.dma("sp", xres[:, t, :], x_d[t * 128:(t + 1) * 128, :], writes=B("x%d" % t))

    def load_win(l):
        wd = wina_d if l == 0 else winb_d
        ncol = CA if l == 0 else CB
        for kc in range(8):
            S.dma("pool", win[:, kc, 0:ncol], wd[kc * 128:(kc + 1) * 128, :], writes=B("win%d" % kc))

    def load_wout(l):
        for kc in range(8):
            S.dma("pool", wout[:, kc, :], wout_d[l * 1024 + kc * 128:l * 1024 + (kc + 1) * 128, :], writes=B("wout"))

    def load_layer_weights(l):
        load_win(l)
        load_wout(l)

    load_wout(0)
    for t in range(NT):
        ssq_of_tile(t, t)
        rstd_from_ssq(t, 1024)

    def att_gen(l, t, pg, pre_tail=None):
        sample = (t == NT - 1)
        sgT, xqT = sgT2[pg], xqT2[pg]
        SG, XQ = "sgT%d" % pg, "xqT%d" % pg
        scv, scb = [], []
        for bk in (3, 4, 5, 6):
            v_, b_ = PS(bk, 0, 2)
            scv.append(v_)
            scb.append(b_)
        if not sample:
            for h in range(4):
                po = (h % 2) * 64
                if _SUB == 700 and h != 0:
                    continue
                if _SUB == 701 and h != 1:
                    continue
                S.op("pe", lambda e, h=h, po=po: e.matmul(scv[h], xqT[po:po + 64, h // 2, :], mkT[po:po + 64, l, h // 2, :], start=True, stop=True),
                     reads=B(XQ, "mkT"), writes=scb[h])
        else:
            for s in range(16):
                i = s % 2
                r0 = (l * 16 + s) * 256
                S.dma("sp", stgKV[i], cmk_d[r0:r0 + 256, :].rearrange("(mt p) c -> p mt c", p=128), writes=B("stg%d" % i))
                S.op("act", lambda e, i=i: e.copy(ckv[i][:, 0, :, :], stgKV[i]), reads=B("stg%d" % i), writes=B("ckv%d" % i) + (B(*S0BN) if i == 0 else []))
                tv, tb = PSB(0, 0, 2)
                tv3 = tv.rearrange("p (c m) -> p c m", c=2)
                for c in range(2):
                    for mt in range(2):
                        S.op("pe", lambda e, c=c, mt=mt, i=i: e.transpose(tv3[:, c, mt * 128:(mt + 1) * 128], ckv[i][:, 0, mt, c * 128:(c + 1) * 128], identb[:]),
                             reads=B("ckv%d" % i, "identb"), writes=tb)
                S.op("act", lambda e, i=i: e.copy(mkTs[i][:], tv3), reads=tb, writes=B("mkTs%d" % i))
                S.op("pool", lambda e, i=i: e.memset(xqm[i][:].rearrange("p a b -> p (a b)"), 0.0), writes=B("xqm%d" % i))
                S.op("pool", lambda e, i=i, s=s: e.tensor_copy(xqm[i][:, :, s * 8:(s + 1) * 8], xqT[:, :, s * 8:(s + 1) * 8]),
                     reads=B(XQ), writes=B("xqm%d" % i))
                for h in range(4):
                    po = (h % 2) * 64
                    S.op("pe", lambda e, h=h, po=po, i=i: e.matmul(scv[h], xqm[i][po:po + 64, h // 2, :], mkTs[i][po:po + 64, h // 2, :], start=(s == 0), stop=(s == 15)),
                         reads=B("xqm%d" % i, "mkTs%d" % i), writes=scb[h])
                yield
        yield
        ovs = [PS(0, 0, 4), PS(2, 0, 4)]
        for half in range(2):
            for c in range(6):
                S.op("pe", lambda e, c=c, half=half: e.matmul(ovs[half][0], sgT[:, c, :], wout[:, c, half * 512:(half + 1) * 512], start=(c == 0), stop=False),
                     reads=B(SG, "wout"), writes=ovs[half][1], kind="bf", inc=(c == 5))
        yield
        for h in range(4):
            S.op("dve", lambda e, h=h: e.tensor_reduce(sm[:, h:h + 1], scv[h], axis=AX.X, op=ALU.max), reads=scb[h], writes=B("sm"))
        S.op("dve", lambda e: e.tensor_scalar(sm[:, 4:8], sm[:, 0:4], -0.125, None, op0=ALU.mult), reads=B("sm"), writes=B("sm"))
        S.op("dve", lambda e: e.memset(sm[:, 8:12], 0.0), writes=B("sm"))
        yield
        if pre_tail is not None:
            att_tail(*pre_tail)
        for h in range(4):
            S.op("act", lambda e, h=h: e.activation(Pb[:, h, :], scv[h], AF.Exp, bias=sm[:, 4 + h:5 + h], scale=0.125, accum_out=sm[:, 8 + h:9 + h]),
                 reads=scb[h] + B("sm"), writes=B("Pb", "sm"))
        yield
        S.op("dve", lambda e: e.reciprocal(sm[:, 12:16], sm[:, 8:12]), reads=B("sm"), writes=B("sm"))
        S.op("dve", lambda e: e.tensor_tensor(Pb[:], Pb[:], sm[:, 12:16].unsqueeze(2).to_broadcast([128, 4, 256]), op=ALU.mult),
             reads=B("Pb", "sm"), writes=B("Pb"))
        yield
        if _SUB == 71:
            return
        tv, tb = PSB(5, 0, 4)
        tv4 = tv.rearrange("p (h m t) -> p h m t", h=4, m=2)
        for h in range(4):
            for mt in range(2):
                S.op("pe", lambda e, h=h, mt=mt: e.transpose(tv4[:, h, mt, :], Pb[:, h, mt * 128:(mt + 1) * 128], identb[:]),
                     reads=B("Pb", "identb"), writes=tb)
        yield
        S.op("act", lambda e: e.copy(PTb, tv4), reads=tb, writes=B("hb"))
        yield
        if _SUB == 72:
            return
        xo_v, xo_b = PS(6, 0, 4)
        xo4 = xo_v.rearrange("p (c k t) -> p c k t", c=2, k=2)
        if not sample:
            for c in range(2):
                for hh in range(2):
                    h = 2 * c + hh
                    for mt in range(2):
                        S.op("pe", lambda e, h=h, mt=mt, c=c, hh=hh: e.matmul(xo4[:, c, hh, :], mvp[:, l, mt, c * 128:(c + 1) * 128], PTb[:, h, mt, :], start=(mt == 0), stop=(mt == 1)),
                             reads=B("mvp", "hb"), writes=xo_b)
        else:
            for s in range(16):
                i = s % 2
                r0 = (l * 16 + s) * 256
                S.dma("sp", stgKV[i], cmv_d[r0:r0 + 256, :].rearrange("(mt p) c -> p mt c", p=128), writes=B("stg%d" % i))
                S.op("act", lambda e, i=i: e.copy(ckv[i][:, 1, :, :], stgKV[i]), reads=B("stg%d" % i), writes=B("ckv%d" % i) + (B(*S0BN) if i == 0 else []))
                for c in range(2):
                    for hh in range(2):
                        h = 2 * c + hh
                        for mt in range(2):
                            S.op("pe", lambda e, h=h, mt=mt, c=c, hh=hh, i=i, s=s: e.matmul(xo4[:, c, hh, s * 8:(s + 1) * 8], ckv[i][:, 1, mt, c * 128:(c + 1) * 128], PTb[:, h, mt, s * 8:(s + 1) * 8], start=(mt == 0), stop=(mt == 1)),
                                 reads=B("ckv%d" % i, "hb"), writes=xo_b)
                yield
        yield
        for c in range(2):
            for hh in range(2):
                po = hh * 64
                S.op("dve", lambda e, c=c, hh=hh, po=po: e.tensor_tensor(sgT[po:po + 64, 6 + c, :], xo4[po:po + 64, c, hh, :], sgT[po:po + 64, 6 + c, :], op=ALU.mult),
                     reads=xo_b + B(SG), writes=B(SG))
        yield
        for half in range(2):
            for c in (6, 7):
                S.op("pe", lambda e, c=c, half=half: e.matmul(ovs[half][0], sgT[:, c, :], wout[:, c, half * 512:(half + 1) * 512], start=False, stop=(c == 7)),
                     reads=B(SG, "wout"), writes=ovs[half][1], kind="bf", inc=(c == 7))
            S.op("dve", lambda e, half=half: e.tensor_tensor(xres[:, t, half * 512:(half + 1) * 512], xres[:, t, half * 512:(half + 1) * 512], ovs[half][0], op=ALU.add),
                 reads=ovs[half][1] + B("x%d" % t), writes=B("x%d" % t))
    def att_tail(l, t):
        col = (l + 1) * NT + t
        ssq_of_tile(t, col)
        rstd_from_ssq(col, 1024)
        if l == 1:
            st = stg[t % 2]
            for half in range(2):
                S.op("dve", lambda e, half=half, st=st: e.scalar_tensor_tensor(st[:], xres[:, t, half * 512:(half + 1) * 512], rstd[:, col:col + 1], gfin[:, half * 512:(half + 1) * 512], op0=ALU.mult, op1=ALU.mult),
                     reads=B("x%d" % t, "rstd_%d" % col, "KBG", "KD", "VB"), writes=B("stg%d" % (t % 2)))
                S.dma("sp", y_o[t * 128:(t + 1) * 128, half * 512:(half + 1) * 512], st[:], reads=B("stg%d" % (t % 2)))

    def conv_setup_hist(l, t, c, i, nh):
        sample = (t == NT - 1)
        hbuf = hist if l == 0 else hist2
        hname = "hist" if l == 0 else "hist2"
        if not sample:
            S.op("pool", lambda e: e.tensor_copy(xp[i][:, 0:nh], hbuf[:, c, :]), reads=B(hname), writes=B("xp%d" % i))
        else:
            srcd = sgc_d if l == 0 else ssc_d
            nr = 16 * nh
            S.dma("sp", hst[i][0:nr, :], srcd[0:nr, c * 128:(c + 1) * 128], writes=B("hst%d" % i))
            pv, pbf = PS(2, 2, 1)
            S.op("pe", lambda e: e.transpose(pv[:, 0:nr], hst[i][0:nr, :], C(I_F)[0:nr, 0:nr]),
                 reads=B("hst%d" % i, "cst"), writes=pbf)
            L = 8
            xp3 = xp[i][:, 0:16 * (nh + L)].rearrange("p (s k) -> p s k", s=16)
            S.op("act", lambda e: e.copy(xp3[:, :, 0:nh], pv[:, 0:nr].rearrange("p (s j) -> p s j", s=16)), reads=pbf, writes=B("xp%d" % i))

    def xp_views(t, i, nh):
        sample = (t == NT - 1)
        if not sample:
            cur = xp[i][:, nh:nh + 128]
            taps = [xp[i][:, j:j + 128] for j in range(nh + 1)]
            last = xp[i][:, 128:128 + nh]
            return cur, taps, last
        L = 8
        xp3 = xp[i][:, 0:16 * (nh + L)].rearrange("p (s k) -> p s k", s=16)
        cur = xp3[:, :, nh:nh + L]
        taps = [xp3[:, :, j:j + L] for j in range(nh + 1)]
        return cur, taps, None

    def as3(ap, t):
        if t == NT - 1:
            return ap.rearrange("p (s k) -> p s k", s=16)
        return ap

    def front0_a(t, pg):
        sample = (t == NT - 1)
        sgT, xqT = sgT2[pg], xqT2[pg]
        SG, XQ = "sgT%d" % pg, "xqT%d" % pg
        norm_transpose(t, t, 0)
        yield
        for c in range(8):
            pv, pbf = proj_chunk(2316 + c * 128)
            S.op("act", lambda e, c=c, pv=pv: e.activation(sgT[:, c, :], pv, AF.Silu), reads=pbf, writes=B(SG))
            yield
        for c in range(2):
            pv, pbf = proj_chunk(2316 + 1024 + c * 128)
            S.op("act", lambda e, c=c, pv=pv: e.copy(xqT[:, c, :], pv), reads=pbf, writes=B(XQ))
        yield
        def stA(c):
            i = c % 2
            pv, pbf = proj_chunk(c * 128)
            conv_setup_hist(0, t, c, i, 3)
            cur, taps, last = xp_views(t, i, 3)
            S.op("act", lambda e: e.copy(cur, as3(pv, t)), reads=pbf, writes=B("xp%d" % i))
            if not sample:
                S.op("pool", lambda e: e.tensor_copy(hist[:, c, :], last), reads=B("xp%d" % i), writes=B("hist"))
            acc = as3(cacc[i][:], t)
            S.op("act", lambda e: e.activation(acc, taps[3], AF.Identity, scale=cwa[:, c, 3:4]),
                 reads=B("xp%d" % i, "cwa"), writes=B("cacc%d" % i))

        def stB(c):
            i = c % 2
            cur, taps, last = xp_views(t, i, 3)
            acc = as3(cacc[i][:], t)
            for j in range(3):
                S.op("dve", lambda e, j=j: e.scalar_tensor_tensor(acc, taps[j], cwa[:, c, j:j + 1], acc, op0=ALU.mult, op1=ALU.add),
                     reads=B("xp%d" % i, "cwa", "cacc%d" % i), writes=B("cacc%d" % i))

        def stC(c):
            i = c % 2
            if c >= 12:
                dst, h, dn = VT, c - 12, "VT%d" % (c - 12)
            elif c < 6:
                dst, h, dn = QT, c, "QT%d" % c
            else:
                dst, h, dn = KT, c - 6, "KT%d" % (c - 6)
            S.op("act", lambda e: e.activation(dst[:, h, :], cacc[i][:], AF.Silu), reads=B("cacc%d" % i), writes=B(dn))

        stA(0)
        stB(0)
        for c in range(18):
            if c + 1 < 18:
                stA(c + 1)
            stC(c)
            if c + 1 < 18:
                stB(c + 1)
            yield

    def front0_b(t, pg):
        sample = (t == NT - 1)
        sgT, xqT = sgT2[pg], xqT2[pg]
        SG, XQ = "sgT%d" % pg, "xqT%d" % pg
        gcol = gat[:, 24:30]
        Um, SLm = (US_, SLS_) if sample else (U_, SL_)
        eG, eGrev, eGl = cum[:, 0:6], cum[:, 6:12], cum[:, 12:18]
        eGls = cum[:, 18:114].rearrange("p (s h) -> p s h", s=16)

        def gen_norm():
            for g in range(4):
                i = g % 2
                isq = (g < 2)
                c0 = g * 3
                names = [("QT%d" % (c0 + k)) if isq else ("KT%d" % (c0 - 6 + k)) for k in range(3)]
                blk = QKT[:, c0:c0 + 3, :]
                S.op("act", lambda e: e.activation(sqb[i][:], blk, AF.Square), reads=B(*names), writes=B("sqb%d" % i))
                qv, qb = PS((2, 0)[i], 0, 3)
                S.op("pe", lambda e: e.matmul(qv, onesb[:], sqb[i][:].rearrange("p a b -> p (a b)"), start=True, stop=True), reads=B("onesb", "sqb%d" % i), writes=qb, kind="bf")
                yield
                S.op("act", lambda e: e.activation(qv, qv, AF.Ln, bias=epsc[:, 0:1], scale=1.0), reads=qb + B("epsc"), writes=qb)
                S.op("act", lambda e: e.activation(qv, qv, AF.Exp, scale=-0.5), reads=qb, writes=qb)
                qv3 = qv.rearrange("p (a b) -> p a b", a=3)
                if isq:
                    S.op("dve", lambda e: e.scalar_tensor_tensor(blk, blk, float(128.0 ** -0.5), qv3, op0=ALU.mult, op1=ALU.mult), reads=qb + B(*names), writes=B(*names))
                else:
                    S.op("dve", lambda e: e.tensor_tensor(blk, blk, qv3, op=ALU.mult), reads=qb + B(*names), writes=B(*names))
                yield

        def gen_gates():
            bav, bab = PS(3, 0)
            for kc in range(8):
                S.op("pe", lambda e, kc=kc: e.matmul(bav[:, 0:12], hT[:, kc, :], win[:, kc, 2304:2316], start=(kc == 0), stop=(kc == 7)),
                     reads=B("hT", "win%d" % kc), writes=bab, kind="bf")
            yield
            S.op("act", lambda e: e.activation(gat[:, 0:6], bav[:, 0:6], AF.Exp, scale=-1.0), reads=bab, writes=B("gat"))
            S.op("dve", lambda e: e.tensor_tensor(gat[:, 6:12], bav[:, 6:12], small[:, 6:12], op=ALU.add), reads=bab + B("small", "gat"), writes=B("gat"))
            S.op("act", lambda e: e.activation(gat[:, 6:12], gat[:, 6:12], AF.Exp), reads=B("gat"), writes=B("gat"))
            S.op("act", lambda e: e.activation(gat[:, 6:12], gat[:, 6:12], AF.Ln, bias=1.0), reads=B("gat"), writes=B("gat"))
            yield
            S.op("dve", lambda e: e.tensor_scalar(gat[:, 0:6], gat[:, 0:6], 1.0, None, op0=ALU.add), reads=B("gat"), writes=B("gat"))
            S.op("dve", lambda e: e.reciprocal(gat[:, 12:18], gat[:, 0:6]), reads=B("gat"), writes=B("gat"))
            S.op("dve", lambda e: e.tensor_scalar(gat[:, 18:24], gat[:, 12:18], -1.0, None, op0=ALU.mult), reads=B("gat"), writes=B("gat"))
            S.op("dve", lambda e: e.tensor_tensor(gat[:, 24:30], gat[:, 6:12], nA[:], op=ALU.mult), reads=B("gat", "nA"), writes=B("gat"))
            cv, cb = PS(3, 1)
            S.op("pe", lambda e: e.matmul(cv[:, 0:6], C(Um), gcol, start=True, stop=True), reads=B("cst", "gat"), writes=cb, kind="f32")
            S.op("pe", lambda e: e.matmul(cv[:, 6:12], C(SLm), gcol, start=True, stop=True), reads=B("cst", "gat"), writes=cb, kind="f32")
            if not sample:
                S.op("pe", lambda e: e.matmul(cv[:, 12:18], C(ONES_F), gcol, start=True, stop=True), reads=B("cst", "gat"), writes=cb, kind="f32")
                yield
                S.op("act", lambda e: e.activation(cum[:, 0:18], cv[:, 0:18], AF.Exp), reads=cb, writes=B("cum"))
            else:
                gm = cum[:, 18:114].rearrange("p (s h) -> p s h", s=16)
                S.op("dve", lambda e: e.tensor_tensor(gm, gcol.unsqueeze(1).to_broadcast([128, 16, 6]), smask[:].unsqueeze(2).to_broadcast([128, 16, 6]), op=ALU.mult),
                     reads=B("gat", "smask", "cum"), writes=B("cum"))
                cv2, cb2 = PS(3, 2)
                S.op("pe", lambda e: e.matmul(cv2[:, 0:96], C(ONES_F), cum[:, 18:114], start=True, stop=True), reads=B("cst", "cum"), writes=cb2, kind="f32")
                yield
                S.op("act", lambda e: e.activation(cum[:, 0:12], cv[:, 0:12], AF.Exp), reads=cb, writes=B("cum"))
                S.op("act", lambda e: e.activation(cum[:, 18:114], cv2[:, 0:96], AF.Exp), reads=cb2, writes=B("cum"))
            S.op("dve", lambda e: e.tensor_tensor(gat[:, 30:36], gat[:, 12:18], eG, op=ALU.mult), reads=B("gat", "cum"), writes=B("gat"))
            tv_, tb_ = PSB(7, 0, 3)
            tvv = tv_[:, 0:768].rearrange("p (h d) -> p h d", h=6)
            for h in range(6):
                S.op("pe", lambda e, h=h: e.transpose(tvv[:, h, :], VT[:, h, :], identb[:]), reads=B("VT%d" % h, "identb"), writes=tb_, kind="tr")
            yield
            S.op("dve", lambda e: e.tensor_tensor(VB, tvv, gat[:, 12:18].unsqueeze(2).to_broadcast([128, 6, 128]), op=ALU.mult), reads=tb_ + B("gat"), writes=B("VB"))

        run_interleaved([gen_norm(), gen_gates()])
        if _SUB == 2:
            return
        tv, tb = PSB(0, 0, 3)
        tv3 = tv.rearrange("p (h d) -> p h d", h=6)
        for h in range(6):
            S.op("pe", lambda e, h=h: e.transpose(tv3[:, h, :], KT[:, h, :], identb[:]), reads=B("KT%d" % h, "identb"), writes=tb)
        S.op("dve", lambda e: e.tensor_tensor(KBG, tv3, gat[:, 30:36].unsqueeze(2).to_broadcast([128, 6, 128]), op=ALU.mult), reads=tb + B("gat"), writes=B("KBG"))
        S.op("dve", lambda e: e.tensor_tensor(KD, tv3, eGrev.unsqueeze(2).to_broadcast([128, 6, 128]), op=ALU.mult), reads=tb + B("cum"), writes=B("KD"))
        if _SUB == 3:
            return
        if t >= NT - 2:
            for blk in range(5):
                c0 = blk * 512
                w_ = min(512, 2304 - c0)
                pv, pbf = PS(7, 0, 4)
                for kc in range(8):
                    S.op("pe", lambda e, kc=kc, pv=pv, c0=c0, w_=w_: e.matmul(pv[:, 0:w_], hT[:, kc, :], win[:, kc, c0:c0 + w_], start=(kc == 0), stop=(kc == 7)),
                         reads=B("hT", "win%d" % kc), writes=pbf)
                st = stg[blk % 2]
                S.op("act", lambda e, st=st, pv=pv, w_=w_: e.copy(st[:, 0:w_], pv[:, 0:w_]), reads=pbf, writes=B("stg%d" % (blk % 2)))
                if not sample:
                    S.dma("sp", gcp_o[0:3, c0:c0 + w_], st[125:128, 0:w_], reads=B("stg%d" % (blk % 2)))
                else:
                    for s in range(16):
                        S.dma("sp", gcs_o[s * 3:(s + 1) * 3, c0:c0 + w_], st[s * 8 + 5:s * 8 + 8, 0:w_], reads=B("stg%d" % (blk % 2)))
        if t == NT - 1:
            load_win(1)
        MLm, MUm = (MLS_, MUS_) if sample else (ML_, MU_)
        nlev = 3 if sample else 7
        have_state = sample or t > 0
        HB = ((4, 5), (0, 1), (7, 2))

        def head_gen(h, i):
            bA, bB = HB[i]
            dlv, dlb = PS(bA, 0)
            duv, dub = PS(bA, 1)
            kkv, kkb = PS(bB, 0)
            kqv, kqb = PS(bB, 1)
            UGv, SLGv = Dl[i], Du[i]
            S.op("dve", lambda e: e.tensor_scalar(UGv[:], C(Um), gat[:, 24 + h:25 + h], None, op0=ALU.mult), reads=B("cst", "gat"), writes=B("Dl%d" % i))
            S.op("dve", lambda e: e.tensor_scalar(SLGv[:], C(SLm), gat[:, 24 + h:25 + h], None, op0=ALU.mult), reads=B("cst", "gat"), writes=B("Du%d" % i))
            S.op("pe", lambda e: e.matmul(dlv, UGv[:], C(SLm), start=True, stop=True), reads=B("Dl%d" % i, "cst"), writes=dlb, kind="f32")
            S.op("pe", lambda e: e.matmul(duv, SLGv[:], C(Um), start=True, stop=True), reads=B("Du%d" % i, "cst"), writes=dub, kind="f32")
            S.op("pe", lambda e: e.matmul(kkv, KT[:, h, :], KT[:, h, :], start=True, stop=True), reads=B("KT%d" % h), writes=kkb, kind="bf")
            S.op("pe", lambda e: e.matmul(kqv, KT[:, h, :], QT[:, h, :], start=True, stop=True), reads=B("KT%d" % h, "QT%d" % h), writes=kqb, kind="bf")
            yield
            S.op("act", lambda e: e.activation(Dl[i][:], dlv, AF.Exp), reads=dlb, writes=B("Dl%d" % i))
            S.op("act", lambda e: e.activation(Du[i][:], duv, AF.Exp), reads=dub, writes=B("Du%d" % i))
            S.op("dve", lambda e: e.tensor_tensor(Dl[i][:], Dl[i][:], C(MLm), op=ALU.mult), reads=B("Dl%d" % i, "cst"), writes=B("Dl%d" % i))
            S.op("dve", lambda e: e.tensor_tensor(Du[i][:], Du[i][:], C(MUm), op=ALU.mult), reads=B("Du%d" % i, "cst"), writes=B("Du%d" % i))
            R0 = Rb[i][0]
            S.op("dve", lambda e: e.scalar_tensor_tensor(R0[:], kkv, gat[:, 18 + h:19 + h], Dl[i][:], op0=ALU.mult, op1=ALU.mult),
                 reads=kkb + B("gat", "Dl%d" % i), writes=B("R%d_0" % i))
            S.op("dve", lambda e: e.tensor_tensor(AqT[i][:], kqv, Du[i][:], op=ALU.mult), reads=kqb + B("Du%d" % i), writes=B("AqT%d" % i))
            yield
            ntv, ntb = PS(bA, 2)
            S.op("pe", lambda e: e.transpose(ntv, R0[:], C(I_F)), reads=B("R%d_0" % i, "cst"), writes=ntb, kind="tr")
            yield
            S.op("act", lambda e: e.copy(Qb[i][0][:], ntv), reads=ntb, writes=B("Q%d_0" % i))
            S.op("dve", lambda e: e.tensor_tensor(Xb[i][:], ntv, C(I_F), op=ALU.add), reads=ntb + B("cst"), writes=B("X%d" % i))
            for k in range(1, nlev):
                a, b_ = (k - 1) % 2, k % 2
                rv, rb_ = PS(bA, 0)
                qv_, qb_ = PS(bA, 1)
                xv, xb_ = PS(bB, 0)
                S.op("pe", lambda e: e.matmul(rv, Qb[i][a][:], Rb[i][a][:], start=True, stop=True),
                     reads=B("Q%d_%d" % (i, a), "R%d_%d" % (i, a)), writes=rb_, kind="f32")
                if k < nlev - 1:
                    S.op("pe", lambda e: e.matmul(qv_, Rb[i][a][:], Qb[i][a][:], start=True, stop=True),
                         reads=B("Q%d_%d" % (i, a), "R%d_%d" % (i, a)), writes=qb_, kind="f32")
                yield
                S.op("act", lambda e: e.copy(Rb[i][b_][:], rv), reads=rb_, writes=B("R%d_%d" % (i, b_)))
                if k < nlev - 1:
                    S.op("act", lambda e: e.copy(Qb[i][b_][:], qv_), reads=qb_, writes=B("Q%d_%d" % (i, b_)))
                lo = 0 if sample else (1 << k)
                S.op("pe", lambda e: e.matmul(xv[:, lo:128], Rb[i][b_][:], Xb[i][:, lo:128], start=True, stop=True),
                     reads=B("R%d_%d" % (i, b_), "X%d" % i), writes=xb_, kind="f32")
                yield
                S.op("dve", lambda e: e.tensor_tensor(Xb[i][:, lo:128], xv[:, lo:128], Xb[i][:, lo:128], op=ALU.add), reads=xb_ + B("X%d" % i), writes=B("X%d" % i))
            S.op("act", lambda e: e.copy(Xbf[i][:], Xb[i][:]), reads=B("X%d" % i), writes=B("Xbf%d" % i))
            wv, wb = PS(bA, 2)
            S.op("pe", lambda e: e.matmul(wv, KBG[:, h, :], Xbf[i][:], start=True, stop=True), reads=B("KBG", "Xbf%d" % i), writes=wb, kind="bf")
            yield
            S.op("act", lambda e: e.activation(wTn[i][:], wv, AF.Identity, scale=-1.0), reads=wb, writes=B("wTn%d" % i))
            vv, vb = PS(bB, 1)
            qsv, qsb = PS(bB, 2)
            if not sample:
                S.op("pe", lambda e: e.matmul(vv, Xbf[i][:], VB[:, h, :], start=True, stop=(not have_state)), reads=B("Xbf%d" % i, "VB"), writes=vb, kind="bf")
                if have_state:
                    S.op("pe", lambda e: e.matmul(vv, wTn[i][:], Sbf[:, h, :], start=False, stop=True), reads=B("wTn%d" % i, "S%d" % h), writes=vb, kind="bf")
                    S.op("pe", lambda e: e.matmul(qsv, QT[:, h, :], Sbf[:, h, :], start=True, stop=True), reads=B("QT%d" % h, "S%d" % h), writes=qsb, kind="bf")
            else:
                t1v, t1b = PS(bA, 0)
                t2v, t2b = PS(bA, 1)
                for q4 in range(4):
                    j = q4 % 2
                    S.dma("sp", S0v[j], sg4[:, q4 * 4:(q4 + 1) * 4, h, :], writes=B(S0n[j]))
                    S.op("act", lambda e, j=j: e.copy(S0bf[:, j * 4:j * 4 + 4, :], S0v[j]), reads=B(S0n[j]), writes=B(*["S0bf_%d" % (j * 4 + k) for k in range(4)]) + B("ckv0"))
                    for k in range(4):
                        s, s8 = q4 * 4 + k, j * 4 + k
                        S.op("pe", lambda e, s=s, s8=s8: e.matmul(t1v[:, s * 8:(s + 1) * 8], S0bf[:, s8, :], wTn[i][:, s * 8:(s + 1) * 8], start=True, stop=True),
                             reads=B("S0bf_%d" % s8, "wTn%d" % i), writes=t1b, kind="bf")
                        S.op("pe", lambda e, s=s, s8=s8: e.matmul(t2v[:, s * 8:(s + 1) * 8], S0bf[:, s8, :], QT[:, h, s * 8:(s + 1) * 8], start=True, stop=True),
                             reads=B("S0bf_%d" % s8, "QT%d" % h), writes=t2b, kind="bf")
                S.op("act", lambda e: e.copy(tsb[0][:], t1v), reads=t1b, writes=B("tsb0"))
                S.op("act", lambda e: e.copy(tsb[1][:], t2v), reads=t2b, writes=B("tsb1"))
                S.op("pe", lambda e: e.matmul(vv, Xbf[i][:], VB[:, h, :], start=True, stop=False), reads=B("Xbf%d" % i, "VB"), writes=vb, kind="bf")
                S.op("pe", lambda e: e.matmul(vv, tsb[0][:], identb[:], start=False, stop=True), reads=B("tsb0", "identb"), writes=vb, kind="bf")
                S.op("pe", lambda e: e.matmul(qsv, tsb[1][:], identb[:], start=True, stop=True), reads=B("tsb1", "identb"), writes=qsb, kind="bf")
            yield
            S.op("act", lambda e: e.copy(vnew[i][:], vv), reads=vb, writes=B("vnew%d" % i))
            av, ab = PS(bA, 3)
            S.op("pe", lambda e: e.matmul(av, AqT[i][:], vnew[i][:], start=True, stop=True), reads=B("AqT%d" % i, "vnew%d" % i), writes=ab, kind="bf")
            if not sample:
                suv, sub = PS(6, i)
                S.op("pe", lambda e: e.matmul(suv, KD[:, h, :], vnew[i][:], start=True, stop=True), reads=B("KD", "vnew%d" % i), writes=sub, kind="bf")
            yield
            ofpv = Dl[i]
            avsv = Du[i]
            if have_state:
                S.op("act", lambda e: e.copy(avsv[:], av), reads=ab, writes=B("Du%d" % i))
                S.op("dve", lambda e: e.scalar_tensor_tensor(ofpv[:], qsv, eG[:, h:h + 1], avsv[:], op0=ALU.mult, op1=ALU.add),
                     reads=qsb + B("cum", "Du%d" % i), writes=B("Dl%d" % i))
            else:
                S.op("act", lambda e: e.copy(ofpv[:], av), reads=ab, writes=B("Dl%d" % i))
            if not sample:
                if have_state:
                    S.op("dve", lambda e: e.scalar_tensor_tensor(Sst[:, h, :], Sst[:, h, :], eGl[:, h:h + 1], suv, op0=ALU.mult, op1=ALU.add),
                         reads=sub + B("cum", "S%d" % h), writes=B("S%d" % h))
                else:
                    S.op("dve", lambda e: e.tensor_copy(Sst[:, h, :], suv), reads=sub, writes=B("S%d" % h))
                S.op("act", lambda e: e.copy(Sbf[:, h, :], Sst[:, h, :]), reads=B("S%d" % h), writes=B("S%d" % h))
                if t == NT - 2:
                    S.dma("sp", sp_o[h * 128:(h + 1) * 128, :], Sst[:, h, :], reads=B("S%d" % h))
            gn = "gatn%d" % h
            S.op("dve", lambda e: e.memset(gat[:, 36 + h:37 + h], 0.0), writes=B(gn))
            S.op("act", lambda e: e.activation(wTn[i][:], ofpv[:], AF.Square, accum_out=gat[:, 36 + h:37 + h]), reads=B("Dl%d" % i, gn), writes=B("wTn%d" % i, gn))
            S.op("act", lambda e: e.activation(gat[:, 42 + h:43 + h], gat[:, 36 + h:37 + h], AF.Ln, bias=epsc[:, 0:1], scale=1.0 / 128), reads=B(gn, "epsc"), writes=B(gn))
            S.op("act", lambda e: e.activation(gat[:, 42 + h:43 + h], gat[:, 42 + h:43 + h], AF.Exp, scale=-0.5), reads=B(gn), writes=B(gn))
            S.op("dve", lambda e: e.scalar_tensor_tensor(onb[i][:], ofpv[:], gat[:, 42 + h:43 + h], ogb[:], op0=ALU.mult, op1=ALU.mult),
                 reads=B("Dl%d" % i, gn, "ogb"), writes=B("onb%d" % i))
            otv, otb = PSB(3, i, 1)
            S.op("pe", lambda e: e.transpose(otv[:, 0:128], onb[i][:], identb[:]), reads=B("onb%d" % i, "identb"), writes=otb, kind="tr")
            yield
            S.op("dve", lambda e: e.tensor_tensor(sgT[:, h, :], otv[:, 0:128], sgT[:, h, :], op=ALU.mult), reads=otb + B(SG), writes=B(SG))
            if sample:
                def ld_batch(q4):
                    S.dma("sp", S0v[q4 % 2], sg4[:, q4 * 4:(q4 + 1) * 4, h, :], writes=B(S0n[q4 % 2]))

                ld_batch(0)
                for q4 in range(4):
                    j = q4 % 2
                    if q4 + 1 < 4:
                        ld_batch(q4 + 1)
                    def mk_mask(k):
                        s = q4 * 4 + k
                        jj = s % 2
                        S.op("dve", lambda e: e.tensor_scalar(kdm[jj][:], KD[:, h, :], smask[:, s:s + 1], None, op0=ALU.mult),
                             reads=B("KD", "smask"), writes=B("kdm%d" % jj))
                        suv, sub = PS((bA, bB)[jj], 0)
                        S.op("pe", lambda e: e.matmul(suv, kdm[jj][:], vnew[i][:], start=True, stop=True), reads=B("kdm%d" % jj, "vnew%d" % i), writes=sub, kind="bf")
                        return suv, sub

                    def fin(k, suv, sub):
                        s = q4 * 4 + k
                        S.op("dve", lambda e: e.scalar_tensor_tensor(Snw4[j][:, k, :], S0v[j][:, k, :], eGls[:, s, h:h + 1], suv, op0=ALU.mult, op1=ALU.add),
                             reads=sub + B("cum", S0n[j]), writes=B("stg%d" % j))

                    pend = mk_mask(0)
                    for k in range(4):
                        nxt = mk_mask(k + 1) if k + 1 < 4 else None
                        fin(k, *pend)
                        pend = nxt
                    S.dma("act", ss4[:, q4 * 4:(q4 + 1) * 4, h, :], Snw4[j], reads=B("stg%d" % j))

        STAG = 0
        slots = [None, None, None]
        steps = [0, 0, 0]
        nxt_h = 0
        tick = 0
        while nxt_h < 6 or any(g is not None for g in slots):
            for i in range(3):
                if slots[i] is None and nxt_h < 6 and tick >= nxt_h * STAG:
                    slots[i] = head_gen(nxt_h, i)
                    nxt_h += 1
                if slots[i] is not None:
                    try:
                        next(slots[i])
                    except StopIteration:
                        slots[i] = None
            tick += 1

    def front1_a(t, pg):
        sample = (t == NT - 1)
        sgT, xqT = sgT2[pg], xqT2[pg]
        SG, XQ = "sgT%d" % pg, "xqT%d" % pg
        norm_transpose(t, NT + t, 8)
        yield
        for c in range(8):
            pv, pbf = proj_chunk(2304 + c * 128)
            S.op("act", lambda e, c=c, pv=pv: e.activation(sgT[:, c, :], pv, AF.Silu), reads=pbf, writes=B(SG))
            yield
        for c in range(2):
            pv, pbf = proj_chunk(2304 + 1024 + c * 128)
            S.op("act", lambda e, c=c, pv=pv: e.copy(xqT[:, c, :], pv), reads=pbf, writes=B(XQ))
        yield
        def l1A(c):
            i = c % 2
            pv, pbf = proj_chunk(768 + c * 128)
            S.op("act", lambda e: e.copy(sil[i][:], pv), reads=pbf, writes=B("sil%d" % i))
            pv2, pbf2 = proj_chunk(1536 + c * 128)
            conv_setup_hist(1, t, c, i, 2)
            cur, taps, last = xp_views(t, i, 2)
            S.op("dve", lambda e: e.tensor_tensor(cur, as3(pv2, t), as3(sil[i][:], t), op=ALU.mult), reads=pbf2 + B("sil%d" % i), writes=B("xp%d" % i))
            if not sample:
                S.op("pool", lambda e: e.tensor_copy(hist2[:, c, :], last), reads=B("xp%d" % i), writes=B("hist2"))
            acc = as3(cacc[i][:], t)
            S.op("act", lambda e: e.activation(acc, taps[2], AF.Identity, scale=cwb[:, c, 2:3]), reads=B("xp%d" % i, "cwb"), writes=B("cacc%d" % i))

        def l1B(c):
            i = c % 2
            cur, taps, last = xp_views(t, i, 2)
            acc = as3(cacc[i][:], t)
            for j in range(2):
                S.op("dve", lambda e, j=j: e.scalar_tensor_tensor(acc, taps[j], cwb[:, c, j:j + 1], acc, op0=ALU.mult, op1=ALU.add),
                     reads=B("xp%d" % i, "cwb", "cacc%d" % i), writes=B("cacc%d" % i))
            pv3, pbf3 = proj_chunk(c * 128)
            S.op("dve", lambda e: e.tensor_tensor(cacc[i][:], pv3, cacc[i][:], op=ALU.mult), reads=pbf3 + B("cacc%d" % i), writes=B("cacc%d" % i))
            S.op("dve", lambda e: e.tensor_tensor(sgT[:, c, :], cacc[i][:], sgT[:, c, :], op=ALU.mult), reads=B("cacc%d" % i, SG), writes=B(SG))

        l1A(0)
        for c in range(6):
            if c + 1 < 6:
                l1A(c + 1)
            l1B(c)
            yield
        if t >= NT - 2:
            for blk in range(2):
                c0 = blk * 384
                pv, pbf = PS(7, 0, 4)
                pw, pwb = PS(6, 0, 4)
                for kc in range(8):
                    S.op("pe", lambda e, kc=kc, pv=pv, c0=c0: e.matmul(pv[:, 0:384], hT[:, kc, :], win[:, kc, 768 + c0:768 + c0 + 384], start=(kc == 0), stop=(kc == 7)),
                         reads=B("hT", "win%d" % kc), writes=pbf)
                for kc in range(8):
                    S.op("pe", lambda e, kc=kc, pw=pw, c0=c0: e.matmul(pw[:, 0:384], hT[:, kc, :], win[:, kc, 1536 + c0:1536 + c0 + 384], start=(kc == 0), stop=(kc == 7)),
                         reads=B("hT", "win%d" % kc), writes=pwb)
                st = stg[blk % 2]
                S.op("act", lambda e, st=st, pv=pv: e.copy(st[:, 0:384], pv[:, 0:384]), reads=pbf, writes=B("stg%d" % (blk % 2)))
                S.op("dve", lambda e, st=st, pw=pw: e.tensor_tensor(st[:, 0:384], pw[:, 0:384], st[:, 0:384], op=ALU.mult), reads=pwb + B("stg%d" % (blk % 2)), writes=B("stg%d" % (blk % 2)))
                if not sample:
                    S.dma("sp", scp_o[0:2, c0:c0 + 384], st[126:128, 0:384], reads=B("stg%d" % (blk % 2)))
                else:
                    for s in range(16):
                        S.dma("sp", scs_o[s * 2:(s + 1) * 2, c0:c0 + 384], st[s * 8 + 6:s * 8 + 8, 0:384], reads=B("stg%d" % (blk % 2)))

    seq = [(0, t) for t in range(NT)] + [(1, t) for t in range(NT)]
    prev = None
    pend_tail = None
    for k, (l, t) in enumerate(seq):
        pg = k % 2
        fa = front0_a(t, pg) if l == 0 else front1_a(t, pg)
        gens = [fa]
        if prev is not None:
            gens.insert(0, att_gen(prev[0], prev[1], prev[2], pre_tail=pend_tail))
            pend_tail = None
        run_interleaved(gens)
        if prev is not None:
            if prev[0] == 1:
                pend_tail = (prev[0], prev[1])
            else:
                att_tail(prev[0], prev[1])
        if l == 1 and t == 0:
            load_wout(1)
            S.dma("sp", gfin, gfin_d[:, :], writes=B("KBG", "KD", "VB"))
        if l == 0:
            front0_b(t, pg)
        prev = (l, t, pg)
    run_interleaved([att_gen(prev[0], prev[1], prev[2], pre_tail=pend_tail)])
    att_tail(prev[0], prev[1])
    S.finish()
    es.close()
    return nc, S


_CACHE = {}
_STAGE = 99
_SUB = 99


def _consts():
    i = np.arange(128)
    same = (i[:, None] // 8) == (i[None, :] // 8)
    ident = np.eye(128)
    ones = np.ones((128, 128))
    U = (i[:, None] <= i[None, :])
    SL = (i[:, None] > i[None, :])
    ML = (i[:, None] > i[None, :])
    MU = (i[:, None] <= i[None, :])
    mats = [ident, ones, U, SL, U & same, SL & same]
    cst = np.stack([m.astype(np.float32) for m in mats], axis=1).reshape(128, 6 * 128)
    smask = ((i[:, None] // 8) == np.arange(16)[None, :]).astype(np.float32)
    return np.ascontiguousarray(cst), np.ascontiguousarray(smask)


def kernel(x_prompt, x_sample, mem_prompt, state_gdn, state_gdn_conv, state_sconv,
           cache_mem_k, cache_mem_v, norm_g, w_in_a, conv_w_a, a_log, dt_bias, o_norm_g,
           w_in_b, conv_w_b, mem_norm_g, w_mem_kv, w_out, final_norm_g):
    f = lambda a: np.ascontiguousarray(np.asarray(a, dtype=np.float32))
    if "nc" not in _CACHE:
        _CACHE["nc"] = build_program(_STAGE)[0]
    nc = _CACHE["nc"]
    cst, smask = _consts()
    x_prompt, x_sample, mem_prompt = f(x_prompt), f(x_sample), f(mem_prompt)
    state_gdn, state_gdn_conv, state_sconv = f(state_gdn), f(state_gdn_conv), f(state_sconv)
    cache_mem_k, cache_mem_v = f(cache_mem_k), f(cache_mem_v)
    gfm = np.concatenate([f(norm_g).reshape(2, 8, 128), f(mem_norm_g).reshape(1, 8, 128)], axis=0)
    gfm = np.ascontiguousarray(gfm.transpose(2, 0, 1).reshape(128, 24))
    cwa = np.ascontiguousarray(f(conv_w_a)[0].reshape(4, 18, 128).transpose(2, 1, 0).reshape(128, 72))
    cwb = np.ascontiguousarray(f(conv_w_b)[0].reshape(3, 6, 128).transpose(2, 1, 0).reshape(128, 18))
    small = np.ascontiguousarray(np.broadcast_to(np.concatenate([f(a_log)[0], f(dt_bias)[0]])[None, :], (128, 12)))
    ogb = np.ascontiguousarray(np.broadcast_to(f(o_norm_g)[0][None, :], (128, 128)))
    gfin = np.ascontiguousarray(np.broadcast_to(f(final_norm_g)[None, :], (128, 1024)))
    shared = {
        "wina": f(w_in_a)[0], "winb": f(w_in_b)[0], "wout": f(w_out).reshape(2048, 1024),
        "wkv": f(w_mem_kv).reshape(2048, 512), "gfm": gfm, "cwa": cwa, "cwb": cwb, "small": small,
        "ogb": ogb, "gfin": gfin, "cst": cst, "smask": smask,
    }
    in_maps = []
    for c in range(8):
        sl = slice(c * 16, (c + 1) * 16)
        m = dict(shared)
        m["x"] = np.ascontiguousarray(np.concatenate([x_prompt[c], x_sample[sl].reshape(128, 1024)], axis=0))
        m["mem"] = mem_prompt[c]
        m["sgdn"] = np.ascontiguousarray(state_gdn[0, sl].reshape(16 * 6 * 128, 128))
        m["sgc"] = np.ascontiguousarray(state_gdn_conv[0, sl].reshape(48, 2304))
        m["ssc"] = np.ascontiguousarray(state_sconv[0, sl].reshape(32, 768))
        m["cmk"] = np.ascontiguousarray(cache_mem_k[:, sl].reshape(2 * 16 * 256, 256))
        m["cmv"] = np.ascontiguousarray(cache_mem_v[:, sl].reshape(2 * 16 * 256, 256))
        in_maps.append(m)
    res = run_bass_kernel_spmd(nc, in_maps, core_ids=list(range(8)))
    R = res.results
    y = np.stack([r["y"] for r in R])
    y_prompt = np.ascontiguousarray(y[:, :2048])
    y_sample = np.ascontiguousarray(y[:, 2048:].reshape(128, 8, 1024))
    S_p = np.stack([r["sp_o"].reshape(6, 128, 128) for r in R])[None]
    gc_p = np.stack([r["gcp_o"] for r in R])[None]
    sc_p = np.stack([r["scp_o"] for r in R])[None]
    mk_p = np.stack([r["mkp_o"].reshape(2, 256, 4, 64) for r in R], axis=1)
    mv_p = np.stack([r["mvp_o"].reshape(2, 256, 4, 64) for r in R], axis=1)
    S_s = np.concatenate([r["ss_o"].reshape(16, 6, 128, 128) for r in R])[None]
    gc_s = np.concatenate([r["gcs_o"].reshape(16, 3, 2304) for r in R])[None]
    sc_s = np.concatenate([r["scs_o"].reshape(16, 2, 768) for r in R])[None]
    outs = (y_prompt, y_sample, S_p, gc_p, sc_p, mk_p, mv_p, S_s, gc_s, sc_s)
    return tuple(np.ascontiguousarray(o, dtype=np.float32) for o in outs)
```
